# Optimizing a Trainium2 kernel written in Bass

```python
import jax
import jax.numpy as jnp
from jax import lax
import numpy as np


D_MODEL = 1024
BATCH = 4
SEQ = 4096
DEPTH = 2

HEAD_DIM = 64
GROUP_HEADS = 4
GROUP_WIDTH = GROUP_HEADS * HEAD_DIM
N_GROUPS = 4
D_MIX = N_GROUPS * GROUP_WIDTH
ROPE_THETA = 500000.0
EPS = 1e-6
Q_BLOCK = 128
N_MEM = 256
XA_HEADS = 4
XA_WIDTH = XA_HEADS * HEAD_DIM
HG_CHUNK = 64
DSA_LATENT = 128
IDX_HEADS = 8
IDX_DIM = 32
DSA_TOPK_MAX = 256
CMP_BLOCK = 32
CMP_STRIDE = 16
CMP_HIDDEN = 256
SLC_BLOCK = 64
SLC_TOPN = 16
WINDOW = 512
ML_CHUNK = 64
CONV_WIDTH = 4
D_FF = -(-8 * D_MODEL // (3 * 256)) * 256

HG_SPLITS = (GROUP_WIDTH,) * 4
DSA_SPLITS = (GROUP_WIDTH, DSA_LATENT, IDX_HEADS * IDX_DIM, IDX_DIM, IDX_HEADS)
NSA_SPLITS = (GROUP_WIDTH,) + (HEAD_DIM,) * 6 + (3 * GROUP_HEADS,)
ML_SPLITS = (2 * GROUP_WIDTH, GROUP_WIDTH, GROUP_WIDTH, GROUP_HEADS, GROUP_HEADS)
GROUP_COLS = (sum(HG_SPLITS), sum(DSA_SPLITS), sum(NSA_SPLITS), sum(ML_SPLITS))
IN_COLS = sum(GROUP_COLS)

kernel_name = "hybrid_parallel_heads_decoder"

F32 = jnp.float32


def split_cols(t, sizes):
    return jnp.split(t, [int(s) for s in np.cumsum(sizes)[:-1]], axis=-1)


def split_heads(t, n):
    return t.reshape(t.shape[:-1] + (n, t.shape[-1] // n))


def rms_norm(t, g):
    t32 = t.astype(F32)
    y = t32 * lax.rsqrt(jnp.mean(t32 * t32, axis=-1, keepdims=True) + EPS)
    return (y * g.astype(F32)).astype(t.dtype)


def partial_rope(t, pos):
    d = t.shape[-1]
    rd = d // 4
    half = rd // 2
    inv = ROPE_THETA ** (-jnp.arange(half, dtype=F32) * 2.0 / rd)
    ang = pos.astype(F32)[:, None] * inv[None, :]
    cos = jnp.cos(ang)[:, None, :].astype(t.dtype)
    sin = jnp.sin(ang)[:, None, :].astype(t.dtype)
    x1, x2 = t[..., :half], t[..., half:rd]
    return jnp.concatenate([x1 * cos - x2 * sin, x2 * cos + x1 * sin, t[..., rd:]], axis=-1)


def masked_softmax(s, mask):
    s = jnp.where(mask, s.astype(F32), -jnp.inf)
    m = jnp.max(s, axis=-1, keepdims=True)
    m = jnp.where(jnp.isfinite(m), m, 0.0)
    e = jnp.exp(s - m)
    den = jnp.sum(e, axis=-1, keepdims=True)
    return e / jnp.where(den > 0, den, 1.0)


def gather_rows(table, idx):
    return jax.vmap(lambda tb, ix: tb[ix])(table, idx)


def causal_conv(t, w, b):
    y = lax.conv_general_dilated(t, w[:, None, :].astype(t.dtype), window_strides=(1,),
                                 padding=[(CONV_WIDTH - 1, 0)],
                                 dimension_numbers=('NWC', 'WIO', 'NWC'),
                                 feature_group_count=t.shape[-1])
    return y + b.astype(t.dtype)


def to_chunks(t, c):
    b_, l_, h_ = t.shape[:3]
    t = t.reshape((b_, l_ // c, c, h_) + t.shape[3:])
    return jnp.moveaxis(t, (1, 3), (0, 2))


def from_chunks(t):
    t = jnp.moveaxis(t, (0, 2), (1, 3))
    return t.reshape((t.shape[0], t.shape[1] * t.shape[2], t.shape[3]) + t.shape[4:])


def unblock(o):
    o = jnp.moveaxis(o, 0, 1)
    return o.reshape((o.shape[0], o.shape[1] * o.shape[2]) + o.shape[3:])


def hgrn2_mixer(cols, lb, o_gain):
    b_, l_ = cols.shape[:2]
    q, f, i, g = split_cols(cols, HG_SPLITS)
    q = split_heads(jax.nn.silu(q), GROUP_HEADS).astype(F32) * HEAD_DIM ** -0.5
    forget = lb + (1.0 - lb) * jax.nn.sigmoid(f.astype(F32))
    k = split_heads(1.0 - forget, GROUP_HEADS)
    logf = split_heads(jnp.log(forget), GROUP_HEADS)
    v = split_heads(i, GROUP_HEADS).astype(F32)
    tri = jnp.tril(jnp.ones((HG_CHUNK, HG_CHUNK), bool))[:, :, None]

    def step(state, inp):
        qc, kc, vc, gc = inp
        bcum = jnp.cumsum(gc, axis=2)
        o_inter = jnp.einsum('bhtk,bhkv->bhtv', qc * jnp.exp(bcum), state)
        diff = bcum[:, :, :, None, :] - bcum[:, :, None, :, :]
        decay = jnp.exp(jnp.where(tri, diff, -jnp.inf))
        attn = jnp.einsum('bhtk,bhsk,bhtsk->bhts', qc, kc, decay)
        out = o_inter + jnp.einsum('bhts,bhsv->bhtv', attn, vc)
        b_last = bcum[:, :, -1:, :]
        state = (jnp.exp(b_last[:, :, 0, :])[..., None] * state
                 + jnp.einsum('bhsk,bhsv->bhkv', kc * jnp.exp(b_last - bcum), vc))
        return state, out

    s0 = jnp.zeros((b_, GROUP_HEADS, HEAD_DIM, HEAD_DIM), F32)
    xs = (to_chunks(q, HG_CHUNK), to_chunks(k, HG_CHUNK), to_chunks(v, HG_CHUNK), to_chunks(logf, HG_CHUNK))
    _, o = lax.scan(step, s0, xs)
    o = rms_norm(from_chunks(o), o_gain).astype(cols.dtype)
    o = o * jax.nn.silu(split_heads(g, GROUP_HEADS))
    return o.reshape(b_, l_, GROUP_WIDTH)


def dsa_mixer(cols, pos, kv_gain, w_uk, w_uv, q_gain, k_gain, idxk_gain):
    b_, l_ = cols.shape[:2]
    q, ckv, iq, ik, iw = split_cols(cols, DSA_SPLITS)
    q = partial_rope(rms_norm(split_heads(q, GROUP_HEADS), q_gain), pos)
    ckv = rms_norm(ckv, kv_gain)
    k = partial_rope(rms_norm(ckv @ w_uk, k_gain)[:, :, None, :], pos)[:, :, 0]
    v = ckv @ w_uv
    iq = partial_rope(split_heads(iq, IDX_HEADS), pos)
    ik = partial_rope(rms_norm(ik, idxk_gain)[:, :, None, :], pos)[:, :, 0]
    iw = iw * (IDX_HEADS ** -0.5 * IDX_DIM ** -0.5)
    topk = min(DSA_TOPK_MAX, l_ // 4)
    key_pos = jnp.arange(l_)

    def block(bi):
        t0 = bi * Q_BLOCK
        qpos = t0 + jnp.arange(Q_BLOCK)
        qb = lax.dynamic_slice_in_dim(q, t0, Q_BLOCK, axis=1)
        iqb = lax.dynamic_slice_in_dim(iq, t0, Q_BLOCK, axis=1)
        iwb = lax.dynamic_slice_in_dim(iw, t0, Q_BLOCK, axis=1)
        rel = jax.nn.relu(jnp.einsum('bthd,bsd->bhts', iqb, ik))
        score = jnp.einsum('bhts,bth->bts', rel, iwb).astype(F32)
        causal = key_pos[None, :] <= qpos[:, None]
        score = jnp.where(causal[None], score, -jnp.inf)
        _, idx = lax.top_k(score, topk)
        k_sel = gather_rows(k, idx)
        v_sel = gather_rows(v, idx)
        s = jnp.einsum('bthd,btkd->bhtk', qb, k_sel) * HEAD_DIM ** -0.5
        p = masked_softmax(s, (idx <= qpos[None, :, None])[:, None]).astype(v.dtype)
        return jnp.einsum('bhtk,btkd->bthd', p, v_sel)

    o = unblock(lax.map(block, jnp.arange(l_ // Q_BLOCK)))
    return o.reshape(b_, l_, GROUP_WIDTH)


def nsa_mixer(cols, pos, pos_k, pos_v, k_w1, k_w2, v_w1, v_w2, q_gain, k_gains):
    b_, l_ = cols.shape[:2]
    scale = HEAD_DIM ** -0.5
    q, kc, vc, ks, vs, kw, vw, gates = split_cols(cols, NSA_SPLITS)
    q = rms_norm(split_heads(q, GROUP_HEADS), q_gain)
    q_rot = partial_rope(q, pos)
    gates = jax.nn.sigmoid(gates.reshape(b_, l_, 3, GROUP_HEADS, 1))

    n_cmp = (l_ - CMP_BLOCK) // CMP_STRIDE + 1
    cmp_idx = np.arange(n_cmp)[:, None] * CMP_STRIDE + np.arange(CMP_BLOCK)[None, :]

    def compress(t, pe, w1, w2):
        blk = (t[:, cmp_idx] + pe).reshape(b_, n_cmp, CMP_BLOCK * HEAD_DIM)
        return jax.nn.relu(blk @ w1) @ w2

    k_cmp = rms_norm(compress(kc, pos_k, k_w1, k_w2), k_gains[0])
    v_cmp = compress(vc, pos_v, v_w1, v_w2)
    cmp_vis = cmp_idx[:, -1][None, :] <= np.arange(l_)[:, None]
    p_cmp = masked_softmax(jnp.einsum('bthd,bjd->bhtj', q, k_cmp) * scale, cmp_vis)
    o_cmp = jnp.einsum('bhtj,bjd->bthd', p_cmp.astype(v_cmp.dtype), v_cmp)

    n_slc = l_ // SLC_BLOCK
    n_sel = min(SLC_TOPN, n_slc)
    st_c = np.arange(n_cmp) * CMP_STRIDE
    st_s = np.arange(n_slc) * SLC_BLOCK
    overlap = ((st_c[:, None] < st_s[None, :] + SLC_BLOCK)
               & (st_c[:, None] + CMP_BLOCK > st_s[None, :])).astype(np.float32)
    imp = jnp.einsum('bhtj,jn->btn', p_cmp, overlap)
    cur = np.arange(l_)[:, None] // SLC_BLOCK
    blk_id = np.arange(n_slc)[None, :]
    forced = (blk_id == 0) | (blk_id == cur) | (blk_id == cur - 1)
    imp = jnp.where(forced, jnp.inf, jnp.where(blk_id > cur, -jnp.inf, imp))
    _, sel = lax.top_k(imp, n_sel)

    k_s = partial_rope(rms_norm(ks, k_gains[1])[:, :, None], pos)[:, :, 0]
    k_blocks = k_s.reshape(b_, n_slc, SLC_BLOCK, HEAD_DIM)
    v_blocks = vs.reshape(b_, n_slc, SLC_BLOCK, HEAD_DIM)
    k_w = partial_rope(rms_norm(kw, k_gains[2])[:, :, None], pos)[:, :, 0]
    k_pad = jnp.pad(k_w, ((0, 0), (WINDOW, 0), (0, 0)))
    v_pad = jnp.pad(vw, ((0, 0), (WINDOW, 0), (0, 0)))

    def block(bi):
        t0 = bi * Q_BLOCK
        qpos = t0 + jnp.arange(Q_BLOCK)
        qb = lax.dynamic_slice_in_dim(q_rot, t0, Q_BLOCK, axis=1)
        sb = lax.dynamic_slice_in_dim(sel, t0, Q_BLOCK, axis=1)
        kb = gather_rows(k_blocks, sb)
        vb = gather_rows(v_blocks, sb)
        kpos = sb[..., None] * SLC_BLOCK + jnp.arange(SLC_BLOCK)
        s = jnp.einsum('bthd,btnsd->bhtns', qb, kb) * scale
        valid = (kpos <= qpos[None, :, None, None])[:, None]
        p = masked_softmax(s.reshape(b_, GROUP_HEADS, Q_BLOCK, n_sel * SLC_BLOCK),
                           valid.reshape(b_, 1, Q_BLOCK, n_sel * SLC_BLOCK))
        p = p.reshape(b_, GROUP_HEADS, Q_BLOCK, n_sel, SLC_BLOCK).astype(vb.dtype)
        o_slc = jnp.einsum('bhtns,btnsd->bthd', p, vb)
        kwb = lax.dynamic_slice_in_dim(k_pad, t0, WINDOW + Q_BLOCK, axis=1)
        vwb = lax.dynamic_slice_in_dim(v_pad, t0, WINDOW + Q_BLOCK, axis=1)
        wpos = t0 - WINDOW + jnp.arange(WINDOW + Q_BLOCK)
        dist = qpos[:, None] - wpos[None, :]
        wvalid = (dist >= 0) & (dist < WINDOW) & (wpos[None, :] >= 0)
        sw = jnp.einsum('bthd,bsd->bhts', qb, kwb) * scale
        pw = masked_softmax(sw, wvalid[None, None]).astype(vwb.dtype)
        o_swa = jnp.einsum('bhts,bsd->bthd', pw, vwb)
        return o_slc, o_swa

    o_slc, o_swa = lax.map(block, jnp.arange(l_ // Q_BLOCK))
    o = gates[:, :, 0] * o_cmp + gates[:, :, 1] * unblock(o_slc) + gates[:, :, 2] * unblock(o_swa)
    return o.reshape(b_, l_, GROUP_WIDTH)


def mlstm_mixer(cols, conv_w, conv_b, i_bias, f_bias, o_gain):
    b_, l_ = cols.shape[:2]
    qk, v, og, ig, fg = split_cols(cols, ML_SPLITS)
    qk = jax.nn.silu(causal_conv(qk, conv_w, conv_b))
    q, k = jnp.split(qk, 2, axis=-1)
    q = split_heads(q, GROUP_HEADS).astype(F32)
    k = split_heads(k, GROUP_HEADS).astype(F32) * HEAD_DIM ** -0.5
    v = split_heads(v, GROUP_HEADS).astype(F32)
    log_i = (ig + i_bias).astype(F32)
    log_f = jax.nn.log_sigmoid((fg + f_bias).astype(F32))
    tri = jnp.tril(jnp.ones((ML_CHUNK, ML_CHUNK), bool))

    def step(carry, inp):
        cmat, nvec, m = carry
        qc, kc, vc, li, lf = inp
        bcum = jnp.cumsum(lf, axis=-1)
        log_d = jnp.where(tri, bcum[..., :, None] - bcum[..., None, :] + li[..., None, :], -jnp.inf)
        inter = bcum + m[..., None]
        m_t = jnp.maximum(inter, jnp.max(log_d, axis=-1))
        d_mat = jnp.exp(log_d - m_t[..., None])
        w_inter = jnp.exp(inter - m_t)
        s = jnp.einsum('bhtd,bhsd->bhts', qc, kc) * d_mat
        num = (w_inter[..., None] * jnp.einsum('bhtd,bhdv->bhtv', qc, cmat)
               + jnp.einsum('bhts,bhsv->bhtv', s, vc))
        den = w_inter * jnp.einsum('bhtd,bhd->bht', qc, nvec) + jnp.sum(s, axis=-1)
        h = num / jnp.maximum(jnp.abs(den), jnp.exp(-m_t))[..., None]
        b_last = bcum[..., -1]
        log_w = b_last[..., None] - bcum + li
        m_new = jnp.maximum(b_last + m, jnp.max(log_w, axis=-1))
        w_s = jnp.exp(log_w - m_new[..., None])
        decay = jnp.exp(b_last + m - m_new)
        cmat = decay[..., None, None] * cmat + jnp.einsum('bhs,bhsd,bhsv->bhdv', w_s, kc, vc)
        nvec = decay[..., None] * nvec + jnp.einsum('bhs,bhsd->bhd', w_s, kc)
        return (cmat, nvec, m_new), h

    init = (jnp.zeros((b_, GROUP_HEADS, HEAD_DIM, HEAD_DIM), F32),
            jnp.zeros((b_, GROUP_HEADS, HEAD_DIM), F32),
            jnp.full((b_, GROUP_HEADS), -1e30, F32))
    xs = (to_chunks(q, ML_CHUNK), to_chunks(k, ML_CHUNK), to_chunks(v, ML_CHUNK),
          to_chunks(log_i, ML_CHUNK), to_chunks(log_f, ML_CHUNK))
    _, h = lax.scan(step, init, xs)
    h = rms_norm(from_chunks(h), o_gain).astype(cols.dtype)
    h = h * jax.nn.sigmoid(split_heads(og, GROUP_HEADS))
    return h.reshape(b_, l_, GROUP_WIDTH)


def cross_attention(h, m, wq, wkv, wo, q_gain, k_gain):
    b_, l_ = h.shape[:2]
    q = rms_norm(split_heads(h @ wq, XA_HEADS), q_gain)
    k, v = jnp.split(m @ wkv, 2, axis=-1)
    k = rms_norm(split_heads(k, XA_HEADS), k_gain)
    v = split_heads(v, XA_HEADS)
    s = jnp.einsum('bthd,bmhd->bhtm', q, k) * HEAD_DIM ** -0.5
    p = jax.nn.softmax(s.astype(F32), axis=-1).astype(v.dtype)
    o = jnp.einsum('bhtm,bmhd->bthd', p, v).reshape(b_, l_, XA_WIDTH)
    return o @ wo


def swiglu(h, w13, w2):
    a, b = jnp.split(h @ w13, 2, axis=-1)
    return (jax.nn.silu(a) * b) @ w2


def setup_inputs(seed: int = 0) -> dict:
    key = jax.random.key(seed)
    keys = iter(jax.random.split(key, 48))

    def nrm(shape, scale):
        return jax.random.normal(next(keys), shape, F32) * scale

    def gain(shape):
        return 1.0 + nrm(shape, 0.02)

    L = DEPTH
    return {
        'x': nrm((BATCH, SEQ, D_MODEL), 1.0),
        'mem': nrm((BATCH, N_MEM, D_MODEL), 1.0),
        'lb_param': nrm((L, GROUP_WIDTH), 0.5),
        'norm_mix': gain((L, D_MODEL)),
        'w_in': nrm((L, D_MODEL, IN_COLS), D_MODEL ** -0.5),
        'w_out': nrm((L, D_MIX, D_MODEL), D_MIX ** -0.5),
        'hg_o_gain': gain((L, HEAD_DIM)),
        'dsa_kv_gain': gain((L, DSA_LATENT)),
        'dsa_w_uk': nrm((L, DSA_LATENT, HEAD_DIM), DSA_LATENT ** -0.5),
        'dsa_w_uv': nrm((L, DSA_LATENT, HEAD_DIM), DSA_LATENT ** -0.5),
        'dsa_q_gain': gain((L, HEAD_DIM)),
        'dsa_k_gain': gain((L, HEAD_DIM)),
        'dsa_idxk_gain': gain((L, IDX_DIM)),
        'nsa_pos_k': nrm((L, CMP_BLOCK, HEAD_DIM), 0.1),
        'nsa_pos_v': nrm((L, CMP_BLOCK, HEAD_DIM), 0.1),
        'nsa_k_w1': nrm((L, CMP_BLOCK * HEAD_DIM, CMP_HIDDEN), (CMP_BLOCK * HEAD_DIM) ** -0.5),
        'nsa_k_w2': nrm((L, CMP_HIDDEN, HEAD_DIM), CMP_HIDDEN ** -0.5),
        'nsa_v_w1': nrm((L, CMP_BLOCK * HEAD_DIM, CMP_HIDDEN), (CMP_BLOCK * HEAD_DIM) ** -0.5),
        'nsa_v_w2': nrm((L, CMP_HIDDEN, HEAD_DIM), CMP_HIDDEN ** -0.5),
        'nsa_q_gain': gain((L, HEAD_DIM)),
        'nsa_k_gains': gain((L, 3, HEAD_DIM)),
        'ml_conv_w': nrm((L, CONV_WIDTH, 2 * GROUP_WIDTH), CONV_WIDTH ** -0.5),
        'ml_conv_b': nrm((L, 2 * GROUP_WIDTH), 0.02),
        'ml_i_bias': nrm((L, GROUP_HEADS), 0.1),
        'ml_f_bias': jnp.linspace(3.0, 6.0, GROUP_HEADS, dtype=F32)[None, :] + nrm((L, GROUP_HEADS), 0.1),
        'ml_o_gain': gain((L, HEAD_DIM)),
        'norm_xa': gain((L, D_MODEL)),
        'norm_mem': gain((L, D_MODEL)),
        'xa_wq': nrm((L, D_MODEL, XA_WIDTH), D_MODEL ** -0.5),
        'xa_wkv': nrm((L, D_MODEL, 2 * XA_WIDTH), D_MODEL ** -0.5),
        'xa_wo': nrm((L, XA_WIDTH, D_MODEL), XA_WIDTH ** -0.5),
        'xa_q_gain': gain((L, HEAD_DIM)),
        'xa_k_gain': gain((L, HEAD_DIM)),
        'norm_ffn': gain((L, D_MODEL)),
        'ffn_w13': nrm((L, D_MODEL, 2 * D_FF), D_MODEL ** -0.5),
        'ffn_w2': nrm((L, D_FF, D_MODEL), D_FF ** -0.5),
    }


def reference(x, mem, lb_param, norm_mix, w_in, w_out, hg_o_gain,
              dsa_kv_gain, dsa_w_uk, dsa_w_uv, dsa_q_gain, dsa_k_gain, dsa_idxk_gain,
              nsa_pos_k, nsa_pos_v, nsa_k_w1, nsa_k_w2, nsa_v_w1, nsa_v_w2, nsa_q_gain, nsa_k_gains,
              ml_conv_w, ml_conv_b, ml_i_bias, ml_f_bias, ml_o_gain,
              norm_xa, norm_mem, xa_wq, xa_wkv, xa_wo, xa_q_gain, xa_k_gain,
              norm_ffn, ffn_w13, ffn_w2):
    pos = jnp.arange(x.shape[1])
    lb_all = jnp.cumsum(jax.nn.softmax(lb_param.astype(F32), axis=0), axis=0)
    lb_all = lb_all - lb_all[:1]
    for l in range(DEPTH):
        h = rms_norm(x, norm_mix[l])
        c_hg, c_dsa, c_nsa, c_ml = split_cols(h @ w_in[l], GROUP_COLS)
        mixed = jnp.concatenate([
            hgrn2_mixer(c_hg, lb_all[l], hg_o_gain[l]),
            dsa_mixer(c_dsa, pos, dsa_kv_gain[l], dsa_w_uk[l], dsa_w_uv[l],
                      dsa_q_gain[l], dsa_k_gain[l], dsa_idxk_gain[l]),
            nsa_mixer(c_nsa, pos, nsa_pos_k[l], nsa_pos_v[l], nsa_k_w1[l], nsa_k_w2[l],
                      nsa_v_w1[l], nsa_v_w2[l], nsa_q_gain[l], nsa_k_gains[l]),
            mlstm_mixer(c_ml, ml_conv_w[l], ml_conv_b[l], ml_i_bias[l], ml_f_bias[l], ml_o_gain[l]),
        ], axis=-1)
        x = x + mixed @ w_out[l]
        x = x + cross_attention(rms_norm(x, norm_xa[l]), rms_norm(mem, norm_mem[l]),
                                xa_wq[l], xa_wkv[l], xa_wo[l], xa_q_gain[l], xa_k_gain[l])
        x = x + swiglu(rms_norm(x, norm_ffn[l]), ffn_w13[l], ffn_w2[l])
    return x
```

```python
import numpy as np
import ml_dtypes
from contextlib import ExitStack
import concourse.bass as bass
import concourse.mybir as mybir
from concourse.bass_utils import run_bass_kernel_spmd

F32 = mybir.dt.float32
BF16 = mybir.dt.bfloat16
ALU = mybir.AluOpType
AF = mybir.ActivationFunctionType
AX = mybir.AxisListType

D = 1024
L = 4096
NT = L // 128
NMEM = 256
DFF = 2816
NFC = DFF // 128
INC = 3388
EPS = 1e-6
NEG = -30000.0
THETA = 500000.0
N_BIS = 14
TOPK = 256


class Buf:
    __slots__ = ("name", "last_w", "readers", "t", "psum")

    def __init__(self, name, t=None, psum=False):
        self.name = name
        self.last_w = None
        self.readers = []
        self.t = t
        self.psum = psum

    def __getitem__(self, k):
        return View([self], self.t[k])

    @property
    def v(self):
        return View([self], self.t[:])


class View:
    __slots__ = ("bufs", "ap")

    def __init__(self, bufs, ap):
        self.bufs = bufs
        self.ap = ap

    def __getitem__(self, k):
        return View(self.bufs, self.ap[k])

    def rr(self, s, **kw):
        return View(self.bufs, self.ap.rearrange(s, **kw))

    def bc(self, shape):
        return View(self.bufs, self.ap.to_broadcast(list(shape)))

    def us(self, ax):
        return View(self.bufs, self.ap.unsqueeze(ax))

    @property
    def shape(self):
        return self.ap.shape


class Op:
    __slots__ = ("idx", "eng", "emit", "deps", "signal", "ticket", "is_dma", "dsem", "dval")


WRITE_KEYS = ("out", "accum_out", "ap")


class Prog:
    ENGS = ("pe", "act", "dve", "pool", "sp")
    NRING = 24

    def __init__(self, nc):
        self.nc = nc
        self.ops = []
        self.ndma = 0
        self.lastdma = {}
        self.lasteng = {}

    def add(self, eng, emit, reads=(), writes=(), dma=False):
        op = Op()
        op.idx = len(self.ops)
        op.eng = eng
        op.emit = emit
        op.is_dma = dma
        op.signal = False
        op.ticket = None
        deps = {}
        for b in reads:
            if b.last_w is not None:
                deps[b.last_w.idx] = (b.last_w, True)
            if b.psum:
                for r in b.readers:
                    if r.eng != eng and r.idx not in deps:
                        deps[r.idx] = (r, False)
        for b in writes:
            if b.last_w is not None and b.last_w.idx not in deps:
                deps[b.last_w.idx] = (b.last_w, False)
            for r in b.readers:
                if r.idx not in deps:
                    deps[r.idx] = (r, False)
        op.deps = list(deps.values())
        for b in reads:
            if not dma:
                b.readers = [r for r in b.readers if r.is_dma or r.eng != eng]
            b.readers.append(op)
        for b in writes:
            b.last_w = op
            b.readers = []
        if dma:
            k = self.ndma
            self.ndma += 1
            op.dsem = k % self.NRING
            op.dval = 16 * (k // self.NRING + 1)
            self.lastdma[op.dsem] = op
        if emit is not None:
            self.lasteng[eng] = op
        self.ops.append(op)
        return op

    def op(self, eng, name, r=(), w=(), **kw):
        reads = list(r)
        writes = list(w)
        args = {}
        for k, v in kw.items():
            if isinstance(v, View):
                if k in WRITE_KEYS:
                    writes.extend(v.bufs)
                else:
                    reads.extend(v.bufs)
                args[k] = v.ap
            else:
                args[k] = v

        if name == "matmul":
            args.setdefault("skip_group_check", True)

        def emit(e, name=name, args=args):
            return getattr(e, name)(**args)

        return self.add(eng, emit, reads, writes, dma=(name == "dma_start"))

    def dma(self, out, in_, eng="sp"):
        return self.op(eng, "dma_start", out=out, in_=in_)

    def barrier(self):
        prev = list(self.lasteng.values()) + list(self.lastdma.values())
        for e in self.ENGS:
            op = self.add(e, None)
            for o in prev:
                if o.eng != e or o.is_dma:
                    op.deps.append((o, True))

    def emit_all(self, sems, ring):
        nc = self.nc
        engobj = {"pe": nc.tensor, "act": nc.scalar, "dve": nc.vector, "pool": nc.gpsimd, "sp": nc.sync}
        for op in self.ops:
            for d, raw in op.deps:
                if d.is_dma:
                    continue
                if d.eng == op.eng and d.eng == "pe":
                    continue
                d.signal = True
        cnt = {e: 0 for e in self.ENGS}
        for op in self.ops:
            if op.signal and not op.is_dma:
                cnt[op.eng] += 1
                op.ticket = cnt[op.eng]
        seen = {e: {} for e in self.ENGS}
        nw = 0
        for op in self.ops:
            e = engobj[op.eng]
            sn = seen[op.eng]
            need = {}
            for d, raw in op.deps:
                if d.is_dma:
                    key = ("r", d.dsem)
                    val = d.dval
                else:
                    if d.eng == op.eng and d.eng == "pe":
                        continue
                    key = ("e", d.eng)
                    val = d.ticket
                if sn.get(key, 0) >= val:
                    continue
                if need.get(key, 0) < val:
                    need[key] = val
            if op.is_dma and op.dval > 16:
                key = ("r", op.dsem)
                val = op.dval - 16
                if sn.get(key, 0) < val and need.get(key, 0) < val:
                    need[key] = val
            for key, val in need.items():
                s = ring[key[1]] if key[0] == "r" else sems[key[1]]
                e.wait_ge(s, val)
                sn[key] = val
                nw += 1
            if op.emit is None:
                continue
            ins = op.emit(e)
            if op.is_dma:
                ins.then_inc(ring[op.dsem], 16)
            elif op.signal:
                ins.then_inc(sems[op.eng], 1)
        return nw, cnt


class DT:
    def __init__(self, name, ap, ntile=1):
        self.name = name
        self.ap = ap
        self.bufs = [Buf("%s_%d" % (name, i)) for i in range(ntile)]

    def tile(self, i, ap):
        return View([self.bufs[i]], ap)

    def all(self, ap=None):
        return View(list(self.bufs), self.ap if ap is None else ap)


def host_consts():
    bf = ml_dtypes.bfloat16
    c = {}
    eye = np.eye(128, dtype=np.float32)
    c["c_identb"] = eye.astype(bf)
    c["c_identf"] = eye
    c["c_ident4"] = np.tile(eye, (1, 4)).astype(bf)
    s = np.arange(128)[:, None]
    t = np.arange(128)[None, :]
    triu = (s <= t).astype(np.float32)
    c["c_triuf"] = triu
    c["c_triub"] = triu.astype(bf)
    tt = np.arange(128)[:, None]
    ss = np.arange(128)[None, :]
    c["c_cb"] = np.where(ss <= tt, 0.0, NEG).astype(np.float32)
    c["c_ab"] = np.where(ss > tt, 0.0, NEG).astype(np.float32)
    pos = np.arange(L, dtype=np.float32)
    for nm, rd in (("c_cs64", 16), ("c_cs32", 8)):
        half = rd // 2
        inv = (np.float32(THETA) ** (-np.arange(half, dtype=np.float32) * np.float32(2.0) / np.float32(rd))).astype(np.float32)
        ang = (pos[:, None] * inv[None, :]).astype(np.float32)
        cs = np.concatenate([np.cos(ang), np.sin(ang)], axis=1).astype(np.float32)
        c[nm] = np.ascontiguousarray(cs.reshape(NT, 128, rd).transpose(1, 0, 2))
    j = np.arange(256)[None, :]
    tq = np.arange(L)[:, None]
    vis = (16 * j + 31 <= tq) & (j < 255)
    c["c_cmpmask"] = np.where(vis, 0.0, NEG).astype(bf)
    n = np.arange(64)[None, :]
    cur = tq // 64
    forced = (n == 0) | (n == cur) | (n == cur - 1)
    fb = np.where(forced, 1.0e6 + 64.0 * n, np.where(n > cur, -1.0e6, 0.0))
    c["c_fb"] = fb.astype(np.float32)
    st_c = np.arange(255) * 16
    st_s = np.arange(64) * 64
    ov = ((st_c[:, None] < st_s[None, :] + 64) & (st_c[:, None] + 32 > st_s[None, :])).astype(np.float32)
    ovp = np.zeros((256, 64), np.float32)
    ovp[:255] = ov
    c["c_overlap"] = ovp.astype(bf)
    c["c_pow2"] = np.tile((1.0078125 * 2.0 ** -np.arange(N_BIS, dtype=np.float32))[None, :], (128, 1)).astype(np.float32)
    return c


PARAM_SHAPES = {
    "lb_param": (2, 256), "norm_mix": (2, 1024), "w_in": (2, 1024, INC), "w_out": (2, 1024, 1024),
    "hg_o_gain": (2, 64), "dsa_kv_gain": (2, 128), "dsa_w_uk": (2, 128, 64), "dsa_w_uv": (2, 128, 64),
    "dsa_q_gain": (2, 64), "dsa_k_gain": (2, 64), "dsa_idxk_gain": (2, 32),
    "nsa_pos_k": (2, 32, 64), "nsa_pos_v": (2, 32, 64), "nsa_k_w1": (2, 2048, 256), "nsa_k_w2": (2, 256, 64),
    "nsa_v_w1": (2, 2048, 256), "nsa_v_w2": (2, 256, 64), "nsa_q_gain": (2, 64), "nsa_k_gains": (2, 3, 64),
    "ml_conv_w": (2, 4, 512), "ml_conv_b": (2, 512), "ml_i_bias": (2, 4), "ml_f_bias": (2, 4), "ml_o_gain": (2, 64),
    "norm_xa": (2, 1024), "norm_mem": (2, 1024), "xa_wq": (2, 1024, 256), "xa_wkv": (2, 1024, 512),
    "xa_wo": (2, 256, 1024), "xa_q_gain": (2, 64), "xa_k_gain": (2, 64), "norm_ffn": (2, 1024),
    "ffn_w13": (2, 1024, 2 * DFF), "ffn_w2": (2, DFF, 1024),
}


HG0, DS0, NS0, ML0 = 0, 1024, 1704, 2356
TM_GROUPS = [(512, 512), (1024, 512), (1536, 168), (1704, 256), (2088, 268), (2868, 512), (3380, 8)]
TM_OFF = [0, 512, 1024, 1192, 1448, 1716, 2228]
TMW = 2236
FM_COLS = [0, 128, 256, 384, 2356, 2484, 2612, 2740, 1960]
C_HGI, C_HGG, C_DQ, C_CKV, C_IQ, C_IK, C_IW = 0, 256, 512, 768, 896, 1152, 1184
C_NQ, C_KS, C_VS, C_KW, C_VW, C_NG = 1192, 1448, 1512, 1576, 1640, 1704
C_MV, C_OG, C_IG, C_FG = 1716, 1972, 2228, 2232
B_HGV, B_DV, B_NVS, B_NVW, TMBW = 0, 256, 321, 386, 452
F_HGG, F_MV, F_OG, F_SGN, F_NG, F_GT, TMFW = 0, 256, 512, 768, 776, 788, 796


class KB:
    def __init__(self, dump=(), layers=(0, 1), phases=None, ntl=NT):
        self.ntl = ntl
        import os
        self.alvl = int(os.environ.get("KDBG_A", "99"))
        self.dump = set(dump)
        self.layers = layers
        self.phases = phases
        nc = bass.Bass("TRN2", target_bir_lowering=False)
        self.nc = nc
        self.P = Prog(nc)
        self.es = None
        self.din = {}

    def sb(self, name, shape, dt):
        t = self.es.enter_context(self.nc.sbuf_tensor(name + "_%d" % self.uid(), list(shape), dt))
        return Buf(name, t)

    def ps(self, name, shape, dt):
        nel = 2048 // (4 if dt == F32 else 2)
        full = self.es.enter_context(self.nc.psum_tensor(name + "_%d" % self.uid(), [128, nel], dt))
        n = 1
        for d in shape[1:]:
            n *= d
        assert n <= nel, (name, shape)
        ap = full[0:shape[0], 0:n]
        if len(shape) == 3:
            ap = ap.rearrange("p (a b) -> p a b", a=shape[1])
        return Buf(name, ap, psum=True)

    def uid(self):
        self._uid = getattr(self, "_uid", 0) + 1
        return self._uid

    def rot(self, name, shape, dt, n, psum=False):
        return [(self.ps if psum else self.sb)("%s%d" % (name, i), shape, dt) for i in range(n)]

    def dram_in(self, name, shape, dt):
        ap = self.nc.dram_tensor(name, list(shape), dt, kind="ExternalInput").ap()
        d = DT(name, ap, 1)
        self.din[name] = d
        return d

    def scr(self, name, shape, dt, ntile=1):
        kind = "ExternalOutput" if name in self.dump else "Internal"
        ap = self.nc.dram_tensor(name, list(shape), dt, kind=kind).ap()
        return DT(name, ap, ntile)

    def op(self, eng, name, **kw):
        return self.P.op(eng, name, **kw)

    def rms_heads(self, X, H, Dh, gain, out, tmp, ssq, np_=128):
        o = self.op
        o("pool", "tensor_tensor", out=tmp, in0=X, in1=X, op=ALU.mult)
        o("dve", "tensor_reduce", out=ssq, in_=tmp, axis=AX.X, op=ALU.add)
        o("act", "activation", out=ssq, in_=ssq, func=AF.Sqrt, scale=1.0 / Dh, bias=self.epsb[0:np_, 0:1])
        o("dve", "reciprocal", out=ssq, in_=ssq)
        o("dve", "tensor_tensor", out=out, in0=X, in1=ssq.us(2).bc([np_, H, Dh]), op=ALU.mult)
        if gain is not None:
            o("pool", "tensor_tensor", out=out, in0=out, in1=gain.us(1).bc([np_, H, Dh]), op=ALU.mult)

    def rope(self, X, H, hf, cs, out, t1, t2):
        o = self.op
        Dh = X.shape[2]
        cos = cs[:, 0:hf].us(1).bc([128, H, hf])
        sin = cs[:, hf:2 * hf].us(1).bc([128, H, hf])
        x1 = X[:, :, 0:hf]
        x2 = X[:, :, hf:2 * hf]
        o("pool", "tensor_copy", out=out[:, :, 2 * hf:Dh], in_=X[:, :, 2 * hf:Dh])
        o("pool", "tensor_tensor", out=t1, in0=x1, in1=cos, op=ALU.mult)
        o("pool", "tensor_tensor", out=t2, in0=x2, in1=sin, op=ALU.mult)
        o("pool", "tensor_tensor", out=out[:, :, 0:hf], in0=t1, in1=t2, op=ALU.subtract)
        o("dve", "tensor_tensor", out=t1, in0=x2, in1=cos, op=ALU.mult)
        o("dve", "tensor_tensor", out=t2, in0=x1, in1=sin, op=ALU.mult)
        o("dve", "tensor_tensor", out=out[:, :, hf:2 * hf], in0=t1, in1=t2, op=ALU.add)

    def build(self):
        nc = self.nc
        P = self.P
        o = self.op
        self.x_in = self.dram_in("x", [L, D], F32)
        self.x_in.bufs = [Buf("xin%d" % i) for i in range(NT)]
        self.mem_in = self.dram_in("mem", [NMEM, D], F32)
        self.prm = {k: self.dram_in(k, list(s), F32).ap for k, s in PARAM_SHAPES.items()}
        hc = host_consts()
        self.cst = {}
        for k, v in hc.items():
            self.cst[k] = self.dram_in(k, list(v.shape), BF16 if v.dtype != np.float32 else F32)
        kind = "ExternalOutput"
        self.xo = DT("out", nc.dram_tensor("out", [L, D], F32, kind=kind).ap(), NT)
        self.fmT = self.scr("fmT", [1152, L], F32, NT)
        self.tmb = self.scr("tmb", [L, TMBW], BF16, NT)
        self.tmf = self.scr("tmf", [L, TMFW], F32, NT)
        self.kT3 = self.scr("kT3", [3, 64, L], BF16, NT)
        self.ikT = self.scr("ikT", [32, L], BF16, NT)
        self.qT3 = self.scr("qT3", [NT, 64, 3, 512], BF16, NT)
        self.iqT = self.scr("iqT", [NT, 32, 1024], BF16, NT)
        self.mixed = self.scr("mixed", [L, D], BF16, NT)

        with ExitStack() as top:
            self.es = top
            sems = {e: top.enter_context(nc.semaphore("s_" + e)) for e in P.ENGS}
            ring = [top.enter_context(nc.semaphore("r%d" % i)) for i in range(P.NRING)]
            self.identb = self.sb("identb", [128, 128], BF16)
            self.identf = self.sb("identf", [128, 128], F32)
            self.ident4 = self.sb("ident4", [128, 512], BF16)
            self.triuf = self.sb("triuf", [128, 128], F32)
            self.triub = self.sb("triub", [128, 128], BF16)
            self.cbf = self.sb("cbf", [128, 128], F32)
            self.cbb = self.sb("cbb", [128, 128], BF16)
            self.abb = self.sb("abb", [128, 128], BF16)
            self.cs64 = self.sb("cs64", [128, NT, 16], F32)
            self.cs32 = self.sb("cs32", [128, NT, 8], F32)
            self.epsb = self.sb("epsb", [128, 1], F32)
            self.onesf = self.sb("onesf", [128, 128], F32)
            tmpf = self.sb("tmpf", [128, 128], F32)
            for nm, dst in (("c_identb", self.identb), ("c_identf", self.identf), ("c_ident4", self.ident4),
                            ("c_triuf", self.triuf), ("c_triub", self.triub), ("c_cb", self.cbf)):
                P.dma(dst.v, self.cst[nm].all())
            P.dma(tmpf.v, self.cst["c_ab"].all())
            P.dma(self.cs64.v, self.cst["c_cs64"].all())
            P.dma(self.cs32.v, self.cst["c_cs32"].all())
            o("dve", "tensor_copy", out=self.cbb.v, in_=self.cbf.v)
            o("dve", "tensor_copy", out=self.abb.v, in_=tmpf.v)
            o("pool", "memset", ap=self.epsb.v, constant=EPS)
            o("pool", "memset", ap=self.onesf.v, constant=1.0)
            P.barrier()
            for l in self.layers:
                xsrc = self.x_in if l == 0 else self.xo
                ph = self.phases
                if ph is None or "A" in ph:
                    self.phase_A(l, xsrc)
                if ph is None or "HG" in ph:
                    self.phase_HG(l)
                if ph is None or "ML" in ph:
                    self.phase_ML(l)
                if ph is None or "DSA" in ph:
                    self.phase_DSA(l)
                if ph is None or "NSA" in ph:
                    self.phase_NSA(l)
                if ph is None or "C" in ph:
                    self.phase_C(l, xsrc)
            P.barrier()
            self.stats = P.emit_all(sems, ring)
        return nc

    def load_weight_bf16(self, dst, src_ap, nk, ncols, gain=None, chunk=512, stage=None):
        P = self.P
        o = self.op
        src = src_ap.rearrange("(c p) n -> p c n", p=128)
        engs = ["dve", "pool", "act"]
        ei = 0
        j = 0
        for c0 in range(0, ncols, chunk):
            w = min(chunk, ncols - c0)
            for k0 in range(0, nk, 4):
                k1 = min(nk, k0 + 4)
                st = stage[j % len(stage)]
                j += 1
                P.dma(st[:, 0:k1 - k0, 0:w], View([], src[:, k0:k1, c0:c0 + w]))
                for k in range(k0, k1):
                    e = engs[ei % 3]
                    ei += 1
                    if gain is None:
                        if e == "act":
                            o(e, "copy", out=dst[:, k, c0:c0 + w], in_=st[:, k - k0, 0:w])
                        else:
                            o(e, "tensor_copy", out=dst[:, k, c0:c0 + w], in_=st[:, k - k0, 0:w])
                    else:
                        if e == "act":
                            o(e, "activation", out=dst[:, k, c0:c0 + w], in_=st[:, k - k0, 0:w], func=AF.Copy,
                              scale=gain[:, k:k + 1])
                        else:
                            o(e, "tensor_scalar", out=dst[:, k, c0:c0 + w], in0=st[:, k - k0, 0:w],
                              scalar1=gain[:, k:k + 1], scalar2=None, op0=ALU.mult)

    def load_gain_cols(self, dst, vec_ap, nk):
        self.P.dma(dst.v, View([], vec_ap.rearrange("(c p) -> p c", p=128)))

    def bcast_row(self, dst_view, vec_ap):
        np_ = dst_view.shape[0]
        self.P.dma(dst_view, View([], vec_ap.partition_broadcast(np_)))

    def col_load(self, dst_view, vec_ap):
        self.P.dma(dst_view, View([], vec_ap.unsqueeze(1)))

    def phase_A(self, l, xsrc):
        P = self.P
        o = self.op
        prm = self.prm
        top = self.es
        with ExitStack() as es:
            self.es = es
            Wb = self.sb("Wb", [128, 8, INC], BF16)
            with ExitStack() as es2:
                self.es = es2
                stage = self.rot("wst", [128, 4, 512], F32, 3)
                self.load_weight_bf16(Wb, prm["w_in"][l], 8, INC, stage=stage)
                P.barrier()
            self.es = es
            gmix = self.sb("gmix", [128, D], F32)
            self.bcast_row(gmix.v, prm["norm_mix"][l])
            g64 = self.sb("g64", [128, 6, 64], F32)
            self.bcast_row(g64[:, 0, :], prm["dsa_q_gain"][l])
            self.bcast_row(g64[:, 1, :], prm["dsa_k_gain"][l])
            self.bcast_row(g64[:, 2, :], prm["nsa_q_gain"][l])
            self.bcast_row(g64[:, 3, :], prm["nsa_k_gains"][l, 1])
            self.bcast_row(g64[:, 4, :], prm["nsa_k_gains"][l, 2])
            gkv = self.sb("gkv", [128, 128], F32)
            self.bcast_row(gkv.v, prm["dsa_kv_gain"][l])
            gik = self.sb("gik", [128, 32], F32)
            self.bcast_row(gik.v, prm["dsa_idxk_gain"][l])
            wst = self.sb("wukv_st", [128, 128], F32)
            wukv = self.sb("wukv", [128, 128], BF16)
            P.dma(wst[:, 0:64], View([], prm["dsa_w_uk"][l]))
            P.dma(wst[:, 64:128], View([], prm["dsa_w_uv"][l]))
            o("dve", "tensor_copy", out=wukv.v, in_=wst.v)

            xt = self.rot("xt", [128, D], F32, 2)
            junk = self.sb("junk", [128, D], F32)
            ssx = self.rot("ssx", [128, 1], F32, 2)
            hb = self.rot("hb", [128, D], BF16, 2)
            hT = self.rot("hT", [128, 8, 128], BF16, 2)
            ct = self.rot("ct", [128, TMW], F32, 2)
            fm = self.rot("fm", [128, 9, 128], F32, 2)
            tmbS = self.rot("tmbS", [128, TMBW], BF16, 2)
            tmfS = self.rot("tmfS", [128, TMFW], F32, 2)
            kS = self.rot("kS", [64, 3, 128], BF16, 2)
            ikS = self.rot("ikS", [32, 128], BF16, 2)
            qS = self.rot("qS", [64, 3, 512], BF16, 2)
            iqS = self.rot("iqS", [32, 1024], BF16, 2)
            wk = self.sb("wk", [128, 8, 64], F32)
            wk2 = self.sb("wk2", [128, 8, 64], F32)
            t1 = self.sb("t1", [128, 8, 8], F32)
            t2 = self.sb("t2", [128, 8, 8], F32)
            ssq = self.sb("ssq", [128, 8], F32)
            qb = self.sb("qb", [128, 12, 64], BF16)
            kb = self.sb("kb", [128, 3, 64], BF16)
            iqb = self.sb("iqb", [128, 8, 32], BF16)
            ikb = self.sb("ikb", [128, 32], BF16)
            ckvb = self.sb("ckvb", [128, 128], BF16)
            ckvT = self.sb("ckvT", [128, 128], BF16)
            kvf = self.sb("kvf", [128, 128], F32)
            iwa = self.sb("iwa", [128, 8], F32)

            psT = self.ps("psT", [128, 8, 128], BF16)
            psA = self.rot("psA", [128, 512], F32, 2, psum=True)
            psB = self.rot("psB", [128, 4, 128], F32, 2, psum=True)
            psX = self.ps("psX", [128, 8, 128], BF16)
            psY = self.ps("psY", [128, 8, 128], BF16)
            psZ = self.ps("psZ", [128, 8, 128], BF16)

            IWS = float(8 ** -0.5 * 32 ** -0.5)

            def sA(i):
                x_t = xt[i % 2]
                hbt = hb[i % 2]
                hTt = hT[i % 2]
                c = ct[i % 2]
                f = fm[i % 2]
                r0 = i * 128
                if self.alvl < 1:
                    return
                P.dma(x_t.v, xsrc.tile(i, xsrc.ap[r0:r0 + 128, :]))
                ss = ssx[i % 2]
                o("act", "activation", out=junk.v, in_=x_t.v, func=AF.Square, accum_out=ss.v)
                o("act", "activation", out=ss.v, in_=ss.v, func=AF.Sqrt, scale=1.0 / D, bias=self.epsb[:, 0:1])
                o("dve", "reciprocal", out=ss.v, in_=ss.v)
                o("dve", "scalar_tensor_tensor", out=hbt.v, in0=x_t.v, scalar=ss[:, 0:1], in1=gmix.v,
                  op0=ALU.mult, op1=ALU.mult)
                for k in range(8):
                    o("pe", "transpose", out=psT[:, k, :], in_=hbt[:, k * 128:(k + 1) * 128], identity=self.identb.v)
                o("act", "copy", out=hTt.v, in_=psT.v)
                for gi, (c0, w) in enumerate(TM_GROUPS):
                    pa = psA[gi % 2]
                    for k in range(8):
                        o("pe", "matmul", out=pa[:, 0:w], lhsT=hTt[:, k, :], rhs=Wb[:, k, c0:c0 + w],
                          start=(k == 0), stop=(k == 7))
                    off = TM_OFF[gi]
                    if gi % 2 == 0:
                        o("dve", "tensor_copy", out=c[:, off:off + w], in_=pa[:, 0:w])
                    else:
                        o("act", "copy", out=c[:, off:off + w], in_=pa[:, 0:w])
                for ci, c0 in enumerate(FM_COLS):
                    pb = psB[(ci // 4) % 2]
                    for k in range(8):
                        o("pe", "matmul", out=pb[:, ci % 4, :], lhsT=Wb[:, k, c0:c0 + 128], rhs=hTt[:, k, :],
                          start=(k == 0 and ci % 4 == 0), stop=(k == 7))
                    if ci in (1,):
                        o("act", "activation", out=f[:, 0:2, :], in_=pb[:, 0:2, :], func=AF.Silu)
                    elif ci in (3,):
                        o("act", "activation", out=f[:, 2:4, :], in_=pb[:, 2:4, :], func=AF.Sigmoid, scale=-1.0)
                    elif ci == 7:
                        o("dve", "tensor_copy", out=f[:, 4:8, :], in_=pb.v)
                    elif ci == 8:
                        o("dve", "tensor_copy", out=f[:, 8, :], in_=pb[:, 0, :])
                P.dma(self.fmT.tile(i, self.fmT.ap.rearrange("(c p) t -> p c t", p=128)[:, :, r0:r0 + 128]), f.v)


            def sB(i):
                r0 = i * 128
                c = ct[i % 2]
                if self.alvl < 2:
                    return
                tb = tmbS[i % 2]
                tf = tmfS[i % 2]
                cs64 = self.cs64[:, i, :]
                cs32 = self.cs32[:, i, :]
                o("pool", "tensor_copy", out=tb[:, B_HGV:B_HGV + 256], in_=c[:, C_HGI:C_HGI + 256])
                o("act", "activation", out=tf[:, F_HGG:F_HGG + 256], in_=c[:, C_HGG:C_HGG + 256], func=AF.Silu)
                o("pool", "tensor_copy", out=tf[:, F_MV:F_MV + 256], in_=c[:, C_MV:C_MV + 256])
                o("act", "activation", out=tf[:, F_OG:F_OG + 256], in_=c[:, C_OG:C_OG + 256], func=AF.Sigmoid)
                o("pool", "tensor_copy", out=tf[:, F_GT:F_GT + 8], in_=c[:, C_IG:C_IG + 8])
                o("act", "activation", out=tf[:, F_NG:F_NG + 12], in_=c[:, C_NG:C_NG + 12], func=AF.Sigmoid)
                W4 = wk[:, 0:4, :]
                self.rms_heads(c[:, C_DQ:C_DQ + 256].rr("p (h d) -> p h d", h=4), 4, 64, g64[:, 0, :], W4,
                               wk2[:, 0:4, :], ssq[:, 0:4])
                self.rope(W4, 4, 8, cs64, qb[:, 0:4, :], t1[:, 0:4, :], t2[:, 0:4, :])
                W4b = wk[:, 4:8, :]
                self.rms_heads(c[:, C_NQ:C_NQ + 256].rr("p (h d) -> p h d", h=4), 4, 64, g64[:, 2, :], W4b,
                               wk2[:, 4:8, :], ssq[:, 4:8])
                o("act", "copy", out=qb[:, 4:8, :], in_=W4b)
                self.rope(W4b, 4, 8, cs64, qb[:, 8:12, :], t1[:, 4:8, :], t2[:, 4:8, :])
                if self.alvl < 3:
                    return
                self.rms_heads(c[:, C_CKV:C_CKV + 128].rr("p (h d) -> p h d", h=1), 1, 128, gkv.v,
                               wk2[:, 0:2, :].rr("p a b -> p (a b)").rr("p (h d) -> p h d", h=1),
                               wk2[:, 2:4, :].rr("p a b -> p (a b)").rr("p (h d) -> p h d", h=1), ssq[:, 0:1])
                o("act", "copy", out=ckvb.v, in_=wk2[:, 0:2, :].rr("p a b -> p (a b)"))
                o("pe", "transpose", out=psT[:, 0, :], in_=ckvb.v, identity=self.identb.v)
                o("act", "copy", out=ckvT.v, in_=psT[:, 0, :])
                pa = psA[1]
                o("pe", "matmul", out=pa[:, 0:128], lhsT=ckvT.v, rhs=wukv.v, start=True, stop=True)
                o("act", "copy", out=kvf.v, in_=pa[:, 0:128])
                o("pool", "tensor_copy", out=tb[:, B_DV:B_DV + 64], in_=kvf[:, 64:128])
                o("pool", "memset", ap=tb[:, B_DV + 64:B_DV + 65], constant=1.0)
                K1 = wk2[:, 4:5, :]
                self.rms_heads(kvf[:, 0:64].rr("p (h d) -> p h d", h=1), 1, 64, g64[:, 1, :], K1, wk2[:, 5:6, :], ssq[:, 1:2])
                self.rope(K1, 1, 8, cs64, kb[:, 0:1, :], t1[:, 0:1, :], t2[:, 0:1, :])
                K2 = wk2[:, 6:7, :]
                self.rms_heads(c[:, C_KS:C_KS + 64].rr("p (h d) -> p h d", h=1), 1, 64, g64[:, 3, :], K2, wk2[:, 7:8, :], ssq[:, 2:3])
                self.rope(K2, 1, 8, cs64, kb[:, 1:2, :], t1[:, 1:2, :], t2[:, 1:2, :])
                K3 = wk2[:, 4:5, :]
                self.rms_heads(c[:, C_KW:C_KW + 64].rr("p (h d) -> p h d", h=1), 1, 64, g64[:, 4, :], K3, wk2[:, 5:6, :], ssq[:, 3:4])
                self.rope(K3, 1, 8, cs64, kb[:, 2:3, :], t1[:, 2:3, :], t2[:, 2:3, :])
                o("pool", "tensor_copy", out=tb[:, B_NVS:B_NVS + 64], in_=c[:, C_VS:C_VS + 64])
                o("pool", "memset", ap=tb[:, B_NVS + 64:B_NVS + 65], constant=1.0)
                o("pool", "tensor_copy", out=tb[:, B_NVW:B_NVW + 64], in_=c[:, C_VW:C_VW + 64])
                o("pool", "memset", ap=tb[:, B_NVW + 64:B_NVW + 66], constant=1.0)
                if self.alvl < 4:
                    return
                o("dve", "tensor_scalar", out=iwa.v, in0=c[:, C_IW:C_IW + 8], scalar1=IWS, scalar2=None, op0=ALU.mult)
                o("dve", "scalar_tensor_tensor", out=iwa.v, in0=iwa.v, scalar=-1.0, in1=iwa.v, op0=ALU.mult, op1=ALU.max)
                o("act", "activation", out=tf[:, F_SGN:F_SGN + 8], in_=c[:, C_IW:C_IW + 8], func=AF.Sign)
                IQ = wk[:, 0:4, :].rr("p a b -> p (a b)").rr("p (h d) -> p h d", h=8)
                IQ2 = wk[:, 4:8, :].rr("p a b -> p (a b)").rr("p (h d) -> p h d", h=8)
                self.rope(c[:, C_IQ:C_IQ + 256].rr("p (h d) -> p h d", h=8), 8, 4, cs32, IQ, t1[:, :, 0:4], t2[:, :, 0:4])
                o("dve", "tensor_tensor", out=iqb.v, in0=IQ, in1=iwa.v.us(2).bc([128, 8, 32]), op=ALU.mult)
                IK = wk2[:, 0:1, 0:32]
                self.rms_heads(c[:, C_IK:C_IK + 32].rr("p (h d) -> p h d", h=1), 1, 32, gik.v, IK, wk2[:, 1:2, 0:32], ssq[:, 4:5])
                self.rope(IK, 1, 4, cs32, ikb.v.rr("p (h d) -> p h d", h=1), t1[:, 0:1, 0:4], t2[:, 0:1, 0:4])
                if self.alvl < 5:
                    return
                import os
                B = int(os.environ.get("KDBG_B", "99"))
                q_s = qS[i % 2]
                k_s = kS[i % 2]
                for h in range(4):
                    o("pe", "transpose", out=psX[0:64, h, :], in_=qb[:, h, :], identity=self.identb.v)
                if B >= 1:
                    for j in range(3):
                        o("pe", "transpose", out=psX[0:64, 4 + j, :], in_=kb[:, j, :], identity=self.identb.v)
                if B >= 2:
                    o("pe", "transpose", out=psX[0:32, 7, :], in_=ikb.v, identity=self.identb.v)
                o("act", "copy", out=q_s[:, 0, :], in_=psX[0:64, 0:4, :].rr("p h t -> p (h t)"))
                if B >= 1:
                    o("dve", "tensor_copy", out=k_s.v, in_=psX[0:64, 4:7, :])
                if B >= 2:
                    o("dve", "tensor_copy", out=ikS[i % 2].v, in_=psX[0:32, 7, :])
                if B >= 3:
                    for h in range(8):
                        o("pe", "transpose", out=psY[0:64, h, :], in_=qb[:, 4 + h, :], identity=self.identb.v)
                    o("act", "copy", out=q_s[:, 1:3, :].rr("p a n -> p (a n)"), in_=psY[0:64, :, :].rr("p h t -> p (h t)"))
                if B >= 4:
                    for h in range(8):
                        o("pe", "transpose", out=psZ[0:32, h, :], in_=iqb[:, h, :], identity=self.identb.v)
                    o("dve", "tensor_copy", out=iqS[i % 2].v, in_=psZ[0:32, :, :].rr("p h t -> p (h t)"))
                if self.alvl < 6:
                    return
                P.dma(self.tmb.tile(i, self.tmb.ap[r0:r0 + 128, :]), tb.v)
                if self.alvl < 7:
                    return
                P.dma(self.tmf.tile(i, self.tmf.ap[r0:r0 + 128, :]), tf.v)
                if self.alvl < 8:
                    return
                P.dma(self.kT3.tile(i, self.kT3.ap.rearrange("k p t -> p k t")[:, :, r0:r0 + 128]), k_s.v)
                if self.alvl < 9:
                    return
                P.dma(self.ikT.tile(i, self.ikT.ap[:, r0:r0 + 128]), ikS[i % 2].v)
                P.dma(self.qT3.tile(i, self.qT3.ap[i]), q_s.v)
                P.dma(self.iqT.tile(i, self.iqT.ap[i]), iqS[i % 2].v)
            sA(0)
            for i in range(NT):
                if i + 1 < NT:
                    sA(i + 1)
                sB(i)
            P.barrier()
        self.es = top

    def phase_C(self, l, xsrc):
        self.phase_C1(l, xsrc)
        self.phase_C2(l)

    def phase_C1(self, l, xsrc):
        P = self.P
        o = self.op
        prm = self.prm
        top = self.es
        with ExitStack() as es:
            self.es = es
            Wo = self.sb("Wo", [128, 8, D], BF16)
            Wq = self.sb("Wq", [128, 8, 256], BF16)
            Wxo = self.sb("Wxo", [128, 2, D], BF16)
            xkT = self.sb("xkT", [64, 4, NMEM], BF16)
            xv = self.sb("xv", [128, 2, 4, 65], BF16)
            gq = self.sb("gq", [128, 64], F32)
            self.bcast_row(gq.v, prm["xa_q_gain"][l])
            gxa = self.sb("gxa", [128, D], F32)
            self.bcast_row(gxa.v, prm["norm_xa"][l])
            with ExitStack() as es2:
                self.es = es2
                stage = self.rot("wst", [128, 4, 512], F32, 3)
                self.load_weight_bf16(Wo, prm["w_out"][l], 8, D, stage=stage)
                self.load_weight_bf16(Wq, prm["xa_wq"][l], 8, 256, stage=stage)
                self.load_weight_bf16(Wxo, prm["xa_wo"][l], 2, D, stage=stage)
                Wkv = self.sb("Wkv", [128, 8, 512], BF16)
                self.load_weight_bf16(Wkv, prm["xa_wkv"][l], 8, 512, stage=stage)
                gmem = self.sb("gmem", [128, D], F32)
                self.bcast_row(gmem.v, prm["norm_mem"][l])
                gk = self.sb("gk", [128, 64], F32)
                self.bcast_row(gk.v, prm["xa_k_gain"][l])
                mt = self.sb("mt", [128, D], F32)
                mj = self.sb("mj", [128, D], BF16)
                mss = self.sb("mss", [128, 1], F32)
                mb = self.sb("mb", [128, D], BF16)
                mT = self.sb("mT", [128, 8, 128], BF16)
                kvf = self.sb("kvf", [128, 512], F32)
                kn = self.sb("kn", [128, 4, 64], F32)
                ktmp = self.sb("ktmp", [128, 4, 64], F32)
                kss = self.sb("kss", [128, 4], F32)
                knb = self.sb("knb", [128, 4, 64], BF16)
                pT = self.ps("pT", [128, 8, 128], BF16)
                pK = self.ps("pK", [128, 512], F32)
                for m in range(2):
                    P.dma(mt.v, self.mem_in.all(self.mem_in.ap[m * 128:(m + 1) * 128, :]))
                    o("act", "activation", out=mj.v, in_=mt.v, func=AF.Square, accum_out=mss.v)
                    o("act", "activation", out=mss.v, in_=mss.v, func=AF.Sqrt, scale=1.0 / D, bias=self.epsb[:, 0:1])
                    o("dve", "reciprocal", out=mss.v, in_=mss.v)
                    o("dve", "scalar_tensor_tensor", out=mb.v, in0=mt.v, scalar=mss[:, 0:1], in1=gmem.v,
                      op0=ALU.mult, op1=ALU.mult)
                    for k in range(8):
                        o("pe", "transpose", out=pT[:, k, :], in_=mb[:, k * 128:(k + 1) * 128], identity=self.identb.v)
                    o("act", "copy", out=mT.v, in_=pT.v)
                    for k in range(8):
                        o("pe", "matmul", out=pK.v, lhsT=mT[:, k, :], rhs=Wkv[:, k, :], start=(k == 0), stop=(k == 7))
                    o("act", "copy", out=kvf.v, in_=pK.v)
                    self.rms_heads(kvf[:, 0:256].rr("p (h d) -> p h d", h=4), 4, 64, gk.v, kn.v, ktmp.v, kss.v)
                    o("act", "copy", out=knb.v, in_=kn.v)
                    for h in range(4):
                        o("pe", "transpose", out=pT[0:64, h, :], in_=knb[:, h, :], identity=self.identb.v)
                    o("act", "copy", out=xkT[:, :, m * 128:(m + 1) * 128], in_=pT[0:64, 0:4, :])
                    o("dve", "tensor_copy", out=xv[:, m, :, 0:64], in_=kvf[:, 256:512].rr("p (h d) -> p h d", h=4))
                    o("pool", "memset", ap=xv[:, m, :, 64:65], constant=1.0)
                P.barrier()
            self.es = es
            xt = self.rot("xt", [128, D], F32, 2)
            mxb = self.rot("mxb", [128, D], BF16, 2)
            mxT = self.sb("mxT", [128, 8, 128], BF16)
            x1r = self.rot("x1", [128, D], F32, 2)
            junk = self.sb("junk", [128, D], BF16)
            ss = self.sb("ss", [128, 1], F32)
            hb = self.sb("hb", [128, D], BF16)
            hT = self.sb("hT", [128, 8, 128], BF16)
            qf = self.sb("qf", [128, 4, 64], F32)
            qn = self.sb("qn", [128, 4, 64], F32)
            qtmp = self.sb("qtmp", [128, 4, 64], F32)
            qss = self.sb("qss", [128, 4], F32)
            qnb = self.sb("qnb", [128, 4, 64], BF16)
            qT4 = self.sb("qT4", [64, 4, 128], BF16)
            PT = self.rot("PT", [128, 512], BF16, 2)
            rden = self.sb("rden", [128, 4], F32)
            ob = self.sb("ob", [128, 4, 64], BF16)
            oT = self.sb("oT", [128, 2, 128], BF16)
            psT = self.ps("psT", [128, 8, 128], BF16)
            psM = self.rot("psM", [128, 512], F32, 2, psum=True)
            psS = self.rot("psS", [128, 512], F32, 2, psum=True)
            psO = self.ps("psO", [128, 4, 65], F32)
            for i in range(self.ntl):
                r0 = i * 128
                x_t = xt[i % 2]
                mx = mxb[i % 2]
                x1 = x1r[i % 2]
                P.dma(x_t.v, xsrc.tile(i, xsrc.ap[r0:r0 + 128, :]))
                P.dma(mx.v, self.mixed.tile(i, self.mixed.ap[r0:r0 + 128, :]))
                for k in range(8):
                    o("pe", "transpose", out=psT[:, k, :], in_=mx[:, k * 128:(k + 1) * 128], identity=self.identb.v)
                o("act", "copy", out=mxT.v, in_=psT.v)
                for g in range(2):
                    pm = psM[g]
                    for k in range(8):
                        o("pe", "matmul", out=pm.v, lhsT=mxT[:, k, :], rhs=Wo[:, k, g * 512:(g + 1) * 512],
                          start=(k == 0), stop=(k == 7))
                    o("dve", "tensor_tensor", out=x1[:, g * 512:(g + 1) * 512], in0=pm.v, in1=x_t[:, g * 512:(g + 1) * 512], op=ALU.add)
                if "xmix" in self.dump:
                    P.dma(self.dbg_xmix.tile(i, self.dbg_xmix.ap[r0:r0 + 128, :]), x1.v)
                o("act", "activation", out=junk.v, in_=x1.v, func=AF.Square, accum_out=ss.v)
                o("act", "activation", out=ss.v, in_=ss.v, func=AF.Sqrt, scale=1.0 / D, bias=self.epsb[:, 0:1])
                o("dve", "reciprocal", out=ss.v, in_=ss.v)
                o("dve", "scalar_tensor_tensor", out=hb.v, in0=x1.v, scalar=ss[:, 0:1], in1=gxa.v, op0=ALU.mult, op1=ALU.mult)
                for k in range(8):
                    o("pe", "transpose", out=psT[:, k, :], in_=hb[:, k * 128:(k + 1) * 128], identity=self.identb.v)
                o("act", "copy", out=hT.v, in_=psT.v)
                pm = psM[0]
                for k in range(8):
                    o("pe", "matmul", out=pm[:, 0:256], lhsT=hT[:, k, :], rhs=Wq[:, k, :], start=(k == 0), stop=(k == 7))
                o("act", "copy", out=qf.v.rr("p h d -> p (h d)"), in_=pm[:, 0:256])
                self.rms_heads(qf.v, 4, 64, gq.v, qn.v, qtmp.v, qss.v)
                o("act", "copy", out=qnb.v, in_=qn.v)
                for h in range(4):
                    o("pe", "transpose", out=psT[0:64, h, :], in_=qnb[:, h, :], identity=self.identb.v)
                o("act", "copy", out=qT4.v, in_=psT[0:64, 0:4, :])
                for m in range(2):
                    pss = psS[m]
                    for h in range(4):
                        o("pe", "matmul", out=pss[:, h * 128:(h + 1) * 128], lhsT=xkT[:, h, m * 128:(m + 1) * 128],
                          rhs=qT4[:, h, :], start=(h == 0), stop=(h == 3))
                    pt = PT[m]
                    o("act", "activation", out=pt.v, in_=pss.v, func=AF.Exp, scale=0.125)
                    for h in range(4):
                        o("pe", "matmul", out=psO[:, h, :], lhsT=pt[:, h * 128:(h + 1) * 128], rhs=xv[:, m, h, :],
                          start=(m == 0 and h == 0), stop=(m == 1))
                o("dve", "reciprocal", out=rden.v, in_=psO[:, :, 64])
                o("dve", "tensor_tensor", out=ob.v, in0=psO[:, :, 0:64], in1=rden.v.us(2).bc([128, 4, 64]), op=ALU.mult)
                for k in range(2):
                    o("pe", "transpose", out=psT[:, k, :], in_=ob.v.rr("p h d -> p (h d)")[:, k * 128:(k + 1) * 128], identity=self.identb.v)
                o("act", "copy", out=oT.v, in_=psT[:, 0:2, :])
                for g in range(2):
                    pm = psM[g]
                    for k in range(2):
                        o("pe", "matmul", out=pm.v, lhsT=oT[:, k, :], rhs=Wxo[:, k, g * 512:(g + 1) * 512],
                          start=(k == 0), stop=(k == 1))
                    o("dve", "tensor_tensor", out=x1[:, g * 512:(g + 1) * 512], in0=pm.v, in1=x1[:, g * 512:(g + 1) * 512], op=ALU.add)
                P.dma(self.xo.tile(i, self.xo.ap[r0:r0 + 128, :]), x1.v)
            P.barrier()
        self.es = top

    def phase_C2(self, l):
        P = self.P
        o = self.op
        prm = self.prm
        top = self.es
        with ExitStack() as es:
            self.es = es
            W13 = self.sb("W13", [128, 8, 2 * DFF], BF16)
            W2 = self.sb("W2", [128, NFC, D], BF16)
            gff = self.sb("gff", [128, D], F32)
            self.bcast_row(gff.v, prm["norm_ffn"][l])
            with ExitStack() as es2:
                self.es = es2
                stage = self.rot("wst", [128, 4, 512], F32, 3)
                self.load_weight_bf16(W13, prm["ffn_w13"][l], 8, 2 * DFF, stage=stage)
                self.load_weight_bf16(W2, prm["ffn_w2"][l], NFC, D, stage=stage)
                P.barrier()
            self.es = es
            xt = self.rot("xt", [128, D], F32, 2)
            junk = self.sb("junk", [128, D], BF16)
            ss = self.sb("ss", [128, 1], F32)
            hb = self.sb("hb", [128, D], BF16)
            hT = self.sb("hT", [128, 8, 128], BF16)
            gT = self.sb("gT", [128, NFC, 128], BF16)
            sa = self.rot("sa", [128, 4, 128], F32, 2)
            psT = self.ps("psT", [128, 8, 128], BF16)
            psM = self.rot("psM", [128, 512], F32, 2, psum=True)
            psF = self.rot("psF", [128, 4, 128], F32, 4, psum=True)
            for i in range(self.ntl):
                r0 = i * 128
                x2 = xt[i % 2]
                P.dma(x2.v, self.xo.tile(i, self.xo.ap[r0:r0 + 128, :]))
                o("act", "activation", out=junk.v, in_=x2.v, func=AF.Square, accum_out=ss.v)
                o("act", "activation", out=ss.v, in_=ss.v, func=AF.Sqrt, scale=1.0 / D, bias=self.epsb[:, 0:1])
                o("dve", "reciprocal", out=ss.v, in_=ss.v)
                o("dve", "scalar_tensor_tensor", out=hb.v, in0=x2.v, scalar=ss[:, 0:1], in1=gff.v, op0=ALU.mult, op1=ALU.mult)
                for k in range(8):
                    o("pe", "transpose", out=psT[:, k, :], in_=hb[:, k * 128:(k + 1) * 128], identity=self.identb.v)
                o("act", "copy", out=hT.v, in_=psT.v)
                for gi, g0 in enumerate(range(0, NFC, 4)):
                    g1 = min(NFC, g0 + 4)
                    n = g1 - g0
                    pfs = (psF[(gi % 2) * 2], psF[(gi % 2) * 2 + 1])
                    for half in range(2):
                        pf = pfs[half]
                        for c in range(g0, g1):
                            col = half * DFF + c * 128
                            for k in range(8):
                                o("pe", "matmul", out=pf[:, c - g0, :], lhsT=W13[:, k, col:col + 128], rhs=hT[:, k, :],
                                  start=(k == 0 and c == g0), stop=(k == 7))
                    s_a = sa[gi % 2]
                    o("act", "activation", out=s_a[:, 0:n, :], in_=pfs[0][:, 0:n, :], func=AF.Silu)
                    o("dve", "tensor_tensor", out=gT[:, g0:g1, :], in0=pfs[1][:, 0:n, :], in1=s_a[:, 0:n, :], op=ALU.mult)
                for g in range(2):
                    pm = psM[g]
                    for c in range(NFC):
                        o("pe", "matmul", out=pm.v, lhsT=gT[:, c, :], rhs=W2[:, c, g * 512:(g + 1) * 512],
                          start=(c == 0), stop=(c == NFC - 1))
                    o("dve", "tensor_tensor", out=x2[:, g * 512:(g + 1) * 512], in0=pm.v, in1=x2[:, g * 512:(g + 1) * 512], op=ALU.add)
                P.dma(self.xo.tile(i, self.xo.ap[r0:r0 + 128, :]), x2.v)
            P.barrier()
        self.es = top

    def phase_HG(self, l):
        P = self.P
        o = self.op
        prm = self.prm
        top = self.es
        CH = 64
        NCH = L // CH
        with ExitStack() as es:
            self.es = es
            oml = self.sb("oml", [128, 2], F32)
            if l == 0:
                o("pool", "memset", ap=oml.v, constant=1.0)
            else:
                lb0 = self.sb("lb0", [128, 2], F32)
                lb1 = self.sb("lb1", [128, 2], F32)
                for hp in range(2):
                    self.col_load(lb0[:, hp:hp + 1], prm["lb_param"][0, hp * 128:(hp + 1) * 128])
                    self.col_load(lb1[:, hp:hp + 1], prm["lb_param"][1, hp * 128:(hp + 1) * 128])
                o("dve", "tensor_tensor", out=lb0.v, in0=lb0.v, in1=lb1.v, op=ALU.subtract)
                o("act", "activation", out=oml.v, in_=lb0.v, func=AF.Sigmoid)
            gO = self.sb("gO", [128, 64], F32)
            self.bcast_row(gO.v, prm["hg_o_gain"][l])
            hm = self.sb("hm", [128, 2], F32)
            blk = self.sb("blk", [128, 128], F32)
            o("pool", "memset", ap=hm.v, constant=0.0)
            o("pool", "memset", ap=hm[0:64, 0:1], constant=1.0)
            o("pool", "memset", ap=hm[64:128, 1:2], constant=1.0)
            o("pool", "memset", ap=blk.v, constant=0.0)
            o("pool", "memset", ap=blk[0:64, 0:64], constant=1.0)
            o("pool", "memset", ap=blk[64:128, 64:128], constant=1.0)
            msk = self.sb("msk", [128, L], F32)
            o("pool", "memset", ap=msk.v, constant=1.0)
            o("pool", "memset", ap=msk.v.rr("p (c t) -> p c t", t=CH)[:, :, 0:1], constant=0.0)
            A1 = self.sb("A1", [128, L], F32)
            A2 = self.sb("A2", [128, L], F32)
            A3 = self.sb("A3", [128, L], F32)
            A4 = self.sb("A4", [128, L], F32)
            A5 = self.sb("A5", [128, L], F32)
            QT = self.sb("QT", [128, L], BF16)
            KTm = [self.sb("KTm%d" % h, [128, L], BF16) for h in range(2)]
            KL = self.sb("KL", [128, L], BF16)
            EL = self.sb("EL", [128, NCH], F32)
            DL = self.sb("DL", [128, NCH], F32)
            EM = self.sb("EM", [128, NCH], F32)
            S = self.sb("S", [128, 128], F32)
            Sb = self.sb("Sb", [128, 128], BF16)
            Vt = self.rot("Vt", [64, 2, 128], BF16, 2)
            Gt = self.rot("Gt", [64, 2, 128], F32, 2)
            KLt = self.rot("KLt", [64, 128], BF16, 2)
            Wm = self.rot("Wm", [64, 2, 64], BF16, 2)
            Oc = self.rot("Oc", [64, 2, 64], F32, 2)
            On = self.sb("On", [64, 2, 64], F32)
            otmp = self.sb("otmp", [64, 2, 64], F32)
            oss = self.sb("oss", [64, 2], F32)
            Ost = self.rot("Ost", [64, 2, 128], BF16, 2)
            psK = self.rot("psK", [64, 128], BF16, 2, psum=True)
            psS = self.rot("psS", [64, 2, 64], F32, 2, psum=True)
            psO = self.rot("psO", [64, 2, 64], F32, 2, psum=True)
            psD = self.rot("psD", [128, 128], F32, 2, psum=True)
            fm = self.fmT
            c3 = lambda b: b.v.rr("p (c t) -> p c t", t=CH)
            for hp in range(2):
                P.dma(A1.v, fm.all(fm.ap[hp * 128:(hp + 1) * 128, :]))
                P.dma(A2.v, fm.all(fm.ap[256 + hp * 128:256 + (hp + 1) * 128, :]))
                o("pool", "tensor_scalar", out=A2.v, in0=A2.v, scalar1=oml[:, hp:hp + 1], scalar2=None, op0=ALU.mult)
                o("act", "activation", out=A3.v, in_=A2.v, func=AF.Ln, scale=-1.0, bias=self.onesf[:, 0:1])
                o("dve", "tensor_tensor_scan", out=A4.v, data0=msk.v, data1=A3.v, initial=0.0, op0=ALU.mult, op1=ALU.add)
                B3 = c3(A4)
                o("dve", "tensor_tensor", out=c3(A3), in0=B3, in1=B3[:, :, 31:32].bc([128, NCH, CH]), op=ALU.subtract)
                o("act", "activation", out=A5.v, in_=A3.v, func=AF.Exp)
                o("dve", "scalar_tensor_tensor", out=QT.v, in0=A1.v, scalar=0.125, in1=A5.v, op0=ALU.mult, op1=ALU.mult)
                o("act", "activation", out=A5.v, in_=A3.v, func=AF.Exp, scale=-1.0)
                o("pool", "tensor_tensor", out=A5.v, in0=A5.v, in1=A2.v, op=ALU.mult)
                o("act", "activation", out=KTm[0].v, in_=A5.v, func=AF.Copy, scale=hm[:, 0:1])
                o("dve", "tensor_scalar", out=KTm[1].v, in0=A5.v, scalar1=hm[:, 1:2], scalar2=None, op0=ALU.mult)
                o("dve", "tensor_tensor", out=EL.v.us(2), in0=B3[:, :, 63:64], in1=B3[:, :, 31:32], op=ALU.subtract)
                o("act", "activation", out=EL.v, in_=EL.v, func=AF.Exp)
                o("act", "activation", out=DL.v.us(2), in_=B3[:, :, 63:64], func=AF.Exp)
                o("act", "activation", out=EM.v.us(2), in_=B3[:, :, 31:32], func=AF.Exp)
                o("pool", "tensor_tensor", out=c3(KL), in0=c3(A5), in1=EL.v.us(2).bc([128, NCH, CH]), op=ALU.mult)
                o("pool", "memset", ap=S.v, constant=0.0)
                o("pool", "memset", ap=Sb.v, constant=0.0)
                for c in range(2 * self.ntl):
                    ti, ci = c // 2, c % 2
                    r0 = ti * 128
                    cs = slice(c * CH, (c + 1) * CH)
                    if ci == 0:
                        P.dma(Vt[ti % 2].v, self.tmb.tile(ti, self.tmb.ap[r0:r0 + 128, B_HGV + hp * 128:B_HGV + (hp + 1) * 128]
                                                         .rearrange("(c s) d -> s c d", s=CH)))
                        P.dma(Gt[ti % 2].v, self.tmf.tile(ti, self.tmf.ap[r0:r0 + 128, F_HGG + hp * 128:F_HGG + (hp + 1) * 128]
                                                         .rearrange("(c s) d -> s c d", s=CH)))
                    V = Vt[ti % 2]
                    G = Gt[ti % 2]
                    pk = psK[c % 2]
                    o("pe", "transpose", out=pk.v, in_=KL[:, cs], identity=self.identb.v)
                    klt = KLt[c % 2]
                    o("act", "copy", out=klt.v, in_=pk.v)
                    pS = psS[c % 2]
                    for h in range(2):
                        hs = slice(h * 64, (h + 1) * 64)
                        o("pe", "matmul", out=pS[:, h, :], lhsT=KTm[h][:, cs], rhs=QT[:, cs], start=(h == 0), stop=(h == 1))
                    wm = Wm[c % 2]
                    o("dve", "tensor_tensor", out=wm.v, in0=pS.v, in1=self.triuf[0:64, 0:64].us(1).bc([64, 2, 64]), op=ALU.mult)
                    pO = psO[c % 2]
                    for h in range(2):
                        hs = slice(h * 64, (h + 1) * 64)
                        o("pe", "matmul", out=pO[:, h, :], lhsT=wm[:, h, :], rhs=V[:, ci, hs], start=(h == 0), stop=False)
                        o("pe", "matmul", out=pO[:, h, :], lhsT=QT[:, cs], rhs=Sb[:, hs], start=False, stop=True)
                    pD = psD[c % 2]
                    o("pe", "matmul", out=pD.v, lhsT=klt.v, rhs=V[:, ci, :], start=True, stop=True)
                    o("dve", "scalar_tensor_tensor", out=S.v, in0=S.v, scalar=DL[:, c:c + 1], in1=pD.v, op0=ALU.mult, op1=ALU.add)
                    if c + 1 < NCH:
                        o("dve", "scalar_tensor_tensor", out=Sb.v, in0=S.v, scalar=EM[:, c + 1:c + 2], in1=blk.v, op0=ALU.mult, op1=ALU.mult)
                    oc = Oc[c % 2]
                    o("act", "copy", out=oc.v, in_=pO.v)
                    self.rms_heads(oc.v, 2, 64, gO[0:64, :], On.v, otmp.v, oss.v, np_=64)
                    ost = Ost[ti % 2]
                    o("dve", "tensor_tensor", out=ost[:, ci, :], in0=On.v.rr("p h d -> p (h d)"), in1=G[:, ci, :], op=ALU.mult)
                    if ci == 1:
                        P.dma(self.mixed.tile(ti, self.mixed.ap[r0:r0 + 128, hp * 128:(hp + 1) * 128]
                                              .rearrange("(c s) d -> s c d", s=CH)), ost.v)
            P.barrier()
        self.es = top

    def phase_ML(self, l):
        P = self.P
        o = self.op
        prm = self.prm
        top = self.es
        with ExitStack() as es:
            self.es = es
            cw = self.sb("cw", [128, 4, 4], F32)
            cbs = self.sb("cbs", [128, 4], F32)
            for ch in range(4):
                for j in range(4):
                    self.col_load(cw[:, ch, j:j + 1], prm["ml_conv_w"][l, j, ch * 128:(ch + 1) * 128])
                self.col_load(cbs[:, ch:ch + 1], prm["ml_conv_b"][l, ch * 128:(ch + 1) * 128])
            ibf = self.sb("ibf", [128, 8], F32)
            self.bcast_row(ibf[:, 0:4], prm["ml_i_bias"][l])
            self.bcast_row(ibf[:, 4:8], prm["ml_f_bias"][l])
            gO = self.sb("gO", [128, 64], F32)
            self.bcast_row(gO.v, prm["ml_o_gain"][l])
            GT = self.sb("GT", [128, NT, 8], F32)
            for q in range(4):
                P.dma(GT[:, q * 8:(q + 1) * 8, :], self.tmf.all(self.tmf.ap[q * 1024:(q + 1) * 1024, F_GT:F_GT + 8]
                                                                .rearrange("(n p) g -> p n g", p=128)))
            LI = self.sb("LI", [128, NT, 4], F32)
            LF = self.sb("LF", [128, NT, 4], F32)
            Fc = self.sb("Fc", [128, NT, 4], F32)
            E = self.sb("E", [128, NT, 4], F32)
            BV = self.sb("BV", [128, NT, 4], F32)
            Gd = self.sb("Gd", [128, NT, 4], F32)
            GdP = self.sb("GdP", [128, NT, 2], F32)
            psG = self.ps("psG", [128, 128], F32)
            o("dve", "tensor_tensor", out=LI.v, in0=GT[:, :, 0:4], in1=ibf[:, 0:4].us(1).bc([128, NT, 4]), op=ALU.add)
            o("dve", "tensor_tensor", out=LF.v, in0=GT[:, :, 4:8], in1=ibf[:, 4:8].us(1).bc([128, NT, 4]), op=ALU.add)
            o("act", "activation", out=LF.v, in_=LF.v, func=AF.Exp, scale=-1.0)
            o("act", "activation", out=LF.v, in_=LF.v, func=AF.Ln, bias=self.onesf[:, 0:1])
            o("dve", "tensor_scalar", out=LF.v, in0=LF.v, scalar1=-1.0, scalar2=None, op0=ALU.mult)
            lf2 = LF.v.rr("p n h -> p (n h)")
            o("pe", "matmul", out=psG.v, lhsT=self.triuf.v, rhs=lf2, start=True, stop=True)
            o("dve", "tensor_copy", out=Fc.v.rr("p n h -> p (n h)"), in_=psG.v)
            o("act", "activation", out=E.v, in_=Fc.v, func=AF.Exp)
            o("dve", "tensor_tensor", out=BV.v, in0=LI.v, in1=Fc.v, op=ALU.subtract)
            o("act", "activation", out=BV.v, in_=BV.v, func=AF.Exp)
            o("pe", "matmul", out=psG.v, lhsT=self.onesf.v, rhs=lf2, start=True, stop=True)
            o("act", "activation", out=Gd.v.rr("p n h -> p (n h)"), in_=psG.v, func=AF.Exp)
            for pr in range(2):
                o("dve", "tensor_copy", out=GdP[0:64, :, pr:pr + 1], in_=Gd[0:64, :, 2 * pr:2 * pr + 1])
                o("dve", "tensor_copy", out=GdP[64:128, :, pr:pr + 1], in_=Gd[64:128, :, 2 * pr + 1:2 * pr + 2])
            QK = [self.sb("QK%d" % ch, [128, L], BF16) for ch in range(4)]
            X = self.rot("X", [128, L], F32, 2)
            Y = self.sb("Y", [128, L], F32)
            fm = self.fmT
            for ch in range(4):
                x = X[ch % 2]
                P.dma(x.v, fm.all(fm.ap[512 + ch * 128:512 + (ch + 1) * 128, :]))
                o("dve", "tensor_scalar", out=Y.v, in0=x.v, scalar1=cw[:, ch, 3:4], scalar2=cbs[:, ch:ch + 1], op0=ALU.mult, op1=ALU.add)
                for sh in (1, 2, 3):
                    o("dve", "scalar_tensor_tensor", out=Y[:, sh:L], in0=x[:, 0:L - sh], scalar=cw[:, ch, 3 - sh:4 - sh],
                      in1=Y[:, sh:L], op0=ALU.mult, op1=ALU.add)
                o("act", "activation", out=QK[ch].v, in_=Y.v, func=AF.Silu)
                if ch >= 2:
                    o("pool", "tensor_scalar", out=QK[ch].v, in0=QK[ch].v, scalar1=0.125, scalar2=None, op0=ALU.mult)
            hm = self.sb("hm", [128, 2], F32)
            blk = self.sb("blk", [128, 130], F32)
            o("pool", "memset", ap=hm.v, constant=0.0)
            o("pool", "memset", ap=hm[0:64, 0:1], constant=1.0)
            o("pool", "memset", ap=hm[64:128, 1:2], constant=1.0)
            o("pool", "memset", ap=blk.v, constant=0.0)
            o("pool", "memset", ap=blk[0:64, 0:65], constant=1.0)
            o("pool", "memset", ap=blk[64:128, 65:130], constant=1.0)
            km = [[self.sb("km%d%d" % (pr, hl), [128, L], BF16) for hl in range(2)] for pr in range(2)]
            for pr in range(2):
                o("dve", "tensor_scalar", out=km[pr][0].v, in0=QK[2 + pr].v, scalar1=hm[:, 0:1], scalar2=None, op0=ALU.mult)
                o("pool", "tensor_scalar", out=km[pr][1].v, in0=QK[2 + pr].v, scalar1=hm[:, 1:2], scalar2=None, op0=ALU.mult)
            C = [self.sb("C%d" % pr, [128, 130], F32) for pr in range(2)]
            Cb = [self.sb("Cb%d" % pr, [128, 130], BF16) for pr in range(2)]
            Tt = self.sb("Tt", [128, 130], F32)
            for pr in range(2):
                o("pool", "memset", ap=C[pr].v, constant=0.0)
                o("pool", "memset", ap=Cb[pr].v, constant=0.0)
            vo = self.rot("vo", [128, 512], F32, 2)
            Vt = self.rot("Vt", [128, 4, 65], BF16, 2)
            kt = self.rot("kt", [128, 128], BF16, 2)
            Wm = self.rot("Wm", [128, 2, 128], BF16, 2)
            dd = self.sb("dd", [128, 4], F32)
            dd2 = self.sb("dd2", [128, 4], F32)
            Ho = self.sb("Ho", [128, 4, 64], F32)
            Hn = self.sb("Hn", [128, 4, 64], F32)
            htmp = self.sb("htmp", [128, 4, 64], F32)
            hss = self.sb("hss", [128, 4], F32)
            Ost = self.rot("Ost", [128, 256], BF16, 2)
            psK = self.rot("psK", [128, 128], BF16, 2, psum=True)
            psS = self.rot("psS", [128, 2, 128], F32, 2, psum=True)
            psO = self.ps("psO", [128, 4, 65], F32)
            psD = self.rot("psD", [128, 130], F32, 2, psum=True)
            for n in range(self.ntl):
                r0 = n * 128
                ts = slice(r0, r0 + 128)
                v_o = vo[n % 2]
                P.dma(v_o.v, self.tmf.tile(n, self.tmf.ap[r0:r0 + 128, F_MV:F_MV + 512]))
                vt = Vt[n % 2]
                o("dve", "tensor_tensor", out=vt[:, :, 0:64], in0=v_o[:, 0:256].rr("p (h d) -> p h d", h=4),
                  in1=BV[:, n, :].us(2).bc([128, 4, 64]), op=ALU.mult)
                o("pool", "tensor_copy", out=vt[:, :, 64:65], in_=BV[:, n, :].us(2))
                for pr in range(2):
                    kq = QK[2 + pr]
                    qq = QK[pr]
                    pk = psK[pr]
                    o("pe", "transpose", out=pk.v, in_=kq[:, ts], identity=self.identb.v)
                    ktt = kt[pr]
                    o("act", "copy", out=ktt.v, in_=pk.v)
                    pS = psS[pr]
                    for hl in range(2):
                        hs = slice(hl * 64, (hl + 1) * 64)
                        o("pe", "matmul", out=pS[:, hl, :], lhsT=km[pr][hl][:, ts], rhs=qq[:, ts], start=(hl == 0), stop=(hl == 1))
                    wm = Wm[pr]
                    o("dve", "tensor_tensor", out=wm.v, in0=pS.v, in1=self.triuf.v.us(1).bc([128, 2, 128]), op=ALU.mult)
                    for hl in range(2):
                        h = 2 * pr + hl
                        hs = slice(hl * 64, (hl + 1) * 64)
                        o("pe", "matmul", out=psO[:, h, :], lhsT=qq[:, ts], rhs=Cb[pr][:, hl * 65:(hl + 1) * 65],
                          start=(h == 0), stop=False)
                        o("pe", "matmul", out=psO[:, h, :], lhsT=wm[:, hl, :], rhs=vt[:, h, :], start=False, stop=True)
                    pD = psD[pr]
                    o("pe", "matmul", out=pD.v, lhsT=ktt.v, rhs=vt[:, 2 * pr:2 * pr + 2, :].rr("p h d -> p (h d)"), start=True, stop=True)
                    o("dve", "tensor_tensor", out=Tt.v, in0=C[pr].v, in1=pD.v, op=ALU.add)
                    o("dve", "tensor_scalar", out=C[pr].v, in0=Tt.v, scalar1=GdP[:, n, pr:pr + 1], scalar2=None, op0=ALU.mult)
                    o("pool", "tensor_tensor", out=Cb[pr].v, in0=C[pr].v, in1=blk.v, op=ALU.mult)
                o("dve", "tensor_tensor", out=dd.v, in0=psO[:, :, 64], in1=E[:, n, :], op=ALU.mult)
                o("dve", "tensor_scalar", out=dd2.v, in0=dd.v, scalar1=1.0, scalar2=None, op0=ALU.max)
                o("dve", "scalar_tensor_tensor", out=dd.v, in0=dd.v, scalar=-1.0, in1=dd2.v, op0=ALU.mult, op1=ALU.max)
                o("dve", "reciprocal", out=dd.v, in_=dd.v)
                o("dve", "tensor_tensor", out=dd.v, in0=dd.v, in1=E[:, n, :], op=ALU.mult)
                o("dve", "tensor_tensor", out=Ho.v, in0=psO[:, :, 0:64], in1=dd.v.us(2).bc([128, 4, 64]), op=ALU.mult)
                self.rms_heads(Ho.v, 4, 64, gO.v, Hn.v, htmp.v, hss.v)
                ost = Ost[n % 2]
                o("dve", "tensor_tensor", out=ost.v, in0=Hn.v.rr("p h d -> p (h d)"), in1=v_o[:, 256:512], op=ALU.mult)
                P.dma(self.mixed.tile(n, self.mixed.ap[r0:r0 + 128, 768:1024]), ost.v)
            P.barrier()
        self.es = top

    def attn_tiles(self, psS_rot, PT_rot, psO, qT, tiles, extra=None):
        o = self.op
        n = len(tiles)

        def pv(idx, PT, vv, ev):
            for h in range(4):
                o("pe", "matmul", out=psO[:, h, :], lhsT=PT[:, h * 128:(h + 1) * 128], rhs=vv,
                  start=(idx == 0 and h == 0), stop=(idx == n - 1))
            if extra is not None:
                for h in range(4):
                    o("pe", "matmul", out=extra[:, h, :], lhsT=PT[:, h * 128:(h + 1) * 128], rhs=ev,
                      start=(idx == 0 and h == 0), stop=(idx == n - 1))

        pend = None
        for idx, (kv, mv, vv, ev) in enumerate(tiles):
            c = self._actr = getattr(self, "_actr", 0) + 1
            pS = psS_rot[c % len(psS_rot)]
            PT = PT_rot[c % len(PT_rot)]
            o("pe", "matmul", out=pS.v, lhsT=kv, rhs=qT, start=True, stop=(mv is None))
            if mv is not None:
                o("pe", "matmul", out=pS.v, lhsT=mv, rhs=self.ident4.v, start=False, stop=True)
            o("act", "activation", out=PT.v, in_=pS.v, func=AF.Exp, scale=0.125)
            if pend is not None:
                pv(*pend)
            pend = (idx, PT, vv, ev)
        pv(*pend)

    def phase_DSA(self, l):
        P = self.P
        o = self.op
        top = self.es
        with ExitStack() as es:
            self.es = es
            kT = self.sb("kT", [64, L], BF16)
            ikT = self.sb("ikT", [32, L], BF16)
            vA = self.sb("vA", [128, NT, 65], BF16)
            pw = self.sb("pw", [128, N_BIS], F32)
            P.dma(kT.v, self.kT3.all(self.kT3.ap[0]))
            P.dma(ikT.v, self.ikT.all())
            for q in range(4):
                P.dma(vA[:, q * 8:(q + 1) * 8, :], self.tmb.all(self.tmb.ap[q * 1024:(q + 1) * 1024, B_DV:B_DV + 65]
                                                                .rearrange("(n p) d -> p n d", p=128)))
            P.dma(pw.v, self.cst["c_pow2"].all())
            qT4 = self.rot("qT4", [64, 512], BF16, 2)
            iq = self.rot("iq", [32, 1024], BF16, 2)
            sg = self.rot("sg", [128, 8], F32, 2)
            Dg = self.rot("Dg", [128, 8, 128], BF16, 2)
            score = self.rot("score", [128, L], F32, 2)
            junk = self.sb("junk", [128, L], BF16)
            mb = self.rot("mb", [128, L], BF16, 2)
            R = self.rot("R", [128, 512], BF16, 3)
            PT = self.rot("PT", [128, 512], BF16, 3)
            am = self.rot("am", [128, 1], F32, 2)
            Wt = self.rot("Wt", [128, N_BIS], F32, 2)
            W2t = self.rot("W2t", [128, N_BIS], F32, 2)
            mid = self.rot("mid", [128, 1], F32, 2)
            cnt = self.sb("cnt", [128, 1], F32)
            st = self.sb("st", [128, 1], F32)
            rden = self.rot("rden", [128, 4], F32, 2)
            ob = self.rot("ob", [128, 4, 64], BF16, 2)
            psY = self.rot("psY", [128, 512], F32, 2, psum=True)
            psC = self.rot("psC", [128, 512], F32, 2, psum=True)
            psS = self.rot("psS", [128, 512], F32, 2, psum=True)
            psO = self.rot("psO", [128, 4, 65], F32, 2, psum=True)
            self._yc = 0

            def st_score(i):
                r0 = i * 128
                S = (i + 1) * 128
                q4 = qT4[i % 2]
                iqt = iq[i % 2]
                sgt = sg[i % 2]
                dg = Dg[i % 2]
                sc = score[i % 2]
                P.dma(q4.v, self.qT3.tile(i, self.qT3.ap[i, :, 0, :]))
                P.dma(iqt.v, self.iqT.tile(i, self.iqT.ap[i]))
                P.dma(sgt.v, self.tmf.tile(i, self.tmf.ap[r0:r0 + 128, F_SGN:F_SGN + 8]))
                o("pool", "tensor_tensor", out=dg.v, in0=self.identb.v.us(1).bc([128, 8, 128]),
                  in1=sgt.v.us(2).bc([128, 8, 128]), op=ALU.mult)
                for j in range(0, S, 512):
                    w = min(512, S - j)
                    pc = psC[(j // 512) % 2]
                    pend = None
                    for h in range(8):
                        yc = self._yc
                        self._yc += 1
                        py = psY[yc % 2]
                        r = R[yc % 3]
                        o("pe", "matmul", out=py[:, 0:w], lhsT=iqt[:, h * 128:(h + 1) * 128], rhs=ikT[:, j:j + w], start=True, stop=True)
                        o("act", "activation", out=r[:, 0:w], in_=py[:, 0:w], func=AF.Relu)
                        if pend is not None:
                            ph, pr = pend
                            o("pe", "matmul", out=pc[:, 0:w], lhsT=dg[:, ph, :], rhs=pr[:, 0:w], start=(ph == 0), stop=False)
                        pend = (h, r)
                    ph, pr = pend
                    o("pe", "matmul", out=pc[:, 0:w], lhsT=dg[:, ph, :], rhs=pr[:, 0:w], start=False, stop=True)
                    o("act", "copy", out=sc[:, j:j + w], in_=pc[:, 0:w])

            def st_bisect(i):
                r0 = i * 128
                S = (i + 1) * 128
                sc = score[i % 2]
                mbt = mb[i % 2]
                a = am[i % 2]
                o("dve", "tensor_reduce", out=a.v, in_=sc[:, 0:S], axis=AX.X, op=ALU.max, apply_absolute_value=True)
                o("dve", "tensor_tensor", out=sc[:, r0:r0 + 128], in0=sc[:, r0:r0 + 128], in1=self.cbf.v, op=ALU.add)
                wt = Wt[i % 2]
                w2 = W2t[i % 2]
                o("dve", "tensor_scalar", out=wt.v, in0=pw.v, scalar1=a[:, 0:1], scalar2=None, op0=ALU.mult)
                o("dve", "tensor_scalar", out=w2.v, in0=wt.v, scalar1=2.0, scalar2=None, op0=ALU.mult)
                md = mid[i % 2]
                o("dve", "memset", ap=md.v, constant=0.0)
                for k in range(N_BIS - 1):
                    o("dve", "tensor_scalar", out=junk[:, 0:S], in0=sc[:, 0:S], scalar1=md[:, 0:1], scalar2=None,
                      op0=ALU.is_ge, op1=ALU.add, accum_out=cnt.v)
                    o("dve", "scalar_tensor_tensor", out=st.v, in0=cnt.v, scalar=float(TOPK), in1=w2[:, k + 1:k + 2],
                      op0=ALU.is_ge, op1=ALU.mult)
                    o("dve", "scalar_tensor_tensor", out=md.v, in0=st.v, scalar=wt[:, k + 1:k + 2], in1=md.v,
                      op0=ALU.subtract, op1=ALU.add)
                o("dve", "tensor_tensor", out=md.v, in0=md.v, in1=wt[:, N_BIS - 1:N_BIS], op=ALU.subtract)
                o("dve", "tensor_scalar", out=mbt[:, 0:S], in0=sc[:, 0:S], scalar1=md[:, 0:1], scalar2=NEG,
                  op0=ALU.is_lt, op1=ALU.mult)

            def st_attn(i):
                r0 = i * 128
                q4 = qT4[i % 2]
                mbt = mb[i % 2]
                pO = psO[i % 2]
                tiles = [(kT[:, kt * 128:(kt + 1) * 128], mbt[:, kt * 128:(kt + 1) * 128], vA[:, kt, :], None) for kt in range(i + 1)]
                self.attn_tiles(psS, PT, pO, q4.v, tiles)
                rd = rden[i % 2]
                o("dve", "reciprocal", out=rd.v, in_=pO[:, :, 64])
                obt = ob[i % 2]
                o("dve", "tensor_tensor", out=obt.v, in0=pO[:, :, 0:64], in1=rd.v.us(2).bc([128, 4, 64]), op=ALU.mult)
                P.dma(self.mixed.tile(i, self.mixed.ap[r0:r0 + 128, 256:512]), obt.v.rr("p h d -> p (h d)"))

            n_t = self.ntl
            st_score(0)
            for i in range(n_t):
                if i + 1 < n_t:
                    st_score(i + 1)
                st_bisect(i)
                st_attn(i)
            P.barrier()
        self.es = top

    def phase_NSA(self, l):
        P = self.P
        o = self.op
        prm = self.prm
        top = self.es
        with ExitStack() as es:
            self.es = es
            kcmpT = self.sb("kcmpT", [64, 256], BF16)
            vcmp = self.sb("vcmp", [128, 2, 65], BF16)
            ovl = self.sb("ovl", [128, 2, 64], BF16)
            P.dma(ovl.v, self.cst["c_overlap"].all(self.cst["c_overlap"].ap.rearrange("(j p) n -> p j n", p=128)))
            with ExitStack() as es2:
                self.es = es2
                kvc = self.sb("kvc", [128, L], F32)
                P.dma(kvc.v, self.fmT.all(self.fmT.ap[1024:1152, :]))
                pe_t = self.sb("pe_t", [32, 128], F32)
                P.dma(pe_t[:, 0:64], View([], prm["nsa_pos_k"][l]))
                P.dma(pe_t[:, 64:128], View([], prm["nsa_pos_v"][l]))
                psP = self.ps("psP", [128, 32], F32)
                o("pe", "transpose", out=psP.v, in_=pe_t.v, identity=self.identf[0:32, 0:32])
                peT = self.sb("peT", [128, 32], F32)
                o("dve", "tensor_copy", out=peT.v, in_=psP.v)
                Ab = self.sb("Ab", [128, L], BF16)
                Bb = self.sb("Bb", [128, L], BF16)
                k3 = kvc.v.rr("p (b r) -> p b r", r=16)
                o("dve", "tensor_tensor", out=Ab.v.rr("p (b r) -> p b r", r=16), in0=k3, in1=peT[:, 0:16].us(1).bc([128, 256, 16]), op=ALU.add)
                o("pool", "tensor_tensor", out=Bb.v.rr("p (b r) -> p b r", r=16), in0=k3, in1=peT[:, 16:32].us(1).bc([128, 256, 16]), op=ALU.add)
                W1 = self.sb("W1", [128, 32, 256], BF16)
                stg = self.rot("w1st", [128, 8, 256], F32, 2)
                sc = 0
                for pp in range(0, 32, 8):
                    sgt = stg[sc % 2]
                    sc += 1
                    P.dma(sgt[0:64], View([], prm["nsa_k_w1"][l].rearrange("(p d) h -> d p h", d=64)[:, pp:pp + 8, :]))
                    P.dma(sgt[64:128], View([], prm["nsa_v_w1"][l].rearrange("(p d) h -> d p h", d=64)[:, pp:pp + 8, :]))
                    o("dve" if (pp // 8) % 2 == 0 else "pool", "tensor_copy", out=W1[:, pp:pp + 8, :], in_=sgt.v)
                w2s = self.sb("w2s", [128, 2, 128], F32)
                P.dma(w2s[:, :, 0:64], View([], prm["nsa_k_w2"][l].rearrange("(c p) d -> p c d", p=128)))
                P.dma(w2s[:, :, 64:128], View([], prm["nsa_v_w2"][l].rearrange("(c p) d -> p c d", p=128)))
                W2c = self.sb("W2c", [128, 2, 128], BF16)
                o("dve", "tensor_copy", out=W2c.v, in_=w2s.v)
                gk0 = self.sb("gk0", [128, 64], F32)
                self.bcast_row(gk0.v, prm["nsa_k_gains"][l, 0])
                hT = [self.sb("hT%d" % s, [128, 2, 256], BF16) for s in range(2)]
                psH = self.rot("psH", [128, 256], F32, 2, psum=True)
                for s in range(2):
                    sp = slice(s * 64, (s + 1) * 64)
                    o("pool", "memset", ap=hT[s].v, constant=0.0)
                    for hc in range(2):
                        ph = psH[hc]
                        for p in range(32):
                            src = Ab if p < 16 else Bb
                            o("pe", "matmul", out=ph[:, 0:255], lhsT=W1[sp, p, hc * 128:(hc + 1) * 128],
                              rhs=src[sp, p:p + 16 * 254 + 1:16], start=(p == 0), stop=(p == 31))
                        o("act", "activation", out=hT[s][:, hc, 0:255], in_=ph[:, 0:255], func=AF.Relu)
                psC = self.rot("psC", [128, 64], F32, 2, psum=True)
                psT = self.ps("psT", [64, 128], BF16)
                kcf = self.sb("kcf", [128, 1, 64], F32)
                kcn = self.sb("kcn", [128, 1, 64], F32)
                kct = self.sb("kct", [128, 1, 64], F32)
                kcs = self.sb("kcs", [128, 1], F32)
                kcb = self.sb("kcb", [128, 64], BF16)
                for jt in range(2):
                    for s in range(2):
                        pc = psC[s]
                        for hc in range(2):
                            o("pe", "matmul", out=pc.v, lhsT=hT[s][:, hc, jt * 128:(jt + 1) * 128], rhs=W2c[:, hc, s * 64:(s + 1) * 64],
                              start=(hc == 0), stop=(hc == 1))
                    o("act", "copy", out=kcf[:, 0, :], in_=psC[0].v)
                    self.rms_heads(kcf.v, 1, 64, gk0.v, kcn.v, kct.v, kcs.v)
                    o("act", "copy", out=kcb.v, in_=kcn[:, 0, :])
                    o("pe", "transpose", out=psT.v, in_=kcb.v, identity=self.identb.v)
                    o("act", "copy", out=kcmpT[:, jt * 128:(jt + 1) * 128], in_=psT.v)
                    o("dve", "tensor_copy", out=vcmp[:, jt, 0:64], in_=psC[1].v)
                    o("pool", "memset", ap=vcmp[:, jt, 64:65], constant=1.0)
                P.barrier()
            self.es = es
            ksT = self.sb("ksT", [64, L], BF16)
            kwT = self.sb("kwT", [64, L], BF16)
            vS = self.sb("vS", [128, NT, 65], BF16)
            vW = self.sb("vW", [128, NT, 65], BF16)
            P.dma(ksT.v, self.kT3.all(self.kT3.ap[1]))
            P.dma(kwT.v, self.kT3.all(self.kT3.ap[2]))
            for q in range(4):
                P.dma(vS[:, q * 8:(q + 1) * 8, :], self.tmb.all(self.tmb.ap[q * 1024:(q + 1) * 1024, B_NVS:B_NVS + 65]
                                                                .rearrange("(n p) d -> p n d", p=128)))
                P.dma(vW[:, q * 8:(q + 1) * 8, :], self.tmb.all(self.tmb.ap[q * 1024:(q + 1) * 1024, B_NVW:B_NVW + 65]
                                                                .rearrange("(n p) d -> p n d", p=128)))
            q2 = self.rot("q2", [64, 2, 512], BF16, 2)
            gt = self.rot("gt", [128, 12], F32, 2)
            cm = self.rot("cm", [128, 256], BF16, 2)
            fbt = self.rot("fbt", [128, 64], F32, 2)
            mb = self.rot("mb", [128, L], BF16, 2)
            PT = self.rot("PT", [128, 512], BF16, 3)
            rc = self.rot("rc", [128, 12], F32, 2)
            tmp2 = self.sb("tmp2", [128, 4, 64], F32)
            impw = self.sb("impw", [128, 4, 64], F32)
            impa = self.sb("impa", [128, 64], F32)
            zz = self.sb("zz", [128, 64], F32)
            m8 = self.sb("m8", [128, 16], F32)
            mbb = self.sb("mbb", [128, 64], BF16)
            acc = self.sb("acc", [128, 4, 64], F32)
            tmp = self.sb("tmp", [128, 4, 64], F32)
            ob = self.rot("ob", [128, 4, 64], BF16, 2)
            psS = self.rot("psS", [128, 512], F32, 2, psum=True)
            psOc = self.rot("psOc", [128, 4, 65], F32, 2, psum=True)
            psI = self.ps("psI", [128, 4, 64], F32)
            psOs = self.ps("psOs", [128, 4, 65], F32)
            psOw = self.ps("psOw", [128, 4, 65], F32)
            def st_cmp(i):
                r0 = i * 128
                S = (i + 1) * 128
                qq = q2[i % 2]
                g = gt[i % 2]
                cmt = cm[i % 2]
                fb = fbt[i % 2]
                mbt = mb[i % 2]
                pOc = psOc[i % 2]
                rcc = rc[i % 2]
                P.dma(qq.v, self.qT3.tile(i, self.qT3.ap[i, :, 1:3, :]))
                P.dma(g.v, self.tmf.tile(i, self.tmf.ap[r0:r0 + 128, F_NG:F_NG + 12]))
                P.dma(cmt.v, self.cst["c_cmpmask"].all(self.cst["c_cmpmask"].ap[r0:r0 + 128, :]))
                P.dma(fb.v, self.cst["c_fb"].all(self.cst["c_fb"].ap[r0:r0 + 128, :]))
                tiles = [(kcmpT[:, jt * 128:(jt + 1) * 128], cmt[:, jt * 128:(jt + 1) * 128], vcmp[:, jt, :], ovl[:, jt, :]) for jt in range(2)]
                self.attn_tiles(psS, PT, pOc, qq[:, 0, :], tiles, extra=psI)
                o("dve", "tensor_scalar", out=rcc[:, 0:4], in0=pOc[:, :, 64], scalar1=1e-30, scalar2=None, op0=ALU.max)
                o("dve", "reciprocal", out=rcc[:, 0:4], in_=rcc[:, 0:4])
                o("dve", "tensor_tensor", out=impw.v, in0=psI.v, in1=rcc[:, 0:4].us(2).bc([128, 4, 64]), op=ALU.mult)
                o("dve", "tensor_reduce", out=impa.v, in_=impw.v.rr("p h n -> p n h"), axis=AX.X, op=ALU.add)
                o("dve", "tensor_tensor", out=impa.v, in0=impa.v, in1=fb.v, op=ALU.add)
                o("dve", "max", out=m8[:, 0:8], in_=impa.v)
                o("dve", "match_replace", out=zz.v, in_to_replace=m8[:, 0:8], in_values=impa.v, imm_value=-3.0e6)
                o("dve", "max", out=m8[:, 8:16], in_=zz.v)
                o("dve", "tensor_scalar", out=mbb.v, in0=impa.v, scalar1=m8[:, 15:16], scalar2=NEG, op0=ALU.is_lt, op1=ALU.mult)
                nb = S // 64
                o("dve", "tensor_copy", out=mbt[:, 0:S].rr("p (b r) -> p b r", r=64), in_=mbb[:, 0:nb].us(2).bc([128, nb, 64]))
                o("dve", "tensor_tensor", out=mbt[:, r0:r0 + 128], in0=mbt[:, r0:r0 + 128], in1=self.cbb.v, op=ALU.add)

            def st_swa(i):
                qq = q2[i % 2]
                tiles = []
                for kt in range(max(0, i - 4), i + 1):
                    mv = self.cbb.v if kt == i else (self.abb.v if kt == i - 4 else None)
                    tiles.append((kwT[:, kt * 128:(kt + 1) * 128], mv, vW[:, kt, :], None))
                self.attn_tiles(psS, PT, psOw, qq[:, 1, :], tiles)

            def st_sel(i):
                qq = q2[i % 2]
                mbt = mb[i % 2]
                tiles = [(ksT[:, kt * 128:(kt + 1) * 128], mbt[:, kt * 128:(kt + 1) * 128], vS[:, kt, :], None) for kt in range(i + 1)]
                self.attn_tiles(psS, PT, psOs, qq[:, 1, :], tiles)

            def st_comb(i):
                r0 = i * 128
                g = gt[i % 2]
                pOc = psOc[i % 2]
                rcc = rc[i % 2]
                o("dve", "reciprocal", out=rcc[:, 4:8], in_=psOs[:, :, 64])
                o("dve", "reciprocal", out=rcc[:, 8:12], in_=psOw[:, :, 64])
                o("dve", "tensor_tensor", out=rcc.v, in0=rcc.v, in1=g.v, op=ALU.mult)
                o("dve", "tensor_tensor", out=acc.v, in0=pOc[:, :, 0:64], in1=rcc[:, 0:4].us(2).bc([128, 4, 64]), op=ALU.mult)
                o("dve", "tensor_tensor", out=tmp.v, in0=psOs[:, :, 0:64], in1=rcc[:, 4:8].us(2).bc([128, 4, 64]), op=ALU.mult)
                o("pool", "tensor_tensor", out=acc.v, in0=acc.v, in1=tmp.v, op=ALU.add)
                o("dve", "tensor_tensor", out=tmp2.v, in0=psOw[:, :, 0:64], in1=rcc[:, 8:12].us(2).bc([128, 4, 64]), op=ALU.mult)
                obt = ob[i % 2]
                o("pool", "tensor_tensor", out=obt.v, in0=acc.v, in1=tmp2.v, op=ALU.add)
                P.dma(self.mixed.tile(i, self.mixed.ap[r0:r0 + 128, 512:768]), obt.v.rr("p h d -> p (h d)"))

            n_t = self.ntl
            st_cmp(0)
            for i in range(n_t):
                st_swa(i)
                if i + 1 < n_t:
                    st_cmp(i + 1)
                st_sel(i)
                st_comb(i)
            P.barrier()
        self.es = top


_CACHE = {}


def make_in_map(inputs, b, consts):
    m = {"x": np.ascontiguousarray(inputs["x"][b], dtype=np.float32),
         "mem": np.ascontiguousarray(inputs["mem"][b], dtype=np.float32)}
    for k in PARAM_SHAPES:
        m[k] = np.ascontiguousarray(inputs[k], dtype=np.float32)
    m.update(consts)
    return m


def kernel(**inputs):
    if "kb" not in _CACHE:
        kb = KB()
        kb.build()
        _CACHE["kb"] = kb
    kb = _CACHE["kb"]
    consts = host_consts()
    in_maps = [make_in_map(inputs, c % 4, consts) for c in range(8)]
    res = run_bass_kernel_spmd(kb.nc, in_maps, core_ids=list(range(8)))
    out = np.stack([np.asarray(res.results[c]["out"], dtype=np.float32) for c in range(4)], axis=0)
    return out
```

```python
import numpy as np
import ml_dtypes
from contextlib import ExitStack
import concourse.bass as bass
import concourse.mybir as mybir
from concourse.bass_utils import run_bass_kernel_spmd

F32 = mybir.dt.float32
BF16 = mybir.dt.bfloat16
ALU = mybir.AluOpType
AF = mybir.ActivationFunctionType
AX = mybir.AxisListType

D = 1024
L = 4096
NT = L // 128
NMEM = 256
DFF = 2816
NFC = DFF // 128
INC = 3388
EPS = 1e-6
NEG = -30000.0
THETA = 500000.0
N_BIS = 14
TOPK = 256


class Buf:
    __slots__ = ("name", "last_w", "readers", "t", "psum")

    def __init__(self, name, t=None, psum=False):
        self.name = name
        self.last_w = None
        self.readers = []
        self.t = t
        self.psum = psum

    def __getitem__(self, k):
        return View([self], self.t[k])

    @property
    def v(self):
        return View([self], self.t[:])


class View:
    __slots__ = ("bufs", "ap")

    def __init__(self, bufs, ap):
        self.bufs = bufs
        self.ap = ap

    def __getitem__(self, k):
        return View(self.bufs, self.ap[k])

    def rr(self, s, **kw):
        return View(self.bufs, self.ap.rearrange(s, **kw))

    def bc(self, shape):
        return View(self.bufs, self.ap.to_broadcast(list(shape)))

    def us(self, ax):
        return View(self.bufs, self.ap.unsqueeze(ax))

    @property
    def shape(self):
        return self.ap.shape


class Op:
    __slots__ = ("idx", "eng", "emit", "deps", "signal", "ticket", "is_dma", "dsem", "dval")


WRITE_KEYS = ("out", "accum_out", "ap")


class Prog:
    ENGS = ("pe", "act", "dve", "pool", "sp")
    NRING = 24

    def __init__(self, nc):
        self.nc = nc
        self.ops = []
        self.ndma = 0
        self.lastdma = {}
        self.lasteng = {}

    def add(self, eng, emit, reads=(), writes=(), dma=False):
        op = Op()
        op.idx = len(self.ops)
        op.eng = eng
        op.emit = emit
        op.is_dma = dma
        op.signal = False
        op.ticket = None
        deps = {}
        for b in reads:
            if b.last_w is not None:
                deps[b.last_w.idx] = (b.last_w, True)
            if b.psum:
                for r in b.readers:
                    if r.eng != eng and r.idx not in deps:
                        deps[r.idx] = (r, False)
        for b in writes:
            if b.last_w is not None and b.last_w.idx not in deps:
                deps[b.last_w.idx] = (b.last_w, False)
            for r in b.readers:
                if r.idx not in deps:
                    deps[r.idx] = (r, False)
        op.deps = list(deps.values())
        for b in reads:
            if not dma:
                b.readers = [r for r in b.readers if r.is_dma or r.eng != eng]
            b.readers.append(op)
        for b in writes:
            b.last_w = op
            b.readers = []
        if dma:
            k = self.ndma
            self.ndma += 1
            op.dsem = k % self.NRING
            op.dval = 16 * (k // self.NRING + 1)
            self.lastdma[op.dsem] = op
        if emit is not None:
            self.lasteng[eng] = op
        self.ops.append(op)
        return op

    def op(self, eng, name, r=(), w=(), **kw):
        reads = list(r)
        writes = list(w)
        args = {}
        for k, v in kw.items():
            if isinstance(v, View):
                if k in WRITE_KEYS:
                    writes.extend(v.bufs)
                else:
                    reads.extend(v.bufs)
                args[k] = v.ap
            else:
                args[k] = v

        if name == "matmul":
            args.setdefault("skip_group_check", True)

        def emit(e, name=name, args=args):
            return getattr(e, name)(**args)

        return self.add(eng, emit, reads, writes, dma=(name == "dma_start"))

    def dma(self, out, in_, eng="sp"):
        return self.op(eng, "dma_start", out=out, in_=in_)

    def barrier(self):
        prev = list(self.lasteng.values()) + list(self.lastdma.values())
        for e in self.ENGS:
            op = self.add(e, None)
            for o in prev:
                if o.eng != e or o.is_dma:
                    op.deps.append((o, True))

    def emit_all(self, sems, ring):
        nc = self.nc
        engobj = {"pe": nc.tensor, "act": nc.scalar, "dve": nc.vector, "pool": nc.gpsimd, "sp": nc.sync}
        for op in self.ops:
            for d, raw in op.deps:
                if d.is_dma:
                    continue
                if d.eng == op.eng and d.eng == "pe":
                    continue
                d.signal = True
        cnt = {e: 0 for e in self.ENGS}
        for op in self.ops:
            if op.signal and not op.is_dma:
                cnt[op.eng] += 1
                op.ticket = cnt[op.eng]
        seen = {e: {} for e in self.ENGS}
        nw = 0
        for op in self.ops:
            e = engobj[op.eng]
            sn = seen[op.eng]
            need = {}
            for d, raw in op.deps:
                if d.is_dma:
                    key = ("r", d.dsem)
                    val = d.dval
                else:
                    if d.eng == op.eng and d.eng == "pe":
                        continue
                    key = ("e", d.eng)
                    val = d.ticket
                if sn.get(key, 0) >= val:
                    continue
                if need.get(key, 0) < val:
                    need[key] = val
            if op.is_dma and op.dval > 16:
                key = ("r", op.dsem)
                val = op.dval - 16
                if sn.get(key, 0) < val and need.get(key, 0) < val:
                    need[key] = val
            for key, val in need.items():
                s = ring[key[1]] if key[0] == "r" else sems[key[1]]
                e.wait_ge(s, val)
                sn[key] = val
                nw += 1
            if op.emit is None:
                continue
            ins = op.emit(e)
            if op.is_dma:
                ins.then_inc(ring[op.dsem], 16)
            elif op.signal:
                ins.then_inc(sems[op.eng], 1)
        return nw, cnt


class DT:
    def __init__(self, name, ap, ntile=1):
        self.name = name
        self.ap = ap
        self.bufs = [Buf("%s_%d" % (name, i)) for i in range(ntile)]

    def tile(self, i, ap):
        return View([self.bufs[i]], ap)

    def all(self, ap=None):
        return View(list(self.bufs), self.ap if ap is None else ap)


def host_consts():
    bf = ml_dtypes.bfloat16
    c = {}
    eye = np.eye(128, dtype=np.float32)
    c["c_identb"] = eye.astype(bf)
    c["c_identf"] = eye
    c["c_ident4"] = np.tile(eye, (1, 4)).astype(bf)
    s = np.arange(128)[:, None]
    t = np.arange(128)[None, :]
    triu = (s <= t).astype(np.float32)
    c["c_triuf"] = triu
    c["c_triub"] = triu.astype(bf)
    tt = np.arange(128)[:, None]
    ss = np.arange(128)[None, :]
    c["c_cb"] = np.where(ss <= tt, 0.0, NEG).astype(np.float32)
    c["c_ab"] = np.where(ss > tt, 0.0, NEG).astype(np.float32)
    pos = np.arange(L, dtype=np.float32)
    for nm, rd in (("c_cs64", 16), ("c_cs32", 8)):
        half = rd // 2
        inv = (np.float32(THETA) ** (-np.arange(half, dtype=np.float32) * np.float32(2.0) / np.float32(rd))).astype(np.float32)
        ang = (pos[:, None] * inv[None, :]).astype(np.float32)
        cs = np.concatenate([np.cos(ang), np.sin(ang)], axis=1).astype(np.float32)
        c[nm] = np.ascontiguousarray(cs.reshape(NT, 128, rd).transpose(1, 0, 2))
    j = np.arange(256)[None, :]
    tq = np.arange(L)[:, None]
    vis = (16 * j + 31 <= tq) & (j < 255)
    c["c_cmpmask"] = np.where(vis, 0.0, NEG).astype(bf)
    n = np.arange(64)[None, :]
    cur = tq // 64
    forced = (n == 0) | (n == cur) | (n == cur - 1)
    fb = np.where(forced, 1.0e6 + 64.0 * n, np.where(n > cur, -1.0e6, 0.0))
    c["c_fb"] = fb.astype(np.float32)
    st_c = np.arange(255) * 16
    st_s = np.arange(64) * 64
    ov = ((st_c[:, None] < st_s[None, :] + 64) & (st_c[:, None] + 32 > st_s[None, :])).astype(np.float32)
    ovp = np.zeros((256, 64), np.float32)
    ovp[:255] = ov
    c["c_overlap"] = ovp.astype(bf)
    c["c_pow2"] = np.tile((1.0078125 * 2.0 ** -np.arange(N_BIS, dtype=np.float32))[None, :], (128, 1)).astype(np.float32)
    return c


PARAM_SHAPES = {
    "lb_param": (2, 256), "norm_mix": (2, 1024), "w_in": (2, 1024, INC), "w_out": (2, 1024, 1024),
    "hg_o_gain": (2, 64), "dsa_kv_gain": (2, 128), "dsa_w_uk": (2, 128, 64), "dsa_w_uv": (2, 128, 64),
    "dsa_q_gain": (2, 64), "dsa_k_gain": (2, 64), "dsa_idxk_gain": (2, 32),
    "nsa_pos_k": (2, 32, 64), "nsa_pos_v": (2, 32, 64), "nsa_k_w1": (2, 2048, 256), "nsa_k_w2": (2, 256, 64),
    "nsa_v_w1": (2, 2048, 256), "nsa_v_w2": (2, 256, 64), "nsa_q_gain": (2, 64), "nsa_k_gains": (2, 3, 64),
    "ml_conv_w": (2, 4, 512), "ml_conv_b": (2, 512), "ml_i_bias": (2, 4), "ml_f_bias": (2, 4), "ml_o_gain": (2, 64),
    "norm_xa": (2, 1024), "norm_mem": (2, 1024), "xa_wq": (2, 1024, 256), "xa_wkv": (2, 1024, 512),
    "xa_wo": (2, 256, 1024), "xa_q_gain": (2, 64), "xa_k_gain": (2, 64), "norm_ffn": (2, 1024),
    "ffn_w13": (2, 1024, 2 * DFF), "ffn_w2": (2, DFF, 1024),
}


HG0, DS0, NS0, ML0 = 0, 1024, 1704, 2356
TM_GROUPS = [(512, 512), (1024, 512), (1536, 168), (1704, 256), (2088, 268), (2868, 512), (3380, 8)]
TM_OFF = [0, 768, 1280, 512, 1448, 1716, 2228]
TMW = 2236
FM_COLS = [0, 128, 256, 384, 2356, 2484, 2612, 2740, 1960]
C_HGI, C_HGG, C_DQ, C_CKV, C_IQ, C_IK, C_IW = 0, 256, 768, 1024, 1152, 1408, 1440
C_NQ, C_KS, C_VS, C_KW, C_VW, C_NG = 512, 1448, 1512, 1576, 1640, 1704
C_MV, C_OG, C_IG, C_FG = 1716, 1972, 2228, 2232
B_HGV, B_DV, B_NVS, B_NVW, TMBW = 0, 256, 321, 386, 452
F_HGG, F_MV, F_OG, F_SGN, F_NG, F_GT, TMFW = 0, 256, 512, 768, 776, 788, 796


class KB:
    def __init__(self, dump=(), layers=(0, 1), phases=None, ntl=NT):
        self.ntl = ntl
        import os
        self.alvl = int(os.environ.get("KDBG_A", "99"))
        self.dump = set(dump)
        self.layers = layers
        self.phases = phases
        nc = bass.Bass("TRN2", target_bir_lowering=False)
        self.nc = nc
        self.P = Prog(nc)
        self.es = None
        self.din = {}

    def sb(self, name, shape, dt):
        t = self.es.enter_context(self.nc.sbuf_tensor(name + "_%d" % self.uid(), list(shape), dt))
        return Buf(name, t)

    def ps(self, name, shape, dt):
        nel = 2048 // (4 if dt == F32 else 2)
        full = self.es.enter_context(self.nc.psum_tensor(name + "_%d" % self.uid(), [128, nel], dt))
        n = 1
        for d in shape[1:]:
            n *= d
        assert n <= nel, (name, shape)
        ap = full[0:shape[0], 0:n]
        if len(shape) == 3:
            ap = ap.rearrange("p (a b) -> p a b", a=shape[1])
        return Buf(name, ap, psum=True)

    def uid(self):
        self._uid = getattr(self, "_uid", 0) + 1
        return self._uid

    def rot(self, name, shape, dt, n, psum=False):
        return [(self.ps if psum else self.sb)("%s%d" % (name, i), shape, dt) for i in range(n)]

    def dram_in(self, name, shape, dt):
        ap = self.nc.dram_tensor(name, list(shape), dt, kind="ExternalInput").ap()
        d = DT(name, ap, 1)
        self.din[name] = d
        return d

    def scr(self, name, shape, dt, ntile=1):
        kind = "ExternalOutput" if name in self.dump else "Internal"
        ap = self.nc.dram_tensor(name, list(shape), dt, kind=kind).ap()
        return DT(name, ap, ntile)

    def op(self, eng, name, **kw):
        return self.P.op(eng, name, **kw)

    def rms_heads(self, X, H, Dh, gain, out, tmp, ssq, np_=128, gain_full=False):
        o = self.op
        o("pool", "tensor_tensor", out=tmp, in0=X, in1=X, op=ALU.mult)
        o("dve", "tensor_reduce", out=ssq, in_=tmp, axis=AX.X, op=ALU.add)
        o("act", "activation", out=ssq, in_=ssq, func=AF.Sqrt, scale=1.0 / Dh, bias=self.epsb[0:np_, 0:1])
        o("dve", "reciprocal", out=ssq, in_=ssq)
        o("dve", "tensor_tensor", out=out, in0=X, in1=ssq.us(2).bc([np_, H, Dh]), op=ALU.mult)
        if gain is not None:
            o("pool", "tensor_tensor", out=out, in0=out, in1=(gain if gain_full else gain.us(1).bc([np_, H, Dh])), op=ALU.mult)

    def rope(self, X, H, hf, cs, out, t1, t2):
        o = self.op
        Dh = X.shape[2]
        cos = cs[:, 0:hf].us(1).bc([128, H, hf])
        sin = cs[:, hf:2 * hf].us(1).bc([128, H, hf])
        x1 = X[:, :, 0:hf]
        x2 = X[:, :, hf:2 * hf]
        o("pool", "tensor_copy", out=out[:, :, 2 * hf:Dh], in_=X[:, :, 2 * hf:Dh])
        o("pool", "tensor_tensor", out=t1, in0=x1, in1=cos, op=ALU.mult)
        o("pool", "tensor_tensor", out=t2, in0=x2, in1=sin, op=ALU.mult)
        o("pool", "tensor_tensor", out=out[:, :, 0:hf], in0=t1, in1=t2, op=ALU.subtract)
        o("dve", "tensor_tensor", out=t1, in0=x2, in1=cos, op=ALU.mult)
        o("dve", "tensor_tensor", out=t2, in0=x1, in1=sin, op=ALU.mult)
        o("dve", "tensor_tensor", out=out[:, :, hf:2 * hf], in0=t1, in1=t2, op=ALU.add)

    def build(self):
        nc = self.nc
        P = self.P
        o = self.op
        self.x_in = self.dram_in("x", [L, D], F32)
        self.x_in.bufs = [Buf("xin%d" % i) for i in range(NT)]
        self.mem_in = self.dram_in("mem", [NMEM, D], F32)
        self.prm = {k: self.dram_in(k, list(s), F32).ap for k, s in PARAM_SHAPES.items()}
        hc = host_consts()
        self.cst = {}
        for k, v in hc.items():
            self.cst[k] = self.dram_in(k, list(v.shape), BF16 if v.dtype != np.float32 else F32)
        kind = "ExternalOutput"
        self.xo = DT("out", nc.dram_tensor("out", [L, D], F32, kind=kind).ap(), NT)
        self.fmT = self.scr("fmT", [1152, L], F32, NT)
        self.tmb = self.scr("tmb", [L, TMBW], BF16, NT)
        self.tmf = self.scr("tmf", [L, TMFW], F32, NT)
        self.kT3 = self.scr("kT3", [3, 64, L], BF16, NT)
        self.ikT = self.scr("ikT", [32, L], BF16, NT)
        self.qT3 = self.scr("qT3", [NT, 64, 3, 512], BF16, NT)
        self.iqT = self.scr("iqT", [NT, 32, 1024], BF16, NT)
        self.mixed = self.scr("mixed", [L, D], BF16, NT)

        with ExitStack() as top:
            self.es = top
            sems = {e: top.enter_context(nc.semaphore("s_" + e)) for e in P.ENGS}
            ring = [top.enter_context(nc.semaphore("r%d" % i)) for i in range(P.NRING)]
            self.identb = self.sb("identb", [128, 128], BF16)
            self.identf = self.sb("identf", [128, 128], F32)
            self.ident4 = self.sb("ident4", [128, 512], BF16)
            self.triuf = self.sb("triuf", [128, 128], F32)
            self.triub = self.sb("triub", [128, 128], BF16)
            self.cbf = self.sb("cbf", [128, 128], F32)
            self.cbb = self.sb("cbb", [128, 128], BF16)
            self.abb = self.sb("abb", [128, 128], BF16)
            self.cs64 = self.sb("cs64", [128, NT, 16], F32)
            self.cs32 = self.sb("cs32", [128, NT, 8], F32)
            self.epsb = self.sb("epsb", [128, 1], F32)
            self.onesf = self.sb("onesf", [128, 128], F32)
            tmpf = self.sb("tmpf", [128, 128], F32)
            for nm, dst in (("c_identb", self.identb), ("c_identf", self.identf), ("c_ident4", self.ident4),
                            ("c_triuf", self.triuf), ("c_triub", self.triub), ("c_cb", self.cbf)):
                P.dma(dst.v, self.cst[nm].all())
            P.dma(tmpf.v, self.cst["c_ab"].all())
            P.dma(self.cs64.v, self.cst["c_cs64"].all())
            P.dma(self.cs32.v, self.cst["c_cs32"].all())
            o("dve", "tensor_copy", out=self.cbb.v, in_=self.cbf.v)
            o("dve", "tensor_copy", out=self.abb.v, in_=tmpf.v)
            o("pool", "memset", ap=self.epsb.v, constant=EPS)
            o("pool", "memset", ap=self.onesf.v, constant=1.0)
            P.barrier()
            for l in self.layers:
                xsrc = self.x_in if l == 0 else self.xo
                ph = self.phases
                if ph is None or "A" in ph:
                    self.phase_A(l, xsrc)
                if ph is None or "HG" in ph:
                    self.phase_HG(l)
                if ph is None or "ML" in ph:
                    self.phase_ML(l)
                if ph is None or "DSA" in ph:
                    self.phase_DSA(l)
                if ph is None or "NSA" in ph:
                    self.phase_NSA(l)
                if ph is None or "C" in ph:
                    self.phase_C(l, xsrc)
            P.barrier()
            self.stats = P.emit_all(sems, ring)
        return nc

    def load_weight_bf16(self, dst, src_ap, nk, ncols, gain=None, chunk=512, stage=None):
        P = self.P
        o = self.op
        src = src_ap.rearrange("(c p) n -> p c n", p=128)
        engs = ["dve", "pool", "act"]
        ei = 0
        j = 0
        for c0 in range(0, ncols, chunk):
            w = min(chunk, ncols - c0)
            for k0 in range(0, nk, 4):
                k1 = min(nk, k0 + 4)
                st = stage[j % len(stage)]
                j += 1
                P.dma(st[:, 0:k1 - k0, 0:w], View([], src[:, k0:k1, c0:c0 + w]))
                for k in range(k0, k1):
                    e = engs[ei % 3]
                    ei += 1
                    if gain is None:
                        if e == "act":
                            o(e, "copy", out=dst[:, k, c0:c0 + w], in_=st[:, k - k0, 0:w])
                        else:
                            o(e, "tensor_copy", out=dst[:, k, c0:c0 + w], in_=st[:, k - k0, 0:w])
                    else:
                        if e == "act":
                            o(e, "activation", out=dst[:, k, c0:c0 + w], in_=st[:, k - k0, 0:w], func=AF.Copy,
                              scale=gain[:, k:k + 1])
                        else:
                            o(e, "tensor_scalar", out=dst[:, k, c0:c0 + w], in0=st[:, k - k0, 0:w],
                              scalar1=gain[:, k:k + 1], scalar2=None, op0=ALU.mult)

    def load_gain_cols(self, dst, vec_ap, nk):
        self.P.dma(dst.v, View([], vec_ap.rearrange("(c p) -> p c", p=128)))

    def bcast_row(self, dst_view, vec_ap):
        np_ = dst_view.shape[0]
        self.P.dma(dst_view, View([], vec_ap.partition_broadcast(np_)))

    def col_load(self, dst_view, vec_ap):
        self.P.dma(dst_view, View([], vec_ap.unsqueeze(1)))

    def phase_A(self, l, xsrc):
        P = self.P
        o = self.op
        prm = self.prm
        top = self.es
        with ExitStack() as es:
            self.es = es
            Wb = self.sb("Wb", [128, 8, INC], BF16)
            with ExitStack() as es2:
                self.es = es2
                stage = self.rot("wst", [128, 4, 512], F32, 3)
                self.load_weight_bf16(Wb, prm["w_in"][l], 8, INC, stage=stage)
                P.barrier()
            self.es = es
            gmix = self.sb("gmix", [128, D], F32)
            self.bcast_row(gmix.v, prm["norm_mix"][l])
            g8 = self.sb("g8", [128, 8, 64], F32)
            for hh in range(4):
                self.bcast_row(g8[:, hh, :], prm["nsa_q_gain"][l])
                self.bcast_row(g8[:, 4 + hh, :], prm["dsa_q_gain"][l])
            g3 = self.sb("g3", [128, 3, 64], F32)
            self.bcast_row(g3[:, 0, :], prm["dsa_k_gain"][l])
            self.bcast_row(g3[:, 1, :], prm["nsa_k_gains"][l, 1])
            self.bcast_row(g3[:, 2, :], prm["nsa_k_gains"][l, 2])
            K3 = self.sb("K3", [128, 3, 64], F32)
            K3n = self.sb("K3n", [128, 3, 64], F32)
            K3t = self.sb("K3t", [128, 3, 64], F32)
            gkv = self.sb("gkv", [128, 128], F32)
            self.bcast_row(gkv.v, prm["dsa_kv_gain"][l])
            gik = self.sb("gik", [128, 32], F32)
            self.bcast_row(gik.v, prm["dsa_idxk_gain"][l])
            wst = self.sb("wukv_st", [128, 128], F32)
            wukv = self.sb("wukv", [128, 128], BF16)
            P.dma(wst[:, 0:64], View([], prm["dsa_w_uk"][l]))
            P.dma(wst[:, 64:128], View([], prm["dsa_w_uv"][l]))
            o("dve", "tensor_copy", out=wukv.v, in_=wst.v)

            xt = self.rot("xt", [128, D], F32, 2)
            junk = self.sb("junk", [128, D], F32)
            ssx = self.rot("ssx", [128, 1], F32, 2)
            hb = self.rot("hb", [128, D], BF16, 2)
            hT = self.rot("hT", [128, 8, 128], BF16, 2)
            ct = self.rot("ct", [128, TMW], F32, 2)
            fm = self.rot("fm", [128, 9, 128], F32, 2)
            tmbS = self.rot("tmbS", [128, TMBW], BF16, 2)
            tmfS = self.rot("tmfS", [128, TMFW], F32, 2)
            kS = self.rot("kS", [64, 3, 128], BF16, 2)
            ikS = self.rot("ikS", [32, 128], BF16, 2)
            qS = self.rot("qS", [64, 3, 512], BF16, 2)
            iqS = self.rot("iqS", [32, 1024], BF16, 2)
            wk = self.sb("wk", [128, 8, 64], F32)
            wk2 = self.sb("wk2", [128, 8, 64], F32)
            t1 = self.sb("t1", [128, 8, 8], F32)
            t2 = self.sb("t2", [128, 8, 8], F32)
            ssq = self.sb("ssq", [128, 8], F32)
            qb = self.sb("qb", [128, 12, 64], BF16)
            kb = self.sb("kb", [128, 3, 64], BF16)
            iqb = self.sb("iqb", [128, 8, 32], BF16)
            ikb = self.sb("ikb", [128, 32], BF16)
            ckvb = self.sb("ckvb", [128, 128], BF16)
            ckvT = self.sb("ckvT", [128, 128], BF16)
            kvf = self.sb("kvf", [128, 128], F32)
            iwa = self.sb("iwa", [128, 8], F32)

            psT = self.ps("psT", [128, 8, 128], BF16)
            psA = self.rot("psA", [128, 512], F32, 2, psum=True)
            psB = self.rot("psB", [128, 4, 128], F32, 2, psum=True)
            psX = self.ps("psX", [128, 8, 128], BF16)
            psY = self.ps("psY", [128, 8, 128], BF16)
            psZ = self.ps("psZ", [128, 8, 128], BF16)

            IWS = float(8 ** -0.5 * 32 ** -0.5)

            def sA(i):
                x_t = xt[i % 2]
                hbt = hb[i % 2]
                hTt = hT[i % 2]
                c = ct[i % 2]
                f = fm[i % 2]
                r0 = i * 128
                if self.alvl < 1:
                    return
                P.dma(x_t.v, xsrc.tile(i, xsrc.ap[r0:r0 + 128, :]))
                ss = ssx[i % 2]
                o("act", "activation", out=junk.v, in_=x_t.v, func=AF.Square, accum_out=ss.v)
                o("act", "activation", out=ss.v, in_=ss.v, func=AF.Sqrt, scale=1.0 / D, bias=self.epsb[:, 0:1])
                o("dve", "reciprocal", out=ss.v, in_=ss.v)
                o("dve", "scalar_tensor_tensor", out=hbt.v, in0=x_t.v, scalar=ss[:, 0:1], in1=gmix.v,
                  op0=ALU.mult, op1=ALU.mult)
                for k in range(8):
                    o("pe", "transpose", out=psT[:, k, :], in_=hbt[:, k * 128:(k + 1) * 128], identity=self.identb.v)
                o("act", "copy", out=hTt.v, in_=psT.v)
                for gi, (c0, w) in enumerate(TM_GROUPS):
                    pa = psA[gi % 2]
                    for k in range(8):
                        o("pe", "matmul", out=pa[:, 0:w], lhsT=hTt[:, k, :], rhs=Wb[:, k, c0:c0 + w],
                          start=(k == 0), stop=(k == 7))
                    off = TM_OFF[gi]
                    if gi % 2 == 0:
                        o("dve", "tensor_copy", out=c[:, off:off + w], in_=pa[:, 0:w])
                    else:
                        o("act", "copy", out=c[:, off:off + w], in_=pa[:, 0:w])
                for ci, c0 in enumerate(FM_COLS):
                    pb = psB[(ci // 4) % 2]
                    for k in range(8):
                        o("pe", "matmul", out=pb[:, ci % 4, :], lhsT=Wb[:, k, c0:c0 + 128], rhs=hTt[:, k, :],
                          start=(k == 0 and ci % 4 == 0), stop=(k == 7))
                    if ci in (1,):
                        o("act", "activation", out=f[:, 0:2, :], in_=pb[:, 0:2, :], func=AF.Silu)
                    elif ci in (3,):
                        o("act", "activation", out=f[:, 2:4, :], in_=pb[:, 2:4, :], func=AF.Sigmoid, scale=-1.0)
                    elif ci == 7:
                        o("dve", "tensor_copy", out=f[:, 4:8, :], in_=pb.v)
                    elif ci == 8:
                        o("dve", "tensor_copy", out=f[:, 8, :], in_=pb[:, 0, :])
                P.dma(self.fmT.tile(i, self.fmT.ap.rearrange("(c p) t -> p c t", p=128)[:, :, r0:r0 + 128]), f.v)


            def sB(i):
                r0 = i * 128
                c = ct[i % 2]
                if self.alvl < 2:
                    return
                tb = tmbS[i % 2]
                tf = tmfS[i % 2]
                cs64 = self.cs64[:, i, :]
                cs32 = self.cs32[:, i, :]
                o("pool", "tensor_copy", out=tb[:, B_HGV:B_HGV + 256], in_=c[:, C_HGI:C_HGI + 256])
                o("act", "activation", out=tf[:, F_HGG:F_HGG + 256], in_=c[:, C_HGG:C_HGG + 256], func=AF.Silu)
                o("pool", "tensor_copy", out=tf[:, F_MV:F_MV + 256], in_=c[:, C_MV:C_MV + 256])
                o("act", "activation", out=tf[:, F_OG:F_OG + 256], in_=c[:, C_OG:C_OG + 256], func=AF.Sigmoid)
                o("pool", "tensor_copy", out=tf[:, F_GT:F_GT + 8], in_=c[:, C_IG:C_IG + 8])
                o("act", "activation", out=tf[:, F_NG:F_NG + 12], in_=c[:, C_NG:C_NG + 12], func=AF.Sigmoid)
                self.rms_heads(c[:, C_NQ:C_NQ + 512].rr("p (h d) -> p h d", h=8), 8, 64, g8.v, wk.v, wk2.v, ssq.v, gain_full=True)
                o("act", "copy", out=qb[:, 0:4, :], in_=wk[:, 0:4, :])
                self.rope(wk.v, 8, 8, cs64, qb[:, 4:12, :], t1.v, t2.v)
                if self.alvl < 3:
                    return
                self.rms_heads(c[:, C_CKV:C_CKV + 128].rr("p (h d) -> p h d", h=1), 1, 128, gkv.v,
                               wk2[:, 0:2, :].rr("p a b -> p (a b)").rr("p (h d) -> p h d", h=1),
                               wk2[:, 2:4, :].rr("p a b -> p (a b)").rr("p (h d) -> p h d", h=1), ssq[:, 0:1])
                o("act", "copy", out=ckvb.v, in_=wk2[:, 0:2, :].rr("p a b -> p (a b)"))
                o("pe", "transpose", out=psT[:, 0, :], in_=ckvb.v, identity=self.identb.v)
                o("act", "copy", out=ckvT.v, in_=psT[:, 0, :])
                pa = psA[1]
                o("pe", "matmul", out=pa[:, 0:128], lhsT=ckvT.v, rhs=wukv.v, start=True, stop=True)
                o("act", "copy", out=kvf.v, in_=pa[:, 0:128])
                o("pool", "tensor_copy", out=tb[:, B_DV:B_DV + 64], in_=kvf[:, 64:128])
                o("pool", "memset", ap=tb[:, B_DV + 64:B_DV + 65], constant=1.0)
                o("act", "copy", out=K3[:, 0, :], in_=kvf[:, 0:64])
                o("dve", "tensor_copy", out=K3[:, 1, :], in_=c[:, C_KS:C_KS + 64])
                o("pool", "tensor_copy", out=K3[:, 2, :], in_=c[:, C_KW:C_KW + 64])
                self.rms_heads(K3.v, 3, 64, g3.v, K3n.v, K3t.v, ssq[:, 1:4], gain_full=True)
                self.rope(K3n.v, 3, 8, cs64, kb.v, t1[:, 0:3, :], t2[:, 0:3, :])
                o("pool", "tensor_copy", out=tb[:, B_NVS:B_NVS + 64], in_=c[:, C_VS:C_VS + 64])
                o("pool", "memset", ap=tb[:, B_NVS + 64:B_NVS + 65], constant=1.0)
                o("pool", "tensor_copy", out=tb[:, B_NVW:B_NVW + 64], in_=c[:, C_VW:C_VW + 64])
                o("pool", "memset", ap=tb[:, B_NVW + 64:B_NVW + 66], constant=1.0)
                if self.alvl < 4:
                    return
                o("dve", "tensor_scalar", out=iwa.v, in0=c[:, C_IW:C_IW + 8], scalar1=IWS, scalar2=None, op0=ALU.mult)
                o("dve", "scalar_tensor_tensor", out=iwa.v, in0=iwa.v, scalar=-1.0, in1=iwa.v, op0=ALU.mult, op1=ALU.max)
                o("act", "activation", out=tf[:, F_SGN:F_SGN + 8], in_=c[:, C_IW:C_IW + 8], func=AF.Sign)
                IQ = wk[:, 0:4, :].rr("p a b -> p (a b)").rr("p (h d) -> p h d", h=8)
                IQ2 = wk[:, 4:8, :].rr("p a b -> p (a b)").rr("p (h d) -> p h d", h=8)
                self.rope(c[:, C_IQ:C_IQ + 256].rr("p (h d) -> p h d", h=8), 8, 4, cs32, IQ, t1[:, :, 0:4], t2[:, :, 0:4])
                o("dve", "tensor_tensor", out=iqb.v, in0=IQ, in1=iwa.v.us(2).bc([128, 8, 32]), op=ALU.mult)
                IK = wk2[:, 0:1, 0:32]
                self.rms_heads(c[:, C_IK:C_IK + 32].rr("p (h d) -> p h d", h=1), 1, 32, gik.v, IK, wk2[:, 1:2, 0:32], ssq[:, 4:5])
                self.rope(IK, 1, 4, cs32, ikb.v.rr("p (h d) -> p h d", h=1), t1[:, 0:1, 0:4], t2[:, 0:1, 0:4])
                if self.alvl < 5:
                    return
                import os
                B = int(os.environ.get("KDBG_B", "99"))
                q_s = qS[i % 2]
                k_s = kS[i % 2]
                for h in range(4):
                    o("pe", "transpose", out=psX[0:64, h, :], in_=qb[:, 8 + h, :], identity=self.identb.v)
                if B >= 1:
                    for j in range(3):
                        o("pe", "transpose", out=psX[0:64, 4 + j, :], in_=kb[:, j, :], identity=self.identb.v)
                if B >= 2:
                    o("pe", "transpose", out=psX[0:32, 7, :], in_=ikb.v, identity=self.identb.v)
                o("act", "copy", out=q_s[:, 0, :], in_=psX[0:64, 0:4, :].rr("p h t -> p (h t)"))
                if B >= 1:
                    o("dve", "tensor_copy", out=k_s.v, in_=psX[0:64, 4:7, :])
                if B >= 2:
                    o("dve", "tensor_copy", out=ikS[i % 2].v, in_=psX[0:32, 7, :])
                if B >= 3:
                    for h in range(8):
                        o("pe", "transpose", out=psY[0:64, h, :], in_=qb[:, h, :], identity=self.identb.v)
                    o("act", "copy", out=q_s[:, 1:3, :].rr("p a n -> p (a n)"), in_=psY[0:64, :, :].rr("p h t -> p (h t)"))
                if B >= 4:
                    for h in range(8):
                        o("pe", "transpose", out=psZ[0:32, h, :], in_=iqb[:, h, :], identity=self.identb.v)
                    o("dve", "tensor_copy", out=iqS[i % 2].v, in_=psZ[0:32, :, :].rr("p h t -> p (h t)"))
                if self.alvl < 6:
                    return
                P.dma(self.tmb.tile(i, self.tmb.ap[r0:r0 + 128, :]), tb.v)
                if self.alvl < 7:
                    return
                P.dma(self.tmf.tile(i, self.tmf.ap[r0:r0 + 128, :]), tf.v)
                if self.alvl < 8:
                    return
                P.dma(self.kT3.tile(i, self.kT3.ap.rearrange("k p t -> p k t")[:, :, r0:r0 + 128]), k_s.v)
                if self.alvl < 9:
                    return
                P.dma(self.ikT.tile(i, self.ikT.ap[:, r0:r0 + 128]), ikS[i % 2].v)
                P.dma(self.qT3.tile(i, self.qT3.ap[i]), q_s.v)
                P.dma(self.iqT.tile(i, self.iqT.ap[i]), iqS[i % 2].v)
            sA(0)
            for i in range(NT):
                if i + 1 < NT:
                    sA(i + 1)
                sB(i)
            P.barrier()
        self.es = top

    def phase_C(self, l, xsrc):
        self.phase_C1(l, xsrc)
        self.phase_C2(l)

    def phase_C1(self, l, xsrc):
        P = self.P
        o = self.op
        prm = self.prm
        top = self.es
        with ExitStack() as es:
            self.es = es
            Wo = self.sb("Wo", [128, 8, D], BF16)
            Wq = self.sb("Wq", [128, 8, 256], BF16)
            Wxo = self.sb("Wxo", [128, 2, D], BF16)
            xkT = self.sb("xkT", [64, 4, NMEM], BF16)
            xv = self.sb("xv", [128, 2, 4, 65], BF16)
            gq = self.sb("gq", [128, 64], F32)
            self.bcast_row(gq.v, prm["xa_q_gain"][l])
            gxa = self.sb("gxa", [128, D], F32)
            self.bcast_row(gxa.v, prm["norm_xa"][l])
            with ExitStack() as es2:
                self.es = es2
                stage = self.rot("wst", [128, 4, 512], F32, 3)
                self.load_weight_bf16(Wo, prm["w_out"][l], 8, D, stage=stage)
                self.load_weight_bf16(Wq, prm["xa_wq"][l], 8, 256, stage=stage)
                self.load_weight_bf16(Wxo, prm["xa_wo"][l], 2, D, stage=stage)
                Wkv = self.sb("Wkv", [128, 8, 512], BF16)
                self.load_weight_bf16(Wkv, prm["xa_wkv"][l], 8, 512, stage=stage)
                gmem = self.sb("gmem", [128, D], F32)
                self.bcast_row(gmem.v, prm["norm_mem"][l])
                gk = self.sb("gk", [128, 64], F32)
                self.bcast_row(gk.v, prm["xa_k_gain"][l])
                mt = self.sb("mt", [128, D], F32)
                mj = self.sb("mj", [128, D], BF16)
                mss = self.sb("mss", [128, 1], F32)
                mb = self.sb("mb", [128, D], BF16)
                mT = self.sb("mT", [128, 8, 128], BF16)
                kvf = self.sb("kvf", [128, 512], F32)
                kn = self.sb("kn", [128, 4, 64], F32)
                ktmp = self.sb("ktmp", [128, 4, 64], F32)
                kss = self.sb("kss", [128, 4], F32)
                knb = self.sb("knb", [128, 4, 64], BF16)
                pT = self.ps("pT", [128, 8, 128], BF16)
                pK = self.ps("pK", [128, 512], F32)
                for m in range(2):
                    P.dma(mt.v, self.mem_in.all(self.mem_in.ap[m * 128:(m + 1) * 128, :]))
                    o("act", "activation", out=mj.v, in_=mt.v, func=AF.Square, accum_out=mss.v)
                    o("act", "activation", out=mss.v, in_=mss.v, func=AF.Sqrt, scale=1.0 / D, bias=self.epsb[:, 0:1])
                    o("dve", "reciprocal", out=mss.v, in_=mss.v)
                    o("dve", "scalar_tensor_tensor", out=mb.v, in0=mt.v, scalar=mss[:, 0:1], in1=gmem.v,
                      op0=ALU.mult, op1=ALU.mult)
                    for k in range(8):
                        o("pe", "transpose", out=pT[:, k, :], in_=mb[:, k * 128:(k + 1) * 128], identity=self.identb.v)
                    o("act", "copy", out=mT.v, in_=pT.v)
                    for k in range(8):
                        o("pe", "matmul", out=pK.v, lhsT=mT[:, k, :], rhs=Wkv[:, k, :], start=(k == 0), stop=(k == 7))
                    o("act", "copy", out=kvf.v, in_=pK.v)
                    self.rms_heads(kvf[:, 0:256].rr("p (h d) -> p h d", h=4), 4, 64, gk.v, kn.v, ktmp.v, kss.v)
                    o("act", "copy", out=knb.v, in_=kn.v)
                    for h in range(4):
                        o("pe", "transpose", out=pT[0:64, h, :], in_=knb[:, h, :], identity=self.identb.v)
                    o("act", "copy", out=xkT[:, :, m * 128:(m + 1) * 128], in_=pT[0:64, 0:4, :])
                    o("dve", "tensor_copy", out=xv[:, m, :, 0:64], in_=kvf[:, 256:512].rr("p (h d) -> p h d", h=4))
                    o("pool", "memset", ap=xv[:, m, :, 64:65], constant=1.0)
                P.barrier()
            self.es = es
            xt = self.rot("xt", [128, D], F32, 2)
            mxb = self.rot("mxb", [128, D], BF16, 2)
            mxT = self.sb("mxT", [128, 8, 128], BF16)
            x1r = self.rot("x1", [128, D], F32, 2)
            junk = self.sb("junk", [128, D], BF16)
            ss = self.sb("ss", [128, 1], F32)
            hb = self.sb("hb", [128, D], BF16)
            hT = self.sb("hT", [128, 8, 128], BF16)
            qf = self.sb("qf", [128, 4, 64], F32)
            qn = self.sb("qn", [128, 4, 64], F32)
            qtmp = self.sb("qtmp", [128, 4, 64], F32)
            qss = self.sb("qss", [128, 4], F32)
            qnb = self.sb("qnb", [128, 4, 64], BF16)
            qT4 = self.sb("qT4", [64, 4, 128], BF16)
            PT = self.rot("PT", [128, 512], BF16, 2)
            rden = self.sb("rden", [128, 4], F32)
            ob = self.sb("ob", [128, 4, 64], BF16)
            oT = self.sb("oT", [128, 2, 128], BF16)
            psT = self.ps("psT", [128, 8, 128], BF16)
            psM = self.rot("psM", [128, 512], F32, 2, psum=True)
            psS = self.rot("psS", [128, 512], F32, 2, psum=True)
            psO = self.ps("psO", [128, 4, 65], F32)
            for i in range(self.ntl):
                r0 = i * 128
                x_t = xt[i % 2]
                mx = mxb[i % 2]
                x1 = x1r[i % 2]
                P.dma(x_t.v, xsrc.tile(i, xsrc.ap[r0:r0 + 128, :]))
                P.dma(mx.v, self.mixed.tile(i, self.mixed.ap[r0:r0 + 128, :]))
                for k in range(8):
                    o("pe", "transpose", out=psT[:, k, :], in_=mx[:, k * 128:(k + 1) * 128], identity=self.identb.v)
                o("act", "copy", out=mxT.v, in_=psT.v)
                for g in range(2):
                    pm = psM[g]
                    for k in range(8):
                        o("pe", "matmul", out=pm.v, lhsT=mxT[:, k, :], rhs=Wo[:, k, g * 512:(g + 1) * 512],
                          start=(k == 0), stop=(k == 7))
                    o("dve", "tensor_tensor", out=x1[:, g * 512:(g + 1) * 512], in0=pm.v, in1=x_t[:, g * 512:(g + 1) * 512], op=ALU.add)
                if "xmix" in self.dump:
                    P.dma(self.dbg_xmix.tile(i, self.dbg_xmix.ap[r0:r0 + 128, :]), x1.v)
                o("act", "activation", out=junk.v, in_=x1.v, func=AF.Square, accum_out=ss.v)
                o("act", "activation", out=ss.v, in_=ss.v, func=AF.Sqrt, scale=1.0 / D, bias=self.epsb[:, 0:1])
                o("dve", "reciprocal", out=ss.v, in_=ss.v)
                o("dve", "scalar_tensor_tensor", out=hb.v, in0=x1.v, scalar=ss[:, 0:1], in1=gxa.v, op0=ALU.mult, op1=ALU.mult)
                for k in range(8):
                    o("pe", "transpose", out=psT[:, k, :], in_=hb[:, k * 128:(k + 1) * 128], identity=self.identb.v)
                o("act", "copy", out=hT.v, in_=psT.v)
                pm = psM[0]
                for k in range(8):
                    o("pe", "matmul", out=pm[:, 0:256], lhsT=hT[:, k, :], rhs=Wq[:, k, :], start=(k == 0), stop=(k == 7))
                o("act", "copy", out=qf.v.rr("p h d -> p (h d)"), in_=pm[:, 0:256])
                self.rms_heads(qf.v, 4, 64, gq.v, qn.v, qtmp.v, qss.v)
                o("act", "copy", out=qnb.v, in_=qn.v)
                for h in range(4):
                    o("pe", "transpose", out=psT[0:64, h, :], in_=qnb[:, h, :], identity=self.identb.v)
                o("act", "copy", out=qT4.v, in_=psT[0:64, 0:4, :])
                for m in range(2):
                    pss = psS[m]
                    for h in range(4):
                        o("pe", "matmul", out=pss[:, h * 128:(h + 1) * 128], lhsT=xkT[:, h, m * 128:(m + 1) * 128],
                          rhs=qT4[:, h, :], start=(h == 0), stop=(h == 3))
                    pt = PT[m]
                    o("act", "activation", out=pt.v, in_=pss.v, func=AF.Exp, scale=0.125)
                    for h in range(4):
                        o("pe", "matmul", out=psO[:, h, :], lhsT=pt[:, h * 128:(h + 1) * 128], rhs=xv[:, m, h, :],
                          start=(m == 0 and h == 0), stop=(m == 1))
                o("dve", "reciprocal", out=rden.v, in_=psO[:, :, 64])
                o("dve", "tensor_tensor", out=ob.v, in0=psO[:, :, 0:64], in1=rden.v.us(2).bc([128, 4, 64]), op=ALU.mult)
                for k in range(2):
                    o("pe", "transpose", out=psT[:, k, :], in_=ob.v.rr("p h d -> p (h d)")[:, k * 128:(k + 1) * 128], identity=self.identb.v)
                o("act", "copy", out=oT.v, in_=psT[:, 0:2, :])
                for g in range(2):
                    pm = psM[g]
                    for k in range(2):
                        o("pe", "matmul", out=pm.v, lhsT=oT[:, k, :], rhs=Wxo[:, k, g * 512:(g + 1) * 512],
                          start=(k == 0), stop=(k == 1))
                    o("dve", "tensor_tensor", out=x1[:, g * 512:(g + 1) * 512], in0=pm.v, in1=x1[:, g * 512:(g + 1) * 512], op=ALU.add)
                P.dma(self.xo.tile(i, self.xo.ap[r0:r0 + 128, :]), x1.v)
            P.barrier()
        self.es = top

    def phase_C2(self, l):
        P = self.P
        o = self.op
        prm = self.prm
        top = self.es
        with ExitStack() as es:
            self.es = es
            W13 = self.sb("W13", [128, 8, 2 * DFF], BF16)
            W2 = self.sb("W2", [128, NFC, D], BF16)
            gff = self.sb("gff", [128, D], F32)
            self.bcast_row(gff.v, prm["norm_ffn"][l])
            with ExitStack() as es2:
                self.es = es2
                stage = self.rot("wst", [128, 4, 512], F32, 3)
                self.load_weight_bf16(W13, prm["ffn_w13"][l], 8, 2 * DFF, stage=stage)
                self.load_weight_bf16(W2, prm["ffn_w2"][l], NFC, D, stage=stage)
                P.barrier()
            self.es = es
            xt = self.rot("xt", [128, D], F32, 2)
            junk = self.sb("junk", [128, D], BF16)
            ss = self.sb("ss", [128, 1], F32)
            hb = self.sb("hb", [128, D], BF16)
            hT = self.sb("hT", [128, 8, 128], BF16)
            gT = self.sb("gT", [128, NFC, 128], BF16)
            sa = self.rot("sa", [128, 4, 128], F32, 2)
            psT = self.ps("psT", [128, 8, 128], BF16)
            psM = self.rot("psM", [128, 512], F32, 2, psum=True)
            psF = self.rot("psF", [128, 4, 128], F32, 4, psum=True)
            for i in range(self.ntl):
                r0 = i * 128
                x2 = xt[i % 2]
                P.dma(x2.v, self.xo.tile(i, self.xo.ap[r0:r0 + 128, :]))
                o("act", "activation", out=junk.v, in_=x2.v, func=AF.Square, accum_out=ss.v)
                o("act", "activation", out=ss.v, in_=ss.v, func=AF.Sqrt, scale=1.0 / D, bias=self.epsb[:, 0:1])
                o("dve", "reciprocal", out=ss.v, in_=ss.v)
                o("dve", "scalar_tensor_tensor", out=hb.v, in0=x2.v, scalar=ss[:, 0:1], in1=gff.v, op0=ALU.mult, op1=ALU.mult)
                for k in range(8):
                    o("pe", "transpose", out=psT[:, k, :], in_=hb[:, k * 128:(k + 1) * 128], identity=self.identb.v)
                o("act", "copy", out=hT.v, in_=psT.v)
                for gi, g0 in enumerate(range(0, NFC, 4)):
                    g1 = min(NFC, g0 + 4)
                    n = g1 - g0
                    pfs = (psF[(gi % 2) * 2], psF[(gi % 2) * 2 + 1])
                    for half in range(2):
                        pf = pfs[half]
                        for c in range(g0, g1):
                            col = half * DFF + c * 128
                            for k in range(8):
                                o("pe", "matmul", out=pf[:, c - g0, :], lhsT=W13[:, k, col:col + 128], rhs=hT[:, k, :],
                                  start=(k == 0 and c == g0), stop=(k == 7))
                    s_a = sa[gi % 2]
                    o("act", "activation", out=s_a[:, 0:n, :], in_=pfs[0][:, 0:n, :], func=AF.Silu)
                    o("dve", "tensor_tensor", out=gT[:, g0:g1, :], in0=pfs[1][:, 0:n, :], in1=s_a[:, 0:n, :], op=ALU.mult)
                for g in range(2):
                    pm = psM[g]
                    for c in range(NFC):
                        o("pe", "matmul", out=pm.v, lhsT=gT[:, c, :], rhs=W2[:, c, g * 512:(g + 1) * 512],
                          start=(c == 0), stop=(c == NFC - 1))
                    o("dve", "tensor_tensor", out=x2[:, g * 512:(g + 1) * 512], in0=pm.v, in1=x2[:, g * 512:(g + 1) * 512], op=ALU.add)
                P.dma(self.xo.tile(i, self.xo.ap[r0:r0 + 128, :]), x2.v)
            P.barrier()
        self.es = top

    def phase_HG(self, l):
        P = self.P
        o = self.op
        prm = self.prm
        top = self.es
        CH = 64
        NCH = L // CH
        with ExitStack() as es:
            self.es = es
            oml = self.sb("oml", [128, 2], F32)
            if l == 0:
                o("pool", "memset", ap=oml.v, constant=1.0)
            else:
                lb0 = self.sb("lb0", [128, 2], F32)
                lb1 = self.sb("lb1", [128, 2], F32)
                for hp in range(2):
                    self.col_load(lb0[:, hp:hp + 1], prm["lb_param"][0, hp * 128:(hp + 1) * 128])
                    self.col_load(lb1[:, hp:hp + 1], prm["lb_param"][1, hp * 128:(hp + 1) * 128])
                o("dve", "tensor_tensor", out=lb0.v, in0=lb0.v, in1=lb1.v, op=ALU.subtract)
                o("act", "activation", out=oml.v, in_=lb0.v, func=AF.Sigmoid)
            gO = self.sb("gO", [128, 64], F32)
            self.bcast_row(gO.v, prm["hg_o_gain"][l])
            hm = self.sb("hm", [128, 2], F32)
            blk = self.sb("blk", [128, 128], F32)
            o("pool", "memset", ap=hm.v, constant=0.0)
            o("pool", "memset", ap=hm[0:64, 0:1], constant=1.0)
            o("pool", "memset", ap=hm[64:128, 1:2], constant=1.0)
            o("pool", "memset", ap=blk.v, constant=0.0)
            o("pool", "memset", ap=blk[0:64, 0:64], constant=1.0)
            o("pool", "memset", ap=blk[64:128, 64:128], constant=1.0)
            msk = self.sb("msk", [128, L], F32)
            o("pool", "memset", ap=msk.v, constant=1.0)
            o("pool", "memset", ap=msk.v.rr("p (c t) -> p c t", t=CH)[:, :, 0:1], constant=0.0)
            A1 = self.sb("A1", [128, L], F32)
            A2 = self.sb("A2", [128, L], F32)
            A3 = self.sb("A3", [128, L], F32)
            A4 = self.sb("A4", [128, L], F32)
            A5 = self.sb("A5", [128, L], F32)
            QTs = [self.sb("QT%d" % p, [128, L], BF16) for p in range(2)]
            KTms = [[self.sb("KTm%d%d" % (p, h), [128, L], BF16) for h in range(2)] for p in range(2)]
            KLs = [self.sb("KL%d" % p, [128, L], BF16) for p in range(2)]
            ELs = [self.sb("EL%d" % p, [128, NCH], F32) for p in range(2)]
            DLs = [self.sb("DL%d" % p, [128, NCH], F32) for p in range(2)]
            EMs = [self.sb("EM%d" % p, [128, NCH], F32) for p in range(2)]
            Ss = [self.sb("S%d" % p, [128, 128], F32) for p in range(2)]
            Sbs = [self.sb("Sb%d" % p, [128, 128], BF16) for p in range(2)]
            Vts = [self.rot("Vt%d" % p, [64, 2, 128], BF16, 2) for p in range(2)]
            Gts = [self.rot("Gt%d" % p, [64, 2, 128], F32, 2) for p in range(2)]
            KLts = [self.rot("KLt%d" % p, [64, 128], BF16, 2) for p in range(2)]
            Wms = [self.rot("Wm%d" % p, [64, 2, 64], BF16, 2) for p in range(2)]
            Ocs = [self.rot("Oc%d" % p, [64, 2, 64], F32, 2) for p in range(2)]
            Ons = [self.sb("On%d" % p, [64, 2, 64], F32) for p in range(2)]
            otmps = [self.sb("otmp%d" % p, [64, 2, 64], F32) for p in range(2)]
            osss = [self.sb("oss%d" % p, [64, 2], F32) for p in range(2)]
            Osts = [self.rot("Ost%d" % p, [64, 2, 128], BF16, 2) for p in range(2)]
            psKs = [self.ps("psK%d" % p, [64, 128], BF16) for p in range(2)]
            psSs = [self.ps("psS%d" % p, [64, 2, 64], F32) for p in range(2)]
            psOs = [self.ps("psO%d" % p, [64, 2, 64], F32) for p in range(2)]
            psDs = [self.ps("psD%d" % p, [128, 128], F32) for p in range(2)]
            fm = self.fmT
            c3 = lambda b: b.v.rr("p (c t) -> p c t", t=CH)
            for hp in range(2):
                QT, KTm, KL, EL, DL, EM, S, Sb = QTs[hp], KTms[hp], KLs[hp], ELs[hp], DLs[hp], EMs[hp], Ss[hp], Sbs[hp]
                P.dma(A1.v, fm.all(fm.ap[hp * 128:(hp + 1) * 128, :]))
                P.dma(A2.v, fm.all(fm.ap[256 + hp * 128:256 + (hp + 1) * 128, :]))
                o("pool", "tensor_scalar", out=A2.v, in0=A2.v, scalar1=oml[:, hp:hp + 1], scalar2=None, op0=ALU.mult)
                o("act", "activation", out=A3.v, in_=A2.v, func=AF.Ln, scale=-1.0, bias=self.onesf[:, 0:1])
                o("dve", "tensor_tensor_scan", out=A4.v, data0=msk.v, data1=A3.v, initial=0.0, op0=ALU.mult, op1=ALU.add)
                B3 = c3(A4)
                o("dve", "tensor_tensor", out=c3(A3), in0=B3, in1=B3[:, :, 31:32].bc([128, NCH, CH]), op=ALU.subtract)
                o("act", "activation", out=A5.v, in_=A3.v, func=AF.Exp)
                o("dve", "scalar_tensor_tensor", out=QT.v, in0=A1.v, scalar=0.125, in1=A5.v, op0=ALU.mult, op1=ALU.mult)
                o("act", "activation", out=A5.v, in_=A3.v, func=AF.Exp, scale=-1.0)
                o("pool", "tensor_tensor", out=A5.v, in0=A5.v, in1=A2.v, op=ALU.mult)
                o("act", "activation", out=KTm[0].v, in_=A5.v, func=AF.Copy, scale=hm[:, 0:1])
                o("dve", "tensor_scalar", out=KTm[1].v, in0=A5.v, scalar1=hm[:, 1:2], scalar2=None, op0=ALU.mult)
                o("dve", "tensor_tensor", out=EL.v.us(2), in0=B3[:, :, 63:64], in1=B3[:, :, 31:32], op=ALU.subtract)
                o("act", "activation", out=EL.v, in_=EL.v, func=AF.Exp)
                o("act", "activation", out=DL.v.us(2), in_=B3[:, :, 63:64], func=AF.Exp)
                o("act", "activation", out=EM.v.us(2), in_=B3[:, :, 31:32], func=AF.Exp)
                o("pool", "tensor_tensor", out=c3(KL), in0=c3(A5), in1=EL.v.us(2).bc([128, NCH, CH]), op=ALU.mult)
                o("pool", "memset", ap=S.v, constant=0.0)
                o("pool", "memset", ap=Sb.v, constant=0.0)

            def step(hp, c):
                QT, KTm, KL, DL, EM, S, Sb = QTs[hp], KTms[hp], KLs[hp], DLs[hp], EMs[hp], Ss[hp], Sbs[hp]
                ti, ci = c // 2, c % 2
                r0 = ti * 128
                cs = slice(c * CH, (c + 1) * CH)
                if ci == 0:
                    P.dma(Vts[hp][ti % 2].v, self.tmb.tile(ti, self.tmb.ap[r0:r0 + 128, B_HGV + hp * 128:B_HGV + (hp + 1) * 128]
                                                          .rearrange("(c s) d -> s c d", s=CH)))
                    P.dma(Gts[hp][ti % 2].v, self.tmf.tile(ti, self.tmf.ap[r0:r0 + 128, F_HGG + hp * 128:F_HGG + (hp + 1) * 128]
                                                          .rearrange("(c s) d -> s c d", s=CH)))
                V = Vts[hp][ti % 2]
                G = Gts[hp][ti % 2]
                pk = psKs[hp]
                o("pe", "transpose", out=pk.v, in_=KL[:, cs], identity=self.identb.v)
                klt = KLts[hp][c % 2]
                o("act", "copy", out=klt.v, in_=pk.v)
                pS = psSs[hp]
                for h in range(2):
                    o("pe", "matmul", out=pS[:, h, :], lhsT=KTm[h][:, cs], rhs=QT[:, cs], start=(h == 0), stop=(h == 1))
                wm = Wms[hp][c % 2]
                o("dve", "tensor_tensor", out=wm.v, in0=pS.v, in1=self.triuf[0:64, 0:64].us(1).bc([64, 2, 64]), op=ALU.mult)
                pO = psOs[hp]
                for h in range(2):
                    hs = slice(h * 64, (h + 1) * 64)
                    o("pe", "matmul", out=pO[:, h, :], lhsT=wm[:, h, :], rhs=V[:, ci, hs], start=(h == 0), stop=False)
                    o("pe", "matmul", out=pO[:, h, :], lhsT=QT[:, cs], rhs=Sb[:, hs], start=False, stop=True)
                pD = psDs[hp]
                o("pe", "matmul", out=pD.v, lhsT=klt.v, rhs=V[:, ci, :], start=True, stop=True)
                o("dve", "scalar_tensor_tensor", out=S.v, in0=S.v, scalar=DL[:, c:c + 1], in1=pD.v, op0=ALU.mult, op1=ALU.add)
                if c + 1 < NCH:
                    o("dve", "scalar_tensor_tensor", out=Sb.v, in0=S.v, scalar=EM[:, c + 1:c + 2], in1=blk.v, op0=ALU.mult, op1=ALU.mult)
                oc = Ocs[hp][c % 2]
                o("act", "copy", out=oc.v, in_=pO.v)
                self.rms_heads(oc.v, 2, 64, gO[0:64, :], Ons[hp].v, otmps[hp].v, osss[hp].v, np_=64)
                ost = Osts[hp][ti % 2]
                o("dve", "tensor_tensor", out=ost[:, ci, :], in0=Ons[hp].v.rr("p h d -> p (h d)"), in1=G[:, ci, :], op=ALU.mult)
                if ci == 1:
                    P.dma(self.mixed.tile(ti, self.mixed.ap[r0:r0 + 128, hp * 128:(hp + 1) * 128]
                                          .rearrange("(c s) d -> s c d", s=CH)), ost.v)

            for c in range(2 * self.ntl):
                for hp in range(2):
                    step(hp, c)
            P.barrier()
        self.es = top

    def phase_ML(self, l):
        P = self.P
        o = self.op
        prm = self.prm
        top = self.es
        with ExitStack() as es:
            self.es = es
            cw = self.sb("cw", [128, 4, 4], F32)
            cbs = self.sb("cbs", [128, 4], F32)
            for ch in range(4):
                for j in range(4):
                    self.col_load(cw[:, ch, j:j + 1], prm["ml_conv_w"][l, j, ch * 128:(ch + 1) * 128])
                self.col_load(cbs[:, ch:ch + 1], prm["ml_conv_b"][l, ch * 128:(ch + 1) * 128])
            ibf = self.sb("ibf", [128, 8], F32)
            self.bcast_row(ibf[:, 0:4], prm["ml_i_bias"][l])
            self.bcast_row(ibf[:, 4:8], prm["ml_f_bias"][l])
            gO = self.sb("gO", [128, 64], F32)
            self.bcast_row(gO.v, prm["ml_o_gain"][l])
            GT = self.sb("GT", [128, NT, 8], F32)
            for q in range(4):
                P.dma(GT[:, q * 8:(q + 1) * 8, :], self.tmf.all(self.tmf.ap[q * 1024:(q + 1) * 1024, F_GT:F_GT + 8]
                                                                .rearrange("(n p) g -> p n g", p=128)))
            LI = self.sb("LI", [128, NT, 4], F32)
            LF = self.sb("LF", [128, NT, 4], F32)
            Fc = self.sb("Fc", [128, NT, 4], F32)
            E = self.sb("E", [128, NT, 4], F32)
            BV = self.sb("BV", [128, NT, 4], F32)
            Gd = self.sb("Gd", [128, NT, 4], F32)
            GdP = self.sb("GdP", [128, NT, 2], F32)
            psG = self.ps("psG", [128, 128], F32)
            o("dve", "tensor_tensor", out=LI.v, in0=GT[:, :, 0:4], in1=ibf[:, 0:4].us(1).bc([128, NT, 4]), op=ALU.add)
            o("dve", "tensor_tensor", out=LF.v, in0=GT[:, :, 4:8], in1=ibf[:, 4:8].us(1).bc([128, NT, 4]), op=ALU.add)
            o("act", "activation", out=LF.v, in_=LF.v, func=AF.Exp, scale=-1.0)
            o("act", "activation", out=LF.v, in_=LF.v, func=AF.Ln, bias=self.onesf[:, 0:1])
            o("dve", "tensor_scalar", out=LF.v, in0=LF.v, scalar1=-1.0, scalar2=None, op0=ALU.mult)
            lf2 = LF.v.rr("p n h -> p (n h)")
            o("pe", "matmul", out=psG.v, lhsT=self.triuf.v, rhs=lf2, start=True, stop=True)
            o("dve", "tensor_copy", out=Fc.v.rr("p n h -> p (n h)"), in_=psG.v)
            o("act", "activation", out=E.v, in_=Fc.v, func=AF.Exp)
            o("dve", "tensor_tensor", out=BV.v, in0=LI.v, in1=Fc.v, op=ALU.subtract)
            o("act", "activation", out=BV.v, in_=BV.v, func=AF.Exp)
            o("pe", "matmul", out=psG.v, lhsT=self.onesf.v, rhs=lf2, start=True, stop=True)
            o("act", "activation", out=Gd.v.rr("p n h -> p (n h)"), in_=psG.v, func=AF.Exp)
            for pr in range(2):
                o("dve", "tensor_copy", out=GdP[0:64, :, pr:pr + 1], in_=Gd[0:64, :, 2 * pr:2 * pr + 1])
                o("dve", "tensor_copy", out=GdP[64:128, :, pr:pr + 1], in_=Gd[64:128, :, 2 * pr + 1:2 * pr + 2])
            QK = [self.sb("QK%d" % ch, [128, L], BF16) for ch in range(4)]
            X = self.rot("X", [128, L], F32, 2)
            Y = self.sb("Y", [128, L], F32)
            fm = self.fmT
            for ch in range(4):
                x = X[ch % 2]
                P.dma(x.v, fm.all(fm.ap[512 + ch * 128:512 + (ch + 1) * 128, :]))
                o("dve", "tensor_scalar", out=Y.v, in0=x.v, scalar1=cw[:, ch, 3:4], scalar2=cbs[:, ch:ch + 1], op0=ALU.mult, op1=ALU.add)
                for sh in (1, 2, 3):
                    o("dve", "scalar_tensor_tensor", out=Y[:, sh:L], in0=x[:, 0:L - sh], scalar=cw[:, ch, 3 - sh:4 - sh],
                      in1=Y[:, sh:L], op0=ALU.mult, op1=ALU.add)
                o("act", "activation", out=QK[ch].v, in_=Y.v, func=AF.Silu)
                if ch >= 2:
                    o("pool", "tensor_scalar", out=QK[ch].v, in0=QK[ch].v, scalar1=0.125, scalar2=None, op0=ALU.mult)
            hm = self.sb("hm", [128, 2], F32)
            blk = self.sb("blk", [128, 130], F32)
            o("pool", "memset", ap=hm.v, constant=0.0)
            o("pool", "memset", ap=hm[0:64, 0:1], constant=1.0)
            o("pool", "memset", ap=hm[64:128, 1:2], constant=1.0)
            o("pool", "memset", ap=blk.v, constant=0.0)
            o("pool", "memset", ap=blk[0:64, 0:65], constant=1.0)
            o("pool", "memset", ap=blk[64:128, 65:130], constant=1.0)
            km = [[self.sb("km%d%d" % (pr, hl), [128, L], BF16) for hl in range(2)] for pr in range(2)]
            for pr in range(2):
                o("dve", "tensor_scalar", out=km[pr][0].v, in0=QK[2 + pr].v, scalar1=hm[:, 0:1], scalar2=None, op0=ALU.mult)
                o("pool", "tensor_scalar", out=km[pr][1].v, in0=QK[2 + pr].v, scalar1=hm[:, 1:2], scalar2=None, op0=ALU.mult)
            C = [self.sb("C%d" % pr, [128, 130], F32) for pr in range(2)]
            Cb = [self.sb("Cb%d" % pr, [128, 130], BF16) for pr in range(2)]
            Tt = self.sb("Tt", [128, 130], F32)
            for pr in range(2):
                o("pool", "memset", ap=C[pr].v, constant=0.0)
                o("pool", "memset", ap=Cb[pr].v, constant=0.0)
            vo = self.rot("vo", [128, 512], F32, 2)
            Vt = self.rot("Vt", [128, 4, 65], BF16, 2)
            kt = self.rot("kt", [128, 128], BF16, 2)
            Wm = self.rot("Wm", [128, 2, 128], BF16, 2)
            dd = self.sb("dd", [128, 4], F32)
            dd2 = self.sb("dd2", [128, 4], F32)
            Ho = self.sb("Ho", [128, 4, 64], F32)
            Hn = self.sb("Hn", [128, 4, 64], F32)
            htmp = self.sb("htmp", [128, 4, 64], F32)
            hss = self.sb("hss", [128, 4], F32)
            Ost = self.rot("Ost", [128, 256], BF16, 2)
            psK = self.rot("psK", [128, 128], BF16, 2, psum=True)
            psS = self.rot("psS", [128, 2, 128], F32, 2, psum=True)
            psO = self.ps("psO", [128, 4, 65], F32)
            psD = self.rot("psD", [128, 130], F32, 2, psum=True)
            for n in range(self.ntl):
                r0 = n * 128
                ts = slice(r0, r0 + 128)
                v_o = vo[n % 2]
                P.dma(v_o.v, self.tmf.tile(n, self.tmf.ap[r0:r0 + 128, F_MV:F_MV + 512]))
                vt = Vt[n % 2]
                o("dve", "tensor_tensor", out=vt[:, :, 0:64], in0=v_o[:, 0:256].rr("p (h d) -> p h d", h=4),
                  in1=BV[:, n, :].us(2).bc([128, 4, 64]), op=ALU.mult)
                o("pool", "tensor_copy", out=vt[:, :, 64:65], in_=BV[:, n, :].us(2))
                for pr in range(2):
                    kq = QK[2 + pr]
                    qq = QK[pr]
                    pk = psK[pr]
                    o("pe", "transpose", out=pk.v, in_=kq[:, ts], identity=self.identb.v)
                    ktt = kt[pr]
                    o("act", "copy", out=ktt.v, in_=pk.v)
                    pS = psS[pr]
                    for hl in range(2):
                        hs = slice(hl * 64, (hl + 1) * 64)
                        o("pe", "matmul", out=pS[:, hl, :], lhsT=km[pr][hl][:, ts], rhs=qq[:, ts], start=(hl == 0), stop=(hl == 1))
                    wm = Wm[pr]
                    o("dve", "tensor_tensor", out=wm.v, in0=pS.v, in1=self.triuf.v.us(1).bc([128, 2, 128]), op=ALU.mult)
                    for hl in range(2):
                        h = 2 * pr + hl
                        hs = slice(hl * 64, (hl + 1) * 64)
                        o("pe", "matmul", out=psO[:, h, :], lhsT=qq[:, ts], rhs=Cb[pr][:, hl * 65:(hl + 1) * 65],
                          start=(h == 0), stop=False)
                        o("pe", "matmul", out=psO[:, h, :], lhsT=wm[:, hl, :], rhs=vt[:, h, :], start=False, stop=True)
                    pD = psD[pr]
                    o("pe", "matmul", out=pD.v, lhsT=ktt.v, rhs=vt[:, 2 * pr:2 * pr + 2, :].rr("p h d -> p (h d)"), start=True, stop=True)
                    o("dve", "tensor_tensor", out=Tt.v, in0=C[pr].v, in1=pD.v, op=ALU.add)
                    o("dve", "tensor_scalar", out=C[pr].v, in0=Tt.v, scalar1=GdP[:, n, pr:pr + 1], scalar2=None, op0=ALU.mult)
                    o("pool", "tensor_tensor", out=Cb[pr].v, in0=C[pr].v, in1=blk.v, op=ALU.mult)
                o("dve", "tensor_tensor", out=dd.v, in0=psO[:, :, 64], in1=E[:, n, :], op=ALU.mult)
                o("dve", "tensor_scalar", out=dd2.v, in0=dd.v, scalar1=1.0, scalar2=None, op0=ALU.max)
                o("dve", "scalar_tensor_tensor", out=dd.v, in0=dd.v, scalar=-1.0, in1=dd2.v, op0=ALU.mult, op1=ALU.max)
                o("dve", "reciprocal", out=dd.v, in_=dd.v)
                o("dve", "tensor_tensor", out=dd.v, in0=dd.v, in1=E[:, n, :], op=ALU.mult)
                o("dve", "tensor_tensor", out=Ho.v, in0=psO[:, :, 0:64], in1=dd.v.us(2).bc([128, 4, 64]), op=ALU.mult)
                self.rms_heads(Ho.v, 4, 64, gO.v, Hn.v, htmp.v, hss.v)
                ost = Ost[n % 2]
                o("dve", "tensor_tensor", out=ost.v, in0=Hn.v.rr("p h d -> p (h d)"), in1=v_o[:, 256:512], op=ALU.mult)
                P.dma(self.mixed.tile(n, self.mixed.ap[r0:r0 + 128, 768:1024]), ost.v)
            P.barrier()
        self.es = top

    def attn_tiles(self, psS_rot, PT_rot, psO, qT, tiles, extra=None):
        o = self.op
        n = len(tiles)

        def pv(idx, PT, vv, ev):
            for h in range(4):
                o("pe", "matmul", out=psO[:, h, :], lhsT=PT[:, h * 128:(h + 1) * 128], rhs=vv,
                  start=(idx == 0 and h == 0), stop=(idx == n - 1))
            if extra is not None:
                for h in range(4):
                    o("pe", "matmul", out=extra[:, h, :], lhsT=PT[:, h * 128:(h + 1) * 128], rhs=ev,
                      start=(idx == 0 and h == 0), stop=(idx == n - 1))

        pend = None
        for idx, (kv, mv, vv, ev) in enumerate(tiles):
            c = self._actr = getattr(self, "_actr", 0) + 1
            pS = psS_rot[c % len(psS_rot)]
            PT = PT_rot[c % len(PT_rot)]
            o("pe", "matmul", out=pS.v, lhsT=kv, rhs=qT, start=True, stop=(mv is None))
            if mv is not None:
                o("pe", "matmul", out=pS.v, lhsT=mv, rhs=self.ident4.v, start=False, stop=True)
            o("act", "activation", out=PT.v, in_=pS.v, func=AF.Exp, scale=0.125)
            if pend is not None:
                pv(*pend)
            pend = (idx, PT, vv, ev)
        pv(*pend)

    def phase_DSA(self, l):
        P = self.P
        o = self.op
        top = self.es
        with ExitStack() as es:
            self.es = es
            kT = self.sb("kT", [64, L], BF16)
            ikT = self.sb("ikT", [32, L], BF16)
            vA = self.sb("vA", [128, NT, 65], BF16)
            pw = self.sb("pw", [128, N_BIS], F32)
            P.dma(kT.v, self.kT3.all(self.kT3.ap[0]))
            P.dma(ikT.v, self.ikT.all())
            for q in range(4):
                P.dma(vA[:, q * 8:(q + 1) * 8, :], self.tmb.all(self.tmb.ap[q * 1024:(q + 1) * 1024, B_DV:B_DV + 65]
                                                                .rearrange("(n p) d -> p n d", p=128)))
            P.dma(pw.v, self.cst["c_pow2"].all())
            qT4 = self.rot("qT4", [64, 512], BF16, 2)
            iq = self.rot("iq", [32, 1024], BF16, 2)
            sg = self.rot("sg", [128, 8], F32, 2)
            Dg = self.rot("Dg", [128, 8, 128], BF16, 2)
            score = self.rot("score", [128, L], F32, 2)
            junk = self.sb("junk", [128, L], BF16)
            mb = self.rot("mb", [128, L], BF16, 2)
            R = self.rot("R", [128, 512], BF16, 3)
            PT = self.rot("PT", [128, 512], BF16, 3)
            am = self.rot("am", [128, 1], F32, 2)
            Wt = self.rot("Wt", [128, N_BIS], F32, 2)
            W2t = self.rot("W2t", [128, N_BIS], F32, 2)
            mid = self.rot("mid", [128, 1], F32, 2)
            cnt = self.sb("cnt", [128, 1], F32)
            st = self.sb("st", [128, 1], F32)
            rden = self.rot("rden", [128, 4], F32, 2)
            ob = self.rot("ob", [128, 4, 64], BF16, 2)
            psY = self.rot("psY", [128, 512], F32, 2, psum=True)
            psC = self.rot("psC", [128, 512], F32, 2, psum=True)
            psS = self.rot("psS", [128, 512], F32, 2, psum=True)
            psO = self.rot("psO", [128, 4, 65], F32, 2, psum=True)
            self._yc = 0

            def st_score(i):
                r0 = i * 128
                S = (i + 1) * 128
                q4 = qT4[i % 2]
                iqt = iq[i % 2]
                sgt = sg[i % 2]
                dg = Dg[i % 2]
                sc = score[i % 2]
                P.dma(q4.v, self.qT3.tile(i, self.qT3.ap[i, :, 0, :]))
                P.dma(iqt.v, self.iqT.tile(i, self.iqT.ap[i]))
                P.dma(sgt.v, self.tmf.tile(i, self.tmf.ap[r0:r0 + 128, F_SGN:F_SGN + 8]))
                o("pool", "tensor_tensor", out=dg.v, in0=self.identb.v.us(1).bc([128, 8, 128]),
                  in1=sgt.v.us(2).bc([128, 8, 128]), op=ALU.mult)
                for j in range(0, S, 512):
                    w = min(512, S - j)
                    pc = psC[(j // 512) % 2]
                    pend = None
                    for h in range(8):
                        yc = self._yc
                        self._yc += 1
                        py = psY[yc % 2]
                        r = R[yc % 3]
                        o("pe", "matmul", out=py[:, 0:w], lhsT=iqt[:, h * 128:(h + 1) * 128], rhs=ikT[:, j:j + w], start=True, stop=True)
                        o("act", "activation", out=r[:, 0:w], in_=py[:, 0:w], func=AF.Relu)
                        if pend is not None:
                            ph, pr = pend
                            o("pe", "matmul", out=pc[:, 0:w], lhsT=dg[:, ph, :], rhs=pr[:, 0:w], start=(ph == 0), stop=False)
                        pend = (h, r)
                    ph, pr = pend
                    o("pe", "matmul", out=pc[:, 0:w], lhsT=dg[:, ph, :], rhs=pr[:, 0:w], start=False, stop=True)
                    o("act", "copy", out=sc[:, j:j + w], in_=pc[:, 0:w])

            def st_bisect(i):
                r0 = i * 128
                S = (i + 1) * 128
                sc = score[i % 2]
                mbt = mb[i % 2]
                a = am[i % 2]
                o("dve", "tensor_reduce", out=a.v, in_=sc[:, 0:S], axis=AX.X, op=ALU.max, apply_absolute_value=True)
                o("dve", "tensor_tensor", out=sc[:, r0:r0 + 128], in0=sc[:, r0:r0 + 128], in1=self.cbf.v, op=ALU.add)
                wt = Wt[i % 2]
                w2 = W2t[i % 2]
                o("dve", "tensor_scalar", out=wt.v, in0=pw.v, scalar1=a[:, 0:1], scalar2=None, op0=ALU.mult)
                o("dve", "tensor_scalar", out=w2.v, in0=wt.v, scalar1=2.0, scalar2=None, op0=ALU.mult)
                md = mid[i % 2]
                o("dve", "memset", ap=md.v, constant=0.0)
                for k in range(N_BIS - 1):
                    o("dve", "tensor_scalar", out=junk[:, 0:S], in0=sc[:, 0:S], scalar1=md[:, 0:1], scalar2=None,
                      op0=ALU.is_ge, op1=ALU.add, accum_out=cnt.v)
                    o("dve", "scalar_tensor_tensor", out=st.v, in0=cnt.v, scalar=float(TOPK), in1=w2[:, k + 1:k + 2],
                      op0=ALU.is_ge, op1=ALU.mult)
                    o("dve", "scalar_tensor_tensor", out=md.v, in0=st.v, scalar=wt[:, k + 1:k + 2], in1=md.v,
                      op0=ALU.subtract, op1=ALU.add)
                o("dve", "tensor_tensor", out=md.v, in0=md.v, in1=wt[:, N_BIS - 1:N_BIS], op=ALU.subtract)
                o("dve", "tensor_scalar", out=mbt[:, 0:S], in0=sc[:, 0:S], scalar1=md[:, 0:1], scalar2=NEG,
                  op0=ALU.is_lt, op1=ALU.mult)

            def st_attn(i):
                r0 = i * 128
                q4 = qT4[i % 2]
                mbt = mb[i % 2]
                pO = psO[i % 2]
                tiles = [(kT[:, kt * 128:(kt + 1) * 128], mbt[:, kt * 128:(kt + 1) * 128], vA[:, kt, :], None) for kt in range(i + 1)]
                self.attn_tiles(psS, PT, pO, q4.v, tiles)
                rd = rden[i % 2]
                o("dve", "reciprocal", out=rd.v, in_=pO[:, :, 64])
                obt = ob[i % 2]
                o("dve", "tensor_tensor", out=obt.v, in0=pO[:, :, 0:64], in1=rd.v.us(2).bc([128, 4, 64]), op=ALU.mult)
                P.dma(self.mixed.tile(i, self.mixed.ap[r0:r0 + 128, 256:512]), obt.v.rr("p h d -> p (h d)"))

            n_t = self.ntl
            st_score(0)
            for i in range(n_t):
                if i + 1 < n_t:
                    st_score(i + 1)
                st_bisect(i)
                st_attn(i)
            P.barrier()
        self.es = top

    def phase_NSA(self, l):
        P = self.P
        o = self.op
        prm = self.prm
        top = self.es
        with ExitStack() as es:
            self.es = es
            kcmpT = self.sb("kcmpT", [64, 256], BF16)
            vcmp = self.sb("vcmp", [128, 2, 65], BF16)
            ovl = self.sb("ovl", [128, 2, 64], BF16)
            P.dma(ovl.v, self.cst["c_overlap"].all(self.cst["c_overlap"].ap.rearrange("(j p) n -> p j n", p=128)))
            with ExitStack() as es2:
                self.es = es2
                kvc = self.sb("kvc", [128, L], F32)
                P.dma(kvc.v, self.fmT.all(self.fmT.ap[1024:1152, :]))
                pe_t = self.sb("pe_t", [32, 128], F32)
                P.dma(pe_t[:, 0:64], View([], prm["nsa_pos_k"][l]))
                P.dma(pe_t[:, 64:128], View([], prm["nsa_pos_v"][l]))
                psP = self.ps("psP", [128, 32], F32)
                o("pe", "transpose", out=psP.v, in_=pe_t.v, identity=self.identf[0:32, 0:32])
                peT = self.sb("peT", [128, 32], F32)
                o("dve", "tensor_copy", out=peT.v, in_=psP.v)
                Ab = self.sb("Ab", [128, L], BF16)
                Bb = self.sb("Bb", [128, L], BF16)
                k3 = kvc.v.rr("p (b r) -> p b r", r=16)
                o("dve", "tensor_tensor", out=Ab.v.rr("p (b r) -> p b r", r=16), in0=k3, in1=peT[:, 0:16].us(1).bc([128, 256, 16]), op=ALU.add)
                o("pool", "tensor_tensor", out=Bb.v.rr("p (b r) -> p b r", r=16), in0=k3, in1=peT[:, 16:32].us(1).bc([128, 256, 16]), op=ALU.add)
                W1 = self.sb("W1", [128, 32, 256], BF16)
                stg = self.rot("w1st", [128, 8, 256], F32, 2)
                sc = 0
                for pp in range(0, 32, 8):
                    sgt = stg[sc % 2]
                    sc += 1
                    P.dma(sgt[0:64], View([], prm["nsa_k_w1"][l].rearrange("(p d) h -> d p h", d=64)[:, pp:pp + 8, :]))
                    P.dma(sgt[64:128], View([], prm["nsa_v_w1"][l].rearrange("(p d) h -> d p h", d=64)[:, pp:pp + 8, :]))
                    o("dve" if (pp // 8) % 2 == 0 else "pool", "tensor_copy", out=W1[:, pp:pp + 8, :], in_=sgt.v)
                w2s = self.sb("w2s", [128, 2, 128], F32)
                P.dma(w2s[:, :, 0:64], View([], prm["nsa_k_w2"][l].rearrange("(c p) d -> p c d", p=128)))
                P.dma(w2s[:, :, 64:128], View([], prm["nsa_v_w2"][l].rearrange("(c p) d -> p c d", p=128)))
                W2c = self.sb("W2c", [128, 2, 128], BF16)
                o("dve", "tensor_copy", out=W2c.v, in_=w2s.v)
                gk0 = self.sb("gk0", [128, 64], F32)
                self.bcast_row(gk0.v, prm["nsa_k_gains"][l, 0])
                hT = [self.sb("hT%d" % s, [128, 2, 256], BF16) for s in range(2)]
                psH = self.rot("psH", [128, 256], F32, 2, psum=True)
                for s in range(2):
                    sp = slice(s * 64, (s + 1) * 64)
                    o("pool", "memset", ap=hT[s].v, constant=0.0)
                    for hc in range(2):
                        ph = psH[hc]
                        for p in range(32):
                            src = Ab if p < 16 else Bb
                            o("pe", "matmul", out=ph[:, 0:255], lhsT=W1[sp, p, hc * 128:(hc + 1) * 128],
                              rhs=src[sp, p:p + 16 * 254 + 1:16], start=(p == 0), stop=(p == 31))
                        o("act", "activation", out=hT[s][:, hc, 0:255], in_=ph[:, 0:255], func=AF.Relu)
                psC = self.rot("psC", [128, 64], F32, 2, psum=True)
                psT = self.ps("psT", [64, 128], BF16)
                kcf = self.sb("kcf", [128, 1, 64], F32)
                kcn = self.sb("kcn", [128, 1, 64], F32)
                kct = self.sb("kct", [128, 1, 64], F32)
                kcs = self.sb("kcs", [128, 1], F32)
                kcb = self.sb("kcb", [128, 64], BF16)
                for jt in range(2):
                    for s in range(2):
                        pc = psC[s]
                        for hc in range(2):
                            o("pe", "matmul", out=pc.v, lhsT=hT[s][:, hc, jt * 128:(jt + 1) * 128], rhs=W2c[:, hc, s * 64:(s + 1) * 64],
                              start=(hc == 0), stop=(hc == 1))
                    o("act", "copy", out=kcf[:, 0, :], in_=psC[0].v)
                    self.rms_heads(kcf.v, 1, 64, gk0.v, kcn.v, kct.v, kcs.v)
                    o("act", "copy", out=kcb.v, in_=kcn[:, 0, :])
                    o("pe", "transpose", out=psT.v, in_=kcb.v, identity=self.identb.v)
                    o("act", "copy", out=kcmpT[:, jt * 128:(jt + 1) * 128], in_=psT.v)
                    o("dve", "tensor_copy", out=vcmp[:, jt, 0:64], in_=psC[1].v)
                    o("pool", "memset", ap=vcmp[:, jt, 64:65], constant=1.0)
                P.barrier()
            self.es = es
            ksT = self.sb("ksT", [64, L], BF16)
            kwT = self.sb("kwT", [64, L], BF16)
            vS = self.sb("vS", [128, NT, 65], BF16)
            vW = self.sb("vW", [128, NT, 65], BF16)
            P.dma(ksT.v, self.kT3.all(self.kT3.ap[1]))
            P.dma(kwT.v, self.kT3.all(self.kT3.ap[2]))
            for q in range(4):
                P.dma(vS[:, q * 8:(q + 1) * 8, :], self.tmb.all(self.tmb.ap[q * 1024:(q + 1) * 1024, B_NVS:B_NVS + 65]
                                                                .rearrange("(n p) d -> p n d", p=128)))
                P.dma(vW[:, q * 8:(q + 1) * 8, :], self.tmb.all(self.tmb.ap[q * 1024:(q + 1) * 1024, B_NVW:B_NVW + 65]
                                                                .rearrange("(n p) d -> p n d", p=128)))
            q2 = self.rot("q2", [64, 2, 512], BF16, 2)
            gt = self.rot("gt", [128, 12], F32, 2)
            cm = self.rot("cm", [128, 256], BF16, 2)
            fbt = self.rot("fbt", [128, 64], F32, 2)
            mb = self.rot("mb", [128, L], BF16, 2)
            PT = self.rot("PT", [128, 512], BF16, 3)
            rc = self.rot("rc", [128, 12], F32, 2)
            tmp2 = self.sb("tmp2", [128, 4, 64], F32)
            impw = self.sb("impw", [128, 4, 64], F32)
            impa = self.sb("impa", [128, 64], F32)
            zz = self.sb("zz", [128, 64], F32)
            m8 = self.sb("m8", [128, 16], F32)
            mbb = self.sb("mbb", [128, 64], BF16)
            acc = self.sb("acc", [128, 4, 64], F32)
            tmp = self.sb("tmp", [128, 4, 64], F32)
            ob = self.rot("ob", [128, 4, 64], BF16, 2)
            psS = self.rot("psS", [128, 512], F32, 2, psum=True)
            psOc = self.rot("psOc", [128, 4, 65], F32, 2, psum=True)
            psI = self.ps("psI", [128, 4, 64], F32)
            psOs = self.ps("psOs", [128, 4, 65], F32)
            psOw = self.ps("psOw", [128, 4, 65], F32)
            def st_cmp(i):
                r0 = i * 128
                S = (i + 1) * 128
                qq = q2[i % 2]
                g = gt[i % 2]
                cmt = cm[i % 2]
                fb = fbt[i % 2]
                mbt = mb[i % 2]
                pOc = psOc[i % 2]
                rcc = rc[i % 2]
                P.dma(qq.v, self.qT3.tile(i, self.qT3.ap[i, :, 1:3, :]))
                P.dma(g.v, self.tmf.tile(i, self.tmf.ap[r0:r0 + 128, F_NG:F_NG + 12]))
                P.dma(cmt.v, self.cst["c_cmpmask"].all(self.cst["c_cmpmask"].ap[r0:r0 + 128, :]))
                P.dma(fb.v, self.cst["c_fb"].all(self.cst["c_fb"].ap[r0:r0 + 128, :]))
                tiles = [(kcmpT[:, jt * 128:(jt + 1) * 128], cmt[:, jt * 128:(jt + 1) * 128], vcmp[:, jt, :], ovl[:, jt, :]) for jt in range(2)]
                self.attn_tiles(psS, PT, pOc, qq[:, 0, :], tiles, extra=psI)
                o("dve", "tensor_scalar", out=rcc[:, 0:4], in0=pOc[:, :, 64], scalar1=1e-30, scalar2=None, op0=ALU.max)
                o("dve", "reciprocal", out=rcc[:, 0:4], in_=rcc[:, 0:4])
                o("dve", "tensor_tensor", out=impw.v, in0=psI.v, in1=rcc[:, 0:4].us(2).bc([128, 4, 64]), op=ALU.mult)
                o("dve", "tensor_reduce", out=impa.v, in_=impw.v.rr("p h n -> p n h"), axis=AX.X, op=ALU.add)
                o("dve", "tensor_tensor", out=impa.v, in0=impa.v, in1=fb.v, op=ALU.add)
                o("dve", "max", out=m8[:, 0:8], in_=impa.v)
                o("dve", "match_replace", out=zz.v, in_to_replace=m8[:, 0:8], in_values=impa.v, imm_value=-3.0e6)
                o("dve", "max", out=m8[:, 8:16], in_=zz.v)
                o("dve", "tensor_scalar", out=mbb.v, in0=impa.v, scalar1=m8[:, 15:16], scalar2=NEG, op0=ALU.is_lt, op1=ALU.mult)
                nb = S // 64
                o("dve", "tensor_copy", out=mbt[:, 0:S].rr("p (b r) -> p b r", r=64), in_=mbb[:, 0:nb].us(2).bc([128, nb, 64]))
                o("dve", "tensor_tensor", out=mbt[:, r0:r0 + 128], in0=mbt[:, r0:r0 + 128], in1=self.cbb.v, op=ALU.add)

            def st_swa(i):
                qq = q2[i % 2]
                tiles = []
                for kt in range(max(0, i - 4), i + 1):
                    mv = self.cbb.v if kt == i else (self.abb.v if kt == i - 4 else None)
                    tiles.append((kwT[:, kt * 128:(kt + 1) * 128], mv, vW[:, kt, :], None))
                self.attn_tiles(psS, PT, psOw, qq[:, 1, :], tiles)

            def st_sel(i):
                qq = q2[i % 2]
                mbt = mb[i % 2]
                tiles = [(ksT[:, kt * 128:(kt + 1) * 128], mbt[:, kt * 128:(kt + 1) * 128], vS[:, kt, :], None) for kt in range(i + 1)]
                self.attn_tiles(psS, PT, psOs, qq[:, 1, :], tiles)

            def st_comb(i):
                r0 = i * 128
                g = gt[i % 2]
                pOc = psOc[i % 2]
                rcc = rc[i % 2]
                o("dve", "reciprocal", out=rcc[:, 4:8], in_=psOs[:, :, 64])
                o("dve", "reciprocal", out=rcc[:, 8:12], in_=psOw[:, :, 64])
                o("dve", "tensor_tensor", out=rcc.v, in0=rcc.v, in1=g.v, op=ALU.mult)
                o("dve", "tensor_tensor", out=acc.v, in0=pOc[:, :, 0:64], in1=rcc[:, 0:4].us(2).bc([128, 4, 64]), op=ALU.mult)
                o("dve", "tensor_tensor", out=tmp.v, in0=psOs[:, :, 0:64], in1=rcc[:, 4:8].us(2).bc([128, 4, 64]), op=ALU.mult)
                o("pool", "tensor_tensor", out=acc.v, in0=acc.v, in1=tmp.v, op=ALU.add)
                o("dve", "tensor_tensor", out=tmp2.v, in0=psOw[:, :, 0:64], in1=rcc[:, 8:12].us(2).bc([128, 4, 64]), op=ALU.mult)
                obt = ob[i % 2]
                o("pool", "tensor_tensor", out=obt.v, in0=acc.v, in1=tmp2.v, op=ALU.add)
                P.dma(self.mixed.tile(i, self.mixed.ap[r0:r0 + 128, 512:768]), obt.v.rr("p h d -> p (h d)"))

            n_t = self.ntl
            st_cmp(0)
            for i in range(n_t):
                st_swa(i)
                if i + 1 < n_t:
                    st_cmp(i + 1)
                st_sel(i)
                st_comb(i)
            P.barrier()
        self.es = top


_CACHE = {}


def make_in_map(inputs, b, consts):
    m = {"x": np.ascontiguousarray(inputs["x"][b], dtype=np.float32),
         "mem": np.ascontiguousarray(inputs["mem"][b], dtype=np.float32)}
    for k in PARAM_SHAPES:
        m[k] = np.ascontiguousarray(inputs[k], dtype=np.float32)
    m.update(consts)
    return m


def kernel(**inputs):
    if "kb" not in _CACHE:
        kb = KB()
        kb.build()
        _CACHE["kb"] = kb
    kb = _CACHE["kb"]
    consts = host_consts()
    in_maps = [make_in_map(inputs, c % 4, consts) for c in range(8)]
    res = run_bass_kernel_spmd(kb.nc, in_maps, core_ids=list(range(8)))
    out = np.stack([np.asarray(res.results[c]["out"], dtype=np.float32) for c in range(4)], axis=0)
    return out
```

```python
import numpy as np
import ml_dtypes
from contextlib import ExitStack
import concourse.bass as bass
import concourse.mybir as mybir
from concourse.bass_utils import run_bass_kernel_spmd

F32 = mybir.dt.float32
BF16 = mybir.dt.bfloat16
ALU = mybir.AluOpType
AF = mybir.ActivationFunctionType
AX = mybir.AxisListType

D = 1024
L = 4096
NT = L // 128
NMEM = 256
DFF = 2816
NFC = DFF // 128
INC = 3388
EPS = 1e-6
NEG = -30000.0
THETA = 500000.0
N_BIS = 14
TOPK = 256


class Buf:
    __slots__ = ("name", "last_w", "readers", "t", "psum")

    def __init__(self, name, t=None, psum=False):
        self.name = name
        self.last_w = None
        self.readers = []
        self.t = t
        self.psum = psum

    def __getitem__(self, k):
        return View([self], self.t[k])

    @property
    def v(self):
        return View([self], self.t[:])


class View:
    __slots__ = ("bufs", "ap")

    def __init__(self, bufs, ap):
        self.bufs = bufs
        self.ap = ap

    def __getitem__(self, k):
        return View(self.bufs, self.ap[k])

    def rr(self, s, **kw):
        return View(self.bufs, self.ap.rearrange(s, **kw))

    def bc(self, shape):
        return View(self.bufs, self.ap.to_broadcast(list(shape)))

    def us(self, ax):
        return View(self.bufs, self.ap.unsqueeze(ax))

    @property
    def shape(self):
        return self.ap.shape


class Op:
    __slots__ = ("idx", "eng", "emit", "deps", "signal", "ticket", "is_dma", "dsem", "dval")


WRITE_KEYS = ("out", "accum_out", "ap")


class Prog:
    ENGS = ("pe", "act", "dve", "pool", "sp")
    NRING = 24

    def __init__(self, nc):
        self.nc = nc
        self.ops = []
        self.ndma = 0
        self.lastdma = {}
        self.lasteng = {}

    def add(self, eng, emit, reads=(), writes=(), dma=False):
        op = Op()
        op.idx = len(self.ops)
        op.eng = eng
        op.emit = emit
        op.is_dma = dma
        op.signal = False
        op.ticket = None
        deps = {}
        for b in reads:
            if b.last_w is not None:
                deps[b.last_w.idx] = (b.last_w, True)
            if b.psum:
                for r in b.readers:
                    if r.eng != eng and r.idx not in deps:
                        deps[r.idx] = (r, False)
        for b in writes:
            if b.last_w is not None and b.last_w.idx not in deps:
                deps[b.last_w.idx] = (b.last_w, False)
            for r in b.readers:
                if r.idx not in deps:
                    deps[r.idx] = (r, False)
        op.deps = list(deps.values())
        for b in reads:
            if not dma:
                b.readers = [r for r in b.readers if r.is_dma or r.eng != eng]
            b.readers.append(op)
        for b in writes:
            b.last_w = op
            b.readers = []
        if dma:
            k = self.ndma
            self.ndma += 1
            op.dsem = k % self.NRING
            op.dval = 16 * (k // self.NRING + 1)
            self.lastdma[op.dsem] = op
        if emit is not None:
            self.lasteng[eng] = op
        self.ops.append(op)
        return op

    def op(self, eng, name, r=(), w=(), **kw):
        reads = list(r)
        writes = list(w)
        args = {}
        for k, v in kw.items():
            if isinstance(v, View):
                if k in WRITE_KEYS:
                    writes.extend(v.bufs)
                else:
                    reads.extend(v.bufs)
                args[k] = v.ap
            else:
                args[k] = v

        if name == "matmul":
            args.setdefault("skip_group_check", True)

        def emit(e, name=name, args=args):
            return getattr(e, name)(**args)

        return self.add(eng, emit, reads, writes, dma=(name == "dma_start"))

    def dma(self, out, in_, eng="sp"):
        return self.op(eng, "dma_start", out=out, in_=in_)

    def barrier(self):
        prev = list(self.lasteng.values()) + list(self.lastdma.values())
        for e in self.ENGS:
            op = self.add(e, None)
            for o in prev:
                if o.eng != e or o.is_dma:
                    op.deps.append((o, True))

    def emit_all(self, sems, ring):
        nc = self.nc
        engobj = {"pe": nc.tensor, "act": nc.scalar, "dve": nc.vector, "pool": nc.gpsimd, "sp": nc.sync}
        for op in self.ops:
            for d, raw in op.deps:
                if d.is_dma:
                    continue
                if d.eng == op.eng and d.eng == "pe":
                    continue
                d.signal = True
        cnt = {e: 0 for e in self.ENGS}
        for op in self.ops:
            if op.signal and not op.is_dma:
                cnt[op.eng] += 1
                op.ticket = cnt[op.eng]
        seen = {e: {} for e in self.ENGS}
        nw = 0
        for op in self.ops:
            e = engobj[op.eng]
            sn = seen[op.eng]
            need = {}
            for d, raw in op.deps:
                if d.is_dma:
                    key = ("r", d.dsem)
                    val = d.dval
                else:
                    if d.eng == op.eng and d.eng == "pe":
                        continue
                    key = ("e", d.eng)
                    val = d.ticket
                if sn.get(key, 0) >= val:
                    continue
                if need.get(key, 0) < val:
                    need[key] = val
            if op.is_dma and op.dval > 16:
                key = ("r", op.dsem)
                val = op.dval - 16
                if sn.get(key, 0) < val and need.get(key, 0) < val:
                    need[key] = val
            for key, val in need.items():
                s = ring[key[1]] if key[0] == "r" else sems[key[1]]
                e.wait_ge(s, val)
                sn[key] = val
                nw += 1
            if op.emit is None:
                continue
            ins = op.emit(e)
            if op.is_dma:
                ins.then_inc(ring[op.dsem], 16)
            elif op.signal:
                ins.then_inc(sems[op.eng], 1)
        return nw, cnt


class DT:
    def __init__(self, name, ap, ntile=1):
        self.name = name
        self.ap = ap
        self.bufs = [Buf("%s_%d" % (name, i)) for i in range(ntile)]

    def tile(self, i, ap):
        return View([self.bufs[i]], ap)

    def all(self, ap=None):
        return View(list(self.bufs), self.ap if ap is None else ap)


def host_consts():
    bf = ml_dtypes.bfloat16
    c = {}
    eye = np.eye(128, dtype=np.float32)
    c["c_identb"] = eye.astype(bf)
    c["c_identf"] = eye
    c["c_ident4"] = np.tile(eye, (1, 4)).astype(bf)
    s = np.arange(128)[:, None]
    t = np.arange(128)[None, :]
    triu = (s <= t).astype(np.float32)
    c["c_triuf"] = triu
    c["c_triub"] = triu.astype(bf)
    tt = np.arange(128)[:, None]
    ss = np.arange(128)[None, :]
    c["c_cb"] = np.where(ss <= tt, 0.0, NEG).astype(np.float32)
    c["c_ab"] = np.where(ss > tt, 0.0, NEG).astype(np.float32)
    pos = np.arange(L, dtype=np.float32)
    for nm, rd in (("c_cs64", 16), ("c_cs32", 8)):
        half = rd // 2
        inv = (np.float32(THETA) ** (-np.arange(half, dtype=np.float32) * np.float32(2.0) / np.float32(rd))).astype(np.float32)
        ang = (pos[:, None] * inv[None, :]).astype(np.float32)
        cs = np.concatenate([np.cos(ang), np.sin(ang)], axis=1).astype(np.float32)
        c[nm] = np.ascontiguousarray(cs.reshape(NT, 128, rd).transpose(1, 0, 2))
    j = np.arange(256)[None, :]
    tq = np.arange(L)[:, None]
    vis = (16 * j + 31 <= tq) & (j < 255)
    c["c_cmpmask"] = np.where(vis, 0.0, NEG).astype(bf)
    n = np.arange(64)[None, :]
    cur = tq // 64
    forced = (n == 0) | (n == cur) | (n == cur - 1)
    fb = np.where(forced, 1.0e6 + 64.0 * n, np.where(n > cur, -1.0e6, 0.0))
    c["c_fb"] = fb.astype(np.float32)
    st_c = np.arange(255) * 16
    st_s = np.arange(64) * 64
    ov = ((st_c[:, None] < st_s[None, :] + 64) & (st_c[:, None] + 32 > st_s[None, :])).astype(np.float32)
    ovp = np.zeros((256, 64), np.float32)
    ovp[:255] = ov
    c["c_overlap"] = ovp.astype(bf)
    c["c_pow2"] = np.tile((1.0078125 * 2.0 ** -np.arange(N_BIS, dtype=np.float32))[None, :], (128, 1)).astype(np.float32)
    return c


PARAM_SHAPES = {
    "lb_param": (2, 256), "norm_mix": (2, 1024), "w_in": (2, 1024, INC), "w_out": (2, 1024, 1024),
    "hg_o_gain": (2, 64), "dsa_kv_gain": (2, 128), "dsa_w_uk": (2, 128, 64), "dsa_w_uv": (2, 128, 64),
    "dsa_q_gain": (2, 64), "dsa_k_gain": (2, 64), "dsa_idxk_gain": (2, 32),
    "nsa_pos_k": (2, 32, 64), "nsa_pos_v": (2, 32, 64), "nsa_k_w1": (2, 2048, 256), "nsa_k_w2": (2, 256, 64),
    "nsa_v_w1": (2, 2048, 256), "nsa_v_w2": (2, 256, 64), "nsa_q_gain": (2, 64), "nsa_k_gains": (2, 3, 64),
    "ml_conv_w": (2, 4, 512), "ml_conv_b": (2, 512), "ml_i_bias": (2, 4), "ml_f_bias": (2, 4), "ml_o_gain": (2, 64),
    "norm_xa": (2, 1024), "norm_mem": (2, 1024), "xa_wq": (2, 1024, 256), "xa_wkv": (2, 1024, 512),
    "xa_wo": (2, 256, 1024), "xa_q_gain": (2, 64), "xa_k_gain": (2, 64), "norm_ffn": (2, 1024),
    "ffn_w13": (2, 1024, 2 * DFF), "ffn_w2": (2, DFF, 1024),
}


HG0, DS0, NS0, ML0 = 0, 1024, 1704, 2356
TM_GROUPS = [(512, 512), (1024, 512), (1536, 168), (1704, 256), (2088, 268), (2868, 512), (3380, 8)]
TM_OFF = [0, 768, 1280, 512, 1448, 1716, 2228]
TMW = 2236
FM_COLS = [0, 128, 256, 384, 2356, 2484, 2612, 2740, 1960]
C_HGI, C_HGG, C_DQ, C_CKV, C_IQ, C_IK, C_IW = 0, 256, 768, 1024, 1152, 1408, 1440
C_NQ, C_KS, C_VS, C_KW, C_VW, C_NG = 512, 1448, 1512, 1576, 1640, 1704
C_MV, C_OG, C_IG, C_FG = 1716, 1972, 2228, 2232
B_HGV, B_DV, B_NVS, B_NVW, TMBW = 0, 256, 321, 386, 452
F_HGG, F_MV, F_OG, F_SGN, F_NG, F_GT, TMFW = 0, 256, 512, 768, 776, 788, 796


class KB:
    def __init__(self, dump=(), layers=(0, 1), phases=None, ntl=NT):
        self.ntl = ntl
        import os
        self.alvl = int(os.environ.get("KDBG_A", "99"))
        self.dump = set(dump)
        self.layers = layers
        self.phases = phases
        nc = bass.Bass("TRN2", target_bir_lowering=False)
        self.nc = nc
        self.P = Prog(nc)
        self.es = None
        self.din = {}

    def sb(self, name, shape, dt):
        t = self.es.enter_context(self.nc.sbuf_tensor(name + "_%d" % self.uid(), list(shape), dt))
        return Buf(name, t)

    def ps(self, name, shape, dt):
        nel = 2048 // (4 if dt == F32 else 2)
        full = self.es.enter_context(self.nc.psum_tensor(name + "_%d" % self.uid(), [128, nel], dt))
        n = 1
        for d in shape[1:]:
            n *= d
        assert n <= nel, (name, shape)
        ap = full[0:shape[0], 0:n]
        if len(shape) == 3:
            ap = ap.rearrange("p (a b) -> p a b", a=shape[1])
        return Buf(name, ap, psum=True)

    def uid(self):
        self._uid = getattr(self, "_uid", 0) + 1
        return self._uid

    def rot(self, name, shape, dt, n, psum=False):
        return [(self.ps if psum else self.sb)("%s%d" % (name, i), shape, dt) for i in range(n)]

    def dram_in(self, name, shape, dt):
        ap = self.nc.dram_tensor(name, list(shape), dt, kind="ExternalInput").ap()
        d = DT(name, ap, 1)
        self.din[name] = d
        return d

    def scr(self, name, shape, dt, ntile=1):
        kind = "ExternalOutput" if name in self.dump else "Internal"
        ap = self.nc.dram_tensor(name, list(shape), dt, kind=kind).ap()
        return DT(name, ap, ntile)

    def op(self, eng, name, **kw):
        return self.P.op(eng, name, **kw)

    def rms_heads(self, X, H, Dh, gain, out, tmp, ssq, np_=128, gain_full=False):
        o = self.op
        o("pool", "tensor_tensor", out=tmp, in0=X, in1=X, op=ALU.mult)
        o("dve", "tensor_reduce", out=ssq, in_=tmp, axis=AX.X, op=ALU.add)
        o("act", "activation", out=ssq, in_=ssq, func=AF.Sqrt, scale=1.0 / Dh, bias=self.epsb[0:np_, 0:1])
        o("dve", "reciprocal", out=ssq, in_=ssq)
        o("dve", "tensor_tensor", out=out, in0=X, in1=ssq.us(2).bc([np_, H, Dh]), op=ALU.mult)
        if gain is not None:
            o("pool", "tensor_tensor", out=out, in0=out, in1=(gain if gain_full else gain.us(1).bc([np_, H, Dh])), op=ALU.mult)

    def rope(self, X, H, hf, cs, out, t1, t2):
        o = self.op
        Dh = X.shape[2]
        cos = cs[:, 0:hf].us(1).bc([128, H, hf])
        sin = cs[:, hf:2 * hf].us(1).bc([128, H, hf])
        x1 = X[:, :, 0:hf]
        x2 = X[:, :, hf:2 * hf]
        o("pool", "tensor_copy", out=out[:, :, 2 * hf:Dh], in_=X[:, :, 2 * hf:Dh])
        o("pool", "tensor_tensor", out=t1, in0=x1, in1=cos, op=ALU.mult)
        o("pool", "tensor_tensor", out=t2, in0=x2, in1=sin, op=ALU.mult)
        o("pool", "tensor_tensor", out=out[:, :, 0:hf], in0=t1, in1=t2, op=ALU.subtract)
        o("dve", "tensor_tensor", out=t1, in0=x2, in1=cos, op=ALU.mult)
        o("dve", "tensor_tensor", out=t2, in0=x1, in1=sin, op=ALU.mult)
        o("dve", "tensor_tensor", out=out[:, :, hf:2 * hf], in0=t1, in1=t2, op=ALU.add)

    def build(self):
        nc = self.nc
        P = self.P
        o = self.op
        self.x_in = self.dram_in("x", [L, D], F32)
        self.x_in.bufs = [Buf("xin%d" % i) for i in range(NT)]
        self.mem_in = self.dram_in("mem", [NMEM, D], F32)
        self.prm = {k: self.dram_in(k, list(s), F32).ap for k, s in PARAM_SHAPES.items()}
        hc = host_consts()
        self.cst = {}
        for k, v in hc.items():
            self.cst[k] = self.dram_in(k, list(v.shape), BF16 if v.dtype != np.float32 else F32)
        kind = "ExternalOutput"
        self.xo = DT("out", nc.dram_tensor("out", [L, D], F32, kind=kind).ap(), NT)
        self.fmT = self.scr("fmT", [1152, L], F32, NT)
        self.tmb = self.scr("tmb", [L, TMBW], BF16, NT)
        self.tmf = self.scr("tmf", [L, TMFW], F32, NT)
        self.kT3 = self.scr("kT3", [3, 64, L], BF16, NT)
        self.ikT = self.scr("ikT", [32, L], BF16, NT)
        self.qT3 = self.scr("qT3", [NT, 64, 3, 512], BF16, NT)
        self.iqT = self.scr("iqT", [NT, 32, 1024], BF16, NT)
        self.mixed = self.scr("mixed", [L, D], BF16, NT)

        with ExitStack() as top:
            self.es = top
            sems = {e: top.enter_context(nc.semaphore("s_" + e)) for e in P.ENGS}
            ring = [top.enter_context(nc.semaphore("r%d" % i)) for i in range(P.NRING)]
            self.identb = self.sb("identb", [128, 128], BF16)
            self.identf = self.sb("identf", [128, 128], F32)
            self.ident4 = self.sb("ident4", [128, 512], BF16)
            self.triuf = self.sb("triuf", [128, 128], F32)
            self.triub = self.sb("triub", [128, 128], BF16)
            self.cbf = self.sb("cbf", [128, 128], F32)
            self.cbb = self.sb("cbb", [128, 128], BF16)
            self.abb = self.sb("abb", [128, 128], BF16)
            self.cs64 = self.sb("cs64", [128, NT, 16], F32)
            self.cs32 = self.sb("cs32", [128, NT, 8], F32)
            self.epsb = self.sb("epsb", [128, 1], F32)
            self.onesf = self.sb("onesf", [128, 128], F32)
            tmpf = self.sb("tmpf", [128, 128], F32)
            for nm, dst in (("c_identb", self.identb), ("c_identf", self.identf), ("c_ident4", self.ident4),
                            ("c_triuf", self.triuf), ("c_triub", self.triub), ("c_cb", self.cbf)):
                P.dma(dst.v, self.cst[nm].all())
            P.dma(tmpf.v, self.cst["c_ab"].all())
            P.dma(self.cs64.v, self.cst["c_cs64"].all())
            P.dma(self.cs32.v, self.cst["c_cs32"].all())
            o("dve", "tensor_copy", out=self.cbb.v, in_=self.cbf.v)
            o("dve", "tensor_copy", out=self.abb.v, in_=tmpf.v)
            o("pool", "memset", ap=self.epsb.v, constant=EPS)
            o("pool", "memset", ap=self.onesf.v, constant=1.0)
            P.barrier()
            for l in self.layers:
                xsrc = self.x_in if l == 0 else self.xo
                ph = self.phases
                if ph is None or "A" in ph:
                    self.phase_A(l, xsrc)
                if ph is None or "HG" in ph:
                    self.phase_HG(l)
                if ph is None or "ML" in ph:
                    self.phase_ML(l)
                if ph is None or "DSA" in ph:
                    self.phase_DSA(l)
                if ph is None or "NSA" in ph:
                    self.phase_NSA(l)
                if ph is None or "C" in ph:
                    self.phase_C(l, xsrc)
            P.barrier()
            self.stats = P.emit_all(sems, ring)
        return nc

    def load_weight_bf16(self, dst, src_ap, nk, ncols, gain=None, chunk=512, stage=None):
        P = self.P
        o = self.op
        src = src_ap.rearrange("(c p) n -> p c n", p=128)
        engs = ["dve", "pool", "act"]
        ei = 0
        j = 0
        for c0 in range(0, ncols, chunk):
            w = min(chunk, ncols - c0)
            for k0 in range(0, nk, 4):
                k1 = min(nk, k0 + 4)
                st = stage[j % len(stage)]
                j += 1
                P.dma(st[:, 0:k1 - k0, 0:w], View([], src[:, k0:k1, c0:c0 + w]))
                for k in range(k0, k1):
                    e = engs[ei % 3]
                    ei += 1
                    if gain is None:
                        if e == "act":
                            o(e, "copy", out=dst[:, k, c0:c0 + w], in_=st[:, k - k0, 0:w])
                        else:
                            o(e, "tensor_copy", out=dst[:, k, c0:c0 + w], in_=st[:, k - k0, 0:w])
                    else:
                        if e == "act":
                            o(e, "activation", out=dst[:, k, c0:c0 + w], in_=st[:, k - k0, 0:w], func=AF.Copy,
                              scale=gain[:, k:k + 1])
                        else:
                            o(e, "tensor_scalar", out=dst[:, k, c0:c0 + w], in0=st[:, k - k0, 0:w],
                              scalar1=gain[:, k:k + 1], scalar2=None, op0=ALU.mult)

    def load_gain_cols(self, dst, vec_ap, nk):
        self.P.dma(dst.v, View([], vec_ap.rearrange("(c p) -> p c", p=128)))

    def bcast_row(self, dst_view, vec_ap):
        np_ = dst_view.shape[0]
        self.P.dma(dst_view, View([], vec_ap.partition_broadcast(np_)))

    def col_load(self, dst_view, vec_ap):
        self.P.dma(dst_view, View([], vec_ap.unsqueeze(1)))

    def phase_A(self, l, xsrc):
        P = self.P
        o = self.op
        prm = self.prm
        top = self.es
        with ExitStack() as es:
            self.es = es
            Wb = self.sb("Wb", [128, 8, INC], BF16)
            with ExitStack() as es2:
                self.es = es2
                stage = self.rot("wst", [128, 4, 512], F32, 3)
                self.load_weight_bf16(Wb, prm["w_in"][l], 8, INC, stage=stage)
                P.barrier()
            self.es = es
            gmix = self.sb("gmix", [128, D], F32)
            self.bcast_row(gmix.v, prm["norm_mix"][l])
            g8 = self.sb("g8", [128, 8, 64], F32)
            for hh in range(4):
                self.bcast_row(g8[:, hh, :], prm["nsa_q_gain"][l])
                self.bcast_row(g8[:, 4 + hh, :], prm["dsa_q_gain"][l])
            g3 = self.sb("g3", [128, 3, 64], F32)
            self.bcast_row(g3[:, 0, :], prm["dsa_k_gain"][l])
            self.bcast_row(g3[:, 1, :], prm["nsa_k_gains"][l, 1])
            self.bcast_row(g3[:, 2, :], prm["nsa_k_gains"][l, 2])
            K3 = self.sb("K3", [128, 3, 64], F32)
            K3n = self.sb("K3n", [128, 3, 64], F32)
            K3t = self.sb("K3t", [128, 3, 64], F32)
            gkv = self.sb("gkv", [128, 128], F32)
            self.bcast_row(gkv.v, prm["dsa_kv_gain"][l])
            gik = self.sb("gik", [128, 32], F32)
            self.bcast_row(gik.v, prm["dsa_idxk_gain"][l])
            wst = self.sb("wukv_st", [128, 128], F32)
            wukv = self.sb("wukv", [128, 128], BF16)
            P.dma(wst[:, 0:64], View([], prm["dsa_w_uk"][l]))
            P.dma(wst[:, 64:128], View([], prm["dsa_w_uv"][l]))
            o("dve", "tensor_copy", out=wukv.v, in_=wst.v)

            xt = self.rot("xt", [128, D], F32, 2)
            junk = self.sb("junk", [128, D], F32)
            ssx = self.rot("ssx", [128, 1], F32, 2)
            hb = self.rot("hb", [128, D], BF16, 2)
            hT = self.rot("hT", [128, 8, 128], BF16, 2)
            ct = self.rot("ct", [128, TMW], F32, 2)
            fm = self.rot("fm", [128, 9, 128], F32, 2)
            tmbS = self.rot("tmbS", [128, TMBW], BF16, 2)
            tmfS = self.rot("tmfS", [128, TMFW], F32, 2)
            kS = self.rot("kS", [64, 3, 128], BF16, 2)
            ikS = self.rot("ikS", [32, 128], BF16, 2)
            qS = self.rot("qS", [64, 3, 512], BF16, 2)
            iqS = self.rot("iqS", [32, 1024], BF16, 2)
            wk = self.sb("wk", [128, 8, 64], F32)
            wk2 = self.sb("wk2", [128, 8, 64], F32)
            t1 = self.sb("t1", [128, 8, 8], F32)
            t2 = self.sb("t2", [128, 8, 8], F32)
            ssq = self.sb("ssq", [128, 8], F32)
            qb = self.sb("qb", [128, 12, 64], BF16)
            kb = self.sb("kb", [128, 3, 64], BF16)
            iqb = self.sb("iqb", [128, 8, 32], BF16)
            ikb = self.sb("ikb", [128, 32], BF16)
            ckvb = self.sb("ckvb", [128, 128], BF16)
            ckvT = self.sb("ckvT", [128, 128], BF16)
            kvf = self.sb("kvf", [128, 128], F32)
            iwa = self.sb("iwa", [128, 8], F32)

            psT = self.ps("psT", [128, 8, 128], BF16)
            psA = self.rot("psA", [128, 512], F32, 2, psum=True)
            psB = self.rot("psB", [128, 4, 128], F32, 2, psum=True)
            psX = self.ps("psX", [128, 8, 128], BF16)
            psY = self.ps("psY", [128, 8, 128], BF16)
            psZ = self.ps("psZ", [128, 8, 128], BF16)

            IWS = float(8 ** -0.5 * 32 ** -0.5)

            def sA(i):
                x_t = xt[i % 2]
                hbt = hb[i % 2]
                hTt = hT[i % 2]
                c = ct[i % 2]
                f = fm[i % 2]
                r0 = i * 128
                if self.alvl < 1:
                    return
                P.dma(x_t.v, xsrc.tile(i, xsrc.ap[r0:r0 + 128, :]))
                ss = ssx[i % 2]
                o("act", "activation", out=junk.v, in_=x_t.v, func=AF.Square, accum_out=ss.v)
                o("act", "activation", out=ss.v, in_=ss.v, func=AF.Sqrt, scale=1.0 / D, bias=self.epsb[:, 0:1])
                o("dve", "reciprocal", out=ss.v, in_=ss.v)
                o("dve", "scalar_tensor_tensor", out=hbt.v, in0=x_t.v, scalar=ss[:, 0:1], in1=gmix.v,
                  op0=ALU.mult, op1=ALU.mult)
                for k in range(8):
                    o("pe", "transpose", out=psT[:, k, :], in_=hbt[:, k * 128:(k + 1) * 128], identity=self.identb.v)
                o("act", "copy", out=hTt.v, in_=psT.v)
                for gi, (c0, w) in enumerate(TM_GROUPS):
                    pa = psA[gi % 2]
                    for k in range(8):
                        o("pe", "matmul", out=pa[:, 0:w], lhsT=hTt[:, k, :], rhs=Wb[:, k, c0:c0 + w],
                          start=(k == 0), stop=(k == 7))
                    off = TM_OFF[gi]
                    if gi % 2 == 0:
                        o("dve", "tensor_copy", out=c[:, off:off + w], in_=pa[:, 0:w])
                    else:
                        o("act", "copy", out=c[:, off:off + w], in_=pa[:, 0:w])
                for ci, c0 in enumerate(FM_COLS):
                    pb = psB[(ci // 4) % 2]
                    for k in range(8):
                        o("pe", "matmul", out=pb[:, ci % 4, :], lhsT=Wb[:, k, c0:c0 + 128], rhs=hTt[:, k, :],
                          start=(k == 0 and ci % 4 == 0), stop=(k == 7))
                    if ci in (1,):
                        o("act", "activation", out=f[:, 0:2, :], in_=pb[:, 0:2, :], func=AF.Silu)
                    elif ci in (3,):
                        o("act", "activation", out=f[:, 2:4, :], in_=pb[:, 2:4, :], func=AF.Sigmoid, scale=-1.0)
                    elif ci == 7:
                        o("dve", "tensor_copy", out=f[:, 4:8, :], in_=pb.v)
                    elif ci == 8:
                        o("dve", "tensor_copy", out=f[:, 8, :], in_=pb[:, 0, :])
                P.dma(self.fmT.tile(i, self.fmT.ap.rearrange("(c p) t -> p c t", p=128)[:, :, r0:r0 + 128]), f.v)


            def sB(i):
                r0 = i * 128
                c = ct[i % 2]
                if self.alvl < 2:
                    return
                tb = tmbS[i % 2]
                tf = tmfS[i % 2]
                cs64 = self.cs64[:, i, :]
                cs32 = self.cs32[:, i, :]
                o("pool", "tensor_copy", out=tb[:, B_HGV:B_HGV + 256], in_=c[:, C_HGI:C_HGI + 256])
                o("act", "activation", out=tf[:, F_HGG:F_HGG + 256], in_=c[:, C_HGG:C_HGG + 256], func=AF.Silu)
                o("pool", "tensor_copy", out=tf[:, F_MV:F_MV + 256], in_=c[:, C_MV:C_MV + 256])
                o("act", "activation", out=tf[:, F_OG:F_OG + 256], in_=c[:, C_OG:C_OG + 256], func=AF.Sigmoid)
                o("pool", "tensor_copy", out=tf[:, F_GT:F_GT + 8], in_=c[:, C_IG:C_IG + 8])
                o("act", "activation", out=tf[:, F_NG:F_NG + 12], in_=c[:, C_NG:C_NG + 12], func=AF.Sigmoid)
                self.rms_heads(c[:, C_NQ:C_NQ + 512].rr("p (h d) -> p h d", h=8), 8, 64, g8.v, wk.v, wk2.v, ssq.v, gain_full=True)
                o("act", "copy", out=qb[:, 0:4, :], in_=wk[:, 0:4, :])
                self.rope(wk.v, 8, 8, cs64, qb[:, 4:12, :], t1.v, t2.v)
                if self.alvl < 3:
                    return
                self.rms_heads(c[:, C_CKV:C_CKV + 128].rr("p (h d) -> p h d", h=1), 1, 128, gkv.v,
                               wk2[:, 0:2, :].rr("p a b -> p (a b)").rr("p (h d) -> p h d", h=1),
                               wk2[:, 2:4, :].rr("p a b -> p (a b)").rr("p (h d) -> p h d", h=1), ssq[:, 0:1])
                o("act", "copy", out=ckvb.v, in_=wk2[:, 0:2, :].rr("p a b -> p (a b)"))
                o("pe", "transpose", out=psT[:, 0, :], in_=ckvb.v, identity=self.identb.v)
                o("act", "copy", out=ckvT.v, in_=psT[:, 0, :])
                pa = psA[1]
                o("pe", "matmul", out=pa[:, 0:128], lhsT=ckvT.v, rhs=wukv.v, start=True, stop=True)
                o("act", "copy", out=kvf.v, in_=pa[:, 0:128])
                o("pool", "tensor_copy", out=tb[:, B_DV:B_DV + 64], in_=kvf[:, 64:128])
                o("pool", "memset", ap=tb[:, B_DV + 64:B_DV + 65], constant=1.0)
                o("act", "copy", out=K3[:, 0, :], in_=kvf[:, 0:64])
                o("dve", "tensor_copy", out=K3[:, 1, :], in_=c[:, C_KS:C_KS + 64])
                o("pool", "tensor_copy", out=K3[:, 2, :], in_=c[:, C_KW:C_KW + 64])
                self.rms_heads(K3.v, 3, 64, g3.v, K3n.v, K3t.v, ssq[:, 1:4], gain_full=True)
                self.rope(K3n.v, 3, 8, cs64, kb.v, t1[:, 0:3, :], t2[:, 0:3, :])
                o("pool", "tensor_copy", out=tb[:, B_NVS:B_NVS + 64], in_=c[:, C_VS:C_VS + 64])
                o("pool", "memset", ap=tb[:, B_NVS + 64:B_NVS + 65], constant=1.0)
                o("pool", "tensor_copy", out=tb[:, B_NVW:B_NVW + 64], in_=c[:, C_VW:C_VW + 64])
                o("pool", "memset", ap=tb[:, B_NVW + 64:B_NVW + 66], constant=1.0)
                if self.alvl < 4:
                    return
                o("dve", "tensor_scalar", out=iwa.v, in0=c[:, C_IW:C_IW + 8], scalar1=IWS, scalar2=None, op0=ALU.mult)
                o("dve", "scalar_tensor_tensor", out=iwa.v, in0=iwa.v, scalar=-1.0, in1=iwa.v, op0=ALU.mult, op1=ALU.max)
                o("act", "activation", out=tf[:, F_SGN:F_SGN + 8], in_=c[:, C_IW:C_IW + 8], func=AF.Sign)
                IQ = wk[:, 0:4, :].rr("p a b -> p (a b)").rr("p (h d) -> p h d", h=8)
                IQ2 = wk[:, 4:8, :].rr("p a b -> p (a b)").rr("p (h d) -> p h d", h=8)
                self.rope(c[:, C_IQ:C_IQ + 256].rr("p (h d) -> p h d", h=8), 8, 4, cs32, IQ, t1[:, :, 0:4], t2[:, :, 0:4])
                o("dve", "tensor_tensor", out=iqb.v, in0=IQ, in1=iwa.v.us(2).bc([128, 8, 32]), op=ALU.mult)
                IK = wk2[:, 0:1, 0:32]
                self.rms_heads(c[:, C_IK:C_IK + 32].rr("p (h d) -> p h d", h=1), 1, 32, gik.v, IK, wk2[:, 1:2, 0:32], ssq[:, 4:5])
                self.rope(IK, 1, 4, cs32, ikb.v.rr("p (h d) -> p h d", h=1), t1[:, 0:1, 0:4], t2[:, 0:1, 0:4])
                if self.alvl < 5:
                    return
                import os
                B = int(os.environ.get("KDBG_B", "99"))
                q_s = qS[i % 2]
                k_s = kS[i % 2]
                for h in range(4):
                    o("pe", "transpose", out=psX[0:64, h, :], in_=qb[:, 8 + h, :], identity=self.identb.v)
                if B >= 1:
                    for j in range(3):
                        o("pe", "transpose", out=psX[0:64, 4 + j, :], in_=kb[:, j, :], identity=self.identb.v)
                if B >= 2:
                    o("pe", "transpose", out=psX[0:32, 7, :], in_=ikb.v, identity=self.identb.v)
                o("act", "copy", out=q_s[:, 0, :], in_=psX[0:64, 0:4, :].rr("p h t -> p (h t)"))
                if B >= 1:
                    o("dve", "tensor_copy", out=k_s.v, in_=psX[0:64, 4:7, :])
                if B >= 2:
                    o("dve", "tensor_copy", out=ikS[i % 2].v, in_=psX[0:32, 7, :])
                if B >= 3:
                    for h in range(8):
                        o("pe", "transpose", out=psY[0:64, h, :], in_=qb[:, h, :], identity=self.identb.v)
                    o("act", "copy", out=q_s[:, 1:3, :].rr("p a n -> p (a n)"), in_=psY[0:64, :, :].rr("p h t -> p (h t)"))
                if B >= 4:
                    for h in range(8):
                        o("pe", "transpose", out=psZ[0:32, h, :], in_=iqb[:, h, :], identity=self.identb.v)
                    o("dve", "tensor_copy", out=iqS[i % 2].v, in_=psZ[0:32, :, :].rr("p h t -> p (h t)"))
                if self.alvl < 6:
                    return
                P.dma(self.tmb.tile(i, self.tmb.ap[r0:r0 + 128, :]), tb.v)
                if self.alvl < 7:
                    return
                P.dma(self.tmf.tile(i, self.tmf.ap[r0:r0 + 128, :]), tf.v)
                if self.alvl < 8:
                    return
                P.dma(self.kT3.tile(i, self.kT3.ap.rearrange("k p t -> p k t")[:, :, r0:r0 + 128]), k_s.v)
                if self.alvl < 9:
                    return
                P.dma(self.ikT.tile(i, self.ikT.ap[:, r0:r0 + 128]), ikS[i % 2].v)
                P.dma(self.qT3.tile(i, self.qT3.ap[i]), q_s.v)
                P.dma(self.iqT.tile(i, self.iqT.ap[i]), iqS[i % 2].v)
            sA(0)
            for i in range(NT):
                if i + 1 < NT:
                    sA(i + 1)
                sB(i)
            P.barrier()
        self.es = top

    def phase_C(self, l, xsrc):
        self.phase_C1(l, xsrc)
        self.phase_C2(l)

    def phase_C1(self, l, xsrc):
        P = self.P
        o = self.op
        prm = self.prm
        top = self.es
        with ExitStack() as es:
            self.es = es
            Wo = self.sb("Wo", [128, 8, D], BF16)
            Wq = self.sb("Wq", [128, 8, 256], BF16)
            Wxo = self.sb("Wxo", [128, 2, D], BF16)
            xkT = self.sb("xkT", [64, 4, NMEM], BF16)
            xv = self.sb("xv", [128, 2, 4, 65], BF16)
            gq = self.sb("gq", [128, 64], F32)
            self.bcast_row(gq.v, prm["xa_q_gain"][l])
            gxa = self.sb("gxa", [128, D], F32)
            self.bcast_row(gxa.v, prm["norm_xa"][l])
            with ExitStack() as es2:
                self.es = es2
                stage = self.rot("wst", [128, 4, 512], F32, 3)
                self.load_weight_bf16(Wo, prm["w_out"][l], 8, D, stage=stage)
                self.load_weight_bf16(Wq, prm["xa_wq"][l], 8, 256, stage=stage)
                self.load_weight_bf16(Wxo, prm["xa_wo"][l], 2, D, stage=stage)
                Wkv = self.sb("Wkv", [128, 8, 512], BF16)
                self.load_weight_bf16(Wkv, prm["xa_wkv"][l], 8, 512, stage=stage)
                gmem = self.sb("gmem", [128, D], F32)
                self.bcast_row(gmem.v, prm["norm_mem"][l])
                gk = self.sb("gk", [128, 64], F32)
                self.bcast_row(gk.v, prm["xa_k_gain"][l])
                mt = self.sb("mt", [128, D], F32)
                mj = self.sb("mj", [128, D], BF16)
                mss = self.sb("mss", [128, 1], F32)
                mb = self.sb("mb", [128, D], BF16)
                mT = self.sb("mT", [128, 8, 128], BF16)
                kvf = self.sb("kvf", [128, 512], F32)
                kn = self.sb("kn", [128, 4, 64], F32)
                ktmp = self.sb("ktmp", [128, 4, 64], F32)
                kss = self.sb("kss", [128, 4], F32)
                knb = self.sb("knb", [128, 4, 64], BF16)
                pT = self.ps("pT", [128, 8, 128], BF16)
                pK = self.ps("pK", [128, 512], F32)
                for m in range(2):
                    P.dma(mt.v, self.mem_in.all(self.mem_in.ap[m * 128:(m + 1) * 128, :]))
                    o("act", "activation", out=mj.v, in_=mt.v, func=AF.Square, accum_out=mss.v)
                    o("act", "activation", out=mss.v, in_=mss.v, func=AF.Sqrt, scale=1.0 / D, bias=self.epsb[:, 0:1])
                    o("dve", "reciprocal", out=mss.v, in_=mss.v)
                    o("dve", "scalar_tensor_tensor", out=mb.v, in0=mt.v, scalar=mss[:, 0:1], in1=gmem.v,
                      op0=ALU.mult, op1=ALU.mult)
                    for k in range(8):
                        o("pe", "transpose", out=pT[:, k, :], in_=mb[:, k * 128:(k + 1) * 128], identity=self.identb.v)
                    o("act", "copy", out=mT.v, in_=pT.v)
                    for k in range(8):
                        o("pe", "matmul", out=pK.v, lhsT=mT[:, k, :], rhs=Wkv[:, k, :], start=(k == 0), stop=(k == 7))
                    o("act", "copy", out=kvf.v, in_=pK.v)
                    self.rms_heads(kvf[:, 0:256].rr("p (h d) -> p h d", h=4), 4, 64, gk.v, kn.v, ktmp.v, kss.v)
                    o("act", "copy", out=knb.v, in_=kn.v)
                    for h in range(4):
                        o("pe", "transpose", out=pT[0:64, h, :], in_=knb[:, h, :], identity=self.identb.v)
                    o("act", "copy", out=xkT[:, :, m * 128:(m + 1) * 128], in_=pT[0:64, 0:4, :])
                    o("dve", "tensor_copy", out=xv[:, m, :, 0:64], in_=kvf[:, 256:512].rr("p (h d) -> p h d", h=4))
                    o("pool", "memset", ap=xv[:, m, :, 64:65], constant=1.0)
                P.barrier()
            self.es = es
            xt = self.rot("xt", [128, D], F32, 2)
            mxb = self.rot("mxb", [128, D], BF16, 2)
            mxT = self.sb("mxT", [128, 8, 128], BF16)
            x1r = self.rot("x1", [128, D], F32, 3)
            junk = self.sb("junk", [128, D], BF16)
            ss = self.sb("ss", [128, 1], F32)
            hb = self.sb("hb", [128, D], BF16)
            hT = self.sb("hT", [128, 8, 128], BF16)
            qfr = self.rot("qf", [128, 4, 64], F32, 2)
            qn = self.sb("qn", [128, 4, 64], F32)
            qtmp = self.sb("qtmp", [128, 4, 64], F32)
            qss = self.sb("qss", [128, 4], F32)
            qnb = self.sb("qnb", [128, 4, 64], BF16)
            qT4 = self.sb("qT4", [64, 4, 128], BF16)
            PT = self.rot("PT", [128, 512], BF16, 2)
            rden = self.sb("rden", [128, 4], F32)
            ob = self.sb("ob", [128, 4, 64], BF16)
            oT = self.sb("oT", [128, 2, 128], BF16)
            psTa = self.ps("psTa", [128, 8, 128], BF16)
            psTb = self.ps("psTb", [128, 8, 128], BF16)
            psM = self.rot("psM", [128, 512], F32, 2, psum=True)
            psW = self.ps("psW", [128, 512], F32)
            psS = self.rot("psS", [128, 512], F32, 2, psum=True)
            psO = self.ps("psO", [128, 4, 65], F32)

            def s1(i):
                r0 = i * 128
                x_t = xt[i % 2]
                mx = mxb[i % 2]
                x1 = x1r[i % 3]
                P.dma(x_t.v, xsrc.tile(i, xsrc.ap[r0:r0 + 128, :]))
                P.dma(mx.v, self.mixed.tile(i, self.mixed.ap[r0:r0 + 128, :]))
                for k in range(8):
                    o("pe", "transpose", out=psTa[:, k, :], in_=mx[:, k * 128:(k + 1) * 128], identity=self.identb.v)
                o("act", "copy", out=mxT.v, in_=psTa.v)
                for g in range(2):
                    pm = psM[g]
                    for k in range(8):
                        o("pe", "matmul", out=pm.v, lhsT=mxT[:, k, :], rhs=Wo[:, k, g * 512:(g + 1) * 512],
                          start=(k == 0), stop=(k == 7))
                    o("dve", "tensor_tensor", out=x1[:, g * 512:(g + 1) * 512], in0=pm.v, in1=x_t[:, g * 512:(g + 1) * 512], op=ALU.add)
                o("act", "activation", out=junk.v, in_=x1.v, func=AF.Square, accum_out=ss.v)
                o("act", "activation", out=ss.v, in_=ss.v, func=AF.Sqrt, scale=1.0 / D, bias=self.epsb[:, 0:1])
                o("dve", "reciprocal", out=ss.v, in_=ss.v)
                o("dve", "scalar_tensor_tensor", out=hb.v, in0=x1.v, scalar=ss[:, 0:1], in1=gxa.v, op0=ALU.mult, op1=ALU.mult)
                for k in range(8):
                    o("pe", "transpose", out=psTa[:, k, :], in_=hb[:, k * 128:(k + 1) * 128], identity=self.identb.v)
                o("act", "copy", out=hT.v, in_=psTa.v)
                pm = psM[0]
                for k in range(8):
                    o("pe", "matmul", out=pm[:, 0:256], lhsT=hT[:, k, :], rhs=Wq[:, k, :], start=(k == 0), stop=(k == 7))
                o("act", "copy", out=qfr[i % 2].v.rr("p h d -> p (h d)"), in_=pm[:, 0:256])

            def s2(i):
                r0 = i * 128
                x1 = x1r[i % 3]
                self.rms_heads(qfr[i % 2].v, 4, 64, gq.v, qn.v, qtmp.v, qss.v)
                o("act", "copy", out=qnb.v, in_=qn.v)
                for h in range(4):
                    o("pe", "transpose", out=psTb[0:64, h, :], in_=qnb[:, h, :], identity=self.identb.v)
                o("act", "copy", out=qT4.v, in_=psTb[0:64, 0:4, :])
                for m in range(2):
                    pss = psS[m]
                    for h in range(4):
                        o("pe", "matmul", out=pss[:, h * 128:(h + 1) * 128], lhsT=xkT[:, h, m * 128:(m + 1) * 128],
                          rhs=qT4[:, h, :], start=(h == 0), stop=(h == 3))
                    pt = PT[m]
                    o("act", "activation", out=pt.v, in_=pss.v, func=AF.Exp, scale=0.125)
                    for h in range(4):
                        o("pe", "matmul", out=psO[:, h, :], lhsT=pt[:, h * 128:(h + 1) * 128], rhs=xv[:, m, h, :],
                          start=(m == 0 and h == 0), stop=(m == 1))
                o("dve", "reciprocal", out=rden.v, in_=psO[:, :, 64])
                o("dve", "tensor_tensor", out=ob.v, in0=psO[:, :, 0:64], in1=rden.v.us(2).bc([128, 4, 64]), op=ALU.mult)
                for k in range(2):
                    o("pe", "transpose", out=psTb[:, 4 + k, :], in_=ob.v.rr("p h d -> p (h d)")[:, k * 128:(k + 1) * 128], identity=self.identb.v)
                o("act", "copy", out=oT.v, in_=psTb[:, 4:6, :])
                for g in range(2):
                    for k in range(2):
                        o("pe", "matmul", out=psW.v, lhsT=oT[:, k, :], rhs=Wxo[:, k, g * 512:(g + 1) * 512],
                          start=(k == 0), stop=(k == 1))
                    o("dve", "tensor_tensor", out=x1[:, g * 512:(g + 1) * 512], in0=psW.v, in1=x1[:, g * 512:(g + 1) * 512], op=ALU.add)
                P.dma(self.xo.tile(i, self.xo.ap[r0:r0 + 128, :]), x1.v)

            n_t = self.ntl
            s1(0)
            for i in range(n_t):
                if i + 1 < n_t:
                    s1(i + 1)
                s2(i)
            P.barrier()
        self.es = top

    def phase_C2(self, l):
        P = self.P
        o = self.op
        prm = self.prm
        top = self.es
        with ExitStack() as es:
            self.es = es
            W13 = self.sb("W13", [128, 8, 2 * DFF], BF16)
            W2 = self.sb("W2", [128, NFC, D], BF16)
            gff = self.sb("gff", [128, D], F32)
            self.bcast_row(gff.v, prm["norm_ffn"][l])
            with ExitStack() as es2:
                self.es = es2
                stage = self.rot("wst", [128, 4, 512], F32, 3)
                self.load_weight_bf16(W13, prm["ffn_w13"][l], 8, 2 * DFF, stage=stage)
                self.load_weight_bf16(W2, prm["ffn_w2"][l], NFC, D, stage=stage)
                P.barrier()
            self.es = es
            xt = self.rot("xt", [128, D], F32, 3)
            junk = self.sb("junk", [128, D], BF16)
            ss = self.sb("ss", [128, 1], F32)
            hb = self.sb("hb", [128, D], BF16)
            hTr = self.rot("hT", [128, 8, 128], BF16, 2)
            gT = self.sb("gT", [128, NFC, 128], BF16)
            sa = self.rot("sa", [128, 4, 128], F32, 2)
            psT = self.ps("psT", [128, 8, 128], BF16)
            psM = self.rot("psM", [128, 512], F32, 2, psum=True)
            psF = self.rot("psF", [128, 4, 128], F32, 4, psum=True)

            def front(i):
                r0 = i * 128
                x2 = xt[i % 3]
                P.dma(x2.v, self.xo.tile(i, self.xo.ap[r0:r0 + 128, :]))
                o("act", "activation", out=junk.v, in_=x2.v, func=AF.Square, accum_out=ss.v)
                o("act", "activation", out=ss.v, in_=ss.v, func=AF.Sqrt, scale=1.0 / D, bias=self.epsb[:, 0:1])
                o("dve", "reciprocal", out=ss.v, in_=ss.v)
                o("dve", "scalar_tensor_tensor", out=hb.v, in0=x2.v, scalar=ss[:, 0:1], in1=gff.v, op0=ALU.mult, op1=ALU.mult)
                for k in range(8):
                    o("pe", "transpose", out=psT[:, k, :], in_=hb[:, k * 128:(k + 1) * 128], identity=self.identb.v)
                o("act", "copy", out=hTr[i % 2].v, in_=psT.v)

            def ab(i):
                hT = hTr[i % 2]
                for gi, g0 in enumerate(range(0, NFC, 4)):
                    g1 = min(NFC, g0 + 4)
                    n = g1 - g0
                    pfs = (psF[(gi % 2) * 2], psF[(gi % 2) * 2 + 1])
                    for half in range(2):
                        pf = pfs[half]
                        for c in range(g0, g1):
                            col = half * DFF + c * 128
                            for k in range(8):
                                o("pe", "matmul", out=pf[:, c - g0, :], lhsT=W13[:, k, col:col + 128], rhs=hT[:, k, :],
                                  start=(k == 0 and c == g0), stop=(k == 7))
                    s_a = sa[gi % 2]
                    o("act", "activation", out=s_a[:, 0:n, :], in_=pfs[0][:, 0:n, :], func=AF.Silu)
                    o("dve", "tensor_tensor", out=gT[:, g0:g1, :], in0=pfs[1][:, 0:n, :], in1=s_a[:, 0:n, :], op=ALU.mult)

            def w2(i):
                r0 = i * 128
                x2 = xt[i % 3]
                for g in range(2):
                    pm = psM[g]
                    for c in range(NFC):
                        o("pe", "matmul", out=pm.v, lhsT=gT[:, c, :], rhs=W2[:, c, g * 512:(g + 1) * 512],
                          start=(c == 0), stop=(c == NFC - 1))
                    o("dve", "tensor_tensor", out=x2[:, g * 512:(g + 1) * 512], in0=pm.v, in1=x2[:, g * 512:(g + 1) * 512], op=ALU.add)
                P.dma(self.xo.tile(i, self.xo.ap[r0:r0 + 128, :]), x2.v)

            n_t = self.ntl
            front(0)
            for i in range(n_t):
                ab(i)
                if i + 1 < n_t:
                    front(i + 1)
                w2(i)
            P.barrier()
        self.es = top

    def phase_HG(self, l):
        P = self.P
        o = self.op
        prm = self.prm
        top = self.es
        CH = 64
        NCH = L // CH
        with ExitStack() as es:
            self.es = es
            oml = self.sb("oml", [128, 2], F32)
            if l == 0:
                o("pool", "memset", ap=oml.v, constant=1.0)
            else:
                lb0 = self.sb("lb0", [128, 2], F32)
                lb1 = self.sb("lb1", [128, 2], F32)
                for hp in range(2):
                    self.col_load(lb0[:, hp:hp + 1], prm["lb_param"][0, hp * 128:(hp + 1) * 128])
                    self.col_load(lb1[:, hp:hp + 1], prm["lb_param"][1, hp * 128:(hp + 1) * 128])
                o("dve", "tensor_tensor", out=lb0.v, in0=lb0.v, in1=lb1.v, op=ALU.subtract)
                o("act", "activation", out=oml.v, in_=lb0.v, func=AF.Sigmoid)
            gO = self.sb("gO", [128, 64], F32)
            self.bcast_row(gO.v, prm["hg_o_gain"][l])
            hm = self.sb("hm", [128, 2], F32)
            blk = self.sb("blk", [128, 128], F32)
            o("pool", "memset", ap=hm.v, constant=0.0)
            o("pool", "memset", ap=hm[0:64, 0:1], constant=1.0)
            o("pool", "memset", ap=hm[64:128, 1:2], constant=1.0)
            o("pool", "memset", ap=blk.v, constant=0.0)
            o("pool", "memset", ap=blk[0:64, 0:64], constant=1.0)
            o("pool", "memset", ap=blk[64:128, 64:128], constant=1.0)
            msk = self.sb("msk", [128, L], F32)
            o("pool", "memset", ap=msk.v, constant=1.0)
            o("pool", "memset", ap=msk.v.rr("p (c t) -> p c t", t=CH)[:, :, 0:1], constant=0.0)
            A1 = self.sb("A1", [128, L], F32)
            A2 = self.sb("A2", [128, L], F32)
            A3 = self.sb("A3", [128, L], F32)
            A4 = self.sb("A4", [128, L], F32)
            A5 = self.sb("A5", [128, L], F32)
            QTs = [self.sb("QT%d" % p, [128, L], BF16) for p in range(2)]
            KTms = [[self.sb("KTm%d%d" % (p, h), [128, L], BF16) for h in range(2)] for p in range(2)]
            KLs = [self.sb("KL%d" % p, [128, L], BF16) for p in range(2)]
            ELs = [self.sb("EL%d" % p, [128, NCH], F32) for p in range(2)]
            DLs = [self.sb("DL%d" % p, [128, NCH], F32) for p in range(2)]
            EMs = [self.sb("EM%d" % p, [128, NCH], F32) for p in range(2)]
            Ss = [self.sb("S%d" % p, [128, 128], F32) for p in range(2)]
            Sbs = [self.sb("Sb%d" % p, [128, 128], BF16) for p in range(2)]
            Vts = [self.rot("Vt%d" % p, [64, 2, 128], BF16, 2) for p in range(2)]
            Gts = [self.rot("Gt%d" % p, [64, 2, 128], F32, 2) for p in range(2)]
            KLts = [self.rot("KLt%d" % p, [64, 128], BF16, 2) for p in range(2)]
            Wms = [self.rot("Wm%d" % p, [64, 2, 64], BF16, 2) for p in range(2)]
            Ocs = [self.rot("Oc%d" % p, [64, 2, 64], F32, 2) for p in range(2)]
            Ons = [self.sb("On%d" % p, [64, 2, 64], F32) for p in range(2)]
            otmps = [self.sb("otmp%d" % p, [64, 2, 64], F32) for p in range(2)]
            osss = [self.sb("oss%d" % p, [64, 2], F32) for p in range(2)]
            Osts = [self.rot("Ost%d" % p, [64, 2, 128], BF16, 2) for p in range(2)]
            psKs = [self.ps("psK%d" % p, [64, 128], BF16) for p in range(2)]
            psSs = [self.ps("psS%d" % p, [64, 2, 64], F32) for p in range(2)]
            psOs = [self.ps("psO%d" % p, [64, 2, 64], F32) for p in range(2)]
            psDs = [self.ps("psD%d" % p, [128, 128], F32) for p in range(2)]
            fm = self.fmT
            c3 = lambda b: b.v.rr("p (c t) -> p c t", t=CH)
            for hp in range(2):
                QT, KTm, KL, EL, DL, EM, S, Sb = QTs[hp], KTms[hp], KLs[hp], ELs[hp], DLs[hp], EMs[hp], Ss[hp], Sbs[hp]
                P.dma(A1.v, fm.all(fm.ap[hp * 128:(hp + 1) * 128, :]))
                P.dma(A2.v, fm.all(fm.ap[256 + hp * 128:256 + (hp + 1) * 128, :]))
                o("pool", "tensor_scalar", out=A2.v, in0=A2.v, scalar1=oml[:, hp:hp + 1], scalar2=None, op0=ALU.mult)
                o("act", "activation", out=A3.v, in_=A2.v, func=AF.Ln, scale=-1.0, bias=self.onesf[:, 0:1])
                o("dve", "tensor_tensor_scan", out=A4.v, data0=msk.v, data1=A3.v, initial=0.0, op0=ALU.mult, op1=ALU.add)
                B3 = c3(A4)
                o("dve", "tensor_tensor", out=c3(A3), in0=B3, in1=B3[:, :, 31:32].bc([128, NCH, CH]), op=ALU.subtract)
                o("act", "activation", out=A5.v, in_=A3.v, func=AF.Exp)
                o("dve", "scalar_tensor_tensor", out=QT.v, in0=A1.v, scalar=0.125, in1=A5.v, op0=ALU.mult, op1=ALU.mult)
                o("act", "activation", out=A5.v, in_=A3.v, func=AF.Exp, scale=-1.0)
                o("pool", "tensor_tensor", out=A5.v, in0=A5.v, in1=A2.v, op=ALU.mult)
                o("act", "activation", out=KTm[0].v, in_=A5.v, func=AF.Copy, scale=hm[:, 0:1])
                o("dve", "tensor_scalar", out=KTm[1].v, in0=A5.v, scalar1=hm[:, 1:2], scalar2=None, op0=ALU.mult)
                o("dve", "tensor_tensor", out=EL.v.us(2), in0=B3[:, :, 63:64], in1=B3[:, :, 31:32], op=ALU.subtract)
                o("act", "activation", out=EL.v, in_=EL.v, func=AF.Exp)
                o("act", "activation", out=DL.v.us(2), in_=B3[:, :, 63:64], func=AF.Exp)
                o("act", "activation", out=EM.v.us(2), in_=B3[:, :, 31:32], func=AF.Exp)
                o("pool", "tensor_tensor", out=c3(KL), in0=c3(A5), in1=EL.v.us(2).bc([128, NCH, CH]), op=ALU.mult)
                o("pool", "memset", ap=S.v, constant=0.0)
                o("pool", "memset", ap=Sb.v, constant=0.0)

            def step(hp, c):
                QT, KTm, KL, DL, EM, S, Sb = QTs[hp], KTms[hp], KLs[hp], DLs[hp], EMs[hp], Ss[hp], Sbs[hp]
                ti, ci = c // 2, c % 2
                r0 = ti * 128
                cs = slice(c * CH, (c + 1) * CH)
                if ci == 0:
                    P.dma(Vts[hp][ti % 2].v, self.tmb.tile(ti, self.tmb.ap[r0:r0 + 128, B_HGV + hp * 128:B_HGV + (hp + 1) * 128]
                                                          .rearrange("(c s) d -> s c d", s=CH)))
                    P.dma(Gts[hp][ti % 2].v, self.tmf.tile(ti, self.tmf.ap[r0:r0 + 128, F_HGG + hp * 128:F_HGG + (hp + 1) * 128]
                                                          .rearrange("(c s) d -> s c d", s=CH)))
                V = Vts[hp][ti % 2]
                G = Gts[hp][ti % 2]
                pk = psKs[hp]
                o("pe", "transpose", out=pk.v, in_=KL[:, cs], identity=self.identb.v)
                klt = KLts[hp][c % 2]
                o("act", "copy", out=klt.v, in_=pk.v)
                pS = psSs[hp]
                for h in range(2):
                    o("pe", "matmul", out=pS[:, h, :], lhsT=KTm[h][:, cs], rhs=QT[:, cs], start=(h == 0), stop=(h == 1))
                wm = Wms[hp][c % 2]
                o("dve", "tensor_tensor", out=wm.v, in0=pS.v, in1=self.triuf[0:64, 0:64].us(1).bc([64, 2, 64]), op=ALU.mult)
                pO = psOs[hp]
                for h in range(2):
                    hs = slice(h * 64, (h + 1) * 64)
                    o("pe", "matmul", out=pO[:, h, :], lhsT=wm[:, h, :], rhs=V[:, ci, hs], start=(h == 0), stop=False)
                    o("pe", "matmul", out=pO[:, h, :], lhsT=QT[:, cs], rhs=Sb[:, hs], start=False, stop=True)
                pD = psDs[hp]
                o("pe", "matmul", out=pD.v, lhsT=klt.v, rhs=V[:, ci, :], start=True, stop=True)
                o("dve", "scalar_tensor_tensor", out=S.v, in0=S.v, scalar=DL[:, c:c + 1], in1=pD.v, op0=ALU.mult, op1=ALU.add)
                if c + 1 < NCH:
                    o("dve", "scalar_tensor_tensor", out=Sb.v, in0=S.v, scalar=EM[:, c + 1:c + 2], in1=blk.v, op0=ALU.mult, op1=ALU.mult)
                oc = Ocs[hp][c % 2]
                o("act", "copy", out=oc.v, in_=pO.v)
                self.rms_heads(oc.v, 2, 64, gO[0:64, :], Ons[hp].v, otmps[hp].v, osss[hp].v, np_=64)
                ost = Osts[hp][ti % 2]
                o("dve", "tensor_tensor", out=ost[:, ci, :], in0=Ons[hp].v.rr("p h d -> p (h d)"), in1=G[:, ci, :], op=ALU.mult)
                if ci == 1:
                    P.dma(self.mixed.tile(ti, self.mixed.ap[r0:r0 + 128, hp * 128:(hp + 1) * 128]
                                          .rearrange("(c s) d -> s c d", s=CH)), ost.v)

            for c in range(2 * self.ntl):
                for hp in range(2):
                    step(hp, c)
            P.barrier()
        self.es = top

    def phase_ML(self, l):
        P = self.P
        o = self.op
        prm = self.prm
        top = self.es
        with ExitStack() as es:
            self.es = es
            cw = self.sb("cw", [128, 4, 4], F32)
            cbs = self.sb("cbs", [128, 4], F32)
            for ch in range(4):
                for j in range(4):
                    self.col_load(cw[:, ch, j:j + 1], prm["ml_conv_w"][l, j, ch * 128:(ch + 1) * 128])
                self.col_load(cbs[:, ch:ch + 1], prm["ml_conv_b"][l, ch * 128:(ch + 1) * 128])
            ibf = self.sb("ibf", [128, 8], F32)
            self.bcast_row(ibf[:, 0:4], prm["ml_i_bias"][l])
            self.bcast_row(ibf[:, 4:8], prm["ml_f_bias"][l])
            gO = self.sb("gO", [128, 64], F32)
            self.bcast_row(gO.v, prm["ml_o_gain"][l])
            GT = self.sb("GT", [128, NT, 8], F32)
            for q in range(4):
                P.dma(GT[:, q * 8:(q + 1) * 8, :], self.tmf.all(self.tmf.ap[q * 1024:(q + 1) * 1024, F_GT:F_GT + 8]
                                                                .rearrange("(n p) g -> p n g", p=128)))
            LI = self.sb("LI", [128, NT, 4], F32)
            LF = self.sb("LF", [128, NT, 4], F32)
            Fc = self.sb("Fc", [128, NT, 4], F32)
            E = self.sb("E", [128, NT, 4], F32)
            BV = self.sb("BV", [128, NT, 4], F32)
            Gd = self.sb("Gd", [128, NT, 4], F32)
            GdP = self.sb("GdP", [128, NT, 2], F32)
            psG = self.ps("psG", [128, 128], F32)
            o("dve", "tensor_tensor", out=LI.v, in0=GT[:, :, 0:4], in1=ibf[:, 0:4].us(1).bc([128, NT, 4]), op=ALU.add)
            o("dve", "tensor_tensor", out=LF.v, in0=GT[:, :, 4:8], in1=ibf[:, 4:8].us(1).bc([128, NT, 4]), op=ALU.add)
            o("act", "activation", out=LF.v, in_=LF.v, func=AF.Exp, scale=-1.0)
            o("act", "activation", out=LF.v, in_=LF.v, func=AF.Ln, bias=self.onesf[:, 0:1])
            o("dve", "tensor_scalar", out=LF.v, in0=LF.v, scalar1=-1.0, scalar2=None, op0=ALU.mult)
            lf2 = LF.v.rr("p n h -> p (n h)")
            o("pe", "matmul", out=psG.v, lhsT=self.triuf.v, rhs=lf2, start=True, stop=True)
            o("dve", "tensor_copy", out=Fc.v.rr("p n h -> p (n h)"), in_=psG.v)
            o("act", "activation", out=E.v, in_=Fc.v, func=AF.Exp)
            o("dve", "tensor_tensor", out=BV.v, in0=LI.v, in1=Fc.v, op=ALU.subtract)
            o("act", "activation", out=BV.v, in_=BV.v, func=AF.Exp)
            o("pe", "matmul", out=psG.v, lhsT=self.onesf.v, rhs=lf2, start=True, stop=True)
            o("act", "activation", out=Gd.v.rr("p n h -> p (n h)"), in_=psG.v, func=AF.Exp)
            for pr in range(2):
                o("dve", "tensor_copy", out=GdP[0:64, :, pr:pr + 1], in_=Gd[0:64, :, 2 * pr:2 * pr + 1])
                o("dve", "tensor_copy", out=GdP[64:128, :, pr:pr + 1], in_=Gd[64:128, :, 2 * pr + 1:2 * pr + 2])
            QK = [self.sb("QK%d" % ch, [128, L], BF16) for ch in range(4)]
            X = self.rot("X", [128, L], F32, 2)
            Y = self.sb("Y", [128, L], F32)
            fm = self.fmT
            for ch in range(4):
                x = X[ch % 2]
                P.dma(x.v, fm.all(fm.ap[512 + ch * 128:512 + (ch + 1) * 128, :]))
                o("dve", "tensor_scalar", out=Y.v, in0=x.v, scalar1=cw[:, ch, 3:4], scalar2=cbs[:, ch:ch + 1], op0=ALU.mult, op1=ALU.add)
                for sh in (1, 2, 3):
                    o("dve", "scalar_tensor_tensor", out=Y[:, sh:L], in0=x[:, 0:L - sh], scalar=cw[:, ch, 3 - sh:4 - sh],
                      in1=Y[:, sh:L], op0=ALU.mult, op1=ALU.add)
                o("act", "activation", out=QK[ch].v, in_=Y.v, func=AF.Silu)
                if ch >= 2:
                    o("pool", "tensor_scalar", out=QK[ch].v, in0=QK[ch].v, scalar1=0.125, scalar2=None, op0=ALU.mult)
            hm = self.sb("hm", [128, 2], F32)
            blk = self.sb("blk", [128, 130], F32)
            o("pool", "memset", ap=hm.v, constant=0.0)
            o("pool", "memset", ap=hm[0:64, 0:1], constant=1.0)
            o("pool", "memset", ap=hm[64:128, 1:2], constant=1.0)
            o("pool", "memset", ap=blk.v, constant=0.0)
            o("pool", "memset", ap=blk[0:64, 0:65], constant=1.0)
            o("pool", "memset", ap=blk[64:128, 65:130], constant=1.0)
            km = [[self.sb("km%d%d" % (pr, hl), [128, L], BF16) for hl in range(2)] for pr in range(2)]
            for pr in range(2):
                o("dve", "tensor_scalar", out=km[pr][0].v, in0=QK[2 + pr].v, scalar1=hm[:, 0:1], scalar2=None, op0=ALU.mult)
                o("pool", "tensor_scalar", out=km[pr][1].v, in0=QK[2 + pr].v, scalar1=hm[:, 1:2], scalar2=None, op0=ALU.mult)
            C = [self.sb("C%d" % pr, [128, 130], F32) for pr in range(2)]
            Cb = [self.sb("Cb%d" % pr, [128, 130], BF16) for pr in range(2)]
            Tt = self.sb("Tt", [128, 130], F32)
            for pr in range(2):
                o("pool", "memset", ap=C[pr].v, constant=0.0)
                o("pool", "memset", ap=Cb[pr].v, constant=0.0)
            vo = self.rot("vo", [128, 512], F32, 2)
            Vt = self.rot("Vt", [128, 4, 65], BF16, 2)
            kt = self.rot("kt", [128, 128], BF16, 2)
            Wm = self.rot("Wm", [128, 2, 128], BF16, 2)
            dd = self.sb("dd", [128, 4], F32)
            dd2 = self.sb("dd2", [128, 4], F32)
            Ho = self.sb("Ho", [128, 4, 64], F32)
            Hn = self.sb("Hn", [128, 4, 64], F32)
            htmp = self.sb("htmp", [128, 4, 64], F32)
            hss = self.sb("hss", [128, 4], F32)
            Ost = self.rot("Ost", [128, 256], BF16, 2)
            psK = self.rot("psK", [128, 128], BF16, 2, psum=True)
            psS = self.rot("psS", [128, 2, 128], F32, 2, psum=True)
            psO = self.ps("psO", [128, 4, 65], F32)
            psD = self.rot("psD", [128, 130], F32, 2, psum=True)
            for n in range(self.ntl):
                r0 = n * 128
                ts = slice(r0, r0 + 128)
                v_o = vo[n % 2]
                P.dma(v_o.v, self.tmf.tile(n, self.tmf.ap[r0:r0 + 128, F_MV:F_MV + 512]))
                vt = Vt[n % 2]
                o("dve", "tensor_tensor", out=vt[:, :, 0:64], in0=v_o[:, 0:256].rr("p (h d) -> p h d", h=4),
                  in1=BV[:, n, :].us(2).bc([128, 4, 64]), op=ALU.mult)
                o("pool", "tensor_copy", out=vt[:, :, 64:65], in_=BV[:, n, :].us(2))
                for pr in range(2):
                    kq = QK[2 + pr]
                    qq = QK[pr]
                    pk = psK[pr]
                    o("pe", "transpose", out=pk.v, in_=kq[:, ts], identity=self.identb.v)
                    ktt = kt[pr]
                    o("act", "copy", out=ktt.v, in_=pk.v)
                    pS = psS[pr]
                    for hl in range(2):
                        hs = slice(hl * 64, (hl + 1) * 64)
                        o("pe", "matmul", out=pS[:, hl, :], lhsT=km[pr][hl][:, ts], rhs=qq[:, ts], start=(hl == 0), stop=(hl == 1))
                    wm = Wm[pr]
                    o("dve", "tensor_tensor", out=wm.v, in0=pS.v, in1=self.triuf.v.us(1).bc([128, 2, 128]), op=ALU.mult)
                    for hl in range(2):
                        h = 2 * pr + hl
                        hs = slice(hl * 64, (hl + 1) * 64)
                        o("pe", "matmul", out=psO[:, h, :], lhsT=qq[:, ts], rhs=Cb[pr][:, hl * 65:(hl + 1) * 65],
                          start=(h == 0), stop=False)
                        o("pe", "matmul", out=psO[:, h, :], lhsT=wm[:, hl, :], rhs=vt[:, h, :], start=False, stop=True)
                    pD = psD[pr]
                    o("pe", "matmul", out=pD.v, lhsT=ktt.v, rhs=vt[:, 2 * pr:2 * pr + 2, :].rr("p h d -> p (h d)"), start=True, stop=True)
                    o("dve", "tensor_tensor", out=Tt.v, in0=C[pr].v, in1=pD.v, op=ALU.add)
                    o("dve", "tensor_scalar", out=C[pr].v, in0=Tt.v, scalar1=GdP[:, n, pr:pr + 1], scalar2=None, op0=ALU.mult)
                    o("pool", "tensor_tensor", out=Cb[pr].v, in0=C[pr].v, in1=blk.v, op=ALU.mult)
                o("dve", "tensor_tensor", out=dd.v, in0=psO[:, :, 64], in1=E[:, n, :], op=ALU.mult)
                o("dve", "tensor_scalar", out=dd2.v, in0=dd.v, scalar1=1.0, scalar2=None, op0=ALU.max)
                o("dve", "scalar_tensor_tensor", out=dd.v, in0=dd.v, scalar=-1.0, in1=dd2.v, op0=ALU.mult, op1=ALU.max)
                o("dve", "reciprocal", out=dd.v, in_=dd.v)
                o("dve", "tensor_tensor", out=dd.v, in0=dd.v, in1=E[:, n, :], op=ALU.mult)
                o("dve", "tensor_tensor", out=Ho.v, in0=psO[:, :, 0:64], in1=dd.v.us(2).bc([128, 4, 64]), op=ALU.mult)
                self.rms_heads(Ho.v, 4, 64, gO.v, Hn.v, htmp.v, hss.v)
                ost = Ost[n % 2]
                o("dve", "tensor_tensor", out=ost.v, in0=Hn.v.rr("p h d -> p (h d)"), in1=v_o[:, 256:512], op=ALU.mult)
                P.dma(self.mixed.tile(n, self.mixed.ap[r0:r0 + 128, 768:1024]), ost.v)
            P.barrier()
        self.es = top

    def attn_tiles(self, psS_rot, PT_rot, psO, qT, tiles, extra=None):
        o = self.op
        n = len(tiles)

        def pv(idx, PT, vv, ev):
            for h in range(4):
                o("pe", "matmul", out=psO[:, h, :], lhsT=PT[:, h * 128:(h + 1) * 128], rhs=vv,
                  start=(idx == 0 and h == 0), stop=(idx == n - 1))
            if extra is not None:
                for h in range(4):
                    o("pe", "matmul", out=extra[:, h, :], lhsT=PT[:, h * 128:(h + 1) * 128], rhs=ev,
                      start=(idx == 0 and h == 0), stop=(idx == n - 1))

        pend = None
        for idx, (kv, mv, vv, ev) in enumerate(tiles):
            c = self._actr = getattr(self, "_actr", 0) + 1
            pS = psS_rot[c % len(psS_rot)]
            PT = PT_rot[c % len(PT_rot)]
            o("pe", "matmul", out=pS.v, lhsT=kv, rhs=qT, start=True, stop=(mv is None))
            if mv is not None:
                o("pe", "matmul", out=pS.v, lhsT=mv, rhs=self.ident4.v, start=False, stop=True)
            o("act", "activation", out=PT.v, in_=pS.v, func=AF.Exp, scale=0.125)
            if pend is not None:
                pv(*pend)
            pend = (idx, PT, vv, ev)
        pv(*pend)

    def phase_DSA(self, l):
        P = self.P
        o = self.op
        top = self.es
        with ExitStack() as es:
            self.es = es
            kT = self.sb("kT", [64, L], BF16)
            ikT = self.sb("ikT", [32, L], BF16)
            vA = self.sb("vA", [128, NT, 65], BF16)
            pw = self.sb("pw", [128, N_BIS], F32)
            P.dma(kT.v, self.kT3.all(self.kT3.ap[0]))
            P.dma(ikT.v, self.ikT.all())
            for q in range(4):
                P.dma(vA[:, q * 8:(q + 1) * 8, :], self.tmb.all(self.tmb.ap[q * 1024:(q + 1) * 1024, B_DV:B_DV + 65]
                                                                .rearrange("(n p) d -> p n d", p=128)))
            P.dma(pw.v, self.cst["c_pow2"].all())
            qT4 = self.rot("qT4", [64, 512], BF16, 2)
            iq = self.rot("iq", [32, 1024], BF16, 2)
            sg = self.rot("sg", [128, 8], F32, 2)
            Dg = self.rot("Dg", [128, 8, 128], BF16, 2)
            score = self.rot("score", [128, L], F32, 2)
            junk = self.sb("junk", [128, L], BF16)
            mb = self.rot("mb", [128, L], BF16, 2)
            R = self.rot("R", [128, 512], BF16, 3)
            PT = self.rot("PT", [128, 512], BF16, 3)
            am = self.rot("am", [128, 1], F32, 2)
            Wt = self.rot("Wt", [128, N_BIS], F32, 2)
            W2t = self.rot("W2t", [128, N_BIS], F32, 2)
            mid = self.rot("mid", [128, 1], F32, 2)
            cnt = self.sb("cnt", [128, 1], F32)
            st = self.sb("st", [128, 1], F32)
            rden = self.rot("rden", [128, 4], F32, 2)
            ob = self.rot("ob", [128, 4, 64], BF16, 2)
            psY = self.rot("psY", [128, 512], F32, 2, psum=True)
            psC = self.rot("psC", [128, 512], F32, 2, psum=True)
            psS = self.rot("psS", [128, 512], F32, 2, psum=True)
            psO = self.rot("psO", [128, 4, 65], F32, 2, psum=True)
            self._yc = 0

            def st_score(i):
                r0 = i * 128
                S = (i + 1) * 128
                q4 = qT4[i % 2]
                iqt = iq[i % 2]
                sgt = sg[i % 2]
                dg = Dg[i % 2]
                sc = score[i % 2]
                P.dma(q4.v, self.qT3.tile(i, self.qT3.ap[i, :, 0, :]))
                P.dma(iqt.v, self.iqT.tile(i, self.iqT.ap[i]))
                P.dma(sgt.v, self.tmf.tile(i, self.tmf.ap[r0:r0 + 128, F_SGN:F_SGN + 8]))
                o("pool", "tensor_tensor", out=dg.v, in0=self.identb.v.us(1).bc([128, 8, 128]),
                  in1=sgt.v.us(2).bc([128, 8, 128]), op=ALU.mult)
                for j in range(0, S, 512):
                    w = min(512, S - j)
                    pc = psC[(j // 512) % 2]
                    pend = None
                    for h in range(8):
                        yc = self._yc
                        self._yc += 1
                        py = psY[yc % 2]
                        r = R[yc % 3]
                        o("pe", "matmul", out=py[:, 0:w], lhsT=iqt[:, h * 128:(h + 1) * 128], rhs=ikT[:, j:j + w], start=True, stop=True)
                        o("act", "activation", out=r[:, 0:w], in_=py[:, 0:w], func=AF.Relu)
                        if pend is not None:
                            ph, pr = pend
                            o("pe", "matmul", out=pc[:, 0:w], lhsT=dg[:, ph, :], rhs=pr[:, 0:w], start=(ph == 0), stop=False)
                        pend = (h, r)
                    ph, pr = pend
                    o("pe", "matmul", out=pc[:, 0:w], lhsT=dg[:, ph, :], rhs=pr[:, 0:w], start=False, stop=True)
                    o("act", "copy", out=sc[:, j:j + w], in_=pc[:, 0:w])

            def st_bisect(i):
                r0 = i * 128
                S = (i + 1) * 128
                sc = score[i % 2]
                mbt = mb[i % 2]
                a = am[i % 2]
                o("dve", "tensor_reduce", out=a.v, in_=sc[:, 0:S], axis=AX.X, op=ALU.max, apply_absolute_value=True)
                o("dve", "tensor_tensor", out=sc[:, r0:r0 + 128], in0=sc[:, r0:r0 + 128], in1=self.cbf.v, op=ALU.add)
                wt = Wt[i % 2]
                w2 = W2t[i % 2]
                o("dve", "tensor_scalar", out=wt.v, in0=pw.v, scalar1=a[:, 0:1], scalar2=None, op0=ALU.mult)
                o("dve", "tensor_scalar", out=w2.v, in0=wt.v, scalar1=2.0, scalar2=None, op0=ALU.mult)
                md = mid[i % 2]
                o("dve", "memset", ap=md.v, constant=0.0)
                for k in range(N_BIS - 1):
                    o("dve", "tensor_scalar", out=junk[:, 0:S], in0=sc[:, 0:S], scalar1=md[:, 0:1], scalar2=None,
                      op0=ALU.is_ge, op1=ALU.add, accum_out=cnt.v)
                    o("dve", "scalar_tensor_tensor", out=st.v, in0=cnt.v, scalar=float(TOPK), in1=w2[:, k + 1:k + 2],
                      op0=ALU.is_ge, op1=ALU.mult)
                    o("dve", "scalar_tensor_tensor", out=md.v, in0=st.v, scalar=wt[:, k + 1:k + 2], in1=md.v,
                      op0=ALU.subtract, op1=ALU.add)
                o("dve", "tensor_tensor", out=md.v, in0=md.v, in1=wt[:, N_BIS - 1:N_BIS], op=ALU.subtract)
                o("dve", "tensor_scalar", out=mbt[:, 0:S], in0=sc[:, 0:S], scalar1=md[:, 0:1], scalar2=NEG,
                  op0=ALU.is_lt, op1=ALU.mult)

            def st_attn(i):
                r0 = i * 128
                q4 = qT4[i % 2]
                mbt = mb[i % 2]
                pO = psO[i % 2]
                tiles = [(kT[:, kt * 128:(kt + 1) * 128], mbt[:, kt * 128:(kt + 1) * 128], vA[:, kt, :], None) for kt in range(i + 1)]
                self.attn_tiles(psS, PT, pO, q4.v, tiles)
                rd = rden[i % 2]
                o("dve", "reciprocal", out=rd.v, in_=pO[:, :, 64])
                obt = ob[i % 2]
                o("dve", "tensor_tensor", out=obt.v, in0=pO[:, :, 0:64], in1=rd.v.us(2).bc([128, 4, 64]), op=ALU.mult)
                P.dma(self.mixed.tile(i, self.mixed.ap[r0:r0 + 128, 256:512]), obt.v.rr("p h d -> p (h d)"))

            n_t = self.ntl
            st_score(0)
            for i in range(n_t):
                if i + 1 < n_t:
                    st_score(i + 1)
                st_bisect(i)
                st_attn(i)
            P.barrier()
        self.es = top

    def phase_NSA(self, l):
        P = self.P
        o = self.op
        prm = self.prm
        top = self.es
        with ExitStack() as es:
            self.es = es
            kcmpT = self.sb("kcmpT", [64, 256], BF16)
            vcmp = self.sb("vcmp", [128, 2, 65], BF16)
            ovl = self.sb("ovl", [128, 2, 64], BF16)
            P.dma(ovl.v, self.cst["c_overlap"].all(self.cst["c_overlap"].ap.rearrange("(j p) n -> p j n", p=128)))
            with ExitStack() as es2:
                self.es = es2
                kvc = self.sb("kvc", [128, L], F32)
                P.dma(kvc.v, self.fmT.all(self.fmT.ap[1024:1152, :]))
                pe_t = self.sb("pe_t", [32, 128], F32)
                P.dma(pe_t[:, 0:64], View([], prm["nsa_pos_k"][l]))
                P.dma(pe_t[:, 64:128], View([], prm["nsa_pos_v"][l]))
                psP = self.ps("psP", [128, 32], F32)
                o("pe", "transpose", out=psP.v, in_=pe_t.v, identity=self.identf[0:32, 0:32])
                peT = self.sb("peT", [128, 32], F32)
                o("dve", "tensor_copy", out=peT.v, in_=psP.v)
                Ab = self.sb("Ab", [128, L], BF16)
                Bb = self.sb("Bb", [128, L], BF16)
                k3 = kvc.v.rr("p (b r) -> p b r", r=16)
                o("dve", "tensor_tensor", out=Ab.v.rr("p (b r) -> p b r", r=16), in0=k3, in1=peT[:, 0:16].us(1).bc([128, 256, 16]), op=ALU.add)
                o("pool", "tensor_tensor", out=Bb.v.rr("p (b r) -> p b r", r=16), in0=k3, in1=peT[:, 16:32].us(1).bc([128, 256, 16]), op=ALU.add)
                W1 = self.sb("W1", [128, 32, 256], BF16)
                stg = self.rot("w1st", [128, 8, 256], F32, 2)
                sc = 0
                for pp in range(0, 32, 8):
                    sgt = stg[sc % 2]
                    sc += 1
                    P.dma(sgt[0:64], View([], prm["nsa_k_w1"][l].rearrange("(p d) h -> d p h", d=64)[:, pp:pp + 8, :]))
                    P.dma(sgt[64:128], View([], prm["nsa_v_w1"][l].rearrange("(p d) h -> d p h", d=64)[:, pp:pp + 8, :]))
                    o("dve" if (pp // 8) % 2 == 0 else "pool", "tensor_copy", out=W1[:, pp:pp + 8, :], in_=sgt.v)
                w2s = self.sb("w2s", [128, 2, 128], F32)
                P.dma(w2s[:, :, 0:64], View([], prm["nsa_k_w2"][l].rearrange("(c p) d -> p c d", p=128)))
                P.dma(w2s[:, :, 64:128], View([], prm["nsa_v_w2"][l].rearrange("(c p) d -> p c d", p=128)))
                W2c = self.sb("W2c", [128, 2, 128], BF16)
                o("dve", "tensor_copy", out=W2c.v, in_=w2s.v)
                gk0 = self.sb("gk0", [128, 64], F32)
                self.bcast_row(gk0.v, prm["nsa_k_gains"][l, 0])
                hT = [self.sb("hT%d" % s, [128, 2, 256], BF16) for s in range(2)]
                psH = self.rot("psH", [128, 256], F32, 2, psum=True)
                for s in range(2):
                    sp = slice(s * 64, (s + 1) * 64)
                    o("pool", "memset", ap=hT[s].v, constant=0.0)
                    for hc in range(2):
                        ph = psH[hc]
                        for p in range(32):
                            src = Ab if p < 16 else Bb
                            o("pe", "matmul", out=ph[:, 0:255], lhsT=W1[sp, p, hc * 128:(hc + 1) * 128],
                              rhs=src[sp, p:p + 16 * 254 + 1:16], start=(p == 0), stop=(p == 31))
                        o("act", "activation", out=hT[s][:, hc, 0:255], in_=ph[:, 0:255], func=AF.Relu)
                psC = self.rot("psC", [128, 64], F32, 2, psum=True)
                psT = self.ps("psT", [64, 128], BF16)
                kcf = self.sb("kcf", [128, 1, 64], F32)
                kcn = self.sb("kcn", [128, 1, 64], F32)
                kct = self.sb("kct", [128, 1, 64], F32)
                kcs = self.sb("kcs", [128, 1], F32)
                kcb = self.sb("kcb", [128, 64], BF16)
                for jt in range(2):
                    for s in range(2):
                        pc = psC[s]
                        for hc in range(2):
                            o("pe", "matmul", out=pc.v, lhsT=hT[s][:, hc, jt * 128:(jt + 1) * 128], rhs=W2c[:, hc, s * 64:(s + 1) * 64],
                              start=(hc == 0), stop=(hc == 1))
                    o("act", "copy", out=kcf[:, 0, :], in_=psC[0].v)
                    self.rms_heads(kcf.v, 1, 64, gk0.v, kcn.v, kct.v, kcs.v)
                    o("act", "copy", out=kcb.v, in_=kcn[:, 0, :])
                    o("pe", "transpose", out=psT.v, in_=kcb.v, identity=self.identb.v)
                    o("act", "copy", out=kcmpT[:, jt * 128:(jt + 1) * 128], in_=psT.v)
                    o("dve", "tensor_copy", out=vcmp[:, jt, 0:64], in_=psC[1].v)
                    o("pool", "memset", ap=vcmp[:, jt, 64:65], constant=1.0)
                P.barrier()
            self.es = es
            ksT = self.sb("ksT", [64, L], BF16)
            kwT = self.sb("kwT", [64, L], BF16)
            vS = self.sb("vS", [128, NT, 65], BF16)
            vW = self.sb("vW", [128, NT, 65], BF16)
            P.dma(ksT.v, self.kT3.all(self.kT3.ap[1]))
            P.dma(kwT.v, self.kT3.all(self.kT3.ap[2]))
            for q in range(4):
                P.dma(vS[:, q * 8:(q + 1) * 8, :], self.tmb.all(self.tmb.ap[q * 1024:(q + 1) * 1024, B_NVS:B_NVS + 65]
                                                                .rearrange("(n p) d -> p n d", p=128)))
                P.dma(vW[:, q * 8:(q + 1) * 8, :], self.tmb.all(self.tmb.ap[q * 1024:(q + 1) * 1024, B_NVW:B_NVW + 65]
                                                                .rearrange("(n p) d -> p n d", p=128)))
            q2 = self.rot("q2", [64, 2, 512], BF16, 2)
            gt = self.rot("gt", [128, 12], F32, 2)
            cm = self.rot("cm", [128, 256], BF16, 2)
            fbt = self.rot("fbt", [128, 64], F32, 2)
            mb = self.rot("mb", [128, L], BF16, 2)
            PT = self.rot("PT", [128, 512], BF16, 3)
            rc = self.rot("rc", [128, 12], F32, 2)
            tmp2 = self.sb("tmp2", [128, 4, 64], F32)
            impw = self.sb("impw", [128, 4, 64], F32)
            impa = self.sb("impa", [128, 64], F32)
            zz = self.sb("zz", [128, 64], F32)
            m8 = self.sb("m8", [128, 16], F32)
            mbb = self.sb("mbb", [128, 64], BF16)
            acc = self.sb("acc", [128, 4, 64], F32)
            tmp = self.sb("tmp", [128, 4, 64], F32)
            ob = self.rot("ob", [128, 4, 64], BF16, 2)
            psS = self.rot("psS", [128, 512], F32, 2, psum=True)
            psOc = self.rot("psOc", [128, 4, 65], F32, 2, psum=True)
            psI = self.ps("psI", [128, 4, 64], F32)
            psOs = self.ps("psOs", [128, 4, 65], F32)
            psOw = self.ps("psOw", [128, 4, 65], F32)
            def st_cmp(i):
                r0 = i * 128
                S = (i + 1) * 128
                qq = q2[i % 2]
                g = gt[i % 2]
                cmt = cm[i % 2]
                fb = fbt[i % 2]
                mbt = mb[i % 2]
                pOc = psOc[i % 2]
                rcc = rc[i % 2]
                P.dma(qq.v, self.qT3.tile(i, self.qT3.ap[i, :, 1:3, :]))
                P.dma(g.v, self.tmf.tile(i, self.tmf.ap[r0:r0 + 128, F_NG:F_NG + 12]))
                P.dma(cmt.v, self.cst["c_cmpmask"].all(self.cst["c_cmpmask"].ap[r0:r0 + 128, :]))
                P.dma(fb.v, self.cst["c_fb"].all(self.cst["c_fb"].ap[r0:r0 + 128, :]))
                tiles = [(kcmpT[:, jt * 128:(jt + 1) * 128], cmt[:, jt * 128:(jt + 1) * 128], vcmp[:, jt, :], ovl[:, jt, :]) for jt in range(2)]
                self.attn_tiles(psS, PT, pOc, qq[:, 0, :], tiles, extra=psI)
                o("dve", "tensor_scalar", out=rcc[:, 0:4], in0=pOc[:, :, 64], scalar1=1e-30, scalar2=None, op0=ALU.max)
                o("dve", "reciprocal", out=rcc[:, 0:4], in_=rcc[:, 0:4])
                o("dve", "tensor_tensor", out=impw.v, in0=psI.v, in1=rcc[:, 0:4].us(2).bc([128, 4, 64]), op=ALU.mult)
                o("dve", "tensor_reduce", out=impa.v, in_=impw.v.rr("p h n -> p n h"), axis=AX.X, op=ALU.add)
                o("dve", "tensor_tensor", out=impa.v, in0=impa.v, in1=fb.v, op=ALU.add)
                o("dve", "max", out=m8[:, 0:8], in_=impa.v)
                o("dve", "match_replace", out=zz.v, in_to_replace=m8[:, 0:8], in_values=impa.v, imm_value=-3.0e6)
                o("dve", "max", out=m8[:, 8:16], in_=zz.v)
                o("dve", "tensor_scalar", out=mbb.v, in0=impa.v, scalar1=m8[:, 15:16], scalar2=NEG, op0=ALU.is_lt, op1=ALU.mult)
                nb = S // 64
                o("dve", "tensor_copy", out=mbt[:, 0:S].rr("p (b r) -> p b r", r=64), in_=mbb[:, 0:nb].us(2).bc([128, nb, 64]))
                o("dve", "tensor_tensor", out=mbt[:, r0:r0 + 128], in0=mbt[:, r0:r0 + 128], in1=self.cbb.v, op=ALU.add)

            def st_swa(i):
                qq = q2[i % 2]
                tiles = []
                for kt in range(max(0, i - 4), i + 1):
                    mv = self.cbb.v if kt == i else (self.abb.v if kt == i - 4 else None)
                    tiles.append((kwT[:, kt * 128:(kt + 1) * 128], mv, vW[:, kt, :], None))
                self.attn_tiles(psS, PT, psOw, qq[:, 1, :], tiles)

            def st_sel(i):
                qq = q2[i % 2]
                mbt = mb[i % 2]
                tiles = [(ksT[:, kt * 128:(kt + 1) * 128], mbt[:, kt * 128:(kt + 1) * 128], vS[:, kt, :], None) for kt in range(i + 1)]
                self.attn_tiles(psS, PT, psOs, qq[:, 1, :], tiles)

            def st_comb(i):
                r0 = i * 128
                g = gt[i % 2]
                pOc = psOc[i % 2]
                rcc = rc[i % 2]
                o("dve", "reciprocal", out=rcc[:, 4:8], in_=psOs[:, :, 64])
                o("dve", "reciprocal", out=rcc[:, 8:12], in_=psOw[:, :, 64])
                o("dve", "tensor_tensor", out=rcc.v, in0=rcc.v, in1=g.v, op=ALU.mult)
                o("dve", "tensor_tensor", out=acc.v, in0=pOc[:, :, 0:64], in1=rcc[:, 0:4].us(2).bc([128, 4, 64]), op=ALU.mult)
                o("dve", "tensor_tensor", out=tmp.v, in0=psOs[:, :, 0:64], in1=rcc[:, 4:8].us(2).bc([128, 4, 64]), op=ALU.mult)
                o("pool", "tensor_tensor", out=acc.v, in0=acc.v, in1=tmp.v, op=ALU.add)
                o("dve", "tensor_tensor", out=tmp2.v, in0=psOw[:, :, 0:64], in1=rcc[:, 8:12].us(2).bc([128, 4, 64]), op=ALU.mult)
                obt = ob[i % 2]
                o("pool", "tensor_tensor", out=obt.v, in0=acc.v, in1=tmp2.v, op=ALU.add)
                P.dma(self.mixed.tile(i, self.mixed.ap[r0:r0 + 128, 512:768]), obt.v.rr("p h d -> p (h d)"))

            n_t = self.ntl
            st_cmp(0)
            for i in range(n_t):
                st_swa(i)
                if i + 1 < n_t:
                    st_cmp(i + 1)
                st_sel(i)
                st_comb(i)
            P.barrier()
        self.es = top


_CACHE = {}


def make_in_map(inputs, b, consts):
    m = {"x": np.ascontiguousarray(inputs["x"][b], dtype=np.float32),
         "mem": np.ascontiguousarray(inputs["mem"][b], dtype=np.float32)}
    for k in PARAM_SHAPES:
        m[k] = np.ascontiguousarray(inputs[k], dtype=np.float32)
    m.update(consts)
    return m


def kernel(**inputs):
    if "kb" not in _CACHE:
        kb = KB()
        kb.build()
        _CACHE["kb"] = kb
    kb = _CACHE["kb"]
    consts = host_consts()
    in_maps = [make_in_map(inputs, c % 4, consts) for c in range(8)]
    res = run_bass_kernel_spmd(kb.nc, in_maps, core_ids=list(range(8)))
    out = np.stack([np.asarray(res.results[c]["out"], dtype=np.float32) for c in range(4)], axis=0)
    return out
```

```python
import numpy as np
import ml_dtypes
from contextlib import ExitStack
import concourse.bass as bass
import concourse.mybir as mybir
from concourse.bass_utils import run_bass_kernel_spmd

F32 = mybir.dt.float32
BF16 = mybir.dt.bfloat16
ALU = mybir.AluOpType
AF = mybir.ActivationFunctionType
AX = mybir.AxisListType

D = 1024
L = 4096
NT = L // 128
NMEM = 256
DFF = 2816
NFC = DFF // 128
INC = 3388
EPS = 1e-6
NEG = -30000.0
THETA = 500000.0
N_BIS = 14
TOPK = 256


class Buf:
    __slots__ = ("name", "last_w", "readers", "t", "psum")

    def __init__(self, name, t=None, psum=False):
        self.name = name
        self.last_w = None
        self.readers = []
        self.t = t
        self.psum = psum

    def __getitem__(self, k):
        return View([self], self.t[k])

    @property
    def v(self):
        return View([self], self.t[:])


class View:
    __slots__ = ("bufs", "ap")

    def __init__(self, bufs, ap):
        self.bufs = bufs
        self.ap = ap

    def __getitem__(self, k):
        return View(self.bufs, self.ap[k])

    def rr(self, s, **kw):
        return View(self.bufs, self.ap.rearrange(s, **kw))

    def bc(self, shape):
        return View(self.bufs, self.ap.to_broadcast(list(shape)))

    def us(self, ax):
        return View(self.bufs, self.ap.unsqueeze(ax))

    @property
    def shape(self):
        return self.ap.shape


class Op:
    __slots__ = ("idx", "eng", "emit", "deps", "signal", "ticket", "is_dma", "dsem", "dval")


WRITE_KEYS = ("out", "accum_out", "ap")


class Prog:
    ENGS = ("pe", "act", "dve", "pool", "sp")
    NRING = 24

    def __init__(self, nc):
        self.nc = nc
        self.ops = []
        self.ndma = 0
        self.lastdma = {}
        self.lasteng = {}

    def add(self, eng, emit, reads=(), writes=(), dma=False):
        op = Op()
        op.idx = len(self.ops)
        op.eng = eng
        op.emit = emit
        op.is_dma = dma
        op.signal = False
        op.ticket = None
        deps = {}
        for b in reads:
            if b.last_w is not None:
                deps[b.last_w.idx] = (b.last_w, True)
            if b.psum:
                for r in b.readers:
                    if r.eng != eng and r.idx not in deps:
                        deps[r.idx] = (r, False)
        for b in writes:
            if b.last_w is not None and b.last_w.idx not in deps:
                deps[b.last_w.idx] = (b.last_w, False)
            for r in b.readers:
                if r.idx not in deps:
                    deps[r.idx] = (r, False)
        op.deps = list(deps.values())
        for b in reads:
            if not dma:
                b.readers = [r for r in b.readers if r.is_dma or r.eng != eng]
            b.readers.append(op)
        for b in writes:
            b.last_w = op
            b.readers = []
        if dma:
            k = self.ndma
            self.ndma += 1
            op.dsem = k % self.NRING
            op.dval = 16 * (k // self.NRING + 1)
            self.lastdma[op.dsem] = op
        if emit is not None:
            self.lasteng[eng] = op
        self.ops.append(op)
        return op

    def op(self, eng, name, r=(), w=(), **kw):
        reads = list(r)
        writes = list(w)
        args = {}
        for k, v in kw.items():
            if isinstance(v, View):
                if k in WRITE_KEYS:
                    writes.extend(v.bufs)
                else:
                    reads.extend(v.bufs)
                args[k] = v.ap
            else:
                args[k] = v

        if name == "matmul":
            args.setdefault("skip_group_check", True)

        def emit(e, name=name, args=args):
            return getattr(e, name)(**args)

        return self.add(eng, emit, reads, writes, dma=(name == "dma_start"))

    def dma(self, out, in_, eng="sp"):
        return self.op(eng, "dma_start", out=out, in_=in_)

    def barrier(self):
        prev = list(self.lasteng.values()) + list(self.lastdma.values())
        for e in self.ENGS:
            op = self.add(e, None)
            for o in prev:
                if o.eng != e or o.is_dma:
                    op.deps.append((o, True))

    def emit_all(self, sems, ring):
        nc = self.nc
        engobj = {"pe": nc.tensor, "act": nc.scalar, "dve": nc.vector, "pool": nc.gpsimd, "sp": nc.sync}
        for op in self.ops:
            for d, raw in op.deps:
                if d.is_dma:
                    continue
                if d.eng == op.eng and d.eng == "pe":
                    continue
                d.signal = True
        cnt = {e: 0 for e in self.ENGS}
        for op in self.ops:
            if op.signal and not op.is_dma:
                cnt[op.eng] += 1
                op.ticket = cnt[op.eng]
        seen = {e: {} for e in self.ENGS}
        nw = 0
        for op in self.ops:
            e = engobj[op.eng]
            sn = seen[op.eng]
            need = {}
            for d, raw in op.deps:
                if d.is_dma:
                    key = ("r", d.dsem)
                    val = d.dval
                else:
                    if d.eng == op.eng and d.eng == "pe":
                        continue
                    key = ("e", d.eng)
                    val = d.ticket
                if sn.get(key, 0) >= val:
                    continue
                if need.get(key, 0) < val:
                    need[key] = val
            if op.is_dma and op.dval > 16:
                key = ("r", op.dsem)
                val = op.dval - 16
                if sn.get(key, 0) < val and need.get(key, 0) < val:
                    need[key] = val
            for key, val in need.items():
                s = ring[key[1]] if key[0] == "r" else sems[key[1]]
                e.wait_ge(s, val)
                sn[key] = val
                nw += 1
            if op.emit is None:
                continue
            ins = op.emit(e)
            if op.is_dma:
                ins.then_inc(ring[op.dsem], 16)
            elif op.signal:
                ins.then_inc(sems[op.eng], 1)
        return nw, cnt


class DT:
    def __init__(self, name, ap, ntile=1):
        self.name = name
        self.ap = ap
        self.bufs = [Buf("%s_%d" % (name, i)) for i in range(ntile)]

    def tile(self, i, ap):
        return View([self.bufs[i]], ap)

    def all(self, ap=None):
        return View(list(self.bufs), self.ap if ap is None else ap)


def host_consts():
    bf = ml_dtypes.bfloat16
    c = {}
    eye = np.eye(128, dtype=np.float32)
    c["c_identb"] = eye.astype(bf)
    c["c_identf"] = eye
    c["c_ident4"] = np.tile(eye, (1, 4)).astype(bf)
    s = np.arange(128)[:, None]
    t = np.arange(128)[None, :]
    triu = (s <= t).astype(np.float32)
    c["c_triuf"] = triu
    c["c_triub"] = triu.astype(bf)
    tt = np.arange(128)[:, None]
    ss = np.arange(128)[None, :]
    c["c_cb"] = np.where(ss <= tt, 0.0, NEG).astype(np.float32)
    c["c_ab"] = np.where(ss > tt, 0.0, NEG).astype(np.float32)
    pos = np.arange(L, dtype=np.float32)
    for nm, rd in (("c_cs64", 16), ("c_cs32", 8)):
        half = rd // 2
        inv = (np.float32(THETA) ** (-np.arange(half, dtype=np.float32) * np.float32(2.0) / np.float32(rd))).astype(np.float32)
        ang = (pos[:, None] * inv[None, :]).astype(np.float32)
        cs = np.concatenate([np.cos(ang), np.sin(ang)], axis=1).astype(np.float32)
        c[nm] = np.ascontiguousarray(cs.reshape(NT, 128, rd).transpose(1, 0, 2))
    j = np.arange(256)[None, :]
    tq = np.arange(L)[:, None]
    vis = (16 * j + 31 <= tq) & (j < 255)
    c["c_cmpmask"] = np.where(vis, 0.0, NEG).astype(bf)
    n = np.arange(64)[None, :]
    cur = tq // 64
    forced = (n == 0) | (n == cur) | (n == cur - 1)
    fb = np.where(forced, 1.0e6 + 64.0 * n, np.where(n > cur, -1.0e6, 0.0))
    c["c_fb"] = fb.astype(np.float32)
    st_c = np.arange(255) * 16
    st_s = np.arange(64) * 64
    ov = ((st_c[:, None] < st_s[None, :] + 64) & (st_c[:, None] + 32 > st_s[None, :])).astype(np.float32)
    ovp = np.zeros((256, 64), np.float32)
    ovp[:255] = ov
    c["c_overlap"] = ovp.astype(bf)
    c["c_pow2"] = np.tile((1.0078125 * 2.0 ** -np.arange(N_BIS, dtype=np.float32))[None, :], (128, 1)).astype(np.float32)
    return c


PARAM_SHAPES = {
    "lb_param": (2, 256), "norm_mix": (2, 1024), "w_in": (2, 1024, INC), "w_out": (2, 1024, 1024),
    "hg_o_gain": (2, 64), "dsa_kv_gain": (2, 128), "dsa_w_uk": (2, 128, 64), "dsa_w_uv": (2, 128, 64),
    "dsa_q_gain": (2, 64), "dsa_k_gain": (2, 64), "dsa_idxk_gain": (2, 32),
    "nsa_pos_k": (2, 32, 64), "nsa_pos_v": (2, 32, 64), "nsa_k_w1": (2, 2048, 256), "nsa_k_w2": (2, 256, 64),
    "nsa_v_w1": (2, 2048, 256), "nsa_v_w2": (2, 256, 64), "nsa_q_gain": (2, 64), "nsa_k_gains": (2, 3, 64),
    "ml_conv_w": (2, 4, 512), "ml_conv_b": (2, 512), "ml_i_bias": (2, 4), "ml_f_bias": (2, 4), "ml_o_gain": (2, 64),
    "norm_xa": (2, 1024), "norm_mem": (2, 1024), "xa_wq": (2, 1024, 256), "xa_wkv": (2, 1024, 512),
    "xa_wo": (2, 256, 1024), "xa_q_gain": (2, 64), "xa_k_gain": (2, 64), "norm_ffn": (2, 1024),
    "ffn_w13": (2, 1024, 2 * DFF), "ffn_w2": (2, DFF, 1024),
}


HG0, DS0, NS0, ML0 = 0, 1024, 1704, 2356
TM_GROUPS = [(512, 512), (1024, 512), (1536, 168), (1704, 256), (2088, 268), (2868, 512), (3380, 8)]
TM_OFF = [0, 768, 1280, 512, 1448, 1716, 2228]
TMW = 2236
FM_COLS = [0, 128, 256, 384, 2356, 2484, 2612, 2740, 1960]
C_HGI, C_HGG, C_DQ, C_CKV, C_IQ, C_IK, C_IW = 0, 256, 768, 1024, 1152, 1408, 1440
C_NQ, C_KS, C_VS, C_KW, C_VW, C_NG = 512, 1448, 1512, 1576, 1640, 1704
C_MV, C_OG, C_IG, C_FG = 1716, 1972, 2228, 2232
B_HGV, B_DV, B_NVS, B_NVW, TMBW = 0, 256, 321, 386, 452
F_HGG, F_MV, F_OG, F_SGN, F_NG, F_GT, TMFW = 0, 256, 512, 768, 776, 788, 796


class KB:
    def __init__(self, dump=(), layers=(0, 1), phases=None, ntl=NT):
        self.ntl = ntl
        import os
        self.alvl = int(os.environ.get("KDBG_A", "99"))
        self.dump = set(dump)
        self.layers = layers
        self.phases = phases
        nc = bass.Bass("TRN2", target_bir_lowering=False)
        self.nc = nc
        self.P = Prog(nc)
        self.es = None
        self.din = {}

    def sb(self, name, shape, dt):
        t = self.es.enter_context(self.nc.sbuf_tensor(name + "_%d" % self.uid(), list(shape), dt))
        return Buf(name, t)

    def ps(self, name, shape, dt):
        nel = 2048 // (4 if dt == F32 else 2)
        full = self.es.enter_context(self.nc.psum_tensor(name + "_%d" % self.uid(), [128, nel], dt))
        n = 1
        for d in shape[1:]:
            n *= d
        assert n <= nel, (name, shape)
        ap = full[0:shape[0], 0:n]
        if len(shape) == 3:
            ap = ap.rearrange("p (a b) -> p a b", a=shape[1])
        return Buf(name, ap, psum=True)

    def uid(self):
        self._uid = getattr(self, "_uid", 0) + 1
        return self._uid

    def rot(self, name, shape, dt, n, psum=False):
        return [(self.ps if psum else self.sb)("%s%d" % (name, i), shape, dt) for i in range(n)]

    def dram_in(self, name, shape, dt):
        ap = self.nc.dram_tensor(name, list(shape), dt, kind="ExternalInput").ap()
        d = DT(name, ap, 1)
        self.din[name] = d
        return d

    def scr(self, name, shape, dt, ntile=1):
        kind = "ExternalOutput" if name in self.dump else "Internal"
        ap = self.nc.dram_tensor(name, list(shape), dt, kind=kind).ap()
        return DT(name, ap, ntile)

    def op(self, eng, name, **kw):
        return self.P.op(eng, name, **kw)

    def rms_heads(self, X, H, Dh, gain, out, tmp, ssq, np_=128, gain_full=False):
        o = self.op
        o("pool", "tensor_tensor", out=tmp, in0=X, in1=X, op=ALU.mult)
        o("dve", "tensor_reduce", out=ssq, in_=tmp, axis=AX.X, op=ALU.add)
        o("act", "activation", out=ssq, in_=ssq, func=AF.Sqrt, scale=1.0 / Dh, bias=self.epsb[0:np_, 0:1])
        o("dve", "reciprocal", out=ssq, in_=ssq)
        o("dve", "tensor_tensor", out=out, in0=X, in1=ssq.us(2).bc([np_, H, Dh]), op=ALU.mult)
        if gain is not None:
            o("pool", "tensor_tensor", out=out, in0=out, in1=(gain if gain_full else gain.us(1).bc([np_, H, Dh])), op=ALU.mult)

    def rope(self, X, H, hf, cs, out, t1, t2):
        o = self.op
        Dh = X.shape[2]
        cos = cs[:, 0:hf].us(1).bc([128, H, hf])
        sin = cs[:, hf:2 * hf].us(1).bc([128, H, hf])
        x1 = X[:, :, 0:hf]
        x2 = X[:, :, hf:2 * hf]
        o("pool", "tensor_copy", out=out[:, :, 2 * hf:Dh], in_=X[:, :, 2 * hf:Dh])
        o("pool", "tensor_tensor", out=t1, in0=x1, in1=cos, op=ALU.mult)
        o("pool", "tensor_tensor", out=t2, in0=x2, in1=sin, op=ALU.mult)
        o("pool", "tensor_tensor", out=out[:, :, 0:hf], in0=t1, in1=t2, op=ALU.subtract)
        o("dve", "tensor_tensor", out=t1, in0=x2, in1=cos, op=ALU.mult)
        o("dve", "tensor_tensor", out=t2, in0=x1, in1=sin, op=ALU.mult)
        o("dve", "tensor_tensor", out=out[:, :, hf:2 * hf], in0=t1, in1=t2, op=ALU.add)

    def build(self):
        nc = self.nc
        P = self.P
        o = self.op
        self.x_in = self.dram_in("x", [L, D], F32)
        self.x_in.bufs = [Buf("xin%d" % i) for i in range(NT)]
        self.mem_in = self.dram_in("mem", [NMEM, D], F32)
        self.prm = {k: self.dram_in(k, list(s), F32).ap for k, s in PARAM_SHAPES.items()}
        hc = host_consts()
        self.cst = {}
        for k, v in hc.items():
            self.cst[k] = self.dram_in(k, list(v.shape), BF16 if v.dtype != np.float32 else F32)
        kind = "ExternalOutput"
        self.xo = DT("out", nc.dram_tensor("out", [L, D], F32, kind=kind).ap(), NT)
        self.fmT = self.scr("fmT", [1152, L], F32, NT)
        self.tmb = self.scr("tmb", [L, TMBW], BF16, NT)
        self.tmf = self.scr("tmf", [L, TMFW], F32, NT)
        self.kT3 = self.scr("kT3", [3, 64, L], BF16, NT)
        self.ikT = self.scr("ikT", [32, L], BF16, NT)
        self.qT3 = self.scr("qT3", [NT, 64, 3, 512], BF16, NT)
        self.iqT = self.scr("iqT", [NT, 32, 1024], BF16, NT)
        self.mixed = self.scr("mixed", [L, D], BF16, NT)

        with ExitStack() as top:
            self.es = top
            sems = {e: top.enter_context(nc.semaphore("s_" + e)) for e in P.ENGS}
            ring = [top.enter_context(nc.semaphore("r%d" % i)) for i in range(P.NRING)]
            self.identb = self.sb("identb", [128, 128], BF16)
            self.identf = self.sb("identf", [128, 128], F32)
            self.ident4 = self.sb("ident4", [128, 512], BF16)
            self.triuf = self.sb("triuf", [128, 128], F32)
            self.triub = self.sb("triub", [128, 128], BF16)
            self.cbf = self.sb("cbf", [128, 128], F32)
            self.cbb = self.sb("cbb", [128, 128], BF16)
            self.abb = self.sb("abb", [128, 128], BF16)
            self.cs64 = self.sb("cs64", [128, NT, 16], F32)
            self.cs32 = self.sb("cs32", [128, NT, 8], F32)
            self.epsb = self.sb("epsb", [128, 1], F32)
            self.onesf = self.sb("onesf", [128, 128], F32)
            tmpf = self.sb("tmpf", [128, 128], F32)
            for nm, dst in (("c_identb", self.identb), ("c_identf", self.identf), ("c_ident4", self.ident4),
                            ("c_triuf", self.triuf), ("c_triub", self.triub), ("c_cb", self.cbf)):
                P.dma(dst.v, self.cst[nm].all())
            P.dma(tmpf.v, self.cst["c_ab"].all())
            P.dma(self.cs64.v, self.cst["c_cs64"].all())
            P.dma(self.cs32.v, self.cst["c_cs32"].all())
            o("dve", "tensor_copy", out=self.cbb.v, in_=self.cbf.v)
            o("dve", "tensor_copy", out=self.abb.v, in_=tmpf.v)
            o("pool", "memset", ap=self.epsb.v, constant=EPS)
            o("pool", "memset", ap=self.onesf.v, constant=1.0)
            P.barrier()
            for l in self.layers:
                xsrc = self.x_in if l == 0 else self.xo
                ph = self.phases
                if ph is None or "A" in ph:
                    self.phase_A(l, xsrc)
                if ph is None or "HG" in ph:
                    self.phase_HG(l)
                if ph is None or "ML" in ph:
                    self.phase_ML(l)
                if ph is None or "DSA" in ph:
                    self.phase_DSA(l)
                if ph is None or "NSA" in ph:
                    self.phase_NSA(l)
                if ph is None or "C" in ph:
                    self.phase_C(l, xsrc)
            P.barrier()
            self.stats = P.emit_all(sems, ring)
        return nc

    def load_weight_bf16(self, dst, src_ap, nk, ncols, gain=None, chunk=512, stage=None):
        P = self.P
        o = self.op
        src = src_ap.rearrange("(c p) n -> p c n", p=128)
        engs = ["dve", "pool", "act"]
        ei = 0
        j = 0
        for c0 in range(0, ncols, chunk):
            w = min(chunk, ncols - c0)
            for k0 in range(0, nk, 4):
                k1 = min(nk, k0 + 4)
                st = stage[j % len(stage)]
                j += 1
                P.dma(st[:, 0:k1 - k0, 0:w], View([], src[:, k0:k1, c0:c0 + w]))
                for k in range(k0, k1):
                    e = engs[ei % 3]
                    ei += 1
                    if gain is None:
                        if e == "act":
                            o(e, "copy", out=dst[:, k, c0:c0 + w], in_=st[:, k - k0, 0:w])
                        else:
                            o(e, "tensor_copy", out=dst[:, k, c0:c0 + w], in_=st[:, k - k0, 0:w])
                    else:
                        if e == "act":
                            o(e, "activation", out=dst[:, k, c0:c0 + w], in_=st[:, k - k0, 0:w], func=AF.Copy,
                              scale=gain[:, k:k + 1])
                        else:
                            o(e, "tensor_scalar", out=dst[:, k, c0:c0 + w], in0=st[:, k - k0, 0:w],
                              scalar1=gain[:, k:k + 1], scalar2=None, op0=ALU.mult)

    def load_gain_cols(self, dst, vec_ap, nk):
        self.P.dma(dst.v, View([], vec_ap.rearrange("(c p) -> p c", p=128)))

    def bcast_row(self, dst_view, vec_ap):
        np_ = dst_view.shape[0]
        self.P.dma(dst_view, View([], vec_ap.partition_broadcast(np_)))

    def col_load(self, dst_view, vec_ap):
        self.P.dma(dst_view, View([], vec_ap.unsqueeze(1)))

    def phase_A(self, l, xsrc):
        P = self.P
        o = self.op
        prm = self.prm
        top = self.es
        with ExitStack() as es:
            self.es = es
            Wb = self.sb("Wb", [128, 8, INC], BF16)
            with ExitStack() as es2:
                self.es = es2
                stage = self.rot("wst", [128, 4, 512], F32, 3)
                self.load_weight_bf16(Wb, prm["w_in"][l], 8, INC, stage=stage)
                P.barrier()
            self.es = es
            gmix = self.sb("gmix", [128, D], F32)
            self.bcast_row(gmix.v, prm["norm_mix"][l])
            g8 = self.sb("g8", [128, 8, 64], F32)
            for hh in range(4):
                self.bcast_row(g8[:, hh, :], prm["nsa_q_gain"][l])
                self.bcast_row(g8[:, 4 + hh, :], prm["dsa_q_gain"][l])
            g3 = self.sb("g3", [128, 3, 64], F32)
            self.bcast_row(g3[:, 0, :], prm["dsa_k_gain"][l])
            self.bcast_row(g3[:, 1, :], prm["nsa_k_gains"][l, 1])
            self.bcast_row(g3[:, 2, :], prm["nsa_k_gains"][l, 2])
            K3 = self.sb("K3", [128, 3, 64], F32)
            K3n = self.sb("K3n", [128, 3, 64], F32)
            K3t = self.sb("K3t", [128, 3, 64], F32)
            gkv = self.sb("gkv", [128, 128], F32)
            self.bcast_row(gkv.v, prm["dsa_kv_gain"][l])
            gik = self.sb("gik", [128, 32], F32)
            self.bcast_row(gik.v, prm["dsa_idxk_gain"][l])
            wst = self.sb("wukv_st", [128, 128], F32)
            wukv = self.sb("wukv", [128, 128], BF16)
            P.dma(wst[:, 0:64], View([], prm["dsa_w_uk"][l]))
            P.dma(wst[:, 64:128], View([], prm["dsa_w_uv"][l]))
            o("dve", "tensor_copy", out=wukv.v, in_=wst.v)

            xt = self.rot("xt", [128, D], F32, 2)
            junk = self.sb("junk", [128, D], F32)
            ssx = self.rot("ssx", [128, 1], F32, 2)
            hb = self.rot("hb", [128, D], BF16, 2)
            hT = self.rot("hT", [128, 8, 128], BF16, 2)
            ct = self.rot("ct", [128, TMW], F32, 2)
            fm = self.rot("fm", [128, 9, 128], F32, 2)
            tmbS = self.rot("tmbS", [128, TMBW], BF16, 2)
            tmfS = self.rot("tmfS", [128, TMFW], F32, 2)
            kS = self.rot("kS", [64, 3, 128], BF16, 2)
            ikS = self.rot("ikS", [32, 128], BF16, 2)
            qS = self.rot("qS", [64, 3, 512], BF16, 2)
            iqS = self.rot("iqS", [32, 1024], BF16, 2)
            wk = self.sb("wk", [128, 8, 64], F32)
            wk2 = self.sb("wk2", [128, 8, 64], F32)
            t1 = self.sb("t1", [128, 8, 8], F32)
            t2 = self.sb("t2", [128, 8, 8], F32)
            ssq = self.sb("ssq", [128, 8], F32)
            qb = self.sb("qb", [128, 12, 64], BF16)
            kb = self.sb("kb", [128, 3, 64], BF16)
            iqb = self.sb("iqb", [128, 8, 32], BF16)
            ikb = self.sb("ikb", [128, 32], BF16)
            ckvb = self.sb("ckvb", [128, 128], BF16)
            ckvT = self.sb("ckvT", [128, 128], BF16)
            kvf = self.sb("kvf", [128, 128], F32)
            iwa = self.sb("iwa", [128, 8], F32)

            psT = self.ps("psT", [128, 8, 128], BF16)
            psA = self.rot("psA", [128, 512], F32, 2, psum=True)
            psB = self.rot("psB", [128, 4, 128], F32, 2, psum=True)
            psX = self.ps("psX", [128, 8, 128], BF16)
            psY = self.ps("psY", [128, 8, 128], BF16)
            psZ = self.ps("psZ", [128, 8, 128], BF16)

            IWS = float(8 ** -0.5 * 32 ** -0.5)

            def sA(i):
                x_t = xt[i % 2]
                hbt = hb[i % 2]
                hTt = hT[i % 2]
                c = ct[i % 2]
                f = fm[i % 2]
                r0 = i * 128
                if self.alvl < 1:
                    return
                P.dma(x_t.v, xsrc.tile(i, xsrc.ap[r0:r0 + 128, :]))
                ss = ssx[i % 2]
                o("act", "activation", out=junk.v, in_=x_t.v, func=AF.Square, accum_out=ss.v)
                o("act", "activation", out=ss.v, in_=ss.v, func=AF.Sqrt, scale=1.0 / D, bias=self.epsb[:, 0:1])
                o("dve", "reciprocal", out=ss.v, in_=ss.v)
                o("dve", "scalar_tensor_tensor", out=hbt.v, in0=x_t.v, scalar=ss[:, 0:1], in1=gmix.v,
                  op0=ALU.mult, op1=ALU.mult)
                for k in range(8):
                    o("pe", "transpose", out=psT[:, k, :], in_=hbt[:, k * 128:(k + 1) * 128], identity=self.identb.v)
                o("act", "copy", out=hTt.v, in_=psT.v)
                for gi, (c0, w) in enumerate(TM_GROUPS):
                    pa = psA[gi % 2]
                    for k in range(8):
                        o("pe", "matmul", out=pa[:, 0:w], lhsT=hTt[:, k, :], rhs=Wb[:, k, c0:c0 + w],
                          start=(k == 0), stop=(k == 7))
                    off = TM_OFF[gi]
                    if gi % 2 == 0:
                        o("dve", "tensor_copy", out=c[:, off:off + w], in_=pa[:, 0:w])
                    else:
                        o("act", "copy", out=c[:, off:off + w], in_=pa[:, 0:w])
                for ci, c0 in enumerate(FM_COLS):
                    pb = psB[(ci // 4) % 2]
                    for k in range(8):
                        o("pe", "matmul", out=pb[:, ci % 4, :], lhsT=Wb[:, k, c0:c0 + 128], rhs=hTt[:, k, :],
                          start=(k == 0 and ci % 4 == 0), stop=(k == 7))
                    if ci in (1,):
                        o("act", "activation", out=f[:, 0:2, :], in_=pb[:, 0:2, :], func=AF.Silu)
                    elif ci in (3,):
                        o("act", "activation", out=f[:, 2:4, :], in_=pb[:, 2:4, :], func=AF.Sigmoid, scale=-1.0)
                    elif ci == 7:
                        o("dve", "tensor_copy", out=f[:, 4:8, :], in_=pb.v)
                    elif ci == 8:
                        o("dve", "tensor_copy", out=f[:, 8, :], in_=pb[:, 0, :])
                P.dma(self.fmT.tile(i, self.fmT.ap.rearrange("(c p) t -> p c t", p=128)[:, :, r0:r0 + 128]), f.v)


            def sB(i):
                r0 = i * 128
                c = ct[i % 2]
                if self.alvl < 2:
                    return
                tb = tmbS[i % 2]
                tf = tmfS[i % 2]
                cs64 = self.cs64[:, i, :]
                cs32 = self.cs32[:, i, :]
                o("pool", "tensor_copy", out=tb[:, B_HGV:B_HGV + 256], in_=c[:, C_HGI:C_HGI + 256])
                o("act", "activation", out=tf[:, F_HGG:F_HGG + 256], in_=c[:, C_HGG:C_HGG + 256], func=AF.Silu)
                o("pool", "tensor_copy", out=tf[:, F_MV:F_MV + 256], in_=c[:, C_MV:C_MV + 256])
                o("act", "activation", out=tf[:, F_OG:F_OG + 256], in_=c[:, C_OG:C_OG + 256], func=AF.Sigmoid)
                o("pool", "tensor_copy", out=tf[:, F_GT:F_GT + 8], in_=c[:, C_IG:C_IG + 8])
                o("act", "activation", out=tf[:, F_NG:F_NG + 12], in_=c[:, C_NG:C_NG + 12], func=AF.Sigmoid)
                self.rms_heads(c[:, C_NQ:C_NQ + 512].rr("p (h d) -> p h d", h=8), 8, 64, g8.v, wk.v, wk2.v, ssq.v, gain_full=True)
                o("act", "copy", out=qb[:, 0:4, :], in_=wk[:, 0:4, :])
                self.rope(wk.v, 8, 8, cs64, qb[:, 4:12, :], t1.v, t2.v)
                if self.alvl < 3:
                    return
                self.rms_heads(c[:, C_CKV:C_CKV + 128].rr("p (h d) -> p h d", h=1), 1, 128, gkv.v,
                               wk2[:, 0:2, :].rr("p a b -> p (a b)").rr("p (h d) -> p h d", h=1),
                               wk2[:, 2:4, :].rr("p a b -> p (a b)").rr("p (h d) -> p h d", h=1), ssq[:, 0:1])
                o("act", "copy", out=ckvb.v, in_=wk2[:, 0:2, :].rr("p a b -> p (a b)"))
                o("pe", "transpose", out=psT[:, 0, :], in_=ckvb.v, identity=self.identb.v)
                o("act", "copy", out=ckvT.v, in_=psT[:, 0, :])
                pa = psA[1]
                o("pe", "matmul", out=pa[:, 0:128], lhsT=ckvT.v, rhs=wukv.v, start=True, stop=True)
                o("act", "copy", out=kvf.v, in_=pa[:, 0:128])
                o("pool", "tensor_copy", out=tb[:, B_DV:B_DV + 64], in_=kvf[:, 64:128])
                o("pool", "memset", ap=tb[:, B_DV + 64:B_DV + 65], constant=1.0)
                o("act", "copy", out=K3[:, 0, :], in_=kvf[:, 0:64])
                o("dve", "tensor_copy", out=K3[:, 1, :], in_=c[:, C_KS:C_KS + 64])
                o("pool", "tensor_copy", out=K3[:, 2, :], in_=c[:, C_KW:C_KW + 64])
                self.rms_heads(K3.v, 3, 64, g3.v, K3n.v, K3t.v, ssq[:, 1:4], gain_full=True)
                self.rope(K3n.v, 3, 8, cs64, kb.v, t1[:, 0:3, :], t2[:, 0:3, :])
                o("pool", "tensor_copy", out=tb[:, B_NVS:B_NVS + 64], in_=c[:, C_VS:C_VS + 64])
                o("pool", "memset", ap=tb[:, B_NVS + 64:B_NVS + 65], constant=1.0)
                o("pool", "tensor_copy", out=tb[:, B_NVW:B_NVW + 64], in_=c[:, C_VW:C_VW + 64])
                o("pool", "memset", ap=tb[:, B_NVW + 64:B_NVW + 66], constant=1.0)
                if self.alvl < 4:
                    return
                o("dve", "tensor_scalar", out=iwa.v, in0=c[:, C_IW:C_IW + 8], scalar1=IWS, scalar2=None, op0=ALU.mult)
                o("dve", "scalar_tensor_tensor", out=iwa.v, in0=iwa.v, scalar=-1.0, in1=iwa.v, op0=ALU.mult, op1=ALU.max)
                o("act", "activation", out=tf[:, F_SGN:F_SGN + 8], in_=c[:, C_IW:C_IW + 8], func=AF.Sign)
                IQ = wk[:, 0:4, :].rr("p a b -> p (a b)").rr("p (h d) -> p h d", h=8)
                IQ2 = wk[:, 4:8, :].rr("p a b -> p (a b)").rr("p (h d) -> p h d", h=8)
                self.rope(c[:, C_IQ:C_IQ + 256].rr("p (h d) -> p h d", h=8), 8, 4, cs32, IQ, t1[:, :, 0:4], t2[:, :, 0:4])
                o("dve", "tensor_tensor", out=iqb.v, in0=IQ, in1=iwa.v.us(2).bc([128, 8, 32]), op=ALU.mult)
                IK = wk2[:, 0:1, 0:32]
                self.rms_heads(c[:, C_IK:C_IK + 32].rr("p (h d) -> p h d", h=1), 1, 32, gik.v, IK, wk2[:, 1:2, 0:32], ssq[:, 4:5])
                self.rope(IK, 1, 4, cs32, ikb.v.rr("p (h d) -> p h d", h=1), t1[:, 0:1, 0:4], t2[:, 0:1, 0:4])
                if self.alvl < 5:
                    return
                import os
                B = int(os.environ.get("KDBG_B", "99"))
                q_s = qS[i % 2]
                k_s = kS[i % 2]
                for h in range(4):
                    o("pe", "transpose", out=psX[0:64, h, :], in_=qb[:, 8 + h, :], identity=self.identb.v)
                if B >= 1:
                    for j in range(3):
                        o("pe", "transpose", out=psX[0:64, 4 + j, :], in_=kb[:, j, :], identity=self.identb.v)
                if B >= 2:
                    o("pe", "transpose", out=psX[0:32, 7, :], in_=ikb.v, identity=self.identb.v)
                o("act", "copy", out=q_s[:, 0, :], in_=psX[0:64, 0:4, :].rr("p h t -> p (h t)"))
                if B >= 1:
                    o("dve", "tensor_copy", out=k_s.v, in_=psX[0:64, 4:7, :])
                if B >= 2:
                    o("dve", "tensor_copy", out=ikS[i % 2].v, in_=psX[0:32, 7, :])
                if B >= 3:
                    for h in range(8):
                        o("pe", "transpose", out=psY[0:64, h, :], in_=qb[:, h, :], identity=self.identb.v)
                    o("act", "copy", out=q_s[:, 1:3, :].rr("p a n -> p (a n)"), in_=psY[0:64, :, :].rr("p h t -> p (h t)"))
                if B >= 4:
                    for h in range(8):
                        o("pe", "transpose", out=psZ[0:32, h, :], in_=iqb[:, h, :], identity=self.identb.v)
                    o("dve", "tensor_copy", out=iqS[i % 2].v, in_=psZ[0:32, :, :].rr("p h t -> p (h t)"))
                if self.alvl < 6:
                    return
                P.dma(self.tmb.tile(i, self.tmb.ap[r0:r0 + 128, :]), tb.v)
                if self.alvl < 7:
                    return
                P.dma(self.tmf.tile(i, self.tmf.ap[r0:r0 + 128, :]), tf.v)
                if self.alvl < 8:
                    return
                P.dma(self.kT3.tile(i, self.kT3.ap.rearrange("k p t -> p k t")[:, :, r0:r0 + 128]), k_s.v)
                if self.alvl < 9:
                    return
                P.dma(self.ikT.tile(i, self.ikT.ap[:, r0:r0 + 128]), ikS[i % 2].v)
                P.dma(self.qT3.tile(i, self.qT3.ap[i]), q_s.v)
                P.dma(self.iqT.tile(i, self.iqT.ap[i]), iqS[i % 2].v)
            sA(0)
            for i in range(NT):
                if i + 1 < NT:
                    sA(i + 1)
                sB(i)
            P.barrier()
        self.es = top

    def phase_C(self, l, xsrc):
        self.phase_C1(l, xsrc)
        self.phase_C2(l)

    def phase_C1(self, l, xsrc):
        P = self.P
        o = self.op
        prm = self.prm
        top = self.es
        with ExitStack() as es:
            self.es = es
            Wo = self.sb("Wo", [128, 8, D], BF16)
            Wq = self.sb("Wq", [128, 8, 256], BF16)
            Wxo = self.sb("Wxo", [128, 2, D], BF16)
            xkT = self.sb("xkT", [64, 4, NMEM], BF16)
            xv = self.sb("xv", [128, 2, 4, 65], BF16)
            gq = self.sb("gq", [128, 64], F32)
            self.bcast_row(gq.v, prm["xa_q_gain"][l])
            gxa = self.sb("gxa", [128, D], F32)
            self.bcast_row(gxa.v, prm["norm_xa"][l])
            with ExitStack() as es2:
                self.es = es2
                stage = self.rot("wst", [128, 4, 512], F32, 3)
                self.load_weight_bf16(Wo, prm["w_out"][l], 8, D, stage=stage)
                self.load_weight_bf16(Wq, prm["xa_wq"][l], 8, 256, stage=stage)
                self.load_weight_bf16(Wxo, prm["xa_wo"][l], 2, D, stage=stage)
                Wkv = self.sb("Wkv", [128, 8, 512], BF16)
                self.load_weight_bf16(Wkv, prm["xa_wkv"][l], 8, 512, stage=stage)
                gmem = self.sb("gmem", [128, D], F32)
                self.bcast_row(gmem.v, prm["norm_mem"][l])
                gk = self.sb("gk", [128, 64], F32)
                self.bcast_row(gk.v, prm["xa_k_gain"][l])
                mt = self.sb("mt", [128, D], F32)
                mj = self.sb("mj", [128, D], BF16)
                mss = self.sb("mss", [128, 1], F32)
                mb = self.sb("mb", [128, D], BF16)
                mT = self.sb("mT", [128, 8, 128], BF16)
                kvf = self.sb("kvf", [128, 512], F32)
                kn = self.sb("kn", [128, 4, 64], F32)
                ktmp = self.sb("ktmp", [128, 4, 64], F32)
                kss = self.sb("kss", [128, 4], F32)
                knb = self.sb("knb", [128, 4, 64], BF16)
                pT = self.ps("pT", [128, 8, 128], BF16)
                pK = self.ps("pK", [128, 512], F32)
                for m in range(2):
                    P.dma(mt.v, self.mem_in.all(self.mem_in.ap[m * 128:(m + 1) * 128, :]))
                    o("act", "activation", out=mj.v, in_=mt.v, func=AF.Square, accum_out=mss.v)
                    o("act", "activation", out=mss.v, in_=mss.v, func=AF.Sqrt, scale=1.0 / D, bias=self.epsb[:, 0:1])
                    o("dve", "reciprocal", out=mss.v, in_=mss.v)
                    o("dve", "scalar_tensor_tensor", out=mb.v, in0=mt.v, scalar=mss[:, 0:1], in1=gmem.v,
                      op0=ALU.mult, op1=ALU.mult)
                    for k in range(8):
                        o("pe", "transpose", out=pT[:, k, :], in_=mb[:, k * 128:(k + 1) * 128], identity=self.identb.v)
                    o("act", "copy", out=mT.v, in_=pT.v)
                    for k in range(8):
                        o("pe", "matmul", out=pK.v, lhsT=mT[:, k, :], rhs=Wkv[:, k, :], start=(k == 0), stop=(k == 7))
                    o("act", "copy", out=kvf.v, in_=pK.v)
                    self.rms_heads(kvf[:, 0:256].rr("p (h d) -> p h d", h=4), 4, 64, gk.v, kn.v, ktmp.v, kss.v)
                    o("act", "copy", out=knb.v, in_=kn.v)
                    for h in range(4):
                        o("pe", "transpose", out=pT[0:64, h, :], in_=knb[:, h, :], identity=self.identb.v)
                    o("act", "copy", out=xkT[:, :, m * 128:(m + 1) * 128], in_=pT[0:64, 0:4, :])
                    o("dve", "tensor_copy", out=xv[:, m, :, 0:64], in_=kvf[:, 256:512].rr("p (h d) -> p h d", h=4))
                    o("pool", "memset", ap=xv[:, m, :, 64:65], constant=1.0)
                P.barrier()
            self.es = es
            xt = self.rot("xt", [128, D], F32, 2)
            mxb = self.rot("mxb", [128, D], BF16, 2)
            mxT = self.sb("mxT", [128, 8, 128], BF16)
            x1r = self.rot("x1", [128, D], F32, 3)
            junk = self.sb("junk", [128, D], BF16)
            ss = self.sb("ss", [128, 1], F32)
            hb = self.sb("hb", [128, D], BF16)
            hT = self.sb("hT", [128, 8, 128], BF16)
            qfr = self.rot("qf", [128, 4, 64], F32, 2)
            qn = self.sb("qn", [128, 4, 64], F32)
            qtmp = self.sb("qtmp", [128, 4, 64], F32)
            qss = self.sb("qss", [128, 4], F32)
            qnb = self.sb("qnb", [128, 4, 64], BF16)
            qT4 = self.sb("qT4", [64, 4, 128], BF16)
            PT = self.rot("PT", [128, 512], BF16, 2)
            rden = self.sb("rden", [128, 4], F32)
            ob = self.sb("ob", [128, 4, 64], BF16)
            oT = self.sb("oT", [128, 2, 128], BF16)
            psTa = self.ps("psTa", [128, 8, 128], BF16)
            psTb = self.ps("psTb", [128, 8, 128], BF16)
            psM = self.rot("psM", [128, 512], F32, 2, psum=True)
            psW = self.ps("psW", [128, 512], F32)
            psS = self.rot("psS", [128, 512], F32, 2, psum=True)
            psO = self.ps("psO", [128, 4, 65], F32)

            def s1(i):
                r0 = i * 128
                x_t = xt[i % 2]
                mx = mxb[i % 2]
                x1 = x1r[i % 3]
                P.dma(x_t.v, xsrc.tile(i, xsrc.ap[r0:r0 + 128, :]))
                P.dma(mx.v, self.mixed.tile(i, self.mixed.ap[r0:r0 + 128, :]))
                for k in range(8):
                    o("pe", "transpose", out=psTa[:, k, :], in_=mx[:, k * 128:(k + 1) * 128], identity=self.identb.v)
                o("act", "copy", out=mxT.v, in_=psTa.v)
                for g in range(2):
                    pm = psM[g]
                    for k in range(8):
                        o("pe", "matmul", out=pm.v, lhsT=mxT[:, k, :], rhs=Wo[:, k, g * 512:(g + 1) * 512],
                          start=(k == 0), stop=(k == 7))
                    o("dve", "tensor_tensor", out=x1[:, g * 512:(g + 1) * 512], in0=pm.v, in1=x_t[:, g * 512:(g + 1) * 512], op=ALU.add)
                o("act", "activation", out=junk.v, in_=x1.v, func=AF.Square, accum_out=ss.v)
                o("act", "activation", out=ss.v, in_=ss.v, func=AF.Sqrt, scale=1.0 / D, bias=self.epsb[:, 0:1])
                o("dve", "reciprocal", out=ss.v, in_=ss.v)
                o("dve", "scalar_tensor_tensor", out=hb.v, in0=x1.v, scalar=ss[:, 0:1], in1=gxa.v, op0=ALU.mult, op1=ALU.mult)
                for k in range(8):
                    o("pe", "transpose", out=psTa[:, k, :], in_=hb[:, k * 128:(k + 1) * 128], identity=self.identb.v)
                o("act", "copy", out=hT.v, in_=psTa.v)
                pm = psM[0]
                for k in range(8):
                    o("pe", "matmul", out=pm[:, 0:256], lhsT=hT[:, k, :], rhs=Wq[:, k, :], start=(k == 0), stop=(k == 7))
                o("act", "copy", out=qfr[i % 2].v.rr("p h d -> p (h d)"), in_=pm[:, 0:256])

            def s2(i):
                r0 = i * 128
                x1 = x1r[i % 3]
                self.rms_heads(qfr[i % 2].v, 4, 64, gq.v, qn.v, qtmp.v, qss.v)
                o("act", "copy", out=qnb.v, in_=qn.v)
                for h in range(4):
                    o("pe", "transpose", out=psTb[0:64, h, :], in_=qnb[:, h, :], identity=self.identb.v)
                o("act", "copy", out=qT4.v, in_=psTb[0:64, 0:4, :])
                for m in range(2):
                    pss = psS[m]
                    for h in range(4):
                        o("pe", "matmul", out=pss[:, h * 128:(h + 1) * 128], lhsT=xkT[:, h, m * 128:(m + 1) * 128],
                          rhs=qT4[:, h, :], start=(h == 0), stop=(h == 3))
                    pt = PT[m]
                    o("act", "activation", out=pt.v, in_=pss.v, func=AF.Exp, scale=0.125)
                    for h in range(4):
                        o("pe", "matmul", out=psO[:, h, :], lhsT=pt[:, h * 128:(h + 1) * 128], rhs=xv[:, m, h, :],
                          start=(m == 0 and h == 0), stop=(m == 1))
                o("dve", "reciprocal", out=rden.v, in_=psO[:, :, 64])
                o("dve", "tensor_tensor", out=ob.v, in0=psO[:, :, 0:64], in1=rden.v.us(2).bc([128, 4, 64]), op=ALU.mult)
                for k in range(2):
                    o("pe", "transpose", out=psTb[:, 4 + k, :], in_=ob.v.rr("p h d -> p (h d)")[:, k * 128:(k + 1) * 128], identity=self.identb.v)
                o("act", "copy", out=oT.v, in_=psTb[:, 4:6, :])
                for g in range(2):
                    for k in range(2):
                        o("pe", "matmul", out=psW.v, lhsT=oT[:, k, :], rhs=Wxo[:, k, g * 512:(g + 1) * 512],
                          start=(k == 0), stop=(k == 1))
                    o("dve", "tensor_tensor", out=x1[:, g * 512:(g + 1) * 512], in0=psW.v, in1=x1[:, g * 512:(g + 1) * 512], op=ALU.add)
                P.dma(self.xo.tile(i, self.xo.ap[r0:r0 + 128, :]), x1.v)

            n_t = self.ntl
            s1(0)
            for i in range(n_t):
                if i + 1 < n_t:
                    s1(i + 1)
                s2(i)
            P.barrier()
        self.es = top

    def phase_C2(self, l):
        P = self.P
        o = self.op
        prm = self.prm
        top = self.es
        with ExitStack() as es:
            self.es = es
            W13 = self.sb("W13", [128, 8, 2 * DFF], BF16)
            W2 = self.sb("W2", [128, NFC, D], BF16)
            gff = self.sb("gff", [128, D], F32)
            self.bcast_row(gff.v, prm["norm_ffn"][l])
            with ExitStack() as es2:
                self.es = es2
                stage = self.rot("wst", [128, 4, 512], F32, 3)
                self.load_weight_bf16(W13, prm["ffn_w13"][l], 8, 2 * DFF, stage=stage)
                self.load_weight_bf16(W2, prm["ffn_w2"][l], NFC, D, stage=stage)
                P.barrier()
            self.es = es
            xt = self.rot("xt", [128, D], F32, 3)
            junk = self.sb("junk", [128, D], BF16)
            ss = self.sb("ss", [128, 1], F32)
            hb = self.sb("hb", [128, D], BF16)
            hTr = self.rot("hT", [128, 8, 128], BF16, 2)
            gT = self.sb("gT", [128, NFC, 128], BF16)
            sa = self.rot("sa", [128, 4, 128], F32, 2)
            psT = self.ps("psT", [128, 8, 128], BF16)
            psM = self.rot("psM", [128, 512], F32, 2, psum=True)
            psF = self.rot("psF", [128, 4, 128], F32, 4, psum=True)

            def front(i):
                r0 = i * 128
                x2 = xt[i % 3]
                P.dma(x2.v, self.xo.tile(i, self.xo.ap[r0:r0 + 128, :]))
                o("act", "activation", out=junk.v, in_=x2.v, func=AF.Square, accum_out=ss.v)
                o("act", "activation", out=ss.v, in_=ss.v, func=AF.Sqrt, scale=1.0 / D, bias=self.epsb[:, 0:1])
                o("dve", "reciprocal", out=ss.v, in_=ss.v)
                o("dve", "scalar_tensor_tensor", out=hb.v, in0=x2.v, scalar=ss[:, 0:1], in1=gff.v, op0=ALU.mult, op1=ALU.mult)
                for k in range(8):
                    o("pe", "transpose", out=psT[:, k, :], in_=hb[:, k * 128:(k + 1) * 128], identity=self.identb.v)
                o("act", "copy", out=hTr[i % 2].v, in_=psT.v)

            def ab(i):
                hT = hTr[i % 2]
                for gi, g0 in enumerate(range(0, NFC, 4)):
                    g1 = min(NFC, g0 + 4)
                    n = g1 - g0
                    pfs = (psF[(gi % 2) * 2], psF[(gi % 2) * 2 + 1])
                    for half in range(2):
                        pf = pfs[half]
                        for c in range(g0, g1):
                            col = half * DFF + c * 128
                            for k in range(8):
                                o("pe", "matmul", out=pf[:, c - g0, :], lhsT=W13[:, k, col:col + 128], rhs=hT[:, k, :],
                                  start=(k == 0 and c == g0), stop=(k == 7))
                    s_a = sa[gi % 2]
                    o("act", "activation", out=s_a[:, 0:n, :], in_=pfs[0][:, 0:n, :], func=AF.Silu)
                    o("dve", "tensor_tensor", out=gT[:, g0:g1, :], in0=pfs[1][:, 0:n, :], in1=s_a[:, 0:n, :], op=ALU.mult)

            def w2(i):
                r0 = i * 128
                x2 = xt[i % 3]
                for g in range(2):
                    pm = psM[g]
                    for c in range(NFC):
                        o("pe", "matmul", out=pm.v, lhsT=gT[:, c, :], rhs=W2[:, c, g * 512:(g + 1) * 512],
                          start=(c == 0), stop=(c == NFC - 1))
                    o("dve", "tensor_tensor", out=x2[:, g * 512:(g + 1) * 512], in0=pm.v, in1=x2[:, g * 512:(g + 1) * 512], op=ALU.add)
                P.dma(self.xo.tile(i, self.xo.ap[r0:r0 + 128, :]), x2.v)

            n_t = self.ntl
            front(0)
            for i in range(n_t):
                ab(i)
                if i + 1 < n_t:
                    front(i + 1)
                w2(i)
            P.barrier()
        self.es = top

    def phase_HG(self, l):
        P = self.P
        o = self.op
        prm = self.prm
        top = self.es
        CH = 64
        NCH = L // CH
        with ExitStack() as es:
            self.es = es
            oml = self.sb("oml", [128, 2], F32)
            if l == 0:
                o("pool", "memset", ap=oml.v, constant=1.0)
            else:
                lb0 = self.sb("lb0", [128, 2], F32)
                lb1 = self.sb("lb1", [128, 2], F32)
                for hp in range(2):
                    self.col_load(lb0[:, hp:hp + 1], prm["lb_param"][0, hp * 128:(hp + 1) * 128])
                    self.col_load(lb1[:, hp:hp + 1], prm["lb_param"][1, hp * 128:(hp + 1) * 128])
                o("dve", "tensor_tensor", out=lb0.v, in0=lb0.v, in1=lb1.v, op=ALU.subtract)
                o("act", "activation", out=oml.v, in_=lb0.v, func=AF.Sigmoid)
            gO = self.sb("gO", [128, 64], F32)
            self.bcast_row(gO.v, prm["hg_o_gain"][l])
            hm = self.sb("hm", [128, 2], F32)
            blk = self.sb("blk", [128, 128], F32)
            o("pool", "memset", ap=hm.v, constant=0.0)
            o("pool", "memset", ap=hm[0:64, 0:1], constant=1.0)
            o("pool", "memset", ap=hm[64:128, 1:2], constant=1.0)
            o("pool", "memset", ap=blk.v, constant=0.0)
            o("pool", "memset", ap=blk[0:64, 0:64], constant=1.0)
            o("pool", "memset", ap=blk[64:128, 64:128], constant=1.0)
            msk = self.sb("msk", [128, L], F32)
            o("pool", "memset", ap=msk.v, constant=1.0)
            o("pool", "memset", ap=msk.v.rr("p (c t) -> p c t", t=CH)[:, :, 0:1], constant=0.0)
            A1 = self.sb("A1", [128, L], F32)
            A2 = self.sb("A2", [128, L], F32)
            A3 = self.sb("A3", [128, L], F32)
            A4 = self.sb("A4", [128, L], F32)
            A5 = self.sb("A5", [128, L], F32)
            QTs = [self.sb("QT%d" % p, [128, L], BF16) for p in range(2)]
            KTms = [[self.sb("KTm%d%d" % (p, h), [128, L], BF16) for h in range(2)] for p in range(2)]
            KLs = [self.sb("KL%d" % p, [128, L], BF16) for p in range(2)]
            ELs = [self.sb("EL%d" % p, [128, NCH], F32) for p in range(2)]
            DLs = [self.sb("DL%d" % p, [128, NCH], F32) for p in range(2)]
            EMs = [self.sb("EM%d" % p, [128, NCH], F32) for p in range(2)]
            Ss = [self.sb("S%d" % p, [128, 128], F32) for p in range(2)]
            Sbs = [self.sb("Sb%d" % p, [128, 128], BF16) for p in range(2)]
            Vts = [self.rot("Vt%d" % p, [64, 2, 128], BF16, 2) for p in range(2)]
            Gts = [self.rot("Gt%d" % p, [64, 2, 128], F32, 2) for p in range(2)]
            KLts = [self.rot("KLt%d" % p, [64, 128], BF16, 2) for p in range(2)]
            Wms = [self.rot("Wm%d" % p, [64, 2, 64], BF16, 2) for p in range(2)]
            Ocs = [self.rot("Oc%d" % p, [64, 2, 64], F32, 2) for p in range(2)]
            Ons = [self.sb("On%d" % p, [64, 2, 64], F32) for p in range(2)]
            otmps = [self.sb("otmp%d" % p, [64, 2, 64], F32) for p in range(2)]
            osss = [self.sb("oss%d" % p, [64, 2], F32) for p in range(2)]
            Osts = [self.rot("Ost%d" % p, [64, 2, 128], BF16, 2) for p in range(2)]
            psKs = [self.ps("psK%d" % p, [64, 128], BF16) for p in range(2)]
            psSs = [self.ps("psS%d" % p, [64, 2, 64], F32) for p in range(2)]
            psOs = [self.ps("psO%d" % p, [64, 2, 64], F32) for p in range(2)]
            psDs = [self.ps("psD%d" % p, [128, 128], F32) for p in range(2)]
            fm = self.fmT
            c3 = lambda b: b.v.rr("p (c t) -> p c t", t=CH)
            for hp in range(2):
                QT, KTm, KL, EL, DL, EM, S, Sb = QTs[hp], KTms[hp], KLs[hp], ELs[hp], DLs[hp], EMs[hp], Ss[hp], Sbs[hp]
                P.dma(A1.v, fm.all(fm.ap[hp * 128:(hp + 1) * 128, :]))
                P.dma(A2.v, fm.all(fm.ap[256 + hp * 128:256 + (hp + 1) * 128, :]))
                o("pool", "tensor_scalar", out=A2.v, in0=A2.v, scalar1=oml[:, hp:hp + 1], scalar2=None, op0=ALU.mult)
                o("act", "activation", out=A3.v, in_=A2.v, func=AF.Ln, scale=-1.0, bias=self.onesf[:, 0:1])
                o("dve", "tensor_tensor_scan", out=A4.v, data0=msk.v, data1=A3.v, initial=0.0, op0=ALU.mult, op1=ALU.add)
                B3 = c3(A4)
                o("dve", "tensor_tensor", out=c3(A3), in0=B3, in1=B3[:, :, 31:32].bc([128, NCH, CH]), op=ALU.subtract)
                o("act", "activation", out=A5.v, in_=A3.v, func=AF.Exp)
                o("dve", "scalar_tensor_tensor", out=QT.v, in0=A1.v, scalar=0.125, in1=A5.v, op0=ALU.mult, op1=ALU.mult)
                o("act", "activation", out=A5.v, in_=A3.v, func=AF.Exp, scale=-1.0)
                o("pool", "tensor_tensor", out=A5.v, in0=A5.v, in1=A2.v, op=ALU.mult)
                o("act", "activation", out=KTm[0].v, in_=A5.v, func=AF.Copy, scale=hm[:, 0:1])
                o("dve", "tensor_scalar", out=KTm[1].v, in0=A5.v, scalar1=hm[:, 1:2], scalar2=None, op0=ALU.mult)
                o("dve", "tensor_tensor", out=EL.v.us(2), in0=B3[:, :, 63:64], in1=B3[:, :, 31:32], op=ALU.subtract)
                o("act", "activation", out=EL.v, in_=EL.v, func=AF.Exp)
                o("act", "activation", out=DL.v.us(2), in_=B3[:, :, 63:64], func=AF.Exp)
                o("act", "activation", out=EM.v.us(2), in_=B3[:, :, 31:32], func=AF.Exp)
                o("pool", "tensor_tensor", out=c3(KL), in0=c3(A5), in1=EL.v.us(2).bc([128, NCH, CH]), op=ALU.mult)
                o("pool", "memset", ap=S.v, constant=0.0)
                o("pool", "memset", ap=Sb.v, constant=0.0)

            def step(hp, c):
                QT, KTm, KL, DL, EM, S, Sb = QTs[hp], KTms[hp], KLs[hp], DLs[hp], EMs[hp], Ss[hp], Sbs[hp]
                ti, ci = c // 2, c % 2
                r0 = ti * 128
                cs = slice(c * CH, (c + 1) * CH)
                if ci == 0:
                    P.dma(Vts[hp][ti % 2].v, self.tmb.tile(ti, self.tmb.ap[r0:r0 + 128, B_HGV + hp * 128:B_HGV + (hp + 1) * 128]
                                                          .rearrange("(c s) d -> s c d", s=CH)))
                    P.dma(Gts[hp][ti % 2].v, self.tmf.tile(ti, self.tmf.ap[r0:r0 + 128, F_HGG + hp * 128:F_HGG + (hp + 1) * 128]
                                                          .rearrange("(c s) d -> s c d", s=CH)))
                V = Vts[hp][ti % 2]
                G = Gts[hp][ti % 2]
                pk = psKs[hp]
                o("pe", "transpose", out=pk.v, in_=KL[:, cs], identity=self.identb.v)
                klt = KLts[hp][c % 2]
                o("act", "copy", out=klt.v, in_=pk.v)
                pS = psSs[hp]
                for h in range(2):
                    o("pe", "matmul", out=pS[:, h, :], lhsT=KTm[h][:, cs], rhs=QT[:, cs], start=(h == 0), stop=(h == 1))
                wm = Wms[hp][c % 2]
                o("dve", "tensor_tensor", out=wm.v, in0=pS.v, in1=self.triuf[0:64, 0:64].us(1).bc([64, 2, 64]), op=ALU.mult)
                pO = psOs[hp]
                for h in range(2):
                    hs = slice(h * 64, (h + 1) * 64)
                    o("pe", "matmul", out=pO[:, h, :], lhsT=wm[:, h, :], rhs=V[:, ci, hs], start=(h == 0), stop=False)
                    o("pe", "matmul", out=pO[:, h, :], lhsT=QT[:, cs], rhs=Sb[:, hs], start=False, stop=True)
                pD = psDs[hp]
                o("pe", "matmul", out=pD.v, lhsT=klt.v, rhs=V[:, ci, :], start=True, stop=True)
                o("dve", "scalar_tensor_tensor", out=S.v, in0=S.v, scalar=DL[:, c:c + 1], in1=pD.v, op0=ALU.mult, op1=ALU.add)
                if c + 1 < NCH:
                    o("dve", "scalar_tensor_tensor", out=Sb.v, in0=S.v, scalar=EM[:, c + 1:c + 2], in1=blk.v, op0=ALU.mult, op1=ALU.mult)
                oc = Ocs[hp][c % 2]
                o("act", "copy", out=oc.v, in_=pO.v)
                self.rms_heads(oc.v, 2, 64, gO[0:64, :], Ons[hp].v, otmps[hp].v, osss[hp].v, np_=64)
                ost = Osts[hp][ti % 2]
                o("dve", "tensor_tensor", out=ost[:, ci, :], in0=Ons[hp].v.rr("p h d -> p (h d)"), in1=G[:, ci, :], op=ALU.mult)
                if ci == 1:
                    P.dma(self.mixed.tile(ti, self.mixed.ap[r0:r0 + 128, hp * 128:(hp + 1) * 128]
                                          .rearrange("(c s) d -> s c d", s=CH)), ost.v)

            for c in range(2 * self.ntl):
                for hp in range(2):
                    step(hp, c)
            P.barrier()
        self.es = top

    def phase_ML(self, l):
        P = self.P
        o = self.op
        prm = self.prm
        top = self.es
        with ExitStack() as es:
            self.es = es
            cw = self.sb("cw", [128, 4, 4], F32)
            cbs = self.sb("cbs", [128, 4], F32)
            for ch in range(4):
                for j in range(4):
                    self.col_load(cw[:, ch, j:j + 1], prm["ml_conv_w"][l, j, ch * 128:(ch + 1) * 128])
                self.col_load(cbs[:, ch:ch + 1], prm["ml_conv_b"][l, ch * 128:(ch + 1) * 128])
            ibf = self.sb("ibf", [128, 8], F32)
            self.bcast_row(ibf[:, 0:4], prm["ml_i_bias"][l])
            self.bcast_row(ibf[:, 4:8], prm["ml_f_bias"][l])
            gO = self.sb("gO", [128, 64], F32)
            self.bcast_row(gO.v, prm["ml_o_gain"][l])
            GT = self.sb("GT", [128, NT, 8], F32)
            for q in range(4):
                P.dma(GT[:, q * 8:(q + 1) * 8, :], self.tmf.all(self.tmf.ap[q * 1024:(q + 1) * 1024, F_GT:F_GT + 8]
                                                                .rearrange("(n p) g -> p n g", p=128)))
            LI = self.sb("LI", [128, NT, 4], F32)
            LF = self.sb("LF", [128, NT, 4], F32)
            Fc = self.sb("Fc", [128, NT, 4], F32)
            E = self.sb("E", [128, NT, 4], F32)
            BV = self.sb("BV", [128, NT, 4], F32)
            Gd = self.sb("Gd", [128, NT, 4], F32)
            GdP = self.sb("GdP", [128, NT, 2], F32)
            psG = self.ps("psG", [128, 128], F32)
            o("dve", "tensor_tensor", out=LI.v, in0=GT[:, :, 0:4], in1=ibf[:, 0:4].us(1).bc([128, NT, 4]), op=ALU.add)
            o("dve", "tensor_tensor", out=LF.v, in0=GT[:, :, 4:8], in1=ibf[:, 4:8].us(1).bc([128, NT, 4]), op=ALU.add)
            o("act", "activation", out=LF.v, in_=LF.v, func=AF.Exp, scale=-1.0)
            o("act", "activation", out=LF.v, in_=LF.v, func=AF.Ln, bias=self.onesf[:, 0:1])
            o("dve", "tensor_scalar", out=LF.v, in0=LF.v, scalar1=-1.0, scalar2=None, op0=ALU.mult)
            lf2 = LF.v.rr("p n h -> p (n h)")
            o("pe", "matmul", out=psG.v, lhsT=self.triuf.v, rhs=lf2, start=True, stop=True)
            o("dve", "tensor_copy", out=Fc.v.rr("p n h -> p (n h)"), in_=psG.v)
            o("act", "activation", out=E.v, in_=Fc.v, func=AF.Exp)
            o("dve", "tensor_tensor", out=BV.v, in0=LI.v, in1=Fc.v, op=ALU.subtract)
            o("act", "activation", out=BV.v, in_=BV.v, func=AF.Exp)
            o("pe", "matmul", out=psG.v, lhsT=self.onesf.v, rhs=lf2, start=True, stop=True)
            o("act", "activation", out=Gd.v.rr("p n h -> p (n h)"), in_=psG.v, func=AF.Exp)
            for pr in range(2):
                o("dve", "tensor_copy", out=GdP[0:64, :, pr:pr + 1], in_=Gd[0:64, :, 2 * pr:2 * pr + 1])
                o("dve", "tensor_copy", out=GdP[64:128, :, pr:pr + 1], in_=Gd[64:128, :, 2 * pr + 1:2 * pr + 2])
            QK = [self.sb("QK%d" % ch, [128, L], BF16) for ch in range(4)]
            X = self.rot("X", [128, L], F32, 2)
            Y = self.sb("Y", [128, L], F32)
            fm = self.fmT
            for ch in range(4):
                x = X[ch % 2]
                P.dma(x.v, fm.all(fm.ap[512 + ch * 128:512 + (ch + 1) * 128, :]))
                o("dve", "tensor_scalar", out=Y.v, in0=x.v, scalar1=cw[:, ch, 3:4], scalar2=cbs[:, ch:ch + 1], op0=ALU.mult, op1=ALU.add)
                for sh in (1, 2, 3):
                    o("dve", "scalar_tensor_tensor", out=Y[:, sh:L], in0=x[:, 0:L - sh], scalar=cw[:, ch, 3 - sh:4 - sh],
                      in1=Y[:, sh:L], op0=ALU.mult, op1=ALU.add)
                o("act", "activation", out=QK[ch].v, in_=Y.v, func=AF.Silu)
                if ch >= 2:
                    o("pool", "tensor_scalar", out=QK[ch].v, in0=QK[ch].v, scalar1=0.125, scalar2=None, op0=ALU.mult)
            hm = self.sb("hm", [128, 2], F32)
            blk = self.sb("blk", [128, 130], F32)
            o("pool", "memset", ap=hm.v, constant=0.0)
            o("pool", "memset", ap=hm[0:64, 0:1], constant=1.0)
            o("pool", "memset", ap=hm[64:128, 1:2], constant=1.0)
            o("pool", "memset", ap=blk.v, constant=0.0)
            o("pool", "memset", ap=blk[0:64, 0:65], constant=1.0)
            o("pool", "memset", ap=blk[64:128, 65:130], constant=1.0)
            km = [[self.sb("km%d%d" % (pr, hl), [128, L], BF16) for hl in range(2)] for pr in range(2)]
            for pr in range(2):
                o("dve", "tensor_scalar", out=km[pr][0].v, in0=QK[2 + pr].v, scalar1=hm[:, 0:1], scalar2=None, op0=ALU.mult)
                o("pool", "tensor_scalar", out=km[pr][1].v, in0=QK[2 + pr].v, scalar1=hm[:, 1:2], scalar2=None, op0=ALU.mult)
            C = [self.sb("C%d" % pr, [128, 130], F32) for pr in range(2)]
            Cb = [self.sb("Cb%d" % pr, [128, 130], BF16) for pr in range(2)]
            Tts = [self.sb("Tt%d" % pr, [128, 130], F32) for pr in range(2)]
            for pr in range(2):
                o("pool", "memset", ap=C[pr].v, constant=0.0)
                o("pool", "memset", ap=Cb[pr].v, constant=0.0)
            vo = self.rot("vo", [128, 512], F32, 2)
            Vt = self.rot("Vt", [128, 4, 65], BF16, 2)
            kt = self.rot("kt", [128, 128], BF16, 2)
            Wm = self.rot("Wm", [128, 2, 128], BF16, 2)
            dd = self.sb("dd", [128, 4], F32)
            dd2 = self.sb("dd2", [128, 4], F32)
            Ho = self.sb("Ho", [128, 4, 64], F32)
            Hn = self.sb("Hn", [128, 4, 64], F32)
            htmp = self.sb("htmp", [128, 4, 64], F32)
            hss = self.sb("hss", [128, 4], F32)
            Ost = self.rot("Ost", [128, 256], BF16, 2)
            psK = self.rot("psK", [128, 128], BF16, 2, psum=True)
            psS = self.rot("psS", [128, 2, 128], F32, 2, psum=True)
            psO = self.ps("psO", [128, 4, 65], F32)
            psD = self.rot("psD", [128, 130], F32, 2, psum=True)
            for n in range(self.ntl):
                r0 = n * 128
                ts = slice(r0, r0 + 128)
                v_o = vo[n % 2]
                P.dma(v_o.v, self.tmf.tile(n, self.tmf.ap[r0:r0 + 128, F_MV:F_MV + 512]))
                vt = Vt[n % 2]
                o("dve", "tensor_tensor", out=vt[:, :, 0:64], in0=v_o[:, 0:256].rr("p (h d) -> p h d", h=4),
                  in1=BV[:, n, :].us(2).bc([128, 4, 64]), op=ALU.mult)
                o("pool", "tensor_copy", out=vt[:, :, 64:65], in_=BV[:, n, :].us(2))
                for pr in range(2):
                    kq = QK[2 + pr]
                    qq = QK[pr]
                    pk = psK[pr]
                    o("pe", "transpose", out=pk.v, in_=kq[:, ts], identity=self.identb.v)
                    ktt = kt[pr]
                    o("act", "copy", out=ktt.v, in_=pk.v)
                    pS = psS[pr]
                    for hl in range(2):
                        hs = slice(hl * 64, (hl + 1) * 64)
                        o("pe", "matmul", out=pS[:, hl, :], lhsT=km[pr][hl][:, ts], rhs=qq[:, ts], start=(hl == 0), stop=(hl == 1))
                    wm = Wm[pr]
                    o("dve", "tensor_tensor", out=wm.v, in0=pS.v, in1=self.triuf.v.us(1).bc([128, 2, 128]), op=ALU.mult)
                    for hl in range(2):
                        h = 2 * pr + hl
                        hs = slice(hl * 64, (hl + 1) * 64)
                        o("pe", "matmul", out=psO[:, h, :], lhsT=qq[:, ts], rhs=Cb[pr][:, hl * 65:(hl + 1) * 65],
                          start=(h == 0), stop=False)
                        o("pe", "matmul", out=psO[:, h, :], lhsT=wm[:, hl, :], rhs=vt[:, h, :], start=False, stop=True)
                    pD = psD[pr]
                    o("pe", "matmul", out=pD.v, lhsT=ktt.v, rhs=vt[:, 2 * pr:2 * pr + 2, :].rr("p h d -> p (h d)"), start=True, stop=True)
                    tt = Tts[pr]
                    o("dve", "tensor_tensor", out=tt.v, in0=C[pr].v, in1=pD.v, op=ALU.add)
                    o("dve", "scalar_tensor_tensor", out=Cb[pr].v, in0=tt.v, scalar=GdP[:, n, pr:pr + 1], in1=blk.v, op0=ALU.mult, op1=ALU.mult)
                    o("act", "activation", out=C[pr].v, in_=tt.v, func=AF.Copy, scale=GdP[:, n, pr:pr + 1])
                o("dve", "tensor_tensor", out=dd.v, in0=psO[:, :, 64], in1=E[:, n, :], op=ALU.mult)
                o("dve", "tensor_scalar", out=dd2.v, in0=dd.v, scalar1=1.0, scalar2=None, op0=ALU.max)
                o("dve", "scalar_tensor_tensor", out=dd.v, in0=dd.v, scalar=-1.0, in1=dd2.v, op0=ALU.mult, op1=ALU.max)
                o("dve", "reciprocal", out=dd.v, in_=dd.v)
                o("dve", "tensor_tensor", out=dd.v, in0=dd.v, in1=E[:, n, :], op=ALU.mult)
                o("dve", "tensor_tensor", out=Ho.v, in0=psO[:, :, 0:64], in1=dd.v.us(2).bc([128, 4, 64]), op=ALU.mult)
                self.rms_heads(Ho.v, 4, 64, gO.v, Hn.v, htmp.v, hss.v)
                ost = Ost[n % 2]
                o("dve", "tensor_tensor", out=ost.v, in0=Hn.v.rr("p h d -> p (h d)"), in1=v_o[:, 256:512], op=ALU.mult)
                P.dma(self.mixed.tile(n, self.mixed.ap[r0:r0 + 128, 768:1024]), ost.v)
            P.barrier()
        self.es = top

    def attn_tiles(self, psS_rot, PT_rot, psO, qT, tiles, extra=None):
        o = self.op
        n = len(tiles)

        def pv(idx, PT, vv, ev):
            for h in range(4):
                o("pe", "matmul", out=psO[:, h, :], lhsT=PT[:, h * 128:(h + 1) * 128], rhs=vv,
                  start=(idx == 0 and h == 0), stop=(idx == n - 1))
            if extra is not None:
                for h in range(4):
                    o("pe", "matmul", out=extra[:, h, :], lhsT=PT[:, h * 128:(h + 1) * 128], rhs=ev,
                      start=(idx == 0 and h == 0), stop=(idx == n - 1))

        pend = None
        for idx, (kv, mv, vv, ev) in enumerate(tiles):
            c = self._actr = getattr(self, "_actr", 0) + 1
            pS = psS_rot[c % len(psS_rot)]
            PT = PT_rot[c % len(PT_rot)]
            o("pe", "matmul", out=pS.v, lhsT=kv, rhs=qT, start=True, stop=(mv is None))
            if mv is not None:
                o("pe", "matmul", out=pS.v, lhsT=mv, rhs=self.ident4.v, start=False, stop=True)
            o("act", "activation", out=PT.v, in_=pS.v, func=AF.Exp, scale=0.125)
            if pend is not None:
                pv(*pend)
            pend = (idx, PT, vv, ev)
        pv(*pend)

    def phase_DSA(self, l):
        P = self.P
        o = self.op
        top = self.es
        with ExitStack() as es:
            self.es = es
            kT = self.sb("kT", [64, L], BF16)
            ikT = self.sb("ikT", [32, L], BF16)
            vA = self.sb("vA", [128, NT, 65], BF16)
            pw = self.sb("pw", [128, N_BIS], F32)
            P.dma(kT.v, self.kT3.all(self.kT3.ap[0]))
            P.dma(ikT.v, self.ikT.all())
            for q in range(4):
                P.dma(vA[:, q * 8:(q + 1) * 8, :], self.tmb.all(self.tmb.ap[q * 1024:(q + 1) * 1024, B_DV:B_DV + 65]
                                                                .rearrange("(n p) d -> p n d", p=128)))
            P.dma(pw.v, self.cst["c_pow2"].all())
            qT4 = self.rot("qT4", [64, 512], BF16, 2)
            iq = self.rot("iq", [32, 1024], BF16, 2)
            sg = self.rot("sg", [128, 8], F32, 2)
            Dg = self.rot("Dg", [128, 8, 128], BF16, 2)
            score = self.rot("score", [128, L], F32, 2)
            junk = self.sb("junk", [128, L], BF16)
            mb = self.rot("mb", [128, L], BF16, 2)
            R = self.rot("R", [128, 512], BF16, 3)
            PT = self.rot("PT", [128, 512], BF16, 3)
            am = self.rot("am", [128, 1], F32, 2)
            Wt = self.rot("Wt", [128, N_BIS], F32, 2)
            W2t = self.rot("W2t", [128, N_BIS], F32, 2)
            mid = self.rot("mid", [128, 1], F32, 2)
            cnt = self.sb("cnt", [128, 1], F32)
            st = self.sb("st", [128, 1], F32)
            rden = self.rot("rden", [128, 4], F32, 2)
            ob = self.rot("ob", [128, 4, 64], BF16, 2)
            psY = self.rot("psY", [128, 512], F32, 2, psum=True)
            psC = self.rot("psC", [128, 512], F32, 2, psum=True)
            psS = self.rot("psS", [128, 512], F32, 2, psum=True)
            psO = self.rot("psO", [128, 4, 65], F32, 2, psum=True)
            self._yc = 0

            def st_score(i):
                r0 = i * 128
                S = (i + 1) * 128
                q4 = qT4[i % 2]
                iqt = iq[i % 2]
                sgt = sg[i % 2]
                dg = Dg[i % 2]
                sc = score[i % 2]
                P.dma(q4.v, self.qT3.tile(i, self.qT3.ap[i, :, 0, :]))
                P.dma(iqt.v, self.iqT.tile(i, self.iqT.ap[i]))
                P.dma(sgt.v, self.tmf.tile(i, self.tmf.ap[r0:r0 + 128, F_SGN:F_SGN + 8]))
                o("pool", "tensor_tensor", out=dg.v, in0=self.identb.v.us(1).bc([128, 8, 128]),
                  in1=sgt.v.us(2).bc([128, 8, 128]), op=ALU.mult)
                for j in range(0, S, 512):
                    w = min(512, S - j)
                    pc = psC[(j // 512) % 2]
                    pend = None
                    for h in range(8):
                        yc = self._yc
                        self._yc += 1
                        py = psY[yc % 2]
                        r = R[yc % 3]
                        o("pe", "matmul", out=py[:, 0:w], lhsT=iqt[:, h * 128:(h + 1) * 128], rhs=ikT[:, j:j + w], start=True, stop=True)
                        o("act", "activation", out=r[:, 0:w], in_=py[:, 0:w], func=AF.Relu)
                        if pend is not None:
                            ph, pr = pend
                            o("pe", "matmul", out=pc[:, 0:w], lhsT=dg[:, ph, :], rhs=pr[:, 0:w], start=(ph == 0), stop=False)
                        pend = (h, r)
                    ph, pr = pend
                    o("pe", "matmul", out=pc[:, 0:w], lhsT=dg[:, ph, :], rhs=pr[:, 0:w], start=False, stop=True)
                    o("act", "copy", out=sc[:, j:j + w], in_=pc[:, 0:w])

            def st_bisect(i):
                r0 = i * 128
                S = (i + 1) * 128
                sc = score[i % 2]
                mbt = mb[i % 2]
                a = am[i % 2]
                o("dve", "tensor_reduce", out=a.v, in_=sc[:, 0:S], axis=AX.X, op=ALU.max, apply_absolute_value=True)
                o("dve", "tensor_tensor", out=sc[:, r0:r0 + 128], in0=sc[:, r0:r0 + 128], in1=self.cbf.v, op=ALU.add)
                wt = Wt[i % 2]
                w2 = W2t[i % 2]
                o("dve", "tensor_scalar", out=wt.v, in0=pw.v, scalar1=a[:, 0:1], scalar2=None, op0=ALU.mult)
                o("dve", "tensor_scalar", out=w2.v, in0=wt.v, scalar1=2.0, scalar2=None, op0=ALU.mult)
                md = mid[i % 2]
                o("dve", "memset", ap=md.v, constant=0.0)
                for k in range(N_BIS - 1):
                    o("dve", "tensor_scalar", out=junk[:, 0:S], in0=sc[:, 0:S], scalar1=md[:, 0:1], scalar2=None,
                      op0=ALU.is_ge, op1=ALU.add, accum_out=cnt.v)
                    o("dve", "scalar_tensor_tensor", out=st.v, in0=cnt.v, scalar=float(TOPK), in1=w2[:, k + 1:k + 2],
                      op0=ALU.is_ge, op1=ALU.mult)
                    o("dve", "scalar_tensor_tensor", out=md.v, in0=st.v, scalar=wt[:, k + 1:k + 2], in1=md.v,
                      op0=ALU.subtract, op1=ALU.add)
                o("dve", "tensor_tensor", out=md.v, in0=md.v, in1=wt[:, N_BIS - 1:N_BIS], op=ALU.subtract)
                o("dve", "tensor_scalar", out=mbt[:, 0:S], in0=sc[:, 0:S], scalar1=md[:, 0:1], scalar2=NEG,
                  op0=ALU.is_lt, op1=ALU.mult)

            def st_attn(i):
                r0 = i * 128
                q4 = qT4[i % 2]
                mbt = mb[i % 2]
                pO = psO[i % 2]
                tiles = [(kT[:, kt * 128:(kt + 1) * 128], mbt[:, kt * 128:(kt + 1) * 128], vA[:, kt, :], None) for kt in range(i + 1)]
                self.attn_tiles(psS, PT, pO, q4.v, tiles)
                rd = rden[i % 2]
                o("dve", "reciprocal", out=rd.v, in_=pO[:, :, 64])
                obt = ob[i % 2]
                o("dve", "tensor_tensor", out=obt.v, in0=pO[:, :, 0:64], in1=rd.v.us(2).bc([128, 4, 64]), op=ALU.mult)
                P.dma(self.mixed.tile(i, self.mixed.ap[r0:r0 + 128, 256:512]), obt.v.rr("p h d -> p (h d)"))

            n_t = self.ntl
            st_score(0)
            for i in range(n_t):
                if i + 1 < n_t:
                    st_score(i + 1)
                st_bisect(i)
                st_attn(i)
            P.barrier()
        self.es = top

    def phase_NSA(self, l):
        P = self.P
        o = self.op
        prm = self.prm
        top = self.es
        with ExitStack() as es:
            self.es = es
            kcmpT = self.sb("kcmpT", [64, 256], BF16)
            vcmp = self.sb("vcmp", [128, 2, 65], BF16)
            ovl = self.sb("ovl", [128, 2, 64], BF16)
            P.dma(ovl.v, self.cst["c_overlap"].all(self.cst["c_overlap"].ap.rearrange("(j p) n -> p j n", p=128)))
            with ExitStack() as es2:
                self.es = es2
                kvc = self.sb("kvc", [128, L], F32)
                P.dma(kvc.v, self.fmT.all(self.fmT.ap[1024:1152, :]))
                pe_t = self.sb("pe_t", [32, 128], F32)
                P.dma(pe_t[:, 0:64], View([], prm["nsa_pos_k"][l]))
                P.dma(pe_t[:, 64:128], View([], prm["nsa_pos_v"][l]))
                psP = self.ps("psP", [128, 32], F32)
                o("pe", "transpose", out=psP.v, in_=pe_t.v, identity=self.identf[0:32, 0:32])
                peT = self.sb("peT", [128, 32], F32)
                o("dve", "tensor_copy", out=peT.v, in_=psP.v)
                Ab = self.sb("Ab", [128, L], BF16)
                Bb = self.sb("Bb", [128, L], BF16)
                k3 = kvc.v.rr("p (b r) -> p b r", r=16)
                o("dve", "tensor_tensor", out=Ab.v.rr("p (b r) -> p b r", r=16), in0=k3, in1=peT[:, 0:16].us(1).bc([128, 256, 16]), op=ALU.add)
                o("pool", "tensor_tensor", out=Bb.v.rr("p (b r) -> p b r", r=16), in0=k3, in1=peT[:, 16:32].us(1).bc([128, 256, 16]), op=ALU.add)
                W1 = self.sb("W1", [128, 32, 256], BF16)
                stg = self.rot("w1st", [128, 8, 256], F32, 2)
                sc = 0
                for pp in range(0, 32, 8):
                    sgt = stg[sc % 2]
                    sc += 1
                    P.dma(sgt[0:64], View([], prm["nsa_k_w1"][l].rearrange("(p d) h -> d p h", d=64)[:, pp:pp + 8, :]))
                    P.dma(sgt[64:128], View([], prm["nsa_v_w1"][l].rearrange("(p d) h -> d p h", d=64)[:, pp:pp + 8, :]))
                    o("dve" if (pp // 8) % 2 == 0 else "pool", "tensor_copy", out=W1[:, pp:pp + 8, :], in_=sgt.v)
                w2s = self.sb("w2s", [128, 2, 128], F32)
                P.dma(w2s[:, :, 0:64], View([], prm["nsa_k_w2"][l].rearrange("(c p) d -> p c d", p=128)))
                P.dma(w2s[:, :, 64:128], View([], prm["nsa_v_w2"][l].rearrange("(c p) d -> p c d", p=128)))
                W2c = self.sb("W2c", [128, 2, 128], BF16)
                o("dve", "tensor_copy", out=W2c.v, in_=w2s.v)
                gk0 = self.sb("gk0", [128, 64], F32)
                self.bcast_row(gk0.v, prm["nsa_k_gains"][l, 0])
                hT = [self.sb("hT%d" % s, [128, 2, 256], BF16) for s in range(2)]
                psH = self.rot("psH", [128, 256], F32, 2, psum=True)
                for s in range(2):
                    sp = slice(s * 64, (s + 1) * 64)
                    o("pool", "memset", ap=hT[s].v, constant=0.0)
                    for hc in range(2):
                        ph = psH[hc]
                        for p in range(32):
                            src = Ab if p < 16 else Bb
                            o("pe", "matmul", out=ph[:, 0:255], lhsT=W1[sp, p, hc * 128:(hc + 1) * 128],
                              rhs=src[sp, p:p + 16 * 254 + 1:16], start=(p == 0), stop=(p == 31))
                        o("act", "activation", out=hT[s][:, hc, 0:255], in_=ph[:, 0:255], func=AF.Relu)
                psC = self.rot("psC", [128, 64], F32, 2, psum=True)
                psT = self.ps("psT", [64, 128], BF16)
                kcf = self.sb("kcf", [128, 1, 64], F32)
                kcn = self.sb("kcn", [128, 1, 64], F32)
                kct = self.sb("kct", [128, 1, 64], F32)
                kcs = self.sb("kcs", [128, 1], F32)
                kcb = self.sb("kcb", [128, 64], BF16)
                for jt in range(2):
                    for s in range(2):
                        pc = psC[s]
                        for hc in range(2):
                            o("pe", "matmul", out=pc.v, lhsT=hT[s][:, hc, jt * 128:(jt + 1) * 128], rhs=W2c[:, hc, s * 64:(s + 1) * 64],
                              start=(hc == 0), stop=(hc == 1))
                    o("act", "copy", out=kcf[:, 0, :], in_=psC[0].v)
                    self.rms_heads(kcf.v, 1, 64, gk0.v, kcn.v, kct.v, kcs.v)
                    o("act", "copy", out=kcb.v, in_=kcn[:, 0, :])
                    o("pe", "transpose", out=psT.v, in_=kcb.v, identity=self.identb.v)
                    o("act", "copy", out=kcmpT[:, jt * 128:(jt + 1) * 128], in_=psT.v)
                    o("dve", "tensor_copy", out=vcmp[:, jt, 0:64], in_=psC[1].v)
                    o("pool", "memset", ap=vcmp[:, jt, 64:65], constant=1.0)
                P.barrier()
            self.es = es
            ksT = self.sb("ksT", [64, L], BF16)
            kwT = self.sb("kwT", [64, L], BF16)
            vS = self.sb("vS", [128, NT, 65], BF16)
            vW = self.sb("vW", [128, NT, 65], BF16)
            P.dma(ksT.v, self.kT3.all(self.kT3.ap[1]))
            P.dma(kwT.v, self.kT3.all(self.kT3.ap[2]))
            for q in range(4):
                P.dma(vS[:, q * 8:(q + 1) * 8, :], self.tmb.all(self.tmb.ap[q * 1024:(q + 1) * 1024, B_NVS:B_NVS + 65]
                                                                .rearrange("(n p) d -> p n d", p=128)))
                P.dma(vW[:, q * 8:(q + 1) * 8, :], self.tmb.all(self.tmb.ap[q * 1024:(q + 1) * 1024, B_NVW:B_NVW + 65]
                                                                .rearrange("(n p) d -> p n d", p=128)))
            q2 = self.rot("q2", [64, 2, 512], BF16, 2)
            gt = self.rot("gt", [128, 12], F32, 2)
            cm = self.rot("cm", [128, 256], BF16, 2)
            fbt = self.rot("fbt", [128, 64], F32, 2)
            mb = self.rot("mb", [128, L], BF16, 2)
            PT = self.rot("PT", [128, 512], BF16, 3)
            rc = self.rot("rc", [128, 12], F32, 2)
            tmp2 = self.sb("tmp2", [128, 4, 64], F32)
            impw = self.sb("impw", [128, 4, 64], F32)
            impa = self.sb("impa", [128, 64], F32)
            zz = self.sb("zz", [128, 64], F32)
            m8 = self.sb("m8", [128, 16], F32)
            mbb = self.sb("mbb", [128, 64], BF16)
            acc = self.sb("acc", [128, 4, 64], F32)
            tmp = self.sb("tmp", [128, 4, 64], F32)
            ob = self.rot("ob", [128, 4, 64], BF16, 2)
            psS = self.rot("psS", [128, 512], F32, 2, psum=True)
            psOc = self.rot("psOc", [128, 4, 65], F32, 2, psum=True)
            psI = self.ps("psI", [128, 4, 64], F32)
            psOs = self.ps("psOs", [128, 4, 65], F32)
            psOw = self.ps("psOw", [128, 4, 65], F32)
            def st_cmp(i):
                r0 = i * 128
                S = (i + 1) * 128
                qq = q2[i % 2]
                g = gt[i % 2]
                cmt = cm[i % 2]
                fb = fbt[i % 2]
                mbt = mb[i % 2]
                pOc = psOc[i % 2]
                rcc = rc[i % 2]
                P.dma(qq.v, self.qT3.tile(i, self.qT3.ap[i, :, 1:3, :]))
                P.dma(g.v, self.tmf.tile(i, self.tmf.ap[r0:r0 + 128, F_NG:F_NG + 12]))
                P.dma(cmt.v, self.cst["c_cmpmask"].all(self.cst["c_cmpmask"].ap[r0:r0 + 128, :]))
                P.dma(fb.v, self.cst["c_fb"].all(self.cst["c_fb"].ap[r0:r0 + 128, :]))
                tiles = [(kcmpT[:, jt * 128:(jt + 1) * 128], cmt[:, jt * 128:(jt + 1) * 128], vcmp[:, jt, :], ovl[:, jt, :]) for jt in range(2)]
                self.attn_tiles(psS, PT, pOc, qq[:, 0, :], tiles, extra=psI)
                o("dve", "tensor_scalar", out=rcc[:, 0:4], in0=pOc[:, :, 64], scalar1=1e-30, scalar2=None, op0=ALU.max)
                o("dve", "reciprocal", out=rcc[:, 0:4], in_=rcc[:, 0:4])
                o("dve", "tensor_tensor", out=impw.v, in0=psI.v, in1=rcc[:, 0:4].us(2).bc([128, 4, 64]), op=ALU.mult)
                o("dve", "tensor_reduce", out=impa.v, in_=impw.v.rr("p h n -> p n h"), axis=AX.X, op=ALU.add)
                o("dve", "tensor_tensor", out=impa.v, in0=impa.v, in1=fb.v, op=ALU.add)
                o("dve", "max", out=m8[:, 0:8], in_=impa.v)
                o("dve", "match_replace", out=zz.v, in_to_replace=m8[:, 0:8], in_values=impa.v, imm_value=-3.0e6)
                o("dve", "max", out=m8[:, 8:16], in_=zz.v)
                o("dve", "tensor_scalar", out=mbb.v, in0=impa.v, scalar1=m8[:, 15:16], scalar2=NEG, op0=ALU.is_lt, op1=ALU.mult)
                nb = S // 64
                o("dve", "tensor_copy", out=mbt[:, 0:S].rr("p (b r) -> p b r", r=64), in_=mbb[:, 0:nb].us(2).bc([128, nb, 64]))
                o("dve", "tensor_tensor", out=mbt[:, r0:r0 + 128], in0=mbt[:, r0:r0 + 128], in1=self.cbb.v, op=ALU.add)

            def st_swa(i):
                qq = q2[i % 2]
                tiles = []
                for kt in range(max(0, i - 4), i + 1):
                    mv = self.cbb.v if kt == i else (self.abb.v if kt == i - 4 else None)
                    tiles.append((kwT[:, kt * 128:(kt + 1) * 128], mv, vW[:, kt, :], None))
                self.attn_tiles(psS, PT, psOw, qq[:, 1, :], tiles)

            def st_sel(i):
                qq = q2[i % 2]
                mbt = mb[i % 2]
                tiles = [(ksT[:, kt * 128:(kt + 1) * 128], mbt[:, kt * 128:(kt + 1) * 128], vS[:, kt, :], None) for kt in range(i + 1)]
                self.attn_tiles(psS, PT, psOs, qq[:, 1, :], tiles)

            def st_comb(i):
                r0 = i * 128
                g = gt[i % 2]
                pOc = psOc[i % 2]
                rcc = rc[i % 2]
                o("dve", "reciprocal", out=rcc[:, 4:8], in_=psOs[:, :, 64])
                o("dve", "reciprocal", out=rcc[:, 8:12], in_=psOw[:, :, 64])
                o("dve", "tensor_tensor", out=rcc.v, in0=rcc.v, in1=g.v, op=ALU.mult)
                o("dve", "tensor_tensor", out=acc.v, in0=pOc[:, :, 0:64], in1=rcc[:, 0:4].us(2).bc([128, 4, 64]), op=ALU.mult)
                o("dve", "tensor_tensor", out=tmp.v, in0=psOs[:, :, 0:64], in1=rcc[:, 4:8].us(2).bc([128, 4, 64]), op=ALU.mult)
                o("pool", "tensor_tensor", out=acc.v, in0=acc.v, in1=tmp.v, op=ALU.add)
                o("dve", "tensor_tensor", out=tmp2.v, in0=psOw[:, :, 0:64], in1=rcc[:, 8:12].us(2).bc([128, 4, 64]), op=ALU.mult)
                obt = ob[i % 2]
                o("pool", "tensor_tensor", out=obt.v, in0=acc.v, in1=tmp2.v, op=ALU.add)
                P.dma(self.mixed.tile(i, self.mixed.ap[r0:r0 + 128, 512:768]), obt.v.rr("p h d -> p (h d)"))

            n_t = self.ntl
            st_cmp(0)
            for i in range(n_t):
                st_swa(i)
                if i + 1 < n_t:
                    st_cmp(i + 1)
                st_sel(i)
                st_comb(i)
            P.barrier()
        self.es = top


_CACHE = {}


def make_in_map(inputs, b, consts):
    m = {"x": np.ascontiguousarray(inputs["x"][b], dtype=np.float32),
         "mem": np.ascontiguousarray(inputs["mem"][b], dtype=np.float32)}
    for k in PARAM_SHAPES:
        m[k] = np.ascontiguousarray(inputs[k], dtype=np.float32)
    m.update(consts)
    return m


def kernel(**inputs):
    if "kb" not in _CACHE:
        kb = KB()
        kb.build()
        _CACHE["kb"] = kb
    kb = _CACHE["kb"]
    consts = host_consts()
    in_maps = [make_in_map(inputs, c % 4, consts) for c in range(8)]
    res = run_bass_kernel_spmd(kb.nc, in_maps, core_ids=list(range(8)))
    out = np.stack([np.asarray(res.results[c]["out"], dtype=np.float32) for c in range(4)], axis=0)
    return out
```

```python
import numpy as np
import ml_dtypes
from contextlib import ExitStack
import concourse.bass as bass
import concourse.mybir as mybir
from concourse.bass_utils import run_bass_kernel_spmd

F32 = mybir.dt.float32
BF16 = mybir.dt.bfloat16
ALU = mybir.AluOpType
AF = mybir.ActivationFunctionType
AX = mybir.AxisListType

D = 1024
L = 4096
NT = L // 128
NMEM = 256
DFF = 2816
NFC = DFF // 128
INC = 3388
EPS = 1e-6
NEG = -30000.0
THETA = 500000.0
N_BIS = 14
TOPK = 256


class Buf:
    __slots__ = ("name", "last_w", "readers", "t", "psum")

    def __init__(self, name, t=None, psum=False):
        self.name = name
        self.last_w = None
        self.readers = []
        self.t = t
        self.psum = psum

    def __getitem__(self, k):
        return View([self], self.t[k])

    @property
    def v(self):
        return View([self], self.t[:])


class View:
    __slots__ = ("bufs", "ap")

    def __init__(self, bufs, ap):
        self.bufs = bufs
        self.ap = ap

    def __getitem__(self, k):
        return View(self.bufs, self.ap[k])

    def rr(self, s, **kw):
        return View(self.bufs, self.ap.rearrange(s, **kw))

    def bc(self, shape):
        return View(self.bufs, self.ap.to_broadcast(list(shape)))

    def us(self, ax):
        return View(self.bufs, self.ap.unsqueeze(ax))

    @property
    def shape(self):
        return self.ap.shape


class Op:
    __slots__ = ("idx", "eng", "emit", "deps", "signal", "ticket", "is_dma", "dsem", "dval")


WRITE_KEYS = ("out", "accum_out", "ap")


class Prog:
    ENGS = ("pe", "act", "dve", "pool", "sp")
    NRING = 24

    def __init__(self, nc):
        self.nc = nc
        self.ops = []
        self.ndma = 0
        self.lastdma = {}
        self.lasteng = {}

    def add(self, eng, emit, reads=(), writes=(), dma=False):
        op = Op()
        op.idx = len(self.ops)
        op.eng = eng
        op.emit = emit
        op.is_dma = dma
        op.signal = False
        op.ticket = None
        deps = {}
        for b in reads:
            if b.last_w is not None:
                deps[b.last_w.idx] = (b.last_w, True)
            if b.psum:
                for r in b.readers:
                    if r.eng != eng and r.idx not in deps:
                        deps[r.idx] = (r, False)
        for b in writes:
            if b.last_w is not None and b.last_w.idx not in deps:
                deps[b.last_w.idx] = (b.last_w, False)
            for r in b.readers:
                if r.idx not in deps:
                    deps[r.idx] = (r, False)
        op.deps = list(deps.values())
        for b in reads:
            if not dma:
                b.readers = [r for r in b.readers if r.is_dma or r.eng != eng]
            b.readers.append(op)
        for b in writes:
            b.last_w = op
            b.readers = []
        if dma:
            k = self.ndma
            self.ndma += 1
            op.dsem = k % self.NRING
            op.dval = 16 * (k // self.NRING + 1)
            self.lastdma[op.dsem] = op
        if emit is not None:
            self.lasteng[eng] = op
        self.ops.append(op)
        return op

    def op(self, eng, name, r=(), w=(), **kw):
        reads = list(r)
        writes = list(w)
        args = {}
        for k, v in kw.items():
            if isinstance(v, View):
                if k in WRITE_KEYS:
                    writes.extend(v.bufs)
                else:
                    reads.extend(v.bufs)
                args[k] = v.ap
            else:
                args[k] = v

        if name == "matmul":
            args.setdefault("skip_group_check", True)

        def emit(e, name=name, args=args):
            return getattr(e, name)(**args)

        return self.add(eng, emit, reads, writes, dma=(name == "dma_start"))

    def dma(self, out, in_, eng="sp"):
        return self.op(eng, "dma_start", out=out, in_=in_)

    def barrier(self):
        prev = list(self.lasteng.values()) + list(self.lastdma.values())
        for e in self.ENGS:
            op = self.add(e, None)
            for o in prev:
                if o.eng != e or o.is_dma:
                    op.deps.append((o, True))

    def emit_all(self, sems, ring):
        nc = self.nc
        engobj = {"pe": nc.tensor, "act": nc.scalar, "dve": nc.vector, "pool": nc.gpsimd, "sp": nc.sync}
        for op in self.ops:
            for d, raw in op.deps:
                if d.is_dma:
                    continue
                if d.eng == op.eng and d.eng == "pe":
                    continue
                d.signal = True
        cnt = {e: 0 for e in self.ENGS}
        for op in self.ops:
            if op.signal and not op.is_dma:
                cnt[op.eng] += 1
                op.ticket = cnt[op.eng]
        seen = {e: {} for e in self.ENGS}
        nw = 0
        for op in self.ops:
            e = engobj[op.eng]
            sn = seen[op.eng]
            need = {}
            for d, raw in op.deps:
                if d.is_dma:
                    key = ("r", d.dsem)
                    val = d.dval
                else:
                    if d.eng == op.eng and d.eng == "pe":
                        continue
                    key = ("e", d.eng)
                    val = d.ticket
                if sn.get(key, 0) >= val:
                    continue
                if need.get(key, 0) < val:
                    need[key] = val
            if op.is_dma and op.dval > 16:
                key = ("r", op.dsem)
                val = op.dval - 16
                if sn.get(key, 0) < val and need.get(key, 0) < val:
                    need[key] = val
            for key, val in need.items():
                s = ring[key[1]] if key[0] == "r" else sems[key[1]]
                e.wait_ge(s, val)
                sn[key] = val
                nw += 1
            if op.emit is None:
                continue
            ins = op.emit(e)
            if op.is_dma:
                ins.then_inc(ring[op.dsem], 16)
            elif op.signal:
                ins.then_inc(sems[op.eng], 1)
        return nw, cnt


class DT:
    def __init__(self, name, ap, ntile=1):
        self.name = name
        self.ap = ap
        self.bufs = [Buf("%s_%d" % (name, i)) for i in range(ntile)]

    def tile(self, i, ap):
        return View([self.bufs[i]], ap)

    def all(self, ap=None):
        return View(list(self.bufs), self.ap if ap is None else ap)


def host_consts():
    bf = ml_dtypes.bfloat16
    c = {}
    eye = np.eye(128, dtype=np.float32)
    c["c_identb"] = eye.astype(bf)
    c["c_identf"] = eye
    c["c_ident4"] = np.tile(eye, (1, 4)).astype(bf)
    s = np.arange(128)[:, None]
    t = np.arange(128)[None, :]
    triu = (s <= t).astype(np.float32)
    c["c_triuf"] = triu
    c["c_triub"] = triu.astype(bf)
    tt = np.arange(128)[:, None]
    ss = np.arange(128)[None, :]
    c["c_cb"] = np.where(ss <= tt, 0.0, NEG).astype(np.float32)
    c["c_ab"] = np.where(ss > tt, 0.0, NEG).astype(np.float32)
    pos = np.arange(L, dtype=np.float32)
    for nm, rd in (("c_cs64", 16), ("c_cs32", 8)):
        half = rd // 2
        inv = (np.float32(THETA) ** (-np.arange(half, dtype=np.float32) * np.float32(2.0) / np.float32(rd))).astype(np.float32)
        ang = (pos[:, None] * inv[None, :]).astype(np.float32)
        cs = np.concatenate([np.cos(ang), np.sin(ang)], axis=1).astype(np.float32)
        c[nm] = np.ascontiguousarray(cs.reshape(NT, 128, rd).transpose(1, 0, 2))
    j = np.arange(256)[None, :]
    tq = np.arange(L)[:, None]
    vis = (16 * j + 31 <= tq) & (j < 255)
    c["c_cmpmask"] = np.where(vis, 0.0, NEG).astype(bf)
    n = np.arange(64)[None, :]
    cur = tq // 64
    forced = (n == 0) | (n == cur) | (n == cur - 1)
    fb = np.where(forced, 1.0e6 + 64.0 * n, np.where(n > cur, -1.0e6, 0.0))
    c["c_fb"] = fb.astype(np.float32)
    st_c = np.arange(255) * 16
    st_s = np.arange(64) * 64
    ov = ((st_c[:, None] < st_s[None, :] + 64) & (st_c[:, None] + 32 > st_s[None, :])).astype(np.float32)
    ovp = np.zeros((256, 64), np.float32)
    ovp[:255] = ov
    c["c_overlap"] = ovp.astype(bf)
    c["c_pow2"] = np.tile((1.0078125 * 2.0 ** -np.arange(N_BIS, dtype=np.float32))[None, :], (128, 1)).astype(np.float32)
    return c


PARAM_SHAPES = {
    "lb_param": (2, 256), "norm_mix": (2, 1024), "w_in": (2, 1024, INC), "w_out": (2, 1024, 1024),
    "hg_o_gain": (2, 64), "dsa_kv_gain": (2, 128), "dsa_w_uk": (2, 128, 64), "dsa_w_uv": (2, 128, 64),
    "dsa_q_gain": (2, 64), "dsa_k_gain": (2, 64), "dsa_idxk_gain": (2, 32),
    "nsa_pos_k": (2, 32, 64), "nsa_pos_v": (2, 32, 64), "nsa_k_w1": (2, 2048, 256), "nsa_k_w2": (2, 256, 64),
    "nsa_v_w1": (2, 2048, 256), "nsa_v_w2": (2, 256, 64), "nsa_q_gain": (2, 64), "nsa_k_gains": (2, 3, 64),
    "ml_conv_w": (2, 4, 512), "ml_conv_b": (2, 512), "ml_i_bias": (2, 4), "ml_f_bias": (2, 4), "ml_o_gain": (2, 64),
    "norm_xa": (2, 1024), "norm_mem": (2, 1024), "xa_wq": (2, 1024, 256), "xa_wkv": (2, 1024, 512),
    "xa_wo": (2, 256, 1024), "xa_q_gain": (2, 64), "xa_k_gain": (2, 64), "norm_ffn": (2, 1024),
    "ffn_w13": (2, 1024, 2 * DFF), "ffn_w2": (2, DFF, 1024),
}


HG0, DS0, NS0, ML0 = 0, 1024, 1704, 2356
TM_GROUPS = [(512, 512), (1024, 512), (1536, 168), (1704, 256), (2088, 268), (2868, 512), (3380, 8)]
TM_OFF = [0, 768, 1280, 512, 1448, 1716, 2228]
TMW = 2236
FM_COLS = [0, 128, 256, 384, 2356, 2484, 2612, 2740, 1960]
C_HGI, C_HGG, C_DQ, C_CKV, C_IQ, C_IK, C_IW = 0, 256, 768, 1024, 1152, 1408, 1440
C_NQ, C_KS, C_VS, C_KW, C_VW, C_NG = 512, 1448, 1512, 1576, 1640, 1704
C_MV, C_OG, C_IG, C_FG = 1716, 1972, 2228, 2232
B_HGV, B_DV, B_NVS, B_NVW, TMBW = 0, 256, 321, 386, 452
F_HGG, F_MV, F_OG, F_SGN, F_NG, F_GT, TMFW = 0, 256, 512, 768, 776, 788, 796


class KB:
    def __init__(self, dump=(), layers=(0, 1), phases=None, ntl=NT):
        self.ntl = ntl
        import os
        self.alvl = int(os.environ.get("KDBG_A", "99"))
        self.dump = set(dump)
        self.layers = layers
        self.phases = phases
        nc = bass.Bass("TRN2", target_bir_lowering=False)
        self.nc = nc
        self.P = Prog(nc)
        self.es = None
        self.din = {}

    def sb(self, name, shape, dt):
        t = self.es.enter_context(self.nc.sbuf_tensor(name + "_%d" % self.uid(), list(shape), dt))
        return Buf(name, t)

    def ps(self, name, shape, dt):
        nel = 2048 // (4 if dt == F32 else 2)
        full = self.es.enter_context(self.nc.psum_tensor(name + "_%d" % self.uid(), [128, nel], dt))
        n = 1
        for d in shape[1:]:
            n *= d
        assert n <= nel, (name, shape)
        ap = full[0:shape[0], 0:n]
        if len(shape) == 3:
            ap = ap.rearrange("p (a b) -> p a b", a=shape[1])
        return Buf(name, ap, psum=True)

    def uid(self):
        self._uid = getattr(self, "_uid", 0) + 1
        return self._uid

    def rot(self, name, shape, dt, n, psum=False):
        return [(self.ps if psum else self.sb)("%s%d" % (name, i), shape, dt) for i in range(n)]

    def dram_in(self, name, shape, dt):
        ap = self.nc.dram_tensor(name, list(shape), dt, kind="ExternalInput").ap()
        d = DT(name, ap, 1)
        self.din[name] = d
        return d

    def scr(self, name, shape, dt, ntile=1):
        kind = "ExternalOutput" if name in self.dump else "Internal"
        ap = self.nc.dram_tensor(name, list(shape), dt, kind=kind).ap()
        return DT(name, ap, ntile)

    def op(self, eng, name, **kw):
        return self.P.op(eng, name, **kw)

    def rms_heads(self, X, H, Dh, gain, out, tmp, ssq, np_=128, gain_full=False):
        o = self.op
        o("pool", "tensor_tensor", out=tmp, in0=X, in1=X, op=ALU.mult)
        o("dve", "tensor_reduce", out=ssq, in_=tmp, axis=AX.X, op=ALU.add)
        o("act", "activation", out=ssq, in_=ssq, func=AF.Sqrt, scale=1.0 / Dh, bias=self.epsb[0:np_, 0:1])
        o("dve", "reciprocal", out=ssq, in_=ssq)
        o("dve", "tensor_tensor", out=out, in0=X, in1=ssq.us(2).bc([np_, H, Dh]), op=ALU.mult)
        if gain is not None:
            o("pool", "tensor_tensor", out=out, in0=out, in1=(gain if gain_full else gain.us(1).bc([np_, H, Dh])), op=ALU.mult)

    def rope(self, X, H, hf, cs, out, t1, t2):
        o = self.op
        Dh = X.shape[2]
        cos = cs[:, 0:hf].us(1).bc([128, H, hf])
        sin = cs[:, hf:2 * hf].us(1).bc([128, H, hf])
        x1 = X[:, :, 0:hf]
        x2 = X[:, :, hf:2 * hf]
        o("pool", "tensor_copy", out=out[:, :, 2 * hf:Dh], in_=X[:, :, 2 * hf:Dh])
        o("pool", "tensor_tensor", out=t1, in0=x1, in1=cos, op=ALU.mult)
        o("pool", "tensor_tensor", out=t2, in0=x2, in1=sin, op=ALU.mult)
        o("pool", "tensor_tensor", out=out[:, :, 0:hf], in0=t1, in1=t2, op=ALU.subtract)
        o("dve", "tensor_tensor", out=t1, in0=x2, in1=cos, op=ALU.mult)
        o("dve", "tensor_tensor", out=t2, in0=x1, in1=sin, op=ALU.mult)
        o("dve", "tensor_tensor", out=out[:, :, hf:2 * hf], in0=t1, in1=t2, op=ALU.add)

    def build(self):
        nc = self.nc
        P = self.P
        o = self.op
        self.x_in = self.dram_in("x", [L, D], F32)
        self.x_in.bufs = [Buf("xin%d" % i) for i in range(NT)]
        self.mem_in = self.dram_in("mem", [NMEM, D], F32)
        self.prm = {k: self.dram_in(k, list(s), F32).ap for k, s in PARAM_SHAPES.items()}
        hc = host_consts()
        self.cst = {}
        for k, v in hc.items():
            self.cst[k] = self.dram_in(k, list(v.shape), BF16 if v.dtype != np.float32 else F32)
        kind = "ExternalOutput"
        self.xo = DT("out", nc.dram_tensor("out", [L, D], F32, kind=kind).ap(), NT)
        self.fmT = self.scr("fmT", [1152, L], F32, NT)
        self.tmb = self.scr("tmb", [L, TMBW], BF16, NT)
        self.tmf = self.scr("tmf", [L, TMFW], F32, NT)
        self.kT3 = self.scr("kT3", [3, 64, L], BF16, NT)
        self.ikT = self.scr("ikT", [32, L], BF16, NT)
        self.qT3 = self.scr("qT3", [NT, 64, 3, 512], BF16, NT)
        self.iqT = self.scr("iqT", [NT, 32, 1024], BF16, NT)
        self.mixed = self.scr("mixed", [L, D], BF16, NT)

        with ExitStack() as top:
            self.es = top
            sems = {e: top.enter_context(nc.semaphore("s_" + e)) for e in P.ENGS}
            ring = [top.enter_context(nc.semaphore("r%d" % i)) for i in range(P.NRING)]
            self.identb = self.sb("identb", [128, 128], BF16)
            self.identf = self.sb("identf", [128, 128], F32)
            self.ident4 = self.sb("ident4", [128, 512], BF16)
            self.triuf = self.sb("triuf", [128, 128], F32)
            self.triub = self.sb("triub", [128, 128], BF16)
            self.cbf = self.sb("cbf", [128, 128], F32)
            self.cbb = self.sb("cbb", [128, 128], BF16)
            self.abb = self.sb("abb", [128, 128], BF16)
            self.cs64 = self.sb("cs64", [128, NT, 16], F32)
            self.cs32 = self.sb("cs32", [128, NT, 8], F32)
            self.epsb = self.sb("epsb", [128, 1], F32)
            self.onesf = self.sb("onesf", [128, 128], F32)
            tmpf = self.sb("tmpf", [128, 128], F32)
            for nm, dst in (("c_identb", self.identb), ("c_identf", self.identf), ("c_ident4", self.ident4),
                            ("c_triuf", self.triuf), ("c_triub", self.triub), ("c_cb", self.cbf)):
                P.dma(dst.v, self.cst[nm].all())
            P.dma(tmpf.v, self.cst["c_ab"].all())
            P.dma(self.cs64.v, self.cst["c_cs64"].all())
            P.dma(self.cs32.v, self.cst["c_cs32"].all())
            o("dve", "tensor_copy", out=self.cbb.v, in_=self.cbf.v)
            o("dve", "tensor_copy", out=self.abb.v, in_=tmpf.v)
            o("pool", "memset", ap=self.epsb.v, constant=EPS)
            o("pool", "memset", ap=self.onesf.v, constant=1.0)
            P.barrier()
            for l in self.layers:
                xsrc = self.x_in if l == 0 else self.xo
                ph = self.phases
                if ph is None or "A" in ph:
                    self.phase_A(l, xsrc)
                if ph is None or "HG" in ph:
                    self.phase_HG(l)
                if ph is None or "ML" in ph:
                    self.phase_ML(l)
                if ph is None or "DSA" in ph:
                    self.phase_DSA(l)
                if ph is None or "NSA" in ph:
                    self.phase_NSA(l)
                if ph is None or "C" in ph:
                    self.phase_C(l, xsrc)
            P.barrier()
            self.stats = P.emit_all(sems, ring)
        return nc

    def load_weight_bf16(self, dst, src_ap, nk, ncols, gain=None, chunk=512, stage=None):
        P = self.P
        o = self.op
        src = src_ap.rearrange("(c p) n -> p c n", p=128)
        engs = ["dve", "pool", "act"]
        ei = 0
        j = 0
        for c0 in range(0, ncols, chunk):
            w = min(chunk, ncols - c0)
            for k0 in range(0, nk, 4):
                k1 = min(nk, k0 + 4)
                st = stage[j % len(stage)]
                j += 1
                P.dma(st[:, 0:k1 - k0, 0:w], View([], src[:, k0:k1, c0:c0 + w]))
                for k in range(k0, k1):
                    e = engs[ei % 3]
                    ei += 1
                    if gain is None:
                        if e == "act":
                            o(e, "copy", out=dst[:, k, c0:c0 + w], in_=st[:, k - k0, 0:w])
                        else:
                            o(e, "tensor_copy", out=dst[:, k, c0:c0 + w], in_=st[:, k - k0, 0:w])
                    else:
                        if e == "act":
                            o(e, "activation", out=dst[:, k, c0:c0 + w], in_=st[:, k - k0, 0:w], func=AF.Copy,
                              scale=gain[:, k:k + 1])
                        else:
                            o(e, "tensor_scalar", out=dst[:, k, c0:c0 + w], in0=st[:, k - k0, 0:w],
                              scalar1=gain[:, k:k + 1], scalar2=None, op0=ALU.mult)

    def load_gain_cols(self, dst, vec_ap, nk):
        self.P.dma(dst.v, View([], vec_ap.rearrange("(c p) -> p c", p=128)))

    def bcast_row(self, dst_view, vec_ap):
        np_ = dst_view.shape[0]
        self.P.dma(dst_view, View([], vec_ap.partition_broadcast(np_)))

    def col_load(self, dst_view, vec_ap):
        self.P.dma(dst_view, View([], vec_ap.unsqueeze(1)))

    def phase_A(self, l, xsrc):
        P = self.P
        o = self.op
        prm = self.prm
        top = self.es
        with ExitStack() as es:
            self.es = es
            Wb = self.sb("Wb", [128, 8, INC], BF16)
            with ExitStack() as es2:
                self.es = es2
                stage = self.rot("wst", [128, 4, 512], F32, 3)
                self.load_weight_bf16(Wb, prm["w_in"][l], 8, INC, stage=stage)
                P.barrier()
            self.es = es
            gmix = self.sb("gmix", [128, D], F32)
            self.bcast_row(gmix.v, prm["norm_mix"][l])
            g8 = self.sb("g8", [128, 8, 64], F32)
            for hh in range(4):
                self.bcast_row(g8[:, hh, :], prm["nsa_q_gain"][l])
                self.bcast_row(g8[:, 4 + hh, :], prm["dsa_q_gain"][l])
            g3 = self.sb("g3", [128, 3, 64], F32)
            self.bcast_row(g3[:, 0, :], prm["dsa_k_gain"][l])
            self.bcast_row(g3[:, 1, :], prm["nsa_k_gains"][l, 1])
            self.bcast_row(g3[:, 2, :], prm["nsa_k_gains"][l, 2])
            K3 = self.sb("K3", [128, 3, 64], F32)
            K3n = self.sb("K3n", [128, 3, 64], F32)
            K3t = self.sb("K3t", [128, 3, 64], F32)
            gkv = self.sb("gkv", [128, 128], F32)
            self.bcast_row(gkv.v, prm["dsa_kv_gain"][l])
            gik = self.sb("gik", [128, 32], F32)
            self.bcast_row(gik.v, prm["dsa_idxk_gain"][l])
            wst = self.sb("wukv_st", [128, 128], F32)
            wukv = self.sb("wukv", [128, 128], BF16)
            P.dma(wst[:, 0:64], View([], prm["dsa_w_uk"][l]))
            P.dma(wst[:, 64:128], View([], prm["dsa_w_uv"][l]))
            o("dve", "tensor_copy", out=wukv.v, in_=wst.v)

            xt = self.rot("xt", [128, D], F32, 2)
            junk = self.sb("junk", [128, D], F32)
            ssx = self.rot("ssx", [128, 1], F32, 2)
            hb = self.rot("hb", [128, D], BF16, 2)
            hT = self.rot("hT", [128, 8, 128], BF16, 2)
            ct = self.rot("ct", [128, TMW], F32, 2)
            fm = self.rot("fm", [128, 9, 128], F32, 2)
            tmbS = self.rot("tmbS", [128, TMBW], BF16, 2)
            tmfS = self.rot("tmfS", [128, TMFW], F32, 2)
            kS = self.rot("kS", [64, 3, 128], BF16, 2)
            ikS = self.rot("ikS", [32, 128], BF16, 2)
            qS = self.rot("qS", [64, 3, 512], BF16, 2)
            iqS = self.rot("iqS", [32, 1024], BF16, 2)
            wk = self.sb("wk", [128, 8, 64], F32)
            wk2 = self.sb("wk2", [128, 8, 64], F32)
            t1 = self.sb("t1", [128, 8, 8], F32)
            t2 = self.sb("t2", [128, 8, 8], F32)
            ssq = self.sb("ssq", [128, 8], F32)
            qb = self.sb("qb", [128, 12, 64], BF16)
            kb = self.sb("kb", [128, 3, 64], BF16)
            iqb = self.sb("iqb", [128, 8, 32], BF16)
            ikb = self.sb("ikb", [128, 32], BF16)
            ckvb = self.sb("ckvb", [128, 128], BF16)
            ckvT = self.sb("ckvT", [128, 128], BF16)
            kvf = self.sb("kvf", [128, 128], F32)
            iwa = self.sb("iwa", [128, 8], F32)

            psT = self.ps("psT", [128, 8, 128], BF16)
            psA = self.rot("psA", [128, 512], F32, 2, psum=True)
            psB = self.rot("psB", [128, 4, 128], F32, 2, psum=True)
            psX = self.ps("psX", [128, 8, 128], BF16)
            psY = self.ps("psY", [128, 8, 128], BF16)
            psZ = self.ps("psZ", [128, 8, 128], BF16)

            IWS = float(8 ** -0.5 * 32 ** -0.5)

            def sA(i):
                x_t = xt[i % 2]
                hbt = hb[i % 2]
                hTt = hT[i % 2]
                c = ct[i % 2]
                f = fm[i % 2]
                r0 = i * 128
                if self.alvl < 1:
                    return
                P.dma(x_t.v, xsrc.tile(i, xsrc.ap[r0:r0 + 128, :]))
                ss = ssx[i % 2]
                o("act", "activation", out=junk.v, in_=x_t.v, func=AF.Square, accum_out=ss.v)
                o("act", "activation", out=ss.v, in_=ss.v, func=AF.Sqrt, scale=1.0 / D, bias=self.epsb[:, 0:1])
                o("dve", "reciprocal", out=ss.v, in_=ss.v)
                o("dve", "scalar_tensor_tensor", out=hbt.v, in0=x_t.v, scalar=ss[:, 0:1], in1=gmix.v,
                  op0=ALU.mult, op1=ALU.mult)
                for k in range(8):
                    o("pe", "transpose", out=psT[:, k, :], in_=hbt[:, k * 128:(k + 1) * 128], identity=self.identb.v)
                o("act", "copy", out=hTt.v, in_=psT.v)
                for gi, (c0, w) in enumerate(TM_GROUPS):
                    pa = psA[gi % 2]
                    for k in range(8):
                        o("pe", "matmul", out=pa[:, 0:w], lhsT=hTt[:, k, :], rhs=Wb[:, k, c0:c0 + w],
                          start=(k == 0), stop=(k == 7))
                    off = TM_OFF[gi]
                    if gi % 2 == 0:
                        o("dve", "tensor_copy", out=c[:, off:off + w], in_=pa[:, 0:w])
                    else:
                        o("act", "copy", out=c[:, off:off + w], in_=pa[:, 0:w])
                for ci, c0 in enumerate(FM_COLS):
                    pb = psB[(ci // 4) % 2]
                    for k in range(8):
                        o("pe", "matmul", out=pb[:, ci % 4, :], lhsT=Wb[:, k, c0:c0 + 128], rhs=hTt[:, k, :],
                          start=(k == 0 and ci % 4 == 0), stop=(k == 7))
                    if ci in (1,):
                        o("act", "activation", out=f[:, 0:2, :], in_=pb[:, 0:2, :], func=AF.Silu)
                    elif ci in (3,):
                        o("act", "activation", out=f[:, 2:4, :], in_=pb[:, 2:4, :], func=AF.Sigmoid, scale=-1.0)
                    elif ci == 7:
                        o("dve", "tensor_copy", out=f[:, 4:8, :], in_=pb.v)
                    elif ci == 8:
                        o("dve", "tensor_copy", out=f[:, 8, :], in_=pb[:, 0, :])
                P.dma(self.fmT.tile(i, self.fmT.ap.rearrange("(c p) t -> p c t", p=128)[:, :, r0:r0 + 128]), f.v)


            def sB(i):
                r0 = i * 128
                c = ct[i % 2]
                if self.alvl < 2:
                    return
                tb = tmbS[i % 2]
                tf = tmfS[i % 2]
                cs64 = self.cs64[:, i, :]
                cs32 = self.cs32[:, i, :]
                o("pool", "tensor_copy", out=tb[:, B_HGV:B_HGV + 256], in_=c[:, C_HGI:C_HGI + 256])
                o("act", "activation", out=tf[:, F_HGG:F_HGG + 256], in_=c[:, C_HGG:C_HGG + 256], func=AF.Silu)
                o("pool", "tensor_copy", out=tf[:, F_MV:F_MV + 256], in_=c[:, C_MV:C_MV + 256])
                o("act", "activation", out=tf[:, F_OG:F_OG + 256], in_=c[:, C_OG:C_OG + 256], func=AF.Sigmoid)
                o("pool", "tensor_copy", out=tf[:, F_GT:F_GT + 8], in_=c[:, C_IG:C_IG + 8])
                o("act", "activation", out=tf[:, F_NG:F_NG + 12], in_=c[:, C_NG:C_NG + 12], func=AF.Sigmoid)
                self.rms_heads(c[:, C_NQ:C_NQ + 512].rr("p (h d) -> p h d", h=8), 8, 64, g8.v, wk.v, wk2.v, ssq.v, gain_full=True)
                o("act", "copy", out=qb[:, 0:4, :], in_=wk[:, 0:4, :])
                self.rope(wk.v, 8, 8, cs64, qb[:, 4:12, :], t1.v, t2.v)
                if self.alvl < 3:
                    return
                self.rms_heads(c[:, C_CKV:C_CKV + 128].rr("p (h d) -> p h d", h=1), 1, 128, gkv.v,
                               wk2[:, 0:2, :].rr("p a b -> p (a b)").rr("p (h d) -> p h d", h=1),
                               wk2[:, 2:4, :].rr("p a b -> p (a b)").rr("p (h d) -> p h d", h=1), ssq[:, 0:1])
                o("act", "copy", out=ckvb.v, in_=wk2[:, 0:2, :].rr("p a b -> p (a b)"))
                o("pe", "transpose", out=psT[:, 0, :], in_=ckvb.v, identity=self.identb.v)
                o("act", "copy", out=ckvT.v, in_=psT[:, 0, :])
                pa = psA[1]
                o("pe", "matmul", out=pa[:, 0:128], lhsT=ckvT.v, rhs=wukv.v, start=True, stop=True)
                o("act", "copy", out=kvf.v, in_=pa[:, 0:128])
                o("pool", "tensor_copy", out=tb[:, B_DV:B_DV + 64], in_=kvf[:, 64:128])
                o("pool", "memset", ap=tb[:, B_DV + 64:B_DV + 65], constant=1.0)
                o("act", "copy", out=K3[:, 0, :], in_=kvf[:, 0:64])
                o("dve", "tensor_copy", out=K3[:, 1, :], in_=c[:, C_KS:C_KS + 64])
                o("pool", "tensor_copy", out=K3[:, 2, :], in_=c[:, C_KW:C_KW + 64])
                self.rms_heads(K3.v, 3, 64, g3.v, K3n.v, K3t.v, ssq[:, 1:4], gain_full=True)
                self.rope(K3n.v, 3, 8, cs64, kb.v, t1[:, 0:3, :], t2[:, 0:3, :])
                o("pool", "tensor_copy", out=tb[:, B_NVS:B_NVS + 64], in_=c[:, C_VS:C_VS + 64])
                o("pool", "memset", ap=tb[:, B_NVS + 64:B_NVS + 65], constant=1.0)
                o("pool", "tensor_copy", out=tb[:, B_NVW:B_NVW + 64], in_=c[:, C_VW:C_VW + 64])
                o("pool", "memset", ap=tb[:, B_NVW + 64:B_NVW + 66], constant=1.0)
                if self.alvl < 4:
                    return
                o("dve", "tensor_scalar", out=iwa.v, in0=c[:, C_IW:C_IW + 8], scalar1=IWS, scalar2=None, op0=ALU.mult)
                o("dve", "scalar_tensor_tensor", out=iwa.v, in0=iwa.v, scalar=-1.0, in1=iwa.v, op0=ALU.mult, op1=ALU.max)
                o("act", "activation", out=tf[:, F_SGN:F_SGN + 8], in_=c[:, C_IW:C_IW + 8], func=AF.Sign)
                IQ = wk[:, 0:4, :].rr("p a b -> p (a b)").rr("p (h d) -> p h d", h=8)
                IQ2 = wk[:, 4:8, :].rr("p a b -> p (a b)").rr("p (h d) -> p h d", h=8)
                self.rope(c[:, C_IQ:C_IQ + 256].rr("p (h d) -> p h d", h=8), 8, 4, cs32, IQ, t1[:, :, 0:4], t2[:, :, 0:4])
                o("dve", "tensor_tensor", out=iqb.v, in0=IQ, in1=iwa.v.us(2).bc([128, 8, 32]), op=ALU.mult)
                IK = wk2[:, 0:1, 0:32]
                self.rms_heads(c[:, C_IK:C_IK + 32].rr("p (h d) -> p h d", h=1), 1, 32, gik.v, IK, wk2[:, 1:2, 0:32], ssq[:, 4:5])
                self.rope(IK, 1, 4, cs32, ikb.v.rr("p (h d) -> p h d", h=1), t1[:, 0:1, 0:4], t2[:, 0:1, 0:4])
                if self.alvl < 5:
                    return
                import os
                B = int(os.environ.get("KDBG_B", "99"))
                q_s = qS[i % 2]
                k_s = kS[i % 2]
                for h in range(4):
                    o("pe", "transpose", out=psX[0:64, h, :], in_=qb[:, 8 + h, :], identity=self.identb.v)
                if B >= 1:
                    for j in range(3):
                        o("pe", "transpose", out=psX[0:64, 4 + j, :], in_=kb[:, j, :], identity=self.identb.v)
                if B >= 2:
                    o("pe", "transpose", out=psX[0:32, 7, :], in_=ikb.v, identity=self.identb.v)
                o("act", "copy", out=q_s[:, 0, :], in_=psX[0:64, 0:4, :].rr("p h t -> p (h t)"))
                if B >= 1:
                    o("dve", "tensor_copy", out=k_s.v, in_=psX[0:64, 4:7, :])
                if B >= 2:
                    o("dve", "tensor_copy", out=ikS[i % 2].v, in_=psX[0:32, 7, :])
                if B >= 3:
                    for h in range(8):
                        o("pe", "transpose", out=psY[0:64, h, :], in_=qb[:, h, :], identity=self.identb.v)
                    o("act", "copy", out=q_s[:, 1:3, :].rr("p a n -> p (a n)"), in_=psY[0:64, :, :].rr("p h t -> p (h t)"))
                if B >= 4:
                    for h in range(8):
                        o("pe", "transpose", out=psZ[0:32, h, :], in_=iqb[:, h, :], identity=self.identb.v)
                    o("dve", "tensor_copy", out=iqS[i % 2].v, in_=psZ[0:32, :, :].rr("p h t -> p (h t)"))
                if self.alvl < 6:
                    return
                P.dma(self.tmb.tile(i, self.tmb.ap[r0:r0 + 128, :]), tb.v)
                if self.alvl < 7:
                    return
                P.dma(self.tmf.tile(i, self.tmf.ap[r0:r0 + 128, :]), tf.v)
                if self.alvl < 8:
                    return
                P.dma(self.kT3.tile(i, self.kT3.ap.rearrange("k p t -> p k t")[:, :, r0:r0 + 128]), k_s.v)
                if self.alvl < 9:
                    return
                P.dma(self.ikT.tile(i, self.ikT.ap[:, r0:r0 + 128]), ikS[i % 2].v)
                P.dma(self.qT3.tile(i, self.qT3.ap[i]), q_s.v)
                P.dma(self.iqT.tile(i, self.iqT.ap[i]), iqS[i % 2].v)
            sA(0)
            for i in range(NT):
                if i + 1 < NT:
                    sA(i + 1)
                sB(i)
            P.barrier()
        self.es = top

    def phase_C(self, l, xsrc):
        self.phase_C1(l, xsrc)
        self.phase_C2(l)

    def phase_C1(self, l, xsrc):
        P = self.P
        o = self.op
        prm = self.prm
        top = self.es
        with ExitStack() as es:
            self.es = es
            Wo = self.sb("Wo", [128, 8, D], BF16)
            Wq = self.sb("Wq", [128, 8, 256], BF16)
            Wxo = self.sb("Wxo", [128, 2, D], BF16)
            xkT = self.sb("xkT", [64, 4, NMEM], BF16)
            xv = self.sb("xv", [128, 2, 4, 65], BF16)
            gq = self.sb("gq", [128, 64], F32)
            self.bcast_row(gq.v, prm["xa_q_gain"][l])
            gxa = self.sb("gxa", [128, D], F32)
            self.bcast_row(gxa.v, prm["norm_xa"][l])
            with ExitStack() as es2:
                self.es = es2
                stage = self.rot("wst", [128, 4, 512], F32, 3)
                self.load_weight_bf16(Wo, prm["w_out"][l], 8, D, stage=stage)
                self.load_weight_bf16(Wq, prm["xa_wq"][l], 8, 256, stage=stage)
                self.load_weight_bf16(Wxo, prm["xa_wo"][l], 2, D, stage=stage)
                Wkv = self.sb("Wkv", [128, 8, 512], BF16)
                self.load_weight_bf16(Wkv, prm["xa_wkv"][l], 8, 512, stage=stage)
                gmem = self.sb("gmem", [128, D], F32)
                self.bcast_row(gmem.v, prm["norm_mem"][l])
                gk = self.sb("gk", [128, 64], F32)
                self.bcast_row(gk.v, prm["xa_k_gain"][l])
                mt = self.sb("mt", [128, D], F32)
                mj = self.sb("mj", [128, D], BF16)
                mss = self.sb("mss", [128, 1], F32)
                mb = self.sb("mb", [128, D], BF16)
                mT = self.sb("mT", [128, 8, 128], BF16)
                kvf = self.sb("kvf", [128, 512], F32)
                kn = self.sb("kn", [128, 4, 64], F32)
                ktmp = self.sb("ktmp", [128, 4, 64], F32)
                kss = self.sb("kss", [128, 4], F32)
                knb = self.sb("knb", [128, 4, 64], BF16)
                pT = self.ps("pT", [128, 8, 128], BF16)
                pK = self.ps("pK", [128, 512], F32)
                for m in range(2):
                    P.dma(mt.v, self.mem_in.all(self.mem_in.ap[m * 128:(m + 1) * 128, :]))
                    o("act", "activation", out=mj.v, in_=mt.v, func=AF.Square, accum_out=mss.v)
                    o("act", "activation", out=mss.v, in_=mss.v, func=AF.Sqrt, scale=1.0 / D, bias=self.epsb[:, 0:1])
                    o("dve", "reciprocal", out=mss.v, in_=mss.v)
                    o("dve", "scalar_tensor_tensor", out=mb.v, in0=mt.v, scalar=mss[:, 0:1], in1=gmem.v,
                      op0=ALU.mult, op1=ALU.mult)
                    for k in range(8):
                        o("pe", "transpose", out=pT[:, k, :], in_=mb[:, k * 128:(k + 1) * 128], identity=self.identb.v)
                    o("act", "copy", out=mT.v, in_=pT.v)
                    for k in range(8):
                        o("pe", "matmul", out=pK.v, lhsT=mT[:, k, :], rhs=Wkv[:, k, :], start=(k == 0), stop=(k == 7))
                    o("act", "copy", out=kvf.v, in_=pK.v)
                    self.rms_heads(kvf[:, 0:256].rr("p (h d) -> p h d", h=4), 4, 64, gk.v, kn.v, ktmp.v, kss.v)
                    o("act", "copy", out=knb.v, in_=kn.v)
                    for h in range(4):
                        o("pe", "transpose", out=pT[0:64, h, :], in_=knb[:, h, :], identity=self.identb.v)
                    o("act", "copy", out=xkT[:, :, m * 128:(m + 1) * 128], in_=pT[0:64, 0:4, :])
                    o("dve", "tensor_copy", out=xv[:, m, :, 0:64], in_=kvf[:, 256:512].rr("p (h d) -> p h d", h=4))
                    o("pool", "memset", ap=xv[:, m, :, 64:65], constant=1.0)
                P.barrier()
            self.es = es
            xt = self.rot("xt", [128, D], F32, 2)
            mxb = self.rot("mxb", [128, D], BF16, 2)
            mxT = self.sb("mxT", [128, 8, 128], BF16)
            x1r = self.rot("x1", [128, D], F32, 3)
            junk = self.sb("junk", [128, D], BF16)
            ss = self.sb("ss", [128, 1], F32)
            hb = self.sb("hb", [128, D], BF16)
            hT = self.sb("hT", [128, 8, 128], BF16)
            qfr = self.rot("qf", [128, 4, 64], F32, 2)
            qn = self.sb("qn", [128, 4, 64], F32)
            qtmp = self.sb("qtmp", [128, 4, 64], F32)
            qss = self.sb("qss", [128, 4], F32)
            qnb = self.sb("qnb", [128, 4, 64], BF16)
            qT4 = self.sb("qT4", [64, 4, 128], BF16)
            PT = self.rot("PT", [128, 512], BF16, 2)
            rden = self.sb("rden", [128, 4], F32)
            ob = self.sb("ob", [128, 4, 64], BF16)
            oT = self.sb("oT", [128, 2, 128], BF16)
            psTa = self.ps("psTa", [128, 8, 128], BF16)
            psTb = self.ps("psTb", [128, 8, 128], BF16)
            psM = self.rot("psM", [128, 512], F32, 2, psum=True)
            psW = self.ps("psW", [128, 512], F32)
            psS = self.rot("psS", [128, 512], F32, 2, psum=True)
            psO = self.ps("psO", [128, 4, 65], F32)

            def s1(i):
                r0 = i * 128
                x_t = xt[i % 2]
                mx = mxb[i % 2]
                x1 = x1r[i % 3]
                P.dma(x_t.v, xsrc.tile(i, xsrc.ap[r0:r0 + 128, :]))
                P.dma(mx.v, self.mixed.tile(i, self.mixed.ap[r0:r0 + 128, :]))
                for k in range(8):
                    o("pe", "transpose", out=psTa[:, k, :], in_=mx[:, k * 128:(k + 1) * 128], identity=self.identb.v)
                o("act", "copy", out=mxT.v, in_=psTa.v)
                for g in range(2):
                    pm = psM[g]
                    for k in range(8):
                        o("pe", "matmul", out=pm.v, lhsT=mxT[:, k, :], rhs=Wo[:, k, g * 512:(g + 1) * 512],
                          start=(k == 0), stop=(k == 7))
                    o("dve", "tensor_tensor", out=x1[:, g * 512:(g + 1) * 512], in0=pm.v, in1=x_t[:, g * 512:(g + 1) * 512], op=ALU.add)
                o("act", "activation", out=junk.v, in_=x1.v, func=AF.Square, accum_out=ss.v)
                o("act", "activation", out=ss.v, in_=ss.v, func=AF.Sqrt, scale=1.0 / D, bias=self.epsb[:, 0:1])
                o("dve", "reciprocal", out=ss.v, in_=ss.v)
                o("dve", "scalar_tensor_tensor", out=hb.v, in0=x1.v, scalar=ss[:, 0:1], in1=gxa.v, op0=ALU.mult, op1=ALU.mult)
                for k in range(8):
                    o("pe", "transpose", out=psTa[:, k, :], in_=hb[:, k * 128:(k + 1) * 128], identity=self.identb.v)
                o("act", "copy", out=hT.v, in_=psTa.v)
                pm = psM[0]
                for k in range(8):
                    o("pe", "matmul", out=pm[:, 0:256], lhsT=hT[:, k, :], rhs=Wq[:, k, :], start=(k == 0), stop=(k == 7))
                o("act", "copy", out=qfr[i % 2].v.rr("p h d -> p (h d)"), in_=pm[:, 0:256])

            def s2(i):
                r0 = i * 128
                x1 = x1r[i % 3]
                self.rms_heads(qfr[i % 2].v, 4, 64, gq.v, qn.v, qtmp.v, qss.v)
                o("act", "copy", out=qnb.v, in_=qn.v)
                for h in range(4):
                    o("pe", "transpose", out=psTb[0:64, h, :], in_=qnb[:, h, :], identity=self.identb.v)
                o("act", "copy", out=qT4.v, in_=psTb[0:64, 0:4, :])
                for m in range(2):
                    pss = psS[m]
                    for h in range(4):
                        o("pe", "matmul", out=pss[:, h * 128:(h + 1) * 128], lhsT=xkT[:, h, m * 128:(m + 1) * 128],
                          rhs=qT4[:, h, :], start=(h == 0), stop=(h == 3))
                    pt = PT[m]
                    o("act", "activation", out=pt.v, in_=pss.v, func=AF.Exp, scale=0.125)
                    for h in range(4):
                        o("pe", "matmul", out=psO[:, h, :], lhsT=pt[:, h * 128:(h + 1) * 128], rhs=xv[:, m, h, :],
                          start=(m == 0 and h == 0), stop=(m == 1))
                o("dve", "reciprocal", out=rden.v, in_=psO[:, :, 64])
                o("dve", "tensor_tensor", out=ob.v, in0=psO[:, :, 0:64], in1=rden.v.us(2).bc([128, 4, 64]), op=ALU.mult)
                for k in range(2):
                    o("pe", "transpose", out=psTb[:, 4 + k, :], in_=ob.v.rr("p h d -> p (h d)")[:, k * 128:(k + 1) * 128], identity=self.identb.v)
                o("act", "copy", out=oT.v, in_=psTb[:, 4:6, :])
                for g in range(2):
                    for k in range(2):
                        o("pe", "matmul", out=psW.v, lhsT=oT[:, k, :], rhs=Wxo[:, k, g * 512:(g + 1) * 512],
                          start=(k == 0), stop=(k == 1))
                    o("dve", "tensor_tensor", out=x1[:, g * 512:(g + 1) * 512], in0=psW.v, in1=x1[:, g * 512:(g + 1) * 512], op=ALU.add)
                P.dma(self.xo.tile(i, self.xo.ap[r0:r0 + 128, :]), x1.v)

            n_t = self.ntl
            s1(0)
            for i in range(n_t):
                if i + 1 < n_t:
                    s1(i + 1)
                s2(i)
            P.barrier()
        self.es = top

    def phase_C2(self, l):
        P = self.P
        o = self.op
        prm = self.prm
        top = self.es
        with ExitStack() as es:
            self.es = es
            W13 = self.sb("W13", [128, 8, 2 * DFF], BF16)
            W2 = self.sb("W2", [128, NFC, D], BF16)
            gff = self.sb("gff", [128, D], F32)
            self.bcast_row(gff.v, prm["norm_ffn"][l])
            with ExitStack() as es2:
                self.es = es2
                stage = self.rot("wst", [128, 4, 512], F32, 3)
                self.load_weight_bf16(W13, prm["ffn_w13"][l], 8, 2 * DFF, stage=stage)
                self.load_weight_bf16(W2, prm["ffn_w2"][l], NFC, D, stage=stage)
                P.barrier()
            self.es = es
            xt = self.rot("xt", [128, D], F32, 3)
            junk = self.sb("junk", [128, D], BF16)
            ss = self.sb("ss", [128, 1], F32)
            hb = self.sb("hb", [128, D], BF16)
            hTr = self.rot("hT", [128, 8, 128], BF16, 2)
            gT = self.sb("gT", [128, NFC, 128], BF16)
            sa = self.rot("sa", [128, 4, 128], F32, 2)
            psT = self.ps("psT", [128, 8, 128], BF16)
            psM = self.rot("psM", [128, 512], F32, 2, psum=True)
            psF = self.rot("psF", [128, 4, 128], F32, 4, psum=True)

            def front(i):
                r0 = i * 128
                x2 = xt[i % 3]
                P.dma(x2.v, self.xo.tile(i, self.xo.ap[r0:r0 + 128, :]))
                o("act", "activation", out=junk.v, in_=x2.v, func=AF.Square, accum_out=ss.v)
                o("act", "activation", out=ss.v, in_=ss.v, func=AF.Sqrt, scale=1.0 / D, bias=self.epsb[:, 0:1])
                o("dve", "reciprocal", out=ss.v, in_=ss.v)
                o("dve", "scalar_tensor_tensor", out=hb.v, in0=x2.v, scalar=ss[:, 0:1], in1=gff.v, op0=ALU.mult, op1=ALU.mult)
                for k in range(8):
                    o("pe", "transpose", out=psT[:, k, :], in_=hb[:, k * 128:(k + 1) * 128], identity=self.identb.v)
                o("act", "copy", out=hTr[i % 2].v, in_=psT.v)

            def ab(i):
                hT = hTr[i % 2]
                for gi, g0 in enumerate(range(0, NFC, 4)):
                    g1 = min(NFC, g0 + 4)
                    n = g1 - g0
                    pfs = (psF[(gi % 2) * 2], psF[(gi % 2) * 2 + 1])
                    for half in range(2):
                        pf = pfs[half]
                        for c in range(g0, g1):
                            col = half * DFF + c * 128
                            for k in range(8):
                                o("pe", "matmul", out=pf[:, c - g0, :], lhsT=W13[:, k, col:col + 128], rhs=hT[:, k, :],
                                  start=(k == 0 and c == g0), stop=(k == 7))
                    s_a = sa[gi % 2]
                    o("act", "activation", out=s_a[:, 0:n, :], in_=pfs[0][:, 0:n, :], func=AF.Silu)
                    o("dve", "tensor_tensor", out=gT[:, g0:g1, :], in0=pfs[1][:, 0:n, :], in1=s_a[:, 0:n, :], op=ALU.mult)

            def w2(i):
                r0 = i * 128
                x2 = xt[i % 3]
                for g in range(2):
                    pm = psM[g]
                    for c in range(NFC):
                        o("pe", "matmul", out=pm.v, lhsT=gT[:, c, :], rhs=W2[:, c, g * 512:(g + 1) * 512],
                          start=(c == 0), stop=(c == NFC - 1))
                    o("dve", "tensor_tensor", out=x2[:, g * 512:(g + 1) * 512], in0=pm.v, in1=x2[:, g * 512:(g + 1) * 512], op=ALU.add)
                P.dma(self.xo.tile(i, self.xo.ap[r0:r0 + 128, :]), x2.v)

            n_t = self.ntl
            front(0)
            for i in range(n_t):
                ab(i)
                if i + 1 < n_t:
                    front(i + 1)
                w2(i)
            P.barrier()
        self.es = top

    def phase_HG(self, l):
        P = self.P
        o = self.op
        prm = self.prm
        top = self.es
        CH = 64
        NCH = L // CH
        with ExitStack() as es:
            self.es = es
            oml = self.sb("oml", [128, 2], F32)
            if l == 0:
                o("pool", "memset", ap=oml.v, constant=1.0)
            else:
                lb0 = self.sb("lb0", [128, 2], F32)
                lb1 = self.sb("lb1", [128, 2], F32)
                for hp in range(2):
                    self.col_load(lb0[:, hp:hp + 1], prm["lb_param"][0, hp * 128:(hp + 1) * 128])
                    self.col_load(lb1[:, hp:hp + 1], prm["lb_param"][1, hp * 128:(hp + 1) * 128])
                o("dve", "tensor_tensor", out=lb0.v, in0=lb0.v, in1=lb1.v, op=ALU.subtract)
                o("act", "activation", out=oml.v, in_=lb0.v, func=AF.Sigmoid)
            gO = self.sb("gO", [128, 64], F32)
            self.bcast_row(gO.v, prm["hg_o_gain"][l])
            hm = self.sb("hm", [128, 2], F32)
            blk = self.sb("blk", [128, 128], F32)
            o("pool", "memset", ap=hm.v, constant=0.0)
            o("pool", "memset", ap=hm[0:64, 0:1], constant=1.0)
            o("pool", "memset", ap=hm[64:128, 1:2], constant=1.0)
            o("pool", "memset", ap=blk.v, constant=0.0)
            o("pool", "memset", ap=blk[0:64, 0:64], constant=1.0)
            o("pool", "memset", ap=blk[64:128, 64:128], constant=1.0)
            msk = self.sb("msk", [128, L], F32)
            o("pool", "memset", ap=msk.v, constant=1.0)
            o("pool", "memset", ap=msk.v.rr("p (c t) -> p c t", t=CH)[:, :, 0:1], constant=0.0)
            A1 = self.sb("A1", [128, L], F32)
            A2 = self.sb("A2", [128, L], F32)
            A3 = self.sb("A3", [128, L], F32)
            A4 = self.sb("A4", [128, L], F32)
            A5 = self.sb("A5", [128, L], F32)
            QTs = [self.sb("QT%d" % p, [128, L], BF16) for p in range(2)]
            KTms = [[self.sb("KTm%d%d" % (p, h), [128, L], BF16) for h in range(2)] for p in range(2)]
            KLs = [self.sb("KL%d" % p, [128, L], BF16) for p in range(2)]
            ELs = [self.sb("EL%d" % p, [128, NCH], F32) for p in range(2)]
            DLs = [self.sb("DL%d" % p, [128, NCH], F32) for p in range(2)]
            EMs = [self.sb("EM%d" % p, [128, NCH], F32) for p in range(2)]
            Ss = [self.sb("S%d" % p, [128, 128], F32) for p in range(2)]
            Sbs = [self.sb("Sb%d" % p, [128, 128], BF16) for p in range(2)]
            Vts = [self.rot("Vt%d" % p, [64, 2, 128], BF16, 2) for p in range(2)]
            Gts = [self.rot("Gt%d" % p, [64, 2, 128], F32, 2) for p in range(2)]
            KLts = [self.rot("KLt%d" % p, [64, 128], BF16, 2) for p in range(2)]
            Wms = [self.rot("Wm%d" % p, [64, 2, 64], BF16, 2) for p in range(2)]
            Ocs = [self.rot("Oc%d" % p, [64, 4, 64], F32, 2) for p in range(2)]
            Ons = [self.sb("On%d" % p, [64, 4, 64], F32) for p in range(2)]
            otmps = [self.sb("otmp%d" % p, [64, 4, 64], F32) for p in range(2)]
            osss = [self.sb("oss%d" % p, [64, 4], F32) for p in range(2)]
            Osts = [self.rot("Ost%d" % p, [64, 2, 128], BF16, 2) for p in range(2)]
            psKs = [self.ps("psK%d" % p, [64, 128], BF16) for p in range(2)]
            psSs = [self.ps("psS%d" % p, [64, 2, 64], F32) for p in range(2)]
            psOs = [self.ps("psO%d" % p, [64, 2, 64], F32) for p in range(2)]
            psDs = [self.ps("psD%d" % p, [128, 128], F32) for p in range(2)]
            fm = self.fmT
            c3 = lambda b: b.v.rr("p (c t) -> p c t", t=CH)
            for hp in range(2):
                QT, KTm, KL, EL, DL, EM, S, Sb = QTs[hp], KTms[hp], KLs[hp], ELs[hp], DLs[hp], EMs[hp], Ss[hp], Sbs[hp]
                P.dma(A1.v, fm.all(fm.ap[hp * 128:(hp + 1) * 128, :]))
                P.dma(A2.v, fm.all(fm.ap[256 + hp * 128:256 + (hp + 1) * 128, :]))
                o("pool", "tensor_scalar", out=A2.v, in0=A2.v, scalar1=oml[:, hp:hp + 1], scalar2=None, op0=ALU.mult)
                o("act", "activation", out=A3.v, in_=A2.v, func=AF.Ln, scale=-1.0, bias=self.onesf[:, 0:1])
                o("dve", "tensor_tensor_scan", out=A4.v, data0=msk.v, data1=A3.v, initial=0.0, op0=ALU.mult, op1=ALU.add)
                B3 = c3(A4)
                o("dve", "tensor_tensor", out=c3(A3), in0=B3, in1=B3[:, :, 31:32].bc([128, NCH, CH]), op=ALU.subtract)
                o("act", "activation", out=A5.v, in_=A3.v, func=AF.Exp)
                o("dve", "scalar_tensor_tensor", out=QT.v, in0=A1.v, scalar=0.125, in1=A5.v, op0=ALU.mult, op1=ALU.mult)
                o("act", "activation", out=A5.v, in_=A3.v, func=AF.Exp, scale=-1.0)
                o("pool", "tensor_tensor", out=A5.v, in0=A5.v, in1=A2.v, op=ALU.mult)
                o("act", "activation", out=KTm[0].v, in_=A5.v, func=AF.Copy, scale=hm[:, 0:1])
                o("dve", "tensor_scalar", out=KTm[1].v, in0=A5.v, scalar1=hm[:, 1:2], scalar2=None, op0=ALU.mult)
                o("dve", "tensor_tensor", out=EL.v.us(2), in0=B3[:, :, 63:64], in1=B3[:, :, 31:32], op=ALU.subtract)
                o("act", "activation", out=EL.v, in_=EL.v, func=AF.Exp)
                o("act", "activation", out=DL.v.us(2), in_=B3[:, :, 63:64], func=AF.Exp)
                o("act", "activation", out=EM.v.us(2), in_=B3[:, :, 31:32], func=AF.Exp)
                o("pool", "tensor_tensor", out=c3(KL), in0=c3(A5), in1=EL.v.us(2).bc([128, NCH, CH]), op=ALU.mult)
                o("pool", "memset", ap=S.v, constant=0.0)
                o("pool", "memset", ap=Sb.v, constant=0.0)

            def step(hp, c):
                QT, KTm, KL, DL, EM, S, Sb = QTs[hp], KTms[hp], KLs[hp], DLs[hp], EMs[hp], Ss[hp], Sbs[hp]
                ti, ci = c // 2, c % 2
                r0 = ti * 128
                cs = slice(c * CH, (c + 1) * CH)
                if ci == 0:
                    P.dma(Vts[hp][ti % 2].v, self.tmb.tile(ti, self.tmb.ap[r0:r0 + 128, B_HGV + hp * 128:B_HGV + (hp + 1) * 128]
                                                          .rearrange("(c s) d -> s c d", s=CH)))
                    P.dma(Gts[hp][ti % 2].v, self.tmf.tile(ti, self.tmf.ap[r0:r0 + 128, F_HGG + hp * 128:F_HGG + (hp + 1) * 128]
                                                          .rearrange("(c s) d -> s c d", s=CH)))
                V = Vts[hp][ti % 2]
                G = Gts[hp][ti % 2]
                pk = psKs[hp]
                o("pe", "transpose", out=pk.v, in_=KL[:, cs], identity=self.identb.v)
                klt = KLts[hp][c % 2]
                o("act", "copy", out=klt.v, in_=pk.v)
                pS = psSs[hp]
                for h in range(2):
                    o("pe", "matmul", out=pS[:, h, :], lhsT=KTm[h][:, cs], rhs=QT[:, cs], start=(h == 0), stop=(h == 1))
                wm = Wms[hp][c % 2]
                o("dve", "tensor_tensor", out=wm.v, in0=pS.v, in1=self.triuf[0:64, 0:64].us(1).bc([64, 2, 64]), op=ALU.mult)
                pO = psOs[hp]
                for h in range(2):
                    hs = slice(h * 64, (h + 1) * 64)
                    o("pe", "matmul", out=pO[:, h, :], lhsT=wm[:, h, :], rhs=V[:, ci, hs], start=(h == 0), stop=False)
                    o("pe", "matmul", out=pO[:, h, :], lhsT=QT[:, cs], rhs=Sb[:, hs], start=False, stop=True)
                pD = psDs[hp]
                o("pe", "matmul", out=pD.v, lhsT=klt.v, rhs=V[:, ci, :], start=True, stop=True)
                o("dve", "scalar_tensor_tensor", out=S.v, in0=S.v, scalar=DL[:, c:c + 1], in1=pD.v, op0=ALU.mult, op1=ALU.add)
                if c + 1 < NCH:
                    o("dve", "scalar_tensor_tensor", out=Sb.v, in0=S.v, scalar=EM[:, c + 1:c + 2], in1=blk.v, op0=ALU.mult, op1=ALU.mult)
                oc = Ocs[hp][ti % 2]
                o("act", "copy", out=oc[:, 2 * ci:2 * ci + 2, :], in_=pO.v)
                if ci == 1:
                    self.rms_heads(oc.v, 4, 64, gO[0:64, :], Ons[hp].v, otmps[hp].v, osss[hp].v, np_=64)
                    ost = Osts[hp][ti % 2]
                    o("dve", "tensor_tensor", out=ost.v.rr("p c d -> p (c d)"), in0=Ons[hp].v.rr("p h d -> p (h d)"),
                      in1=G.v.rr("p c d -> p (c d)"), op=ALU.mult)
                    P.dma(self.mixed.tile(ti, self.mixed.ap[r0:r0 + 128, hp * 128:(hp + 1) * 128]
                                          .rearrange("(c s) d -> s c d", s=CH)), ost.v)

            for c in range(2 * self.ntl):
                for hp in range(2):
                    step(hp, c)
            P.barrier()
        self.es = top

    def phase_ML(self, l):
        P = self.P
        o = self.op
        prm = self.prm
        top = self.es
        with ExitStack() as es:
            self.es = es
            cw = self.sb("cw", [128, 4, 4], F32)
            cbs = self.sb("cbs", [128, 4], F32)
            for ch in range(4):
                for j in range(4):
                    self.col_load(cw[:, ch, j:j + 1], prm["ml_conv_w"][l, j, ch * 128:(ch + 1) * 128])
                self.col_load(cbs[:, ch:ch + 1], prm["ml_conv_b"][l, ch * 128:(ch + 1) * 128])
            ibf = self.sb("ibf", [128, 8], F32)
            self.bcast_row(ibf[:, 0:4], prm["ml_i_bias"][l])
            self.bcast_row(ibf[:, 4:8], prm["ml_f_bias"][l])
            gO = self.sb("gO", [128, 64], F32)
            self.bcast_row(gO.v, prm["ml_o_gain"][l])
            GT = self.sb("GT", [128, NT, 8], F32)
            for q in range(4):
                P.dma(GT[:, q * 8:(q + 1) * 8, :], self.tmf.all(self.tmf.ap[q * 1024:(q + 1) * 1024, F_GT:F_GT + 8]
                                                                .rearrange("(n p) g -> p n g", p=128)))
            LI = self.sb("LI", [128, NT, 4], F32)
            LF = self.sb("LF", [128, NT, 4], F32)
            Fc = self.sb("Fc", [128, NT, 4], F32)
            E = self.sb("E", [128, NT, 4], F32)
            BV = self.sb("BV", [128, NT, 4], F32)
            Gd = self.sb("Gd", [128, NT, 4], F32)
            GdP = self.sb("GdP", [128, NT, 2], F32)
            psG = self.ps("psG", [128, 128], F32)
            o("dve", "tensor_tensor", out=LI.v, in0=GT[:, :, 0:4], in1=ibf[:, 0:4].us(1).bc([128, NT, 4]), op=ALU.add)
            o("dve", "tensor_tensor", out=LF.v, in0=GT[:, :, 4:8], in1=ibf[:, 4:8].us(1).bc([128, NT, 4]), op=ALU.add)
            o("act", "activation", out=LF.v, in_=LF.v, func=AF.Exp, scale=-1.0)
            o("act", "activation", out=LF.v, in_=LF.v, func=AF.Ln, bias=self.onesf[:, 0:1])
            o("dve", "tensor_scalar", out=LF.v, in0=LF.v, scalar1=-1.0, scalar2=None, op0=ALU.mult)
            lf2 = LF.v.rr("p n h -> p (n h)")
            o("pe", "matmul", out=psG.v, lhsT=self.triuf.v, rhs=lf2, start=True, stop=True)
            o("dve", "tensor_copy", out=Fc.v.rr("p n h -> p (n h)"), in_=psG.v)
            o("act", "activation", out=E.v, in_=Fc.v, func=AF.Exp)
            o("dve", "tensor_tensor", out=BV.v, in0=LI.v, in1=Fc.v, op=ALU.subtract)
            o("act", "activation", out=BV.v, in_=BV.v, func=AF.Exp)
            o("pe", "matmul", out=psG.v, lhsT=self.onesf.v, rhs=lf2, start=True, stop=True)
            o("act", "activation", out=Gd.v.rr("p n h -> p (n h)"), in_=psG.v, func=AF.Exp)
            for pr in range(2):
                o("dve", "tensor_copy", out=GdP[0:64, :, pr:pr + 1], in_=Gd[0:64, :, 2 * pr:2 * pr + 1])
                o("dve", "tensor_copy", out=GdP[64:128, :, pr:pr + 1], in_=Gd[64:128, :, 2 * pr + 1:2 * pr + 2])
            QK = [self.sb("QK%d" % ch, [128, L], BF16) for ch in range(4)]
            X = self.rot("X", [128, L], F32, 2)
            Y = self.sb("Y", [128, L], F32)
            fm = self.fmT
            for ch in range(4):
                x = X[ch % 2]
                P.dma(x.v, fm.all(fm.ap[512 + ch * 128:512 + (ch + 1) * 128, :]))
                o("dve", "tensor_scalar", out=Y.v, in0=x.v, scalar1=cw[:, ch, 3:4], scalar2=cbs[:, ch:ch + 1], op0=ALU.mult, op1=ALU.add)
                for sh in (1, 2, 3):
                    o("dve", "scalar_tensor_tensor", out=Y[:, sh:L], in0=x[:, 0:L - sh], scalar=cw[:, ch, 3 - sh:4 - sh],
                      in1=Y[:, sh:L], op0=ALU.mult, op1=ALU.add)
                o("act", "activation", out=QK[ch].v, in_=Y.v, func=AF.Silu)
                if ch >= 2:
                    o("pool", "tensor_scalar", out=QK[ch].v, in0=QK[ch].v, scalar1=0.125, scalar2=None, op0=ALU.mult)
            hm = self.sb("hm", [128, 2], F32)
            blk = self.sb("blk", [128, 130], F32)
            o("pool", "memset", ap=hm.v, constant=0.0)
            o("pool", "memset", ap=hm[0:64, 0:1], constant=1.0)
            o("pool", "memset", ap=hm[64:128, 1:2], constant=1.0)
            o("pool", "memset", ap=blk.v, constant=0.0)
            o("pool", "memset", ap=blk[0:64, 0:65], constant=1.0)
            o("pool", "memset", ap=blk[64:128, 65:130], constant=1.0)
            km = [[self.sb("km%d%d" % (pr, hl), [128, L], BF16) for hl in range(2)] for pr in range(2)]
            for pr in range(2):
                o("dve", "tensor_scalar", out=km[pr][0].v, in0=QK[2 + pr].v, scalar1=hm[:, 0:1], scalar2=None, op0=ALU.mult)
                o("pool", "tensor_scalar", out=km[pr][1].v, in0=QK[2 + pr].v, scalar1=hm[:, 1:2], scalar2=None, op0=ALU.mult)
            C = [self.sb("C%d" % pr, [128, 130], F32) for pr in range(2)]
            Cb = [self.sb("Cb%d" % pr, [128, 130], BF16) for pr in range(2)]
            Tt = self.sb("Tt", [128, 130], F32)
            for pr in range(2):
                o("pool", "memset", ap=C[pr].v, constant=0.0)
                o("pool", "memset", ap=Cb[pr].v, constant=0.0)
            vo = self.rot("vo", [128, 512], F32, 2)
            Vt = self.rot("Vt", [128, 4, 65], BF16, 2)
            kt = self.rot("kt", [128, 128], BF16, 2)
            Wm = self.rot("Wm", [128, 2, 128], BF16, 2)
            dd = self.sb("dd", [128, 4], F32)
            dd2 = self.sb("dd2", [128, 4], F32)
            Ho = self.sb("Ho", [128, 4, 64], F32)
            Hn = self.sb("Hn", [128, 4, 64], F32)
            htmp = self.sb("htmp", [128, 4, 64], F32)
            hss = self.sb("hss", [128, 4], F32)
            Ost = self.rot("Ost", [128, 256], BF16, 2)
            psK = self.rot("psK", [128, 128], BF16, 2, psum=True)
            psS = self.rot("psS", [128, 2, 128], F32, 2, psum=True)
            psO = self.ps("psO", [128, 4, 65], F32)
            psD = self.rot("psD", [128, 130], F32, 2, psum=True)
            for n in range(self.ntl):
                r0 = n * 128
                ts = slice(r0, r0 + 128)
                v_o = vo[n % 2]
                P.dma(v_o.v, self.tmf.tile(n, self.tmf.ap[r0:r0 + 128, F_MV:F_MV + 512]))
                vt = Vt[n % 2]
                o("dve", "tensor_tensor", out=vt[:, :, 0:64], in0=v_o[:, 0:256].rr("p (h d) -> p h d", h=4),
                  in1=BV[:, n, :].us(2).bc([128, 4, 64]), op=ALU.mult)
                o("pool", "tensor_copy", out=vt[:, :, 64:65], in_=BV[:, n, :].us(2))
                for pr in range(2):
                    kq = QK[2 + pr]
                    qq = QK[pr]
                    pk = psK[pr]
                    o("pe", "transpose", out=pk.v, in_=kq[:, ts], identity=self.identb.v)
                    ktt = kt[pr]
                    o("act", "copy", out=ktt.v, in_=pk.v)
                    pS = psS[pr]
                    for hl in range(2):
                        hs = slice(hl * 64, (hl + 1) * 64)
                        o("pe", "matmul", out=pS[:, hl, :], lhsT=km[pr][hl][:, ts], rhs=qq[:, ts], start=(hl == 0), stop=(hl == 1))
                    wm = Wm[pr]
                    o("dve", "tensor_tensor", out=wm.v, in0=pS.v, in1=self.triuf.v.us(1).bc([128, 2, 128]), op=ALU.mult)
                    for hl in range(2):
                        h = 2 * pr + hl
                        hs = slice(hl * 64, (hl + 1) * 64)
                        o("pe", "matmul", out=psO[:, h, :], lhsT=qq[:, ts], rhs=Cb[pr][:, hl * 65:(hl + 1) * 65],
                          start=(h == 0), stop=False)
                        o("pe", "matmul", out=psO[:, h, :], lhsT=wm[:, hl, :], rhs=vt[:, h, :], start=False, stop=True)
                    pD = psD[pr]
                    o("pe", "matmul", out=pD.v, lhsT=ktt.v, rhs=vt[:, 2 * pr:2 * pr + 2, :].rr("p h d -> p (h d)"), start=True, stop=True)
                    o("dve", "tensor_tensor", out=Tt.v, in0=C[pr].v, in1=pD.v, op=ALU.add)
                    o("dve", "tensor_scalar", out=C[pr].v, in0=Tt.v, scalar1=GdP[:, n, pr:pr + 1], scalar2=None, op0=ALU.mult)
                    o("pool", "tensor_tensor", out=Cb[pr].v, in0=C[pr].v, in1=blk.v, op=ALU.mult)
                o("dve", "tensor_tensor", out=dd.v, in0=psO[:, :, 64], in1=E[:, n, :], op=ALU.mult)
                o("dve", "tensor_scalar", out=dd2.v, in0=dd.v, scalar1=1.0, scalar2=None, op0=ALU.max)
                o("dve", "scalar_tensor_tensor", out=dd.v, in0=dd.v, scalar=-1.0, in1=dd2.v, op0=ALU.mult, op1=ALU.max)
                o("dve", "reciprocal", out=dd.v, in_=dd.v)
                o("dve", "tensor_tensor", out=dd.v, in0=dd.v, in1=E[:, n, :], op=ALU.mult)
                o("dve", "tensor_tensor", out=Ho.v, in0=psO[:, :, 0:64], in1=dd.v.us(2).bc([128, 4, 64]), op=ALU.mult)
                self.rms_heads(Ho.v, 4, 64, gO.v, Hn.v, htmp.v, hss.v)
                ost = Ost[n % 2]
                o("dve", "tensor_tensor", out=ost.v, in0=Hn.v.rr("p h d -> p (h d)"), in1=v_o[:, 256:512], op=ALU.mult)
                P.dma(self.mixed.tile(n, self.mixed.ap[r0:r0 + 128, 768:1024]), ost.v)
            P.barrier()
        self.es = top

    def attn_tiles(self, psS_rot, PT_rot, psO, qT, tiles, extra=None):
        o = self.op
        n = len(tiles)

        def pv(idx, PT, vv, ev):
            for h in range(4):
                o("pe", "matmul", out=psO[:, h, :], lhsT=PT[:, h * 128:(h + 1) * 128], rhs=vv,
                  start=(idx == 0 and h == 0), stop=(idx == n - 1))
            if extra is not None:
                for h in range(4):
                    o("pe", "matmul", out=extra[:, h, :], lhsT=PT[:, h * 128:(h + 1) * 128], rhs=ev,
                      start=(idx == 0 and h == 0), stop=(idx == n - 1))

        pend = None
        for idx, (kv, mv, vv, ev) in enumerate(tiles):
            c = self._actr = getattr(self, "_actr", 0) + 1
            pS = psS_rot[c % len(psS_rot)]
            PT = PT_rot[c % len(PT_rot)]
            o("pe", "matmul", out=pS.v, lhsT=kv, rhs=qT, start=True, stop=(mv is None))
            if mv is not None:
                o("pe", "matmul", out=pS.v, lhsT=mv, rhs=self.ident4.v, start=False, stop=True)
            o("act", "activation", out=PT.v, in_=pS.v, func=AF.Exp, scale=0.125)
            if pend is not None:
                pv(*pend)
            pend = (idx, PT, vv, ev)
        pv(*pend)

    def phase_DSA(self, l):
        P = self.P
        o = self.op
        top = self.es
        with ExitStack() as es:
            self.es = es
            kT = self.sb("kT", [64, L], BF16)
            ikT = self.sb("ikT", [32, L], BF16)
            vA = self.sb("vA", [128, NT, 65], BF16)
            pw = self.sb("pw", [128, N_BIS], F32)
            P.dma(kT.v, self.kT3.all(self.kT3.ap[0]))
            P.dma(ikT.v, self.ikT.all())
            for q in range(4):
                P.dma(vA[:, q * 8:(q + 1) * 8, :], self.tmb.all(self.tmb.ap[q * 1024:(q + 1) * 1024, B_DV:B_DV + 65]
                                                                .rearrange("(n p) d -> p n d", p=128)))
            P.dma(pw.v, self.cst["c_pow2"].all())
            qT4 = self.rot("qT4", [64, 512], BF16, 2)
            iq = self.rot("iq", [32, 1024], BF16, 2)
            sg = self.rot("sg", [128, 8], F32, 2)
            Dg = self.rot("Dg", [128, 8, 128], BF16, 2)
            score = self.rot("score", [128, L], F32, 2)
            junk = self.sb("junk", [128, L], BF16)
            mb = self.rot("mb", [128, L], BF16, 2)
            R = self.rot("R", [128, 512], BF16, 3)
            PT = self.rot("PT", [128, 512], BF16, 3)
            am = self.rot("am", [128, 1], F32, 2)
            Wt = self.rot("Wt", [128, N_BIS], F32, 2)
            W2t = self.rot("W2t", [128, N_BIS], F32, 2)
            mid = self.rot("mid", [128, 1], F32, 2)
            cnt = self.sb("cnt", [128, 1], F32)
            st = self.sb("st", [128, 1], F32)
            rden = self.rot("rden", [128, 4], F32, 2)
            ob = self.rot("ob", [128, 4, 64], BF16, 2)
            psY = self.rot("psY", [128, 512], F32, 2, psum=True)
            psC = self.rot("psC", [128, 512], F32, 2, psum=True)
            psS = self.rot("psS", [128, 512], F32, 2, psum=True)
            psO = self.rot("psO", [128, 4, 65], F32, 2, psum=True)
            self._yc = 0

            def st_score(i):
                r0 = i * 128
                S = (i + 1) * 128
                q4 = qT4[i % 2]
                iqt = iq[i % 2]
                sgt = sg[i % 2]
                dg = Dg[i % 2]
                sc = score[i % 2]
                P.dma(q4.v, self.qT3.tile(i, self.qT3.ap[i, :, 0, :]))
                P.dma(iqt.v, self.iqT.tile(i, self.iqT.ap[i]))
                P.dma(sgt.v, self.tmf.tile(i, self.tmf.ap[r0:r0 + 128, F_SGN:F_SGN + 8]))
                o("pool", "tensor_tensor", out=dg.v, in0=self.identb.v.us(1).bc([128, 8, 128]),
                  in1=sgt.v.us(2).bc([128, 8, 128]), op=ALU.mult)
                for j in range(0, S, 512):
                    w = min(512, S - j)
                    pc = psC[(j // 512) % 2]
                    pend = None
                    for h in range(8):
                        yc = self._yc
                        self._yc += 1
                        py = psY[yc % 2]
                        r = R[yc % 3]
                        o("pe", "matmul", out=py[:, 0:w], lhsT=iqt[:, h * 128:(h + 1) * 128], rhs=ikT[:, j:j + w], start=True, stop=True)
                        o("act", "activation", out=r[:, 0:w], in_=py[:, 0:w], func=AF.Relu)
                        if pend is not None:
                            ph, pr = pend
                            o("pe", "matmul", out=pc[:, 0:w], lhsT=dg[:, ph, :], rhs=pr[:, 0:w], start=(ph == 0), stop=False)
                        pend = (h, r)
                    ph, pr = pend
                    o("pe", "matmul", out=pc[:, 0:w], lhsT=dg[:, ph, :], rhs=pr[:, 0:w], start=False, stop=True)
                    o("act", "copy", out=sc[:, j:j + w], in_=pc[:, 0:w])

            def st_bisect(i):
                r0 = i * 128
                S = (i + 1) * 128
                sc = score[i % 2]
                mbt = mb[i % 2]
                a = am[i % 2]
                o("dve", "tensor_reduce", out=a.v, in_=sc[:, 0:S], axis=AX.X, op=ALU.max, apply_absolute_value=True)
                o("dve", "tensor_tensor", out=sc[:, r0:r0 + 128], in0=sc[:, r0:r0 + 128], in1=self.cbf.v, op=ALU.add)
                wt = Wt[i % 2]
                w2 = W2t[i % 2]
                o("dve", "tensor_scalar", out=wt.v, in0=pw.v, scalar1=a[:, 0:1], scalar2=None, op0=ALU.mult)
                o("dve", "tensor_scalar", out=w2.v, in0=wt.v, scalar1=2.0, scalar2=None, op0=ALU.mult)
                md = mid[i % 2]
                o("dve", "memset", ap=md.v, constant=0.0)
                for k in range(N_BIS - 1):
                    o("dve", "tensor_scalar", out=junk[:, 0:S], in0=sc[:, 0:S], scalar1=md[:, 0:1], scalar2=None,
                      op0=ALU.is_ge, op1=ALU.add, accum_out=cnt.v)
                    o("dve", "scalar_tensor_tensor", out=st.v, in0=cnt.v, scalar=float(TOPK), in1=w2[:, k + 1:k + 2],
                      op0=ALU.is_ge, op1=ALU.mult)
                    o("dve", "scalar_tensor_tensor", out=md.v, in0=st.v, scalar=wt[:, k + 1:k + 2], in1=md.v,
                      op0=ALU.subtract, op1=ALU.add)
                o("dve", "tensor_tensor", out=md.v, in0=md.v, in1=wt[:, N_BIS - 1:N_BIS], op=ALU.subtract)
                o("dve", "tensor_scalar", out=mbt[:, 0:S], in0=sc[:, 0:S], scalar1=md[:, 0:1], scalar2=NEG,
                  op0=ALU.is_lt, op1=ALU.mult)

            def st_attn(i):
                r0 = i * 128
                q4 = qT4[i % 2]
                mbt = mb[i % 2]
                pO = psO[i % 2]
                tiles = [(kT[:, kt * 128:(kt + 1) * 128], mbt[:, kt * 128:(kt + 1) * 128], vA[:, kt, :], None) for kt in range(i + 1)]
                self.attn_tiles(psS, PT, pO, q4.v, tiles)
                rd = rden[i % 2]
                o("dve", "reciprocal", out=rd.v, in_=pO[:, :, 64])
                obt = ob[i % 2]
                o("dve", "tensor_tensor", out=obt.v, in0=pO[:, :, 0:64], in1=rd.v.us(2).bc([128, 4, 64]), op=ALU.mult)
                P.dma(self.mixed.tile(i, self.mixed.ap[r0:r0 + 128, 256:512]), obt.v.rr("p h d -> p (h d)"))

            n_t = self.ntl
            st_score(0)
            for i in range(n_t):
                if i + 1 < n_t:
                    st_score(i + 1)
                st_bisect(i)
                st_attn(i)
            P.barrier()
        self.es = top

    def phase_NSA(self, l):
        P = self.P
        o = self.op
        prm = self.prm
        top = self.es
        with ExitStack() as es:
            self.es = es
            kcmpT = self.sb("kcmpT", [64, 256], BF16)
            vcmp = self.sb("vcmp", [128, 2, 65], BF16)
            ovl = self.sb("ovl", [128, 2, 64], BF16)
            P.dma(ovl.v, self.cst["c_overlap"].all(self.cst["c_overlap"].ap.rearrange("(j p) n -> p j n", p=128)))
            with ExitStack() as es2:
                self.es = es2
                kvc = self.sb("kvc", [128, L], F32)
                P.dma(kvc.v, self.fmT.all(self.fmT.ap[1024:1152, :]))
                pe_t = self.sb("pe_t", [32, 128], F32)
                P.dma(pe_t[:, 0:64], View([], prm["nsa_pos_k"][l]))
                P.dma(pe_t[:, 64:128], View([], prm["nsa_pos_v"][l]))
                psP = self.ps("psP", [128, 32], F32)
                o("pe", "transpose", out=psP.v, in_=pe_t.v, identity=self.identf[0:32, 0:32])
                peT = self.sb("peT", [128, 32], F32)
                o("dve", "tensor_copy", out=peT.v, in_=psP.v)
                Ab = self.sb("Ab", [128, L], BF16)
                Bb = self.sb("Bb", [128, L], BF16)
                k3 = kvc.v.rr("p (b r) -> p b r", r=16)
                o("dve", "tensor_tensor", out=Ab.v.rr("p (b r) -> p b r", r=16), in0=k3, in1=peT[:, 0:16].us(1).bc([128, 256, 16]), op=ALU.add)
                o("pool", "tensor_tensor", out=Bb.v.rr("p (b r) -> p b r", r=16), in0=k3, in1=peT[:, 16:32].us(1).bc([128, 256, 16]), op=ALU.add)
                W1 = self.sb("W1", [128, 32, 256], BF16)
                stg = self.rot("w1st", [128, 8, 256], F32, 2)
                sc = 0
                for pp in range(0, 32, 8):
                    sgt = stg[sc % 2]
                    sc += 1
                    P.dma(sgt[0:64], View([], prm["nsa_k_w1"][l].rearrange("(p d) h -> d p h", d=64)[:, pp:pp + 8, :]))
                    P.dma(sgt[64:128], View([], prm["nsa_v_w1"][l].rearrange("(p d) h -> d p h", d=64)[:, pp:pp + 8, :]))
                    o("dve" if (pp // 8) % 2 == 0 else "pool", "tensor_copy", out=W1[:, pp:pp + 8, :], in_=sgt.v)
                w2s = self.sb("w2s", [128, 2, 128], F32)
                P.dma(w2s[:, :, 0:64], View([], prm["nsa_k_w2"][l].rearrange("(c p) d -> p c d", p=128)))
                P.dma(w2s[:, :, 64:128], View([], prm["nsa_v_w2"][l].rearrange("(c p) d -> p c d", p=128)))
                W2c = self.sb("W2c", [128, 2, 128], BF16)
                o("dve", "tensor_copy", out=W2c.v, in_=w2s.v)
                gk0 = self.sb("gk0", [128, 64], F32)
                self.bcast_row(gk0.v, prm["nsa_k_gains"][l, 0])
                hT = [self.sb("hT%d" % s, [128, 2, 256], BF16) for s in range(2)]
                psH = self.rot("psH", [128, 256], F32, 2, psum=True)
                for s in range(2):
                    sp = slice(s * 64, (s + 1) * 64)
                    o("pool", "memset", ap=hT[s].v, constant=0.0)
                    for hc in range(2):
                        ph = psH[hc]
                        for p in range(32):
                            src = Ab if p < 16 else Bb
                            o("pe", "matmul", out=ph[:, 0:255], lhsT=W1[sp, p, hc * 128:(hc + 1) * 128],
                              rhs=src[sp, p:p + 16 * 254 + 1:16], start=(p == 0), stop=(p == 31))
                        o("act", "activation", out=hT[s][:, hc, 0:255], in_=ph[:, 0:255], func=AF.Relu)
                psC = self.rot("psC", [128, 64], F32, 2, psum=True)
                psT = self.ps("psT", [64, 128], BF16)
                kcf = self.sb("kcf", [128, 1, 64], F32)
                kcn = self.sb("kcn", [128, 1, 64], F32)
                kct = self.sb("kct", [128, 1, 64], F32)
                kcs = self.sb("kcs", [128, 1], F32)
                kcb = self.sb("kcb", [128, 64], BF16)
                for jt in range(2):
                    for s in range(2):
                        pc = psC[s]
                        for hc in range(2):
                            o("pe", "matmul", out=pc.v, lhsT=hT[s][:, hc, jt * 128:(jt + 1) * 128], rhs=W2c[:, hc, s * 64:(s + 1) * 64],
                              start=(hc == 0), stop=(hc == 1))
                    o("act", "copy", out=kcf[:, 0, :], in_=psC[0].v)
                    self.rms_heads(kcf.v, 1, 64, gk0.v, kcn.v, kct.v, kcs.v)
                    o("act", "copy", out=kcb.v, in_=kcn[:, 0, :])
                    o("pe", "transpose", out=psT.v, in_=kcb.v, identity=self.identb.v)
                    o("act", "copy", out=kcmpT[:, jt * 128:(jt + 1) * 128], in_=psT.v)
                    o("dve", "tensor_copy", out=vcmp[:, jt, 0:64], in_=psC[1].v)
                    o("pool", "memset", ap=vcmp[:, jt, 64:65], constant=1.0)
                P.barrier()
            self.es = es
            ksT = self.sb("ksT", [64, L], BF16)
            kwT = self.sb("kwT", [64, L], BF16)
            vS = self.sb("vS", [128, NT, 65], BF16)
            vW = self.sb("vW", [128, NT, 65], BF16)
            P.dma(ksT.v, self.kT3.all(self.kT3.ap[1]))
            P.dma(kwT.v, self.kT3.all(self.kT3.ap[2]))
            for q in range(4):
                P.dma(vS[:, q * 8:(q + 1) * 8, :], self.tmb.all(self.tmb.ap[q * 1024:(q + 1) * 1024, B_NVS:B_NVS + 65]
                                                                .rearrange("(n p) d -> p n d", p=128)))
                P.dma(vW[:, q * 8:(q + 1) * 8, :], self.tmb.all(self.tmb.ap[q * 1024:(q + 1) * 1024, B_NVW:B_NVW + 65]
                                                                .rearrange("(n p) d -> p n d", p=128)))
            q2 = self.rot("q2", [64, 2, 512], BF16, 2)
            gt = self.rot("gt", [128, 12], F32, 2)
            cm = self.rot("cm", [128, 256], BF16, 2)
            fbt = self.rot("fbt", [128, 64], F32, 2)
            mb = self.rot("mb", [128, L], BF16, 2)
            PT = self.rot("PT", [128, 512], BF16, 3)
            rc = self.rot("rc", [128, 12], F32, 2)
            tmp2 = self.sb("tmp2", [128, 4, 64], F32)
            impw = self.sb("impw", [128, 4, 64], F32)
            impa = self.sb("impa", [128, 64], F32)
            zz = self.sb("zz", [128, 64], F32)
            m8 = self.sb("m8", [128, 16], F32)
            mbb = self.sb("mbb", [128, 64], BF16)
            acc = self.sb("acc", [128, 4, 64], F32)
            tmp = self.sb("tmp", [128, 4, 64], F32)
            ob = self.rot("ob", [128, 4, 64], BF16, 2)
            psS = self.rot("psS", [128, 512], F32, 2, psum=True)
            psOc = self.rot("psOc", [128, 4, 65], F32, 2, psum=True)
            psI = self.ps("psI", [128, 4, 64], F32)
            psOs = self.ps("psOs", [128, 4, 65], F32)
            psOw = self.ps("psOw", [128, 4, 65], F32)
            def st_cmp(i):
                r0 = i * 128
                S = (i + 1) * 128
                qq = q2[i % 2]
                g = gt[i % 2]
                cmt = cm[i % 2]
                fb = fbt[i % 2]
                mbt = mb[i % 2]
                pOc = psOc[i % 2]
                rcc = rc[i % 2]
                P.dma(qq.v, self.qT3.tile(i, self.qT3.ap[i, :, 1:3, :]))
                P.dma(g.v, self.tmf.tile(i, self.tmf.ap[r0:r0 + 128, F_NG:F_NG + 12]))
                P.dma(cmt.v, self.cst["c_cmpmask"].all(self.cst["c_cmpmask"].ap[r0:r0 + 128, :]))
                P.dma(fb.v, self.cst["c_fb"].all(self.cst["c_fb"].ap[r0:r0 + 128, :]))
                tiles = [(kcmpT[:, jt * 128:(jt + 1) * 128], cmt[:, jt * 128:(jt + 1) * 128], vcmp[:, jt, :], ovl[:, jt, :]) for jt in range(2)]
                self.attn_tiles(psS, PT, pOc, qq[:, 0, :], tiles, extra=psI)
                o("dve", "tensor_scalar", out=rcc[:, 0:4], in0=pOc[:, :, 64], scalar1=1e-30, scalar2=None, op0=ALU.max)
                o("dve", "reciprocal", out=rcc[:, 0:4], in_=rcc[:, 0:4])
                o("dve", "tensor_tensor", out=impw.v, in0=psI.v, in1=rcc[:, 0:4].us(2).bc([128, 4, 64]), op=ALU.mult)
                o("dve", "tensor_reduce", out=impa.v, in_=impw.v.rr("p h n -> p n h"), axis=AX.X, op=ALU.add)
                o("dve", "tensor_tensor", out=impa.v, in0=impa.v, in1=fb.v, op=ALU.add)
                o("dve", "max", out=m8[:, 0:8], in_=impa.v)
                o("dve", "match_replace", out=zz.v, in_to_replace=m8[:, 0:8], in_values=impa.v, imm_value=-3.0e6)
                o("dve", "max", out=m8[:, 8:16], in_=zz.v)
                o("dve", "tensor_scalar", out=mbb.v, in0=impa.v, scalar1=m8[:, 15:16], scalar2=NEG, op0=ALU.is_lt, op1=ALU.mult)
                nb = S // 64
                o("dve", "tensor_copy", out=mbt[:, 0:S].rr("p (b r) -> p b r", r=64), in_=mbb[:, 0:nb].us(2).bc([128, nb, 64]))
                o("dve", "tensor_tensor", out=mbt[:, r0:r0 + 128], in0=mbt[:, r0:r0 + 128], in1=self.cbb.v, op=ALU.add)

            def st_swa(i):
                qq = q2[i % 2]
                tiles = []
                for kt in range(max(0, i - 4), i + 1):
                    mv = self.cbb.v if kt == i else (self.abb.v if kt == i - 4 else None)
                    tiles.append((kwT[:, kt * 128:(kt + 1) * 128], mv, vW[:, kt, :], None))
                self.attn_tiles(psS, PT, psOw, qq[:, 1, :], tiles)

            def st_sel(i):
                qq = q2[i % 2]
                mbt = mb[i % 2]
                tiles = [(ksT[:, kt * 128:(kt + 1) * 128], mbt[:, kt * 128:(kt + 1) * 128], vS[:, kt, :], None) for kt in range(i + 1)]
                self.attn_tiles(psS, PT, psOs, qq[:, 1, :], tiles)

            def st_comb(i):
                r0 = i * 128
                g = gt[i % 2]
                pOc = psOc[i % 2]
                rcc = rc[i % 2]
                o("dve", "reciprocal", out=rcc[:, 4:8], in_=psOs[:, :, 64])
                o("dve", "reciprocal", out=rcc[:, 8:12], in_=psOw[:, :, 64])
                o("dve", "tensor_tensor", out=rcc.v, in0=rcc.v, in1=g.v, op=ALU.mult)
                o("dve", "tensor_tensor", out=acc.v, in0=pOc[:, :, 0:64], in1=rcc[:, 0:4].us(2).bc([128, 4, 64]), op=ALU.mult)
                o("dve", "tensor_tensor", out=tmp.v, in0=psOs[:, :, 0:64], in1=rcc[:, 4:8].us(2).bc([128, 4, 64]), op=ALU.mult)
                o("pool", "tensor_tensor", out=acc.v, in0=acc.v, in1=tmp.v, op=ALU.add)
                o("dve", "tensor_tensor", out=tmp2.v, in0=psOw[:, :, 0:64], in1=rcc[:, 8:12].us(2).bc([128, 4, 64]), op=ALU.mult)
                obt = ob[i % 2]
                o("pool", "tensor_tensor", out=obt.v, in0=acc.v, in1=tmp2.v, op=ALU.add)
                P.dma(self.mixed.tile(i, self.mixed.ap[r0:r0 + 128, 512:768]), obt.v.rr("p h d -> p (h d)"))

            n_t = self.ntl
            st_cmp(0)
            for i in range(n_t):
                st_swa(i)
                if i + 1 < n_t:
                    st_cmp(i + 1)
                st_sel(i)
                st_comb(i)
            P.barrier()
        self.es = top


_CACHE = {}


def make_in_map(inputs, b, consts):
    m = {"x": np.ascontiguousarray(inputs["x"][b], dtype=np.float32),
         "mem": np.ascontiguousarray(inputs["mem"][b], dtype=np.float32)}
    for k in PARAM_SHAPES:
        m[k] = np.ascontiguousarray(inputs[k], dtype=np.float32)
    m.update(consts)
    return m


def kernel(**inputs):
    if "kb" not in _CACHE:
        kb = KB()
        kb.build()
        _CACHE["kb"] = kb
    kb = _CACHE["kb"]
    consts = host_consts()
    in_maps = [make_in_map(inputs, c % 4, consts) for c in range(8)]
    res = run_bass_kernel_spmd(kb.nc, in_maps, core_ids=list(range(8)))
    out = np.stack([np.asarray(res.results[c]["out"], dtype=np.float32) for c in range(4)], axis=0)
    return out
```
